# Optimizing a Trainium2 kernel written in Bass

```python
import jax, jax.numpy as jnp
from jax import lax
import numpy as np

D_MODEL = 1024
BATCH = 8
SEQ = 4096
DEPTH = 1

CHUNK = 64
D_MIX = D_MODEL
EPS = 1e-6
NEG_INF = -1e30

ATTN_WIDTH = D_MIX // 2
ATTN_HEAD_DIM = 64
ATTN_Q_HEADS = ATTN_WIDTH // ATTN_HEAD_DIM
ATTN_KV_HEADS = 2
ATTN_GROUP = ATTN_Q_HEADS // ATTN_KV_HEADS
WINDOW = 128
WINDOW_CHUNKS = WINDOW // CHUNK
ROPE_THETA = 10000.0

HGRN_WIDTH = D_MIX - ATTN_WIDTH
HGRN_EXPAND = 128
HGRN_HEADS = HGRN_WIDTH // HGRN_EXPAND
HGRN_KEY_DIM = HGRN_EXPAND
HGRN_VAL_DIM = HGRN_WIDTH // HGRN_HEADS

D_FF = 2816
FFN_RES_WEIGHT = 0.5

N_SUBLAYERS = 3
N_MOD = 3 * N_SUBLAYERS

PROJ_WIDTHS = (ATTN_WIDTH, ATTN_KV_HEADS * ATTN_HEAD_DIM, ATTN_KV_HEADS * ATTN_HEAD_DIM,
               HGRN_WIDTH, HGRN_WIDTH, HGRN_WIDTH, HGRN_WIDTH)
PROJ_OFFSETS = tuple(int(o) for o in np.cumsum(PROJ_WIDTHS)[:-1])
D_PROJ = int(sum(PROJ_WIDTHS))

kernel_name = "hybrid_swa_sink_hgrn2_macaron_block"


def _rmsnorm(x, gain):
    xf = x.astype(jnp.float32)
    inv = lax.rsqrt(jnp.mean(xf * xf, axis=-1, keepdims=True) + EPS)
    return (xf * inv * gain.astype(jnp.float32)).astype(x.dtype)


def _rope_tables(positions):
    inv_freq = 1.0 / (ROPE_THETA ** (jnp.arange(0, ATTN_HEAD_DIM, 2, dtype=jnp.float32) / ATTN_HEAD_DIM))
    ang = positions.astype(jnp.float32)[..., None] * inv_freq
    return jnp.cos(ang)[:, :, None, :], jnp.sin(ang)[:, :, None, :]


def _apply_rope(t, cos, sin):
    tf = t.astype(jnp.float32)
    t1, t2 = jnp.split(tf, 2, axis=-1)
    return jnp.concatenate([t1 * cos - t2 * sin, t2 * cos + t1 * sin], axis=-1).astype(t.dtype)


def _band(t):
    B, S, H, d = t.shape
    n_chunks = S // CHUNK
    tp = jnp.pad(t, ((0, 0), (WINDOW_CHUNKS * CHUNK, 0), (0, 0), (0, 0)))
    tp = tp.reshape(B, n_chunks + WINDOW_CHUNKS, CHUNK, H, d)
    return jnp.concatenate([tp[:, j:j + n_chunks] for j in range(WINDOW_CHUNKS + 1)], axis=2)


def _sliding_window_attention(q, k, v, sinks):
    B, S = q.shape[:2]
    n_chunks = S // CHUNK
    band = (WINDOW_CHUNKS + 1) * CHUNK
    qb = q.reshape(B, n_chunks, CHUNK, ATTN_KV_HEADS, ATTN_GROUP, ATTN_HEAD_DIM)
    kb = _band(k)
    vb = _band(v)
    scores = jnp.einsum("bnqhgd,bnkhd->bnhgqk", qb, kb,
                        preferred_element_type=jnp.float32) * (ATTN_HEAD_DIM ** -0.5)
    key_pos = (jnp.arange(n_chunks)[:, None] - WINDOW_CHUNKS) * CHUNK + jnp.arange(band)[None, :]
    valid = (key_pos >= 0)[None, :, None, None, None, :]
    scores = jnp.where(valid, scores, NEG_INF)
    sink_col = jnp.broadcast_to(
        sinks.astype(jnp.float32).reshape(1, 1, ATTN_KV_HEADS, ATTN_GROUP, 1, 1),
        scores.shape[:-1] + (1,))
    probs = jax.nn.softmax(jnp.concatenate([scores, sink_col], axis=-1), axis=-1)[..., :-1]
    out = jnp.einsum("bnhgqk,bnkhd->bnqhgd", probs.astype(v.dtype), vb)
    return out.reshape(B, S, ATTN_WIDTH)


def _hgrn2(hq, hf, hi, hg, lower_bound, gnorm_gain):
    B, S = hq.shape[:2]
    n_chunks = S // CHUNK
    q = jax.nn.silu(hq.astype(jnp.float32)) * (HGRN_KEY_DIM ** -0.5)
    f = lower_bound + (1.0 - lower_bound) * jax.nn.sigmoid(hf.astype(jnp.float32))
    log_f = jnp.log(f)
    k = 1.0 - f
    v = hi.astype(jnp.float32)

    def to_chunks(t, d):
        return t.reshape(B, n_chunks, CHUNK, HGRN_HEADS, d).transpose(1, 0, 3, 2, 4)

    qc = to_chunks(q, HGRN_KEY_DIM)
    kc = to_chunks(k, HGRN_KEY_DIM)
    gc = to_chunks(log_f, HGRN_KEY_DIM)
    vc = to_chunks(v, HGRN_VAL_DIM)
    causal = jnp.tril(jnp.ones((CHUNK, CHUNK), dtype=bool))[:, :, None]

    def step(state, inp):
        q_c, k_c, g_c, v_c = inp
        b = jnp.cumsum(g_c, axis=2)
        o_inter = jnp.einsum("bhtd,bhde->bhte", q_c * jnp.exp(b), state)
        diff = b[:, :, :, None, :] - b[:, :, None, :, :]
        decay = jnp.exp(jnp.where(causal, diff, -jnp.inf))
        scores = jnp.einsum("bhtd,bhsd,bhtsd->bhts", q_c, k_c, decay)
        o_intra = jnp.einsum("bhts,bhse->bhte", scores, v_c)
        b_last = b[:, :, -1:, :]
        new_state = (jnp.exp(b_last[:, :, 0, :])[..., None] * state
                     + jnp.einsum("bhsd,bhse->bhde", k_c * jnp.exp(b_last - b), v_c))
        return new_state, o_inter + o_intra

    state0 = jnp.zeros((B, HGRN_HEADS, HGRN_KEY_DIM, HGRN_VAL_DIM), jnp.float32)
    _, oc = lax.scan(step, state0, (qc, kc, gc, vc))
    o = oc.transpose(1, 0, 3, 2, 4).reshape(B, S, HGRN_HEADS, HGRN_VAL_DIM)
    o = o * lax.rsqrt(jnp.mean(o * o, axis=-1, keepdims=True) + EPS) * gnorm_gain.astype(jnp.float32)
    g = jax.nn.silu(hg.astype(jnp.float32)).reshape(B, S, HGRN_HEADS, HGRN_VAL_DIM)
    return (o * g).reshape(B, S, HGRN_WIDTH).astype(hq.dtype)


def _swiglu(h, w_in, w_out):
    gate, up = jnp.split(h @ w_in, 2, axis=-1)
    return (jax.nn.silu(gate) * up) @ w_out


def _token_mixer(h, w_in, w_out, sinks, lower_bound, gnorm_gain, cos, sin):
    B, S = h.shape[:2]
    aq, ak, av, hq, hf, hi, hg = jnp.split(h @ w_in, PROJ_OFFSETS, axis=-1)
    aq = _apply_rope(aq.reshape(B, S, ATTN_Q_HEADS, ATTN_HEAD_DIM), cos, sin)
    ak = _apply_rope(ak.reshape(B, S, ATTN_KV_HEADS, ATTN_HEAD_DIM), cos, sin)
    av = av.reshape(B, S, ATTN_KV_HEADS, ATTN_HEAD_DIM)
    attn = _sliding_window_attention(aq, ak, av, sinks)
    rec = _hgrn2(hq, hf, hi, hg, lower_bound, gnorm_gain)
    return jnp.concatenate([attn, rec], axis=-1) @ w_out


def setup_inputs(seed: int = 0) -> dict:
    key = jax.random.key(seed)
    ks = jax.random.split(key, 16)
    f32 = jnp.float32
    x = jax.random.normal(ks[0], (BATCH, SEQ, D_MODEL), f32)
    c = jax.random.normal(ks[1], (BATCH, D_MODEL), f32)
    offset = jax.random.randint(ks[2], (BATCH, 1), 0, 64, dtype=jnp.int32) * CHUNK
    positions = (offset + jnp.arange(SEQ, dtype=jnp.int32)[None, :]).astype(jnp.int32)
    w_cond = jax.random.normal(ks[3], (DEPTH, D_MODEL, N_MOD * D_MODEL), f32) * (0.5 * D_MODEL ** -0.5)
    b_cond = jax.random.normal(ks[4], (DEPTH, N_MOD * D_MODEL), f32) * 0.01
    norm_pre = 1.0 + 0.02 * jax.random.normal(ks[5], (DEPTH, N_SUBLAYERS, D_MODEL), f32)
    norm_post = 1.0 + 0.02 * jax.random.normal(ks[6], (DEPTH, N_SUBLAYERS, D_MODEL), f32)
    ffn_w_in = jax.random.normal(ks[7], (DEPTH, 2, D_MODEL, 2 * D_FF), f32) * (D_MODEL ** -0.5)
    ffn_w_out = jax.random.normal(ks[8], (DEPTH, 2, D_FF, D_MODEL), f32) * (D_FF ** -0.5)
    w_mix_in = jax.random.normal(ks[9], (DEPTH, D_MODEL, D_PROJ), f32) * (D_MODEL ** -0.5)
    w_mix_out = jax.random.normal(ks[10], (DEPTH, D_MIX, D_MODEL), f32) * (D_MIX ** -0.5)
    attn_sinks = jax.random.normal(ks[11], (DEPTH, ATTN_Q_HEADS), f32)
    hgrn_lb_logits = 1.0 + 0.1 * jax.random.normal(ks[12], (DEPTH + 1, HGRN_WIDTH), f32)
    hgrn_gnorm = 1.0 + 0.02 * jax.random.normal(ks[13], (DEPTH, HGRN_VAL_DIM), f32)
    return {"x": x, "c": c, "positions": positions, "w_cond": w_cond, "b_cond": b_cond,
            "norm_pre": norm_pre, "norm_post": norm_post, "ffn_w_in": ffn_w_in,
            "ffn_w_out": ffn_w_out, "w_mix_in": w_mix_in, "w_mix_out": w_mix_out,
            "attn_sinks": attn_sinks, "hgrn_lb_logits": hgrn_lb_logits, "hgrn_gnorm": hgrn_gnorm}


def reference(x, c, positions, w_cond, b_cond, norm_pre, norm_post, ffn_w_in, ffn_w_out,
              w_mix_in, w_mix_out, attn_sinks, hgrn_lb_logits, hgrn_gnorm):
    cos, sin = _rope_tables(positions)
    lower_bounds = jnp.cumsum(jax.nn.softmax(hgrn_lb_logits.astype(jnp.float32), axis=0), axis=0)
    c_act = jax.nn.silu(c)
    for layer in range(DEPTH):
        mod = (c_act @ w_cond[layer] + b_cond[layer])[:, None, :]
        sh1, sc1, gt1, sh2, sc2, gt2, sh3, sc3, gt3 = jnp.split(mod, N_MOD, axis=-1)
        h = _rmsnorm(x, norm_pre[layer, 0]) * (1.0 + sc1) + sh1
        y = _swiglu(h, ffn_w_in[layer, 0], ffn_w_out[layer, 0])
        x = x + FFN_RES_WEIGHT * gt1 * _rmsnorm(y, norm_post[layer, 0])
        h = _rmsnorm(x, norm_pre[layer, 1]) * (1.0 + sc2) + sh2
        y = _token_mixer(h, w_mix_in[layer], w_mix_out[layer], attn_sinks[layer],
                         lower_bounds[layer], hgrn_gnorm[layer], cos, sin)
        x = x + gt2 * _rmsnorm(y, norm_post[layer, 1])
        h = _rmsnorm(x, norm_pre[layer, 2]) * (1.0 + sc3) + sh3
        y = _swiglu(h, ffn_w_in[layer, 1], ffn_w_out[layer, 1])
        x = x + FFN_RES_WEIGHT * gt3 * _rmsnorm(y, norm_post[layer, 2])
    return x
```

```python
import numpy as np
from contextlib import ExitStack
import ml_dtypes
import concourse.bass as bass
import concourse.mybir as mybir
from concourse.bass_utils import run_bass_kernel_spmd

F32 = mybir.dt.float32
BF16 = mybir.dt.bfloat16
I32 = mybir.dt.int32
AF = mybir.ActivationFunctionType
ALU = mybir.AluOpType
AX = mybir.AxisListType

S = 4096
D = 1024
DFF = 2816
NF = DFF // 128
TB = 512
NB = S // TB
NT = S // 128
EPS = 1e-6
PI = float(np.pi)
TWO_PI = float(2 * np.pi)


class Tk:
    __slots__ = ("w", "r", "name", "excl")

    def __init__(self, name="", excl=False):
        self.w = None
        self.r = {}
        self.name = name
        self.excl = excl


class Sem:
    def __init__(self, h, name):
        self.h = h
        self.cnt = 0
        self.name = name


class Eng:
    def __init__(self, name, h, sem):
        self.name = name
        self.h = h
        self.sem = sem
        self.seen = {}


class Kern:
    def __init__(self, nc, es):
        self.nc = nc
        self.es = es
        self.sems = []
        self.PE = Eng("pe", nc.tensor, self.newsem("pe"))
        self.ACT = Eng("act", nc.scalar, self.newsem("act"))
        self.DVE = Eng("dve", nc.vector, self.newsem("dve"))
        self.POOL = Eng("pool", nc.gpsimd, self.newsem("pool"))
        self.SP = Eng("sp", nc.sync, self.newsem("sp"))
        self.engs = [self.PE, self.ACT, self.DVE, self.POOL, self.SP]
        self.n_inst = 0

    def newsem(self, name):
        self.sid = getattr(self, "sid", 0) + 1
        s = Sem(self.es.enter_context(self.nc.semaphore("m%d_%s" % (self.sid, name))), name)
        self.sems.append(s)
        return s

    def sb(self, name, shape, dt, es=None):
        self.uid = getattr(self, "uid", 0) + 1
        return (es or self.es).enter_context(self.nc.sbuf_tensor("s%d_%s" % (self.uid, name), shape, dt))

    def _wait(self, E, toks):
        for (s, v) in toks:
            if E.seen.get(id(s), 0) >= v:
                continue
            E.h.wait_ge(s.h, v)
            E.seen[id(s)] = v

    def _deps(self, E, reads, writes):
        toks = []
        for t in reads:
            if t.w is not None:
                toks.append(t.w)
            if t.excl:
                for tok in t.r.values():
                    if tok[0] is not E.sem:
                        toks.append(tok)
        for t in writes:
            if t.w is not None and t.w[0] is not E.sem:
                toks.append(t.w)
            for tok in t.r.values():
                if tok[0] is not E.sem:
                    toks.append(tok)
        return toks

    def _mark(self, tok, reads, writes):
        for t in writes:
            t.w = tok
            t.r = {}
        for t in reads:
            t.r[id(tok[0])] = tok

    def op(self, E, fn, reads=(), writes=()):
        self._wait(E, self._deps(E, reads, writes))
        ins = fn()
        ins.then_inc(E.sem.h, 1)
        E.sem.cnt += 1
        self._mark((E.sem, E.sem.cnt), reads, writes)
        self.n_inst += 1
        return ins

    def group(self, E, fns, reads=(), writes=()):
        self._wait(E, self._deps(E, reads, writes))
        ins = None
        for fn in fns:
            ins = fn()
            self.n_inst += 1
        ins.then_inc(E.sem.h, 1)
        E.sem.cnt += 1
        self._mark((E.sem, E.sem.cnt), reads, writes)

    def dma(self, E, out, in_, reads, writes, sem, **kw):
        self._wait(E, self._deps(E, reads, writes))
        E.h.dma_start(out=out, in_=in_, **kw).then_inc(sem.h, 16)
        sem.cnt += 16
        self._mark((sem, sem.cnt), reads, writes)
        self.n_inst += 1

    def barrier(self):
        toks = [(s, s.cnt) for s in self.sems if s.cnt > 0]
        for E in self.engs:
            self._wait(E, [t for t in toks if t[0] is not E.sem])


class Psum:
    def __init__(self, K):
        nc = K.nc
        self.t = K.es.enter_context(nc.psum_tensor("psum_all", [128, 8 * 512], F32))
        self.tb = self.t.bitcast(BF16)
        self.tk = [Tk("bank%d" % i, excl=True) for i in range(8)]
        self.p = 0
        self.held = [False] * 8

    def try_alloc(self, n=1):
        for i in range(8):
            b = (self.p + i) % 8
            if b % n or b + n > 8:
                continue
            if any(self.held[b:b + n]):
                continue
            for j in range(b, b + n):
                self.held[j] = True
            self.p = (b + n) % 8
            return b, self.tk[b:b + n]
        return None

    def alloc(self, n=1):
        r = self.try_alloc(n)
        assert r is not None, "PSUM exhausted (straight-line code must free before allocating)"
        return r

    def free(self, b, n=1):
        for j in range(b, b + n):
            assert self.held[j]
            self.held[j] = False

    def f32(self, b, n=1):
        return self.t[:, b * 512:(b + n) * 512]

    def bf(self, b, n=1):
        return self.tb[:, b * 1024:(b + n) * 1024]


class Rot:
    def __init__(self, K, es, name, shape, dt, n, dma=False):
        self.items = []
        for i in range(n):
            t = K.sb("%s%d" % (name, i), shape, dt, es)
            self.items.append((t, Tk("%s%d" % (name, i)), K.newsem("%s%d" % (name, i)) if dma else None))
        self.i = 0

    def next(self):
        it = self.items[self.i % len(self.items)]
        self.i += 1
        return it


def build(stop_after=3):
    nc = bass.Bass("TRN2", target_bir_lowering=False)

    def din(name, shape, dt=F32):
        return nc.dram_tensor(name, shape, dt, kind="ExternalInput").ap()

    x_d = din("x", [S, D])
    ccol_d = din("ccol", [128, 8])
    pos_d = din("pos", [128, NT], I32)
    wcond_d = din("w_cond", [D, 9 * D])
    bcond_d = din("b_cond", [128, 9 * D])
    npre_d = din("npre", [128, 3, 8])
    npost_d = din("npost", [128, 3, D])
    win_d = din("w_in", [2, D, 2 * DFF])
    wout_d = din("w_out", [2, DFF, D])
    wmi_d = din("w_mi", [D, 2816])
    wmo_d = din("w_mo", [D, D])
    sinks_d = din("sinks", [128, 8])
    lbl_d = din("lbl", [128, 2, 4])
    gn_d = din("gn", [128, 1])
    ident_d = din("ident", [128, 128], BF16)
    onesb_d = din("onesb", [128, 128], BF16)
    amask_d = din("amask", [128, 2, 2, 512], BF16)
    hmask_d = din("hmask", [128, 4, 128], BF16)
    rmask_d = din("rmask", [128, 512])
    invf_d = din("invf", [128, 32])
    rowm_d = din("rowm", [128, 2])
    out_d = nc.dram_tensor("out", [S, D], F32, kind="ExternalOutput").ap()
    x1_d = nc.dram_tensor("x1s", [S, D], F32).ap()
    x2_d = nc.dram_tensor("x2s", [S, D], F32).ap()
    rope_d = nc.dram_tensor("rope_s", [128, 2, NT * 32], F32).ap()
    w2in_s = nc.dram_tensor("w2in_s", [D, 2 * DFF], BF16).ap()
    w2out_s = nc.dram_tensor("w2out_s", [DFF, D], BF16).ap()
    wmi_s = nc.dram_tensor("wmi_s", [D, 2816], BF16).ap()
    wmo_s = nc.dram_tensor("wmo_s", [D, D], BF16).ap()

    with ExitStack() as es:
        K = Kern(nc, es)
        PE, ACT, DVE, POOL, SP = K.PE, K.ACT, K.DVE, K.POOL, K.SP
        ps = Psum(K)
        dram_tk = {}

        def dtk(name, tile):
            key = (name, tile)
            if key not in dram_tk:
                dram_tk[key] = Tk("%s_%d" % key)
            return dram_tk[key]

        ident = K.sb("ident", [128, 128], BF16)
        onesb = K.sb("onesb", [128, 128], BF16)
        rowm = K.sb("rowm", [128, 2], F32)
        sinks = K.sb("sinks", [128, 8], F32)
        gn = K.sb("gn", [128, 1], F32)
        lb = K.sb("lb", [128, 4], F32)
        oml = K.sb("oml", [128, 4], F32)
        a_col = K.sb("a_col", [128, 3, 8], F32)
        sh_col = K.sb("sh_col", [128, 3, 8], F32)
        G = K.sb("G", [128, 3, D], BF16)
        c01 = K.sb("c01", [128, 2, 4], F32)
        gnh = K.sb("gnh", [128, 1], F32)
        const_tk = Tk("consts")
        par_tk = Tk("params")
        rope_tk = Tk("rope")

        with ExitStack() as ss:
            ld = K.newsem("setup_ld")
            ccol = K.sb("ccol", [128, 8], F32, ss)
            ca = K.sb("ca", [128, 8], F32, ss)
            posi = K.sb("posi", [128, NT], I32, ss)
            npre = K.sb("npre", [128, 3, 8], F32, ss)
            npost = K.sb("npost", [128, 3, D], F32, ss)
            lbl = K.sb("lbl", [128, 2, 4], F32, ss)
            invf = K.sb("invf", [128, 32], F32, ss)
            setup_tk = Tk("setup_in")
            cosT = K.sb("cosT", [128, NT, 32], F32, ss)
            sinT = K.sb("sinT", [128, NT, 32], F32, ss)
            loads = [(ident, ident_d), (onesb, onesb_d),
                     (rowm, rowm_d), (sinks, sinks_d), (gn, gn_d), (ccol, ccol_d), (posi, pos_d), (npre, npre_d),
                     (npost, npost_d), (lbl, lbl_d), (invf, invf_d)]
            for (t, d_) in loads:
                nc.sync.dma_start(out=t[:], in_=d_).then_inc(ld.h, 16)
                ld.cnt += 16
            setup_tk.w = (ld, ld.cnt)
            const_tk.w = (ld, ld.cnt)

            K.op(ACT, lambda: nc.scalar.activation(out=ca[:], in_=ccol[:], func=AF.Silu), reads=[setup_tk], writes=[par_tk])
            K.op(DVE, lambda: nc.vector.tensor_tensor(out=lb[:], in0=lbl[:, 0, :], in1=lbl[:, 1, :], op=ALU.subtract),
                 reads=[setup_tk], writes=[par_tk])
            K.op(ACT, lambda: nc.scalar.activation(out=lb[:], in_=lb[:], func=AF.Sigmoid), reads=[par_tk], writes=[par_tk])
            K.op(DVE, lambda: nc.vector.tensor_scalar(out=oml[:], in0=lb[:], scalar1=-1.0, scalar2=1.0, op0=ALU.mult, op1=ALU.add),
                 reads=[par_tk], writes=[par_tk])
            K.op(DVE, lambda: nc.vector.tensor_scalar(out=c01[:, 1, :], in0=oml[:], scalar1=0.5, scalar2=None, op0=ALU.mult),
                 reads=[par_tk], writes=[par_tk])
            K.op(DVE, lambda: nc.vector.tensor_tensor(out=c01[:, 0, :], in0=lb[:], in1=c01[:, 1, :], op=ALU.add),
                 reads=[par_tk], writes=[par_tk])
            K.op(DVE, lambda: nc.vector.tensor_scalar(out=gnh[:], in0=gn[:], scalar1=0.5, scalar2=None, op0=ALU.mult),
                 reads=[setup_tk, par_tk], writes=[par_tk])

            cab = K.sb("cab", [128, 8, 128], F32, ss)
            identf = K.sb("identf", [128, 128], F32, ss)
            modbc = K.sb("modbc", [128, 9 * D], F32, ss)
            dg = K.sb("dg", [128, 8, 128], F32, ss)
            K.op(DVE, lambda: nc.vector.tensor_copy(out=cab[:], in_=ca[:].unsqueeze(2).broadcast_to([128, 8, 128])),
                 reads=[par_tk], writes=[par_tk])
            K.op(DVE, lambda: nc.vector.tensor_copy(out=identf[:], in_=ident[:]), reads=[setup_tk], writes=[setup_tk])
            wc_rot = Rot(K, ss, "wc", [128, 8, 512], F32, 2, dma=True)
            bc_rot = Rot(K, ss, "bc", [128, 512], F32, 2, dma=True)
            wc_view = wcond_d.rearrange("(k p) n -> p k n", p=128)
            mod_tk = Tk("modbc")
            for cb in range(18):
                wc, wc_tk, wc_sem = wc_rot.next()
                bc, bc_tk, bc_sem = bc_rot.next()
                K.dma(SP, out=wc[:], in_=wc_view[:, :, cb * 512:(cb + 1) * 512], reads=[], writes=[wc_tk], sem=wc_sem)
                K.dma(SP, out=bc[:], in_=bcond_d[:, cb * 512:(cb + 1) * 512], reads=[], writes=[bc_tk], sem=bc_sem)
                b, btk = ps.alloc(1)
                pv = ps.f32(b)
                K.group(PE, [(lambda k=k: nc.tensor.matmul(pv, lhsT=cab[:, k, :], rhs=wc[:, k, :],
                                                         start=(k == 0), stop=(k == 7))) for k in range(8)],
                        reads=[par_tk, wc_tk], writes=btk)
                K.op(DVE, lambda: nc.vector.tensor_tensor(out=modbc[:, cb * 512:(cb + 1) * 512], in0=pv, in1=bc[:], op=ALU.add),
                     reads=btk + [bc_tk, mod_tk], writes=[mod_tk])
                ps.free(b)
            for k in range(3):
                for which, v in (("sh", 3 * k), ("sc", 3 * k + 1)):
                    K.op(DVE, lambda: nc.vector.tensor_tensor(out=dg[:], in0=modbc[:, v * D:(v + 1) * D].rearrange("p (j q) -> p j q", q=128),
                                                              in1=identf[:].unsqueeze(1).broadcast_to([128, 8, 128]), op=ALU.mult),
                         reads=[mod_tk, setup_tk, par_tk], writes=[par_tk])
                    dstc = sh_col if which == "sh" else a_col
                    K.op(DVE, lambda: nc.vector.tensor_reduce(out=dstc[:, k, :], in_=dg[:], axis=AX.X, op=ALU.add),
                         reads=[par_tk], writes=[par_tk])
                K.op(DVE, lambda: nc.vector.scalar_tensor_tensor(out=a_col[:, k, :], in0=a_col[:, k, :], scalar=1.0, in1=npre[:, k, :],
                                                                 op0=ALU.add, op1=ALU.mult),
                     reads=[par_tk, setup_tk], writes=[par_tk])
                resw = 1.0 if k == 1 else 0.5
                K.op(DVE, lambda: nc.vector.scalar_tensor_tensor(out=G[:, k, :], in0=modbc[:, (3 * k + 2) * D:(3 * k + 3) * D], scalar=resw,
                                                                 in1=npost[:, k, :], op0=ALU.mult, op1=ALU.mult),
                     reads=[mod_tk, setup_tk, par_tk], writes=[par_tk])

            posf = K.sb("posf", [128, NT], F32, ss)
            ang = K.sb("ang", [128, NT, 32], F32, ss)
            rr = K.sb("rr", [128, NT, 32], F32, ss)
            rc = K.sb("rc", [128, NT, 32], F32, ss)
            kf = K.sb("kf", [128, NT, 32], F32, ss)
            ki = K.sb("ki", [128, NT, 32], I32, ss)
            mk = K.sb("mk", [128, NT, 32], F32, ss)
            rt = Tk("ropetmp")

            def V(fn, reads=(), writes=()):
                K.op(DVE, fn, reads=list(reads) + [rt, setup_tk], writes=list(writes) + [rt])

            V(lambda: nc.vector.tensor_copy(out=posf[:], in_=posi[:]))
            V(lambda: nc.vector.tensor_tensor(out=ang[:], in0=posf[:].unsqueeze(2).broadcast_to([128, NT, 32]),
                                              in1=invf[:].unsqueeze(1).broadcast_to([128, NT, 32]), op=ALU.mult))
            V(lambda: nc.vector.tensor_scalar(out=kf[:], in0=ang[:], scalar1=1.0 / TWO_PI, scalar2=None, op0=ALU.mult))
            V(lambda: nc.vector.tensor_copy(out=ki[:], in_=kf[:]))
            V(lambda: nc.vector.tensor_copy(out=kf[:], in_=ki[:]))
            C1 = 6.28125
            C2 = TWO_PI - 6.28125
            V(lambda: nc.vector.scalar_tensor_tensor(out=rr[:], in0=kf[:], scalar=-C1, in1=ang[:], op0=ALU.mult, op1=ALU.add))
            V(lambda: nc.vector.scalar_tensor_tensor(out=rr[:], in0=kf[:], scalar=-C2, in1=rr[:], op0=ALU.mult, op1=ALU.add))

            def fold(t):
                V(lambda: nc.vector.tensor_scalar(out=mk[:], in0=t[:], scalar1=PI, scalar2=-TWO_PI, op0=ALU.is_gt, op1=ALU.mult))
                V(lambda: nc.vector.tensor_tensor(out=t[:], in0=t[:], in1=mk[:], op=ALU.add))
                V(lambda: nc.vector.tensor_scalar(out=mk[:], in0=t[:], scalar1=-PI, scalar2=TWO_PI, op0=ALU.is_lt, op1=ALU.mult))
                V(lambda: nc.vector.tensor_tensor(out=t[:], in0=t[:], in1=mk[:], op=ALU.add))
                V(lambda: nc.vector.tensor_scalar(out=t[:], in0=t[:], scalar1=PI, scalar2=-PI, op0=ALU.min, op1=ALU.max))

            fold(rr)
            V(lambda: nc.vector.tensor_scalar(out=rc[:], in0=rr[:], scalar1=PI / 2, scalar2=None, op0=ALU.add))
            fold(rc)
            K.op(ACT, lambda: nc.scalar.activation(out=sinT[:], in_=rr[:], func=AF.Sin), reads=[rt], writes=[rope_tk])
            K.op(ACT, lambda: nc.scalar.activation(out=cosT[:], in_=rc[:], func=AF.Sin), reads=[rt], writes=[rope_tk])
            rsem = K.newsem("ropest")
            K.dma(SP, out=rope_d[:, 0, :], in_=cosT[:].rearrange("p t f -> p (t f)"), reads=[rope_tk], writes=[dtk("rope", 0)], sem=rsem)
            K.dma(SP, out=rope_d[:, 1, :], in_=sinT[:].rearrange("p t f -> p (t f)"), reads=[rope_tk], writes=[dtk("rope", 0)], sem=rsem)
            K.barrier()

        def issue_cast_dma(dst, src, tk, sem):
            K.dma(POOL, out=dst, in_=src, reads=[], writes=[tk], sem=sem)

        eps_t = K.sb("eps_t", [128, 2], F32)
        const2_tk = Tk("const2")
        K.op(DVE, lambda: nc.vector.memset(eps_t[:, 0:1], EPS), writes=[const2_tk])
        K.op(DVE, lambda: nc.vector.memset(eps_t[:, 1:2], 0.0), writes=[const2_tk])

        def rstd_ops(st, st_tk, n, lnexp):
            if lnexp:
                K.op(ACT, lambda: nc.scalar.activation(out=st[:, 1:2], in_=st[:, 0:1], func=AF.Ln, scale=1.0 / n, bias=eps_t[:, 0:1]),
                     reads=[st_tk, const2_tk], writes=[st_tk])
                K.op(ACT, lambda: nc.scalar.activation(out=st[:, 2:3], in_=st[:, 1:2], func=AF.Exp, scale=-0.5),
                     reads=[st_tk], writes=[st_tk])
            else:
                K.op(ACT, lambda: nc.scalar.activation(out=st[:, 1:2], in_=st[:, 0:1], func=AF.Sqrt, scale=1.0 / n, bias=eps_t[:, 0:1]),
                     reads=[st_tk, const2_tk], writes=[st_tk])
                K.op(DVE, lambda: nc.vector.reciprocal(out=st[:, 2:3], in_=st[:, 1:2]), reads=[st_tk], writes=[st_tk])

        class Pro:
            def __init__(self, es_, src, srcname, kidx, lnexp, tag, nxin=2):
                self.src, self.srcname, self.kidx, self.lnexp = src, srcname, kidx, lnexp
                self.xin_rot = Rot(K, es_, tag + "xin", [128, D], F32, nxin, dma=True)
                self.xn_rot = Rot(K, es_, tag + "xn", [128, D], BF16, 2)
                self.st_rot = Rot(K, es_, tag + "st", [128, 4], F32, 4)
                self.tmpf = K.sb(tag + "tmpf", [128, 8, 128], F32, es_)
                self.tmpf_tk = Tk(tag + "tmpf")

            def elem(self, tile):
                xin, xin_tk, xin_sem = self.xin_rot.next()
                K.dma(SP, out=xin[:], in_=self.src[tile * 128:(tile + 1) * 128, :], reads=[dtk(self.srcname, tile)],
                      writes=[xin_tk], sem=xin_sem)
                st, st_tk, _ = self.st_rot.next()
                xn, xn_tk, _ = self.xn_rot.next()
                K.op(ACT, lambda: nc.scalar.activation(out=xn[:], in_=xin[:], func=AF.Square, accum_out=st[:, 0:1]),
                     reads=[xin_tk], writes=[xn_tk, st_tk])
                rstd_ops(st, st_tk, D, self.lnexp)
                K.op(DVE, lambda: nc.vector.tensor_scalar(out=xn[:], in0=xin[:], scalar1=st[:, 2:3], scalar2=None, op0=ALU.mult),
                     reads=[xin_tk, st_tk], writes=[xn_tk])
                return xn, xn_tk

            def pe(self, xn, xn_tk, h, h_tks, tt):
                b, tks = ps.alloc(1)
                tv = ps.bf(b)
                K.group(PE, [(lambda dc=dc: nc.tensor.transpose(tv[:, dc * 128:(dc + 1) * 128], xn[:, dc * 128:(dc + 1) * 128], ident[:]))
                             for dc in range(8)], reads=[xn_tk, const_tk], writes=tks)
                kidx = self.kidx
                K.op(DVE, lambda: nc.vector.tensor_tensor(out=self.tmpf[:], in0=tv.rearrange("p (c t) -> p c t", t=128),
                                                          in1=a_col[:, kidx, :].unsqueeze(2).broadcast_to([128, 8, 128]), op=ALU.mult),
                     reads=tks + [par_tk], writes=[self.tmpf_tk])
                ps.free(b)
                K.op(DVE, lambda: nc.vector.tensor_tensor(out=h[:, :, tt * 128:(tt + 1) * 128], in0=self.tmpf[:],
                                                          in1=sh_col[:, kidx, :].unsqueeze(2).broadcast_to([128, 8, 128]), op=ALU.add),
                     reads=[self.tmpf_tk, par_tk], writes=h_tks)

        class Epi:
            def __init__(self, es_, src, srcname, dst, dstname, kidx, lnexp, tag, junk):
                self.src, self.srcname, self.dst, self.dstname, self.kidx, self.lnexp = src, srcname, dst, dstname, kidx, lnexp
                self.junk = junk
                self.xre_rot = Rot(K, es_, tag + "xre", [128, D], F32, 1, dma=True)
                self.ob_rot = Rot(K, es_, tag + "ob", [128, D], F32, 1, dma=True)
                self.st_rot = Rot(K, es_, tag + "est", [128, 4], F32, 4)

            def run(self, tile, yb, ytk):
                yv = ps.f32(yb, 2)
                kidx = self.kidx
                xre, xre_tk, xre_sem = self.xre_rot.next()
                K.dma(SP, out=xre[:], in_=self.src[tile * 128:(tile + 1) * 128, :], reads=[dtk(self.srcname, tile)], writes=[xre_tk],
                      sem=xre_sem)
                st, st_tk, _ = self.st_rot.next()
                ob, ob_tk, ob_sem = self.ob_rot.next()
                jk, jk_tk, _ = self.junk.next()
                K.op(ACT, lambda: nc.scalar.activation(out=jk[:], in_=yv, func=AF.Square, accum_out=st[:, 0:1]),
                     reads=ytk, writes=[jk_tk, st_tk])
                rstd_ops(st, st_tk, D, self.lnexp)
                K.op(DVE, lambda: nc.vector.scalar_tensor_tensor(out=ob[:], in0=yv, scalar=st[:, 2:3], in1=G[:, kidx, :],
                                                                 op0=ALU.mult, op1=ALU.mult),
                     reads=ytk + [st_tk, par_tk], writes=[ob_tk])
                ps.free(yb, 2)
                K.op(POOL, lambda: nc.gpsimd.tensor_tensor(out=ob[:], in0=ob[:], in1=xre[:], op=ALU.add),
                     reads=[ob_tk, xre_tk], writes=[ob_tk])
                K.dma(POOL, out=self.dst[tile * 128:(tile + 1) * 128, :], in_=ob[:], reads=[ob_tk], writes=[dtk(self.dstname, tile)],
                      sem=ob_sem)

        def interleave(*gens):
            gens = list(gens)
            while gens:
                for g in list(gens):
                    try:
                        next(g)
                    except StopIteration:
                        gens.remove(g)

        def ffn_phase(fi, kidx, src, srcname, dst, dstname):
            with ExitStack() as pe_:
                Win = K.sb("Win", [128, 8, 2 * DFF], BF16, pe_)
                Wout = K.sb("Wout", [128, NF, D], BF16, pe_)
                groups = [(0, 6), (6, 12), (12, 17), (17, 22)]
                win_tk = [Tk("win%d" % g) for g in range(4)]
                wout_tk = [Tk("wout%d" % g) for g in range(2)]
                if fi == 0:
                    wv = win_d[fi].rearrange("(k p) n -> p k n", p=128)
                    wov = wout_d[fi].rearrange("(f p) n -> p f n", p=128)
                    for g, (f0, f1) in enumerate(groups):
                        sem = K.newsem("win%d_%d" % (fi, g))
                        for off in (0, DFF):
                            issue_cast_dma(Win[:, :, off + f0 * 128:off + f1 * 128], wv[:, :, off + f0 * 128:off + f1 * 128],
                                           win_tk[g], sem)
                    for g, (f0, f1) in enumerate([(0, 11), (11, 22)]):
                        sem = K.newsem("wout%d_%d" % (fi, g))
                        issue_cast_dma(Wout[:, f0:f1, :], wov[:, f0:f1, :], wout_tk[g], sem)

                    def precast():
                        pcs = K.newsem("precast")
                        K.dma(POOL, out=wmi_s, in_=wmi_d, reads=[], writes=[dtk("wmi_s", 0)], sem=pcs)
                        K.dma(POOL, out=wmo_s, in_=wmo_d, reads=[], writes=[dtk("wmo_s", 0)], sem=pcs)
                        for r in range(4):
                            K.dma(POOL, out=w2in_s[r * 256:(r + 1) * 256, :], in_=win_d[1][r * 256:(r + 1) * 256, :], reads=[],
                                  writes=[dtk("w2in_s", 0)], sem=pcs)
                        for r in range(2):
                            K.dma(POOL, out=w2out_s[r * 1408:(r + 1) * 1408, :], in_=wout_d[1][r * 1408:(r + 1) * 1408, :], reads=[],
                                  writes=[dtk("w2out_s", 0)], sem=pcs)
                else:
                    wv = w2in_s.rearrange("(k p) n -> p k n", p=128)
                    wov = w2out_s.rearrange("(f p) n -> p f n", p=128)
                    qi = 0
                    for g, (f0, f1) in enumerate(groups):
                        sem = K.newsem("win%d_%d" % (fi, g))
                        for off in (0, DFF):
                            K.dma(ACT, out=Win[:, :, off + f0 * 128:off + f1 * 128],
                                  in_=wv[:, :, off + f0 * 128:off + f1 * 128], reads=[dtk("w2in_s", 0)], writes=[win_tk[g]], sem=sem)
                            qi += 1
                    for g, (f0, f1) in enumerate([(0, 11), (11, 22)]):
                        sem = K.newsem("wout%d_%d" % (fi, g))
                        K.dma(ACT, out=Wout[:, f0:f1, :], in_=wov[:, f0:f1, :], reads=[dtk("w2out_s", 0)],
                              writes=[wout_tk[g]], sem=sem)
                        qi += 1

                def gof(f):
                    for g, (f0, f1) in enumerate(groups):
                        if f0 <= f < f1:
                            return g

                pro = Pro(pe_, src, srcname, kidx, False, "f")
                epi = Epi(pe_, src, srcname, dst, dstname, kidx, False, "f", pro.xn_rot)
                hs = [K.sb("h%d" % i, [128, 8, TB], BF16, pe_) for i in range(2)]
                h_tkss = [[Tk("h%d" % i)] for i in range(2)]
                act = K.sb("act", [128, NF, TB], BF16, pe_)
                act_tk = [Tk("act%d" % f) for f in range(NF)]
                sg_rot = Rot(K, pe_, "sg", [128, TB], F32, 1)

                for tt in range(4):
                    xn, xn_tk = pro.elem(tt)
                    pro.pe(xn, xn_tk, hs[0], h_tkss[0], tt)
                for blk in range(NB):
                    if fi == 0 and blk == 2:
                        precast()
                    h = hs[blk % 2]
                    h_tks = h_tkss[blk % 2]
                    hn = hs[(blk + 1) % 2]
                    hn_tks = h_tkss[(blk + 1) % 2]
                    pend = {}
                    for f in range(NF):
                        bg, tg = ps.alloc(1)
                        bu, tu = ps.alloc(1)
                        pg = ps.f32(bg)
                        pu = ps.f32(bu)
                        fns = []
                        for k in range(8):
                            fns.append(lambda k=k: nc.tensor.matmul(pg, lhsT=Win[:, k, f * 128:(f + 1) * 128], rhs=h[:, k, :],
                                                                    start=(k == 0), stop=(k == 7)))
                        for k in range(8):
                            fns.append(lambda k=k: nc.tensor.matmul(pu, lhsT=Win[:, k, DFF + f * 128:DFF + (f + 1) * 128],
                                                                    rhs=h[:, k, :], start=(k == 0), stop=(k == 7)))
                        K.group(PE, fns, reads=h_tks + [win_tk[gof(f)]], writes=tg + tu)
                        sg, sg_tk, _ = sg_rot.next()
                        K.op(ACT, lambda: nc.scalar.activation(out=sg[:], in_=pg, func=AF.Silu), reads=tg, writes=[sg_tk])
                        K.op(DVE, lambda: nc.vector.tensor_tensor(out=act[:, f, :], in0=sg[:], in1=pu, op=ALU.mult),
                             reads=[sg_tk] + tu, writes=[act_tk[f]])
                        ps.free(bg)
                        ps.free(bu)
                        if blk + 1 < NB:
                            if f % 5 == 1 and f // 5 < 4:
                                tt = f // 5
                                pend[tt] = pro.elem((blk + 1) * 4 + tt)
                            if f % 5 == 4 and f // 5 < 4:
                                tt = f // 5
                                pro.pe(pend[tt][0], pend[tt][1], hn, hn_tks, tt)
                    for tt in range(4):
                        yb, ytk = ps.alloc(2)
                        fns = []
                        for n in range(2):
                            o = ps.f32(yb + n)
                            for f in range(NF):
                                fns.append(lambda o=o, n=n, f=f: nc.tensor.matmul(
                                    o, lhsT=act[:, f, tt * 128:(tt + 1) * 128], rhs=Wout[:, f, n * 512:(n + 1) * 512],
                                    start=(f == 0), stop=(f == NF - 1)))
                        K.group(PE, fns, reads=act_tk + wout_tk, writes=ytk)
                        epi.run(blk * 4 + tt, yb, ytk)
                K.barrier()

        def mixer_phase(src, srcname, dst, dstname):
            kidx = 1
            with ExitStack() as pe_:
                Wmi = K.sb("Wmi", [128, 8, 2816], BF16, pe_)
                Wmo = K.sb("Wmo", [128, 8, D], BF16, pe_)
                wmi_tk = Tk("wmi")
                wmo_tk = Tk("wmo")
                wmiv = wmi_s.rearrange("(k p) n -> p k n", p=128)
                wmov = wmo_s.rearrange("(k p) n -> p k n", p=128)
                s1 = K.newsem("wmi")
                K.dma(ACT, out=Wmi[:, :, 0:1408], in_=wmiv[:, :, 0:1408], reads=[dtk("wmi_s", 0)], writes=[wmi_tk], sem=s1)
                K.dma(ACT, out=Wmi[:, :, 1408:2816], in_=wmiv[:, :, 1408:2816], reads=[dtk("wmi_s", 0)], writes=[wmi_tk], sem=s1)
                s2 = K.newsem("wmo")
                K.dma(ACT, out=Wmo[:], in_=wmov, reads=[dtk("wmo_s", 0)], writes=[wmo_tk], sem=s2)
                amask = K.sb("amask", [128, 2, 2, 512], BF16, pe_)
                hmask = K.sb("hmask", [128, 4, 128], BF16, pe_)
                rmask = K.sb("rmask", [128, 512], F32, pe_)
                cosT = K.sb("mcosT", [128, NT, 32], F32, pe_)
                sinT = K.sb("msinT", [128, NT, 32], F32, pe_)
                mc_tk = Tk("mconst")
                mcs = K.newsem("mconst")
                for (t_, d_) in ((amask[:], amask_d), (hmask[:], hmask_d), (rmask[:], rmask_d)):
                    K.dma(SP, out=t_, in_=d_, reads=[], writes=[mc_tk], sem=mcs)
                K.dma(SP, out=cosT[:].rearrange("p t f -> p (t f)"), in_=rope_d[:, 0, :], reads=[dtk("rope", 0)], writes=[mc_tk], sem=mcs)
                K.dma(SP, out=sinT[:].rearrange("p t f -> p (t f)"), in_=rope_d[:, 1, :], reads=[dtk("rope", 0)], writes=[mc_tk], sem=mcs)

                pro = Pro(pe_, src, srcname, kidx, True, "m", nxin=1)
                epi = Epi(pe_, src, srcname, dst, dstname, kidx, True, "m", pro.xn_rot)
                h = K.sb("mh", [128, 8, TB], BF16, pe_)
                h_tks = [Tk("mh")]
                q2 = K.sb("q2", [128, 4, TB], F32, pe_)
                ff = K.sb("ff", [128, 4, TB], F32, pe_)
                bb = K.sb("bb", [128, 4, TB], F32, pe_)
                q2_tk = [Tk("q2_%d" % c) for c in range(4)]
                ff_tk = [Tk("ff_%d" % c) for c in range(4)]
                bb_tk = [Tk("bb_%d" % c) for c in range(4)]
                tA = Rot(K, pe_, "tA", [128, TB], F32, 2)
                tB = Rot(K, pe_, "tB", [128, TB], F32, 2)
                qt = K.sb("qt", [128, 4, TB], BF16, pe_)
                kt = K.sb("kt", [128, 4, TB], BF16, pe_)
                k2T = K.sb("k2T", [128, 4, TB], BF16, pe_)
                gs = K.sb("gs", [128, 4, TB], BF16, pe_)
                dec = K.sb("dec", [128, 4, 8], F32, pe_)
                hb_tk = [Tk("hgblk%d" % c) for c in range(4)]
                ktw_tk = [Tk("ktw%d" % c) for c in range(4)]
                k2w_tk = [Tk("k2w%d" % c) for c in range(4)]
                catT = K.sb("catT", [128, 8, TB], BF16, pe_)
                cat_tk = [Tk("cat%d" % c) for c in range(8)]
                NCH = 2
                qkr_r = Rot(K, pe_, "qkr", [128, 640], BF16, NCH)
                vtm = K.sb("vtm", [128, 4, 512], BF16, pe_)
                vtm_tk = [Tk("vtm%d" % i) for i in range(4)]
                vbe = K.sb("vbe", [128, 3, 2, 65], BF16, pe_)
                vb_tk = [Tk("vb0"), Tk("vb1"), Tk("vb2")]
                qT_r = Rot(K, pe_, "qT", [128, 8, 128], BF16, NCH)
                kTa = K.sb("kTa", [128, 2, 384], BF16, pe_)
                kT_tk = [Tk("kT0"), Tk("kT1"), Tk("kT2")]
                PT_r = Rot(K, pe_, "PTs", [128, 2, 2, 512], BF16, NCH)
                atm_r = Rot(K, pe_, "atm", [128, 512], BF16, NCH)
                ra_r = Rot(K, pe_, "ra", [128, 10, 32], F32, NCH)
                rb_r = Rot(K, pe_, "rb", [128, 10, 32], F32, NCH)
                dn_r = Rot(K, pe_, "dn", [128, 2, 8], F32, NCH)
                esink = K.sb("esink", [128, 8], F32, pe_)
                negc = K.sb("negc", [128, 1], F32, pe_)
                SHIFT = 16.0
                k2A = K.sb("k2A", [128, 4, 128], BF16, pe_)
                k2B = K.sb("k2B", [128, 4, 128], BF16, pe_)
                k2A_tk = Tk("k2A")
                k2B_tk = Tk("k2B")
                scm = K.sb("scm", [128, 4, 128], BF16, pe_)
                scm_tk = Tk("scm")
                st32 = K.sb("st32", [128, 4, 128], F32, pe_)
                st32_tk = [Tk("st32")]
                stb_rot = Rot(K, pe_, "stb", [128, 4, 128], BF16, 2)
                sqr = K.sb("sqr", [128, 512], BF16, pe_)
                sqr_tk = Tk("sqr")
                rsb = K.sb("rsb", [128, 512], F32, pe_)
                rsb_tk = Tk("rsb")
                t1 = K.sb("t1", [128, 512], F32, pe_)
                t1_tk = Tk("t1")

                for (qT_, qT_tk_, _) in qT_r.items:
                    K.op(POOL, lambda: nc.gpsimd.memset(qT_[:], 0.0), writes=[qT_tk_])
                K.op(POOL, lambda: nc.gpsimd.memset(kTa[:], 0.0), writes=kT_tk)
                K.op(POOL, lambda: nc.gpsimd.memset(vbe[:], 1.0), writes=vb_tk)
                K.op(DVE, lambda: nc.vector.memset(negc[:], -SHIFT), writes=[mc_tk])
                K.op(DVE, lambda: nc.vector.tensor_scalar(out=esink[:], in0=sinks[:], scalar1=-SHIFT, scalar2=None, op0=ALU.add),
                     reads=[const_tk, mc_tk], writes=[mc_tk])
                K.op(ACT, lambda: nc.scalar.activation(out=esink[:], in_=esink[:], func=AF.Exp), reads=[mc_tk], writes=[mc_tk])
                K.op(POOL, lambda: nc.gpsimd.memset(st32[:], 0.0), writes=st32_tk)
                stb0, stb0_tk, _ = stb_rot.next()
                K.op(POOL, lambda: nc.gpsimd.memset(stb0[:], 0.0), writes=[stb0_tk])
                state = {"stb": (stb0, stb0_tk)}
                QSC = float(0.5 * 128 ** -0.5)

                def galloc(n):
                    tries = 0
                    while True:
                        r = ps.try_alloc(n)
                        if r is not None:
                            return r
                        tries += 1
                        assert tries < 10000, "PSUM livelock"
                        yield

                def fm_proj(col0, b, tks):
                    pv = ps.f32(b)
                    K.group(PE, [(lambda k=k: nc.tensor.matmul(pv, lhsT=Wmi[:, k, col0:col0 + 128], rhs=h[:, k, :],
                                                             start=(k == 0), stop=(k == 7))) for k in range(8)],
                            reads=h_tks + [wmi_tk], writes=tks)
                    return pv

                def gen_prologue(blk):
                    for tt in range(4):
                        xn, xn_tk = pro.elem(blk * 4 + tt)
                        yield
                        pro.pe(xn, xn_tk, h, h_tks, tt)
                        yield

                def gen_P(blk):
                    for tt in range(4):
                        cs = slice(tt * 128, (tt + 1) * 128)
                        hb_, htk = yield from galloc(1)
                        K.group(PE, [(lambda k=k: nc.tensor.matmul(ps.f32(hb_), lhsT=h[:, k, cs], rhs=Wmi[:, k, 1792:2304],
                                                                 start=(k == 0), stop=(k == 7))) for k in range(8)],
                                reads=h_tks + [wmi_tk], writes=htk)
                        K.op(ACT, lambda: nc.scalar.activation(out=vtm[:, tt, :], in_=ps.f32(hb_), func=AF.Copy), reads=htk, writes=[vtm_tk[tt]])
                        ps.free(hb_)
                        yield
                    for c in range(4):
                        b, tks = yield from galloc(1)
                        pv = fm_proj(1280 + c * 128, b, tks)
                        a_, a_tk, _ = tA.next()
                        K.op(ACT, lambda: nc.scalar.activation(out=a_[:], in_=pv, func=AF.Tanh, scale=0.5), reads=tks, writes=[a_tk])
                        K.op(DVE, lambda: nc.vector.tensor_scalar(out=ff[:, c, :], in0=a_[:], scalar1=c01[:, 1, c:c + 1], scalar2=c01[:, 0, c:c + 1],
                                                                  op0=ALU.mult, op1=ALU.add), reads=[a_tk, par_tk], writes=[ff_tk[c]])
                        ps.free(b)
                        yield
                        b, tks = yield from galloc(1)
                        pv = fm_proj(768 + c * 128, b, tks)
                        a_, a_tk, _ = tA.next()
                        K.op(ACT, lambda: nc.scalar.activation(out=a_[:], in_=pv, func=AF.Tanh, scale=0.5), reads=tks, writes=[a_tk])
                        K.op(DVE, lambda: nc.vector.scalar_tensor_tensor(out=q2[:, c, :], in0=a_[:], scalar=1.0, in1=pv, op0=ALU.add, op1=ALU.mult),
                             reads=[a_tk] + tks, writes=[q2_tk[c]])
                        ps.free(b)
                        yield
                        b, tks = yield from galloc(1)
                        pv = fm_proj(2304 + c * 128, b, tks)
                        a_, a_tk, _ = tA.next()
                        K.op(ACT, lambda: nc.scalar.activation(out=a_[:], in_=pv, func=AF.Tanh, scale=0.5), reads=tks, writes=[a_tk])
                        K.op(DVE, lambda: nc.vector.scalar_tensor_tensor(out=gs[:, c, :], in0=a_[:], scalar=1.0, in1=pv, op0=ALU.add, op1=ALU.mult),
                             reads=[a_tk] + tks, writes=[hb_tk[c]])
                        ps.free(b)
                        yield
                    for c in range(4):
                        wtk = [hb_tk[c]]
                        a_, a_tk, _ = tA.next()
                        K.op(ACT, lambda: nc.scalar.activation(out=a_[:], in_=ff[:, c, :], func=AF.Ln), reads=[ff_tk[c]], writes=[a_tk])
                        K.op(DVE, lambda: nc.vector.tensor_tensor_scan(out=bb[:, c, :], data0=rmask[:], data1=a_[:], initial=0.0,
                                                                       op0=ALU.mult, op1=ALU.add),
                             reads=[a_tk, mc_tk], writes=[bb_tk[c]])
                        K.op(DVE, lambda: nc.vector.tensor_scalar(out=ff[:, c, :], in0=ff[:, c, :], scalar1=-1.0, scalar2=1.0, op0=ALU.mult, op1=ALU.add),
                             reads=[ff_tk[c], a_tk], writes=[ff_tk[c]])
                        yield
                        e_, e_tk, _ = tB.next()
                        K.op(ACT, lambda: nc.scalar.activation(out=e_[:], in_=bb[:, c, :], func=AF.Exp), reads=[bb_tk[c]], writes=[e_tk])
                        K.op(DVE, lambda: nc.vector.scalar_tensor_tensor(out=qt[:, c, :], in0=q2[:, c, :], scalar=QSC, in1=e_[:],
                                                                         op0=ALU.mult, op1=ALU.mult),
                             reads=[q2_tk[c], e_tk], writes=wtk)
                        K.op(DVE, lambda: nc.vector.tensor_copy(out=dec[:, c, :],
                                                                in_=e_[:].rearrange("p (n t) -> p n t", t=64)[:, :, 63]),
                             reads=[e_tk], writes=wtk)
                        yield
                        x_, x_tk, _ = tB.next()
                        K.op(ACT, lambda: nc.scalar.activation(out=x_[:], in_=bb[:, c, :], func=AF.Exp, scale=-1.0), reads=[bb_tk[c]], writes=[x_tk])
                        K.op(DVE, lambda: nc.vector.tensor_tensor(out=kt[:, c, :], in0=ff[:, c, :], in1=x_[:], op=ALU.mult),
                             reads=[ff_tk[c], x_tk], writes=[ktw_tk[c]])
                        yield
                        d_, d_tk, _ = tA.next()
                        b3 = bb[:, c, :].rearrange("p (n t) -> p n t", t=64)
                        K.op(DVE, lambda: nc.vector.tensor_tensor(out=d_[:].rearrange("p (n t) -> p n t", t=64),
                                                                  in0=b3[:, :, 63:64].broadcast_to([128, 8, 64]), in1=b3, op=ALU.subtract),
                             reads=[bb_tk[c]], writes=[d_tk])
                        K.op(ACT, lambda: nc.scalar.activation(out=d_[:], in_=d_[:], func=AF.Exp), reads=[d_tk], writes=[d_tk])
                        K.op(DVE, lambda: nc.vector.tensor_tensor(out=k2T[:, c, :], in0=ff[:, c, :], in1=d_[:], op=ALU.mult),
                             reads=[ff_tk[c], d_tk], writes=[k2w_tk[c]])
                        yield

                def gen_A(blk, tts):
                    for tt in tts:
                        tile = blk * 4 + tt
                        cs = slice(tt * 128, (tt + 1) * 128)
                        slot = tile % 3
                        pslot = (tile - 1) % 3
                        qkr, qkr_tk, _ = qkr_r.next()
                        qT, qT_tk, _ = qT_r.next()
                        PTs, PT_tk, _ = PT_r.next()
                        atm, atm_tk, _ = atm_r.next()
                        ra, ra_tk, _ = ra_r.next()
                        rb, rb_tk, _ = rb_r.next()
                        dn, dn_tk, _ = dn_r.next()
                        pb, ptk = yield from galloc(2)
                        fns = []
                        for k in range(8):
                            fns.append(lambda k=k: nc.tensor.matmul(ps.f32(pb), lhsT=h[:, k, cs], rhs=Wmi[:, k, 0:512],
                                                                    start=(k == 0), stop=(k == 7)))
                        for k in range(8):
                            fns.append(lambda k=k: nc.tensor.matmul(ps.f32(pb + 1)[:, 0:256], lhsT=h[:, k, cs], rhs=Wmi[:, k, 512:768],
                                                                    start=(k == 0), stop=(k == 7)))
                        K.group(PE, fns, reads=h_tks + [wmi_tk], writes=ptk)
                        yield
                        K.op(ACT, lambda: nc.scalar.activation(out=vbe[:, slot, :, 0:64],
                                                               in_=ps.f32(pb + 1)[:, 128:256].rearrange("p (g d) -> p g d", d=64), func=AF.Copy),
                             reads=ptk, writes=[vb_tk[slot]])
                        qk3 = ps.f32(pb, 2)[:, 0:640].rearrange("p (h d) -> p h d", d=64)
                        cb_ = cosT[:, tile, :].unsqueeze(1).broadcast_to([128, 10, 32])
                        sb_ = sinT[:, tile, :].unsqueeze(1).broadcast_to([128, 10, 32])
                        o3 = qkr[:].rearrange("p (h d) -> p h d", d=64)
                        K.op(DVE, lambda: nc.vector.tensor_tensor(out=ra[:], in0=qk3[:, :, 0:32], in1=cb_, op=ALU.mult), reads=ptk + [mc_tk], writes=[ra_tk])
                        K.op(DVE, lambda: nc.vector.tensor_tensor(out=rb[:], in0=qk3[:, :, 32:64], in1=sb_, op=ALU.mult), reads=ptk + [mc_tk], writes=[rb_tk])
                        K.op(DVE, lambda: nc.vector.tensor_tensor(out=o3[:, :, 0:32], in0=ra[:], in1=rb[:], op=ALU.subtract),
                             reads=[ra_tk, rb_tk], writes=[qkr_tk])
                        yield
                        K.op(DVE, lambda: nc.vector.tensor_tensor(out=ra[:], in0=qk3[:, :, 32:64], in1=cb_, op=ALU.mult), reads=ptk + [mc_tk], writes=[ra_tk])
                        K.op(DVE, lambda: nc.vector.tensor_tensor(out=rb[:], in0=qk3[:, :, 0:32], in1=sb_, op=ALU.mult), reads=ptk + [mc_tk], writes=[rb_tk])
                        K.op(DVE, lambda: nc.vector.tensor_tensor(out=o3[:, :, 32:64], in0=ra[:], in1=rb[:], op=ALU.add),
                             reads=[ra_tk, rb_tk], writes=[qkr_tk])
                        ps.free(pb, 2)
                        yield
                        tb2, ttk = yield from galloc(1)
                        tv = ps.bf(tb2)
                        fns = [(lambda j=j: nc.tensor.transpose(tv[:, j * 128:(j + 1) * 128], qkr[:, j * 128:(j + 1) * 128], ident[:]))
                               for j in range(5)]
                        K.group(PE, fns, reads=[qkr_tk, const_tk], writes=ttk)
                        qv = tv[:, 0:512].rearrange("p (j t) -> p j t", t=128)
                        qT4 = qT[:].rearrange("p (j two) t -> p j two t", two=2)
                        K.op(DVE, lambda: nc.vector.tensor_copy(out=qT4[0:64, :, 0, :], in_=qv[0:64]), reads=ttk, writes=[qT_tk])
                        K.op(DVE, lambda: nc.vector.tensor_copy(out=qT4[0:64, :, 1, :], in_=qv[64:128]), reads=ttk, writes=[qT_tk])
                        K.op(ACT, lambda: nc.scalar.activation(out=kTa[0:64, 0, slot * 128:(slot + 1) * 128], in_=tv[0:64, 512:640], func=AF.Copy),
                             reads=ttk, writes=[kT_tk[slot]])
                        K.op(ACT, lambda: nc.scalar.activation(out=kTa[0:64, 1, slot * 128:(slot + 1) * 128], in_=tv[64:128, 512:640],
                                                               func=AF.Copy), reads=ttk, writes=[kT_tk[slot]])
                        ps.free(tb2)
                        yield
                        kbs = [(1, slot)] if tile == 0 else [(0, pslot), (1, slot)]
                        var = 0
                        for g in range(2):
                            sb2, stk = yield from galloc(2)
                            fns = []
                            for (kb, sl) in kbs:
                                o = ps.f32(sb2 + kb)
                                fns.append(lambda o=o, g=g, sl=sl: nc.tensor.matmul(
                                    o, lhsT=kTa[:, g, sl * 128:(sl + 1) * 128], rhs=qT[:, g * 4:(g + 1) * 4, :].rearrange("p h t -> p (h t)"),
                                    start=True, stop=False))
                                fns.append(lambda o=o, kb=kb: nc.tensor.matmul(o, lhsT=ident[:], rhs=amask[:, var, kb, :], start=False, stop=True))
                            K.group(PE, fns, reads=[qT_tk, const_tk, mc_tk] + kT_tk, writes=stk)
                            for (kb, sl) in kbs:
                                K.op(ACT, lambda: nc.scalar.activation(out=PTs[:, g, kb, :], in_=ps.f32(sb2 + kb), func=AF.Exp, scale=0.125,
                                                                       bias=negc[:, 0:1]),
                                     reads=[stk[kb], mc_tk], writes=[PT_tk])
                            ps.free(sb2, 2)
                            yield
                        ob2, otk = yield from galloc(2)
                        fns = []
                        for hh in range(8):
                            g, hq = hh // 4, hh % 4
                            for i, (kb, sl) in enumerate(kbs):
                                fns.append(lambda g=g, hq=hq, kb=kb, sl=sl, i=i: nc.tensor.matmul(
                                    ps.f32(ob2 + g)[:, hq * 65:(hq + 1) * 65], lhsT=PTs[:, g, kb, hq * 128:(hq + 1) * 128],
                                    rhs=vbe[:, sl, g, :], start=(i == 0), stop=(i == len(kbs) - 1)))
                        K.group(PE, fns, reads=[PT_tk] + vb_tk, writes=otk)
                        yield
                        O4 = ps.f32(ob2, 2).rearrange("p (b c) -> p b c", c=512)[:, :, 0:260].rearrange("p b (h e) -> p b h e", e=65)
                        K.op(DVE, lambda: nc.vector.tensor_tensor(out=dn[:, 0, :].rearrange("p (b h) -> p b h", h=4), in0=O4[:, :, :, 64],
                                                                  in1=esink[:].rearrange("p (b h) -> p b h", h=4), op=ALU.add),
                             reads=otk + [mc_tk], writes=[dn_tk])
                        K.op(DVE, lambda: nc.vector.reciprocal(out=dn[:, 1, :], in_=dn[:, 0, :]), reads=[dn_tk], writes=[dn_tk])
                        K.op(DVE, lambda: nc.vector.tensor_tensor(out=atm[:].rearrange("p (b h d) -> p b h d", h=4, d=64), in0=O4[:, :, :, 0:64],
                                                                  in1=dn[:, 1, :].rearrange("p (b h) -> p b h", h=4).unsqueeze(3).broadcast_to([128, 2, 4, 64]),
                                                                  op=ALU.mult),
                             reads=otk + [dn_tk], writes=[atm_tk])
                        ps.free(ob2, 2)
                        yield
                        ab_, atk = yield from galloc(1)
                        av = ps.bf(ab_)
                        K.group(PE, [(lambda j=j: nc.tensor.transpose(av[:, j * 128:(j + 1) * 128], atm[:, j * 128:(j + 1) * 128], ident[:]))
                                     for j in range(4)], reads=[atm_tk, const_tk], writes=atk)
                        K.op(ACT, lambda: nc.scalar.activation(out=catT[:, 0:4, cs], in_=av[:, 0:512].rearrange("p (j t) -> p j t", t=128),
                                                               func=AF.Copy), reads=atk, writes=cat_tk[0:4])
                        ps.free(ab_)
                        yield

                def state_step(n, ubank, utk, sdst, sdst_tk):
                    K.op(DVE, lambda: nc.vector.tensor_tensor(out=st32[:], in0=st32[:], in1=dec[:, :, n:n + 1].broadcast_to([128, 4, 128]),
                                                              op=ALU.mult), reads=st32_tk + hb_tk, writes=st32_tk)
                    K.op(DVE, lambda: nc.vector.tensor_tensor(out=st32[:], in0=st32[:], in1=ps.f32(ubank).rearrange("p (c e) -> p c e", e=128),
                                                              op=ALU.add), reads=st32_tk + utk, writes=st32_tk)
                    K.op(ACT, lambda: nc.scalar.activation(out=sdst[:], in_=st32[:], func=AF.Copy), reads=st32_tk, writes=[sdst_tk])

                def gen_H(blk):
                    for tt in range(4):
                        cs = slice(tt * 128, (tt + 1) * 128)
                        vt = vtm[:, tt, :]
                        vt_tk = vtm_tk[tt]
                        kb_, ktk = yield from galloc(1)
                        kv = ps.bf(kb_)
                        K.group(PE, [(lambda c=c: nc.tensor.transpose(kv[:, c * 128:(c + 1) * 128], k2T[:, c, cs], ident[:])) for c in range(4)],
                                reads=hb_tk + k2w_tk + [const_tk], writes=ktk)
                        kv3 = kv[:, 0:512].rearrange("p (c d) -> p c d", d=128)
                        K.op(DVE, lambda: nc.vector.tensor_scalar(out=k2A[:], in0=kv3, scalar1=rowm[:, 0:1], scalar2=None, op0=ALU.mult),
                             reads=ktk + [const_tk], writes=[k2A_tk])
                        K.op(DVE, lambda: nc.vector.tensor_scalar(out=k2B[:], in0=kv3, scalar1=rowm[:, 1:2], scalar2=None, op0=ALU.mult),
                             reads=ktk + [const_tk], writes=[k2B_tk])
                        ps.free(kb_)
                        sb_2, sctk = yield from galloc(1)
                        scv = ps.f32(sb_2)
                        K.group(PE, [(lambda c=c: nc.tensor.matmul(scv[:, c * 128:(c + 1) * 128], lhsT=kt[:, c, cs], rhs=qt[:, c, cs],
                                                                 start=True, stop=True)) for c in range(4)],
                                reads=hb_tk + ktw_tk, writes=sctk)
                        yield
                        K.op(DVE, lambda: nc.vector.tensor_tensor(out=scm[:], in0=scv.rearrange("p (c t) -> p c t", t=128), in1=hmask[:], op=ALU.mult),
                             reads=sctk + [mc_tk], writes=[scm_tk])
                        ps.free(sb_2)
                        ua, uatk = yield from galloc(1)
                        ub, ubtk = yield from galloc(1)
                        K.group(PE, [(lambda c=c: nc.tensor.matmul(ps.f32(ua)[:, c * 128:(c + 1) * 128], lhsT=k2A[:, c, :],
                                                                 rhs=vt[:, c * 128:(c + 1) * 128], start=True, stop=True)) for c in range(4)],
                                reads=[k2A_tk, vt_tk], writes=uatk)
                        K.group(PE, [(lambda c=c: nc.tensor.matmul(ps.f32(ub)[:, c * 128:(c + 1) * 128], lhsT=k2B[:, c, :],
                                                                 rhs=vt[:, c * 128:(c + 1) * 128], start=True, stop=True)) for c in range(4)],
                                reads=[k2B_tk, vt_tk], writes=ubtk)
                        yield
                        oo, ootk = yield from galloc(1)
                        oov = ps.f32(oo)
                        stA, stA_tk = state["stb"]
                        fns = []
                        for c in range(4):
                            fns.append(lambda c=c: nc.tensor.matmul(oov[:, c * 128:(c + 1) * 128], lhsT=vt[:, c * 128:(c + 1) * 128],
                                                                    rhs=scm[:, c, :], start=(c == 0), stop=False, skip_group_check=True))
                        for c in range(4):
                            fns.append(lambda c=c: nc.tensor.matmul(oov[:, c * 128:c * 128 + 64], lhsT=stA[:, c, :],
                                                                    rhs=qt[:, c, tt * 128:tt * 128 + 64], start=False, stop=False,
                                                                    skip_group_check=True))
                        K.group(PE, fns, reads=[vt_tk, scm_tk, stA_tk] + hb_tk, writes=ootk)
                        stB, stB_tk, _ = stb_rot.next()
                        state_step(2 * tt, ua, uatk, stB, stB_tk)
                        ps.free(ua)
                        yield
                        K.group(PE, [(lambda c=c: nc.tensor.matmul(oov[:, c * 128 + 64:(c + 1) * 128], lhsT=stB[:, c, :],
                                                                 rhs=qt[:, c, tt * 128 + 64:(tt + 1) * 128], start=False, stop=True,
                                                                 skip_group_check=True)) for c in range(4)],
                                reads=[stB_tk] + hb_tk, writes=ootk)
                        stC, stC_tk, _ = stb_rot.next()
                        state_step(2 * tt + 1, ub, ubtk, stC, stC_tk)
                        ps.free(ub)
                        state["stb"] = (stC, stC_tk)
                        yield
                        K.op(ACT, lambda: nc.scalar.activation(out=sqr[:], in_=oov, func=AF.Square), reads=ootk, writes=[sqr_tk])
                        nb_, ntk = yield from galloc(1)
                        K.group(PE, [lambda: nc.tensor.matmul(ps.f32(nb_), lhsT=onesb[:], rhs=sqr[:], start=True, stop=True)],
                                reads=[sqr_tk, const_tk], writes=ntk)
                        yield
                        K.op(ACT, lambda: nc.scalar.activation(out=rsb[:], in_=ps.f32(nb_), func=AF.Ln, scale=1.0 / 128, bias=eps_t[:, 0:1]),
                             reads=ntk + [const2_tk], writes=[rsb_tk])
                        ps.free(nb_)
                        K.op(ACT, lambda: nc.scalar.activation(out=rsb[:], in_=rsb[:], func=AF.Exp, scale=-0.5), reads=[rsb_tk], writes=[rsb_tk])
                        K.op(DVE, lambda: nc.vector.scalar_tensor_tensor(out=t1[:], in0=oov, scalar=gnh[:, 0:1], in1=rsb[:], op0=ALU.mult, op1=ALU.mult),
                             reads=ootk + [rsb_tk, par_tk], writes=[t1_tk])
                        ps.free(oo)
                        K.op(DVE, lambda: nc.vector.tensor_tensor(out=catT[:, 4:8, cs], in0=t1[:].rearrange("p (c t) -> p c t", t=128),
                                                                  in1=gs[:, :, cs], op=ALU.mult),
                             reads=[t1_tk] + hb_tk, writes=cat_tk[4:8])
                        yield

                def gen_O(blk):
                    for tt in range(4):
                        yb, ytk = yield from galloc(2)
                        fns = []
                        for n in range(2):
                            o = ps.f32(yb + n)
                            for kc in range(8):
                                fns.append(lambda o=o, n=n, kc=kc: nc.tensor.matmul(
                                    o, lhsT=catT[:, kc, tt * 128:(tt + 1) * 128], rhs=Wmo[:, kc, n * 512:(n + 1) * 512],
                                    start=(kc == 0), stop=(kc == 7)))
                        K.group(PE, fns, reads=cat_tk + [wmo_tk], writes=ytk)
                        yield
                        epi.run(blk * 4 + tt, yb, ytk)
                        yield

                interleave(gen_prologue(0))
                for blk in range(NB):
                    interleave(gen_P(blk), gen_A(blk, [0]), gen_A(blk, [1]))
                    interleave(gen_H(blk), gen_A(blk, [2]), gen_A(blk, [3]))
                    if blk + 1 < NB:
                        interleave(gen_O(blk), gen_prologue(blk + 1))
                    else:
                        interleave(gen_O(blk))
                K.barrier()

        dsts = {1: out_d if stop_after == 1 else x1_d, 2: out_d if stop_after == 2 else x2_d, 3: out_d}
        names = {1: "out" if stop_after == 1 else "x1", 2: "out" if stop_after == 2 else "x2", 3: "out"}
        if stop_after == 0:
            dbg = K.sb("dbg", [128, D], F32)
            dbg_tk = Tk("dbg")
            dsem = K.newsem("dbg")
            K.op(DVE, lambda: nc.vector.tensor_copy(out=dbg[:], in_=G[:, 0, :]), reads=[par_tk], writes=[dbg_tk])
            K.dma(SP, out=out_d[0:128, :], in_=dbg[:], reads=[dbg_tk], writes=[], sem=dsem)
            K.op(DVE, lambda: nc.vector.tensor_copy(out=dbg[:, 0:24], in_=a_col[:].rearrange("p a b -> p (a b)")), reads=[par_tk, dbg_tk], writes=[dbg_tk])
            K.op(DVE, lambda: nc.vector.tensor_copy(out=dbg[:, 24:48], in_=sh_col[:].rearrange("p a b -> p (a b)")), reads=[par_tk, dbg_tk], writes=[dbg_tk])
            K.op(DVE, lambda: nc.vector.tensor_copy(out=dbg[:, 48:52], in_=lb[:]), reads=[par_tk, dbg_tk], writes=[dbg_tk])
            K.dma(SP, out=out_d[128:256, :], in_=dbg[:], reads=[dbg_tk], writes=[], sem=dsem)
            K.barrier()
            return nc
        ffn_phase(0, 0, x_d, "x", dsts[1], names[1])
        if stop_after >= 2:
            mixer_phase(x1_d, "x1", dsts[2], names[2])
        if stop_after >= 3:
            ffn_phase(1, 2, x2_d, "x2", out_d, "out")
        K.barrier()
        print("instructions:", K.n_inst, "sems:", len(K.sems))
    return nc


def _consts():
    bf = ml_dtypes.bfloat16
    ident = np.eye(128, dtype=np.float32).astype(bf)
    onesb = np.ones((128, 128), np.float32).astype(bf)
    NEG = -1e30
    am = np.zeros((128, 2, 256), np.float32)
    am[0:64, 0, 192:256] = NEG
    am[64:128, 0, 0:64] = NEG
    am[0:64, 1, 64:128] = NEG
    am[64:128, 1, 128:192] = NEG
    s = np.arange(128)[:, None]
    t = np.arange(128)[None, :]
    hm = ((s // 64 == t // 64) & (s <= t)).astype(np.float32)
    hmask = np.broadcast_to(hm[:, None, :], (128, 4, 128)).copy()
    rmask = np.ones((128, 512), np.float32)
    rmask[:, ::64] = 0.0
    inv_freq = (1.0 / (np.float32(10000.0) ** (np.arange(0, 64, 2, dtype=np.float32) / np.float32(64)))).astype(np.float32)
    invf = np.broadcast_to(inv_freq[None, :], (128, 32)).copy()
    rowm = np.zeros((128, 2), np.float32)
    rowm[0:64, 0] = 1.0
    rowm[64:128, 1] = 1.0
    amT = np.zeros((128, 2, 2, 4, 128), np.float32)
    for v in range(2):
        for kb in range(2):
            amT[:, v, kb, :, :] = am[:, v, kb * 128:(kb + 1) * 128].T[:, None, :]
    amT = amT.reshape(128, 2, 2, 512)
    return dict(ident=ident, onesb=onesb, amask=amT.astype(bf), hmask=hmask.astype(bf), rmask=rmask, invf=invf, rowm=rowm)


def make_in_maps(x, c, positions, w_cond, b_cond, norm_pre, norm_post, ffn_w_in, ffn_w_out,
                 w_mix_in, w_mix_out, attn_sinks, hgrn_lb_logits, hgrn_gnorm):
    f = np.float32
    cst = _consts()
    shared = dict(
        w_cond=np.ascontiguousarray(w_cond[0], f), b_cond=np.ascontiguousarray(np.broadcast_to(np.asarray(b_cond[0], f)[None, :], (128, 9 * D))),
        npre=np.ascontiguousarray(np.asarray(norm_pre[0], f).reshape(3, 8, 128).transpose(2, 0, 1)),
        npost=np.ascontiguousarray(np.broadcast_to(np.asarray(norm_post[0], f)[None], (128, 3, D))),
        w_in=np.ascontiguousarray(ffn_w_in[0], f), w_out=np.ascontiguousarray(ffn_w_out[0], f),
        w_mi=np.ascontiguousarray(w_mix_in[0], f), w_mo=np.ascontiguousarray(w_mix_out[0], f),
        sinks=np.ascontiguousarray(np.broadcast_to(np.asarray(attn_sinks[0], f)[None], (128, 8))),
        lbl=np.ascontiguousarray(np.asarray(hgrn_lb_logits, f).reshape(2, 4, 128).transpose(2, 0, 1)),
        gn=np.ascontiguousarray(np.asarray(hgrn_gnorm[0], f)[:, None]),
        **cst)
    maps = []
    for b in range(8):
        m = dict(shared)
        m["x"] = np.ascontiguousarray(x[b], f)
        m["ccol"] = np.ascontiguousarray(np.asarray(c[b], f).reshape(8, 128).T)
        m["pos"] = np.ascontiguousarray(np.asarray(positions[b], np.int32).reshape(NT, 128).T)
        maps.append(m)
    return maps


def kernel(**inputs):
    nc = build(3)
    maps = make_in_maps(**inputs)
    res = run_bass_kernel_spmd(nc, maps, core_ids=list(range(8)))
    return np.stack([np.asarray(r["out"], np.float32) for r in res.results], axis=0)
```

```python
import numpy as np
from contextlib import ExitStack
import ml_dtypes
import concourse.bass as bass
import concourse.mybir as mybir
from concourse.bass_utils import run_bass_kernel_spmd

F32 = mybir.dt.float32
BF16 = mybir.dt.bfloat16
I32 = mybir.dt.int32
AF = mybir.ActivationFunctionType
ALU = mybir.AluOpType
AX = mybir.AxisListType

S = 4096
D = 1024
DFF = 2816
NF = DFF // 128
TB = 512
NB = S // TB
NT = S // 128
EPS = 1e-6
PI = float(np.pi)
TWO_PI = float(2 * np.pi)


class Tk:
    __slots__ = ("w", "r", "name", "excl")

    def __init__(self, name="", excl=False):
        self.w = None
        self.r = {}
        self.name = name
        self.excl = excl


class Sem:
    def __init__(self, h, name):
        self.h = h
        self.cnt = 0
        self.name = name


class Eng:
    def __init__(self, name, h, sem):
        self.name = name
        self.h = h
        self.sem = sem
        self.seen = {}


class Kern:
    def __init__(self, nc, es):
        self.nc = nc
        self.es = es
        self.sems = []
        self.PE = Eng("pe", nc.tensor, self.newsem("pe"))
        self.ACT = Eng("act", nc.scalar, self.newsem("act"))
        self.DVE = Eng("dve", nc.vector, self.newsem("dve"))
        self.POOL = Eng("pool", nc.gpsimd, self.newsem("pool"))
        self.SP = Eng("sp", nc.sync, self.newsem("sp"))
        self.engs = [self.PE, self.ACT, self.DVE, self.POOL, self.SP]
        self.n_inst = 0

    def newsem(self, name):
        self.sid = getattr(self, "sid", 0) + 1
        s = Sem(self.es.enter_context(self.nc.semaphore("m%d_%s" % (self.sid, name))), name)
        self.sems.append(s)
        return s

    def sb(self, name, shape, dt, es=None):
        self.uid = getattr(self, "uid", 0) + 1
        return (es or self.es).enter_context(self.nc.sbuf_tensor("s%d_%s" % (self.uid, name), shape, dt))

    def _wait(self, E, toks):
        for (s, v) in toks:
            if E.seen.get(id(s), 0) >= v:
                continue
            E.h.wait_ge(s.h, v)
            E.seen[id(s)] = v

    def _deps(self, E, reads, writes):
        toks = []
        for t in reads:
            if t.w is not None:
                toks.append(t.w)
            if t.excl:
                for tok in t.r.values():
                    if tok[0] is not E.sem:
                        toks.append(tok)
        for t in writes:
            if t.w is not None and t.w[0] is not E.sem:
                toks.append(t.w)
            for tok in t.r.values():
                if tok[0] is not E.sem or E is not self.PE:
                    toks.append(tok)
        return toks

    def _mark(self, tok, reads, writes):
        for t in writes:
            t.w = tok
            t.r = {}
        for t in reads:
            t.r[id(tok[0])] = tok

    def op(self, E, fn, reads=(), writes=()):
        self._wait(E, self._deps(E, reads, writes))
        ins = fn()
        ins.then_inc(E.sem.h, 1)
        E.sem.cnt += 1
        self._mark((E.sem, E.sem.cnt), reads, writes)
        self.n_inst += 1
        return ins

    def group(self, E, fns, reads=(), writes=()):
        self._wait(E, self._deps(E, reads, writes))
        ins = None
        for fn in fns:
            ins = fn()
            self.n_inst += 1
        ins.then_inc(E.sem.h, 1)
        E.sem.cnt += 1
        self._mark((E.sem, E.sem.cnt), reads, writes)

    def dma(self, E, out, in_, reads, writes, sem, **kw):
        self._wait(E, self._deps(E, reads, writes))
        E.h.dma_start(out=out, in_=in_, **kw).then_inc(sem.h, 16)
        sem.cnt += 16
        self._mark((sem, sem.cnt), reads, writes)
        self.n_inst += 1

    def barrier(self):
        toks = [(s, s.cnt) for s in self.sems if s.cnt > 0]
        for E in self.engs:
            self._wait(E, [t for t in toks if t[0] is not E.sem])


class Psum:
    def __init__(self, K):
        nc = K.nc
        self.t = K.es.enter_context(nc.psum_tensor("psum_all", [128, 8 * 512], F32))
        self.tb = self.t.bitcast(BF16)
        self.tk = [Tk("bank%d" % i, excl=True) for i in range(8)]
        self.p = 0
        self.held = [False] * 8

    def try_alloc(self, n=1):
        for i in range(8):
            b = (self.p + i) % 8
            if b % n or b + n > 8:
                continue
            if any(self.held[b:b + n]):
                continue
            for j in range(b, b + n):
                self.held[j] = True
            self.p = (b + n) % 8
            return b, self.tk[b:b + n]
        return None

    def alloc(self, n=1):
        r = self.try_alloc(n)
        assert r is not None, "PSUM exhausted (straight-line code must free before allocating)"
        return r

    def free(self, b, n=1):
        for j in range(b, b + n):
            assert self.held[j]
            self.held[j] = False

    def f32(self, b, n=1):
        return self.t[:, b * 512:(b + n) * 512]

    def bf(self, b, n=1):
        return self.tb[:, b * 1024:(b + n) * 1024]


class Rot:
    def __init__(self, K, es, name, shape, dt, n, dma=False):
        self.items = []
        for i in range(n):
            t = K.sb("%s%d" % (name, i), shape, dt, es)
            self.items.append((t, Tk("%s%d" % (name, i)), K.newsem("%s%d" % (name, i)) if dma else None))
        self.i = 0

    def next(self):
        it = self.items[self.i % len(self.items)]
        self.i += 1
        return it


def build(stop_after=3):
    nc = bass.Bass("TRN2", target_bir_lowering=False)

    def din(name, shape, dt=F32):
        return nc.dram_tensor(name, shape, dt, kind="ExternalInput").ap()

    x_d = din("x", [S, D])
    ccol_d = din("ccol", [128, 8])
    pos_d = din("pos", [128, NT], I32)
    wcond_d = din("w_cond", [D, 9 * D])
    bcond_d = din("b_cond", [128, 9 * D])
    npre_d = din("npre", [128, 3, 8])
    npost_d = din("npost", [128, 3, D])
    win_d = din("w_in", [2, D, 2 * DFF])
    wout_d = din("w_out", [2, DFF, D])
    wmi_d = din("w_mi", [D, 2816])
    wmo_d = din("w_mo", [D, D])
    sinks_d = din("sinks", [128, 8])
    lbl_d = din("lbl", [128, 2, 4])
    gn_d = din("gn", [128, 1])
    ident_d = din("ident", [128, 128], BF16)
    onesb_d = din("onesb", [128, 128], BF16)
    amask_d = din("amask", [128, 2, 2, 512], BF16)
    hmask_d = din("hmask", [128, 4, 128], BF16)
    rmask_d = din("rmask", [128, 512])
    invf_d = din("invf", [128, 32])
    rowm_d = din("rowm", [128, 2])
    out_d = nc.dram_tensor("out", [S, D], F32, kind="ExternalOutput").ap()
    x1_d = nc.dram_tensor("x1s", [S, D], F32).ap()
    x2_d = nc.dram_tensor("x2s", [S, D], F32).ap()
    rope_d = nc.dram_tensor("rope_s", [128, 2, NT * 32], F32).ap()
    w2in_s = nc.dram_tensor("w2in_s", [D, 2 * DFF], BF16).ap()
    w2out_s = nc.dram_tensor("w2out_s", [DFF, D], BF16).ap()
    wmi_s = nc.dram_tensor("wmi_s", [D, 2816], BF16).ap()
    wmo_s = nc.dram_tensor("wmo_s", [D, D], BF16).ap()

    with ExitStack() as es:
        K = Kern(nc, es)
        PE, ACT, DVE, POOL, SP = K.PE, K.ACT, K.DVE, K.POOL, K.SP
        ps = Psum(K)
        dram_tk = {}

        def dtk(name, tile):
            key = (name, tile)
            if key not in dram_tk:
                dram_tk[key] = Tk("%s_%d" % key)
            return dram_tk[key]

        ident = K.sb("ident", [128, 128], BF16)
        onesb = K.sb("onesb", [128, 128], BF16)
        rowm = K.sb("rowm", [128, 2], F32)
        sinks = K.sb("sinks", [128, 8], F32)
        gn = K.sb("gn", [128, 1], F32)
        lb = K.sb("lb", [128, 4], F32)
        oml = K.sb("oml", [128, 4], F32)
        a_col = K.sb("a_col", [128, 3, 8], F32)
        sh_col = K.sb("sh_col", [128, 3, 8], F32)
        G = K.sb("G", [128, 3, D], BF16)
        c01 = K.sb("c01", [128, 2, 4], F32)
        gnh = K.sb("gnh", [128, 1], F32)
        const_tk = Tk("consts")
        par_tk = Tk("params")
        rope_tk = Tk("rope")

        with ExitStack() as ss:
            ld = K.newsem("setup_ld")
            ccol = K.sb("ccol", [128, 8], F32, ss)
            ca = K.sb("ca", [128, 8], F32, ss)
            posi = K.sb("posi", [128, NT], I32, ss)
            npre = K.sb("npre", [128, 3, 8], F32, ss)
            npost = K.sb("npost", [128, 3, D], F32, ss)
            lbl = K.sb("lbl", [128, 2, 4], F32, ss)
            invf = K.sb("invf", [128, 32], F32, ss)
            setup_tk = Tk("setup_in")
            cosT = K.sb("cosT", [128, NT, 32], F32, ss)
            sinT = K.sb("sinT", [128, NT, 32], F32, ss)
            loads = [(ident, ident_d), (onesb, onesb_d),
                     (rowm, rowm_d), (sinks, sinks_d), (gn, gn_d), (ccol, ccol_d), (posi, pos_d), (npre, npre_d),
                     (npost, npost_d), (lbl, lbl_d), (invf, invf_d)]
            for (t, d_) in loads:
                nc.sync.dma_start(out=t[:], in_=d_).then_inc(ld.h, 16)
                ld.cnt += 16
            setup_tk.w = (ld, ld.cnt)
            const_tk.w = (ld, ld.cnt)

            K.op(ACT, lambda: nc.scalar.activation(out=ca[:], in_=ccol[:], func=AF.Silu), reads=[setup_tk], writes=[par_tk])
            K.op(DVE, lambda: nc.vector.tensor_tensor(out=lb[:], in0=lbl[:, 0, :], in1=lbl[:, 1, :], op=ALU.subtract),
                 reads=[setup_tk], writes=[par_tk])
            K.op(ACT, lambda: nc.scalar.activation(out=lb[:], in_=lb[:], func=AF.Sigmoid), reads=[par_tk], writes=[par_tk])
            K.op(DVE, lambda: nc.vector.tensor_scalar(out=oml[:], in0=lb[:], scalar1=-1.0, scalar2=1.0, op0=ALU.mult, op1=ALU.add),
                 reads=[par_tk], writes=[par_tk])
            K.op(DVE, lambda: nc.vector.tensor_scalar(out=c01[:, 1, :], in0=oml[:], scalar1=0.5, scalar2=None, op0=ALU.mult),
                 reads=[par_tk], writes=[par_tk])
            K.op(DVE, lambda: nc.vector.tensor_tensor(out=c01[:, 0, :], in0=lb[:], in1=c01[:, 1, :], op=ALU.add),
                 reads=[par_tk], writes=[par_tk])
            K.op(DVE, lambda: nc.vector.tensor_scalar(out=gnh[:], in0=gn[:], scalar1=0.5, scalar2=None, op0=ALU.mult),
                 reads=[setup_tk, par_tk], writes=[par_tk])

            cab = K.sb("cab", [128, 8, 128], F32, ss)
            identf = K.sb("identf", [128, 128], F32, ss)
            modbc = K.sb("modbc", [128, 9 * D], F32, ss)
            dg = K.sb("dg", [128, 8, 128], F32, ss)
            K.op(DVE, lambda: nc.vector.tensor_copy(out=cab[:], in_=ca[:].unsqueeze(2).broadcast_to([128, 8, 128])),
                 reads=[par_tk], writes=[par_tk])
            K.op(DVE, lambda: nc.vector.tensor_copy(out=identf[:], in_=ident[:]), reads=[setup_tk], writes=[setup_tk])
            wc_rot = Rot(K, ss, "wc", [128, 8, 512], F32, 2, dma=True)
            bc_rot = Rot(K, ss, "bc", [128, 512], F32, 2, dma=True)
            wc_view = wcond_d.rearrange("(k p) n -> p k n", p=128)
            mod_tk = Tk("modbc")
            for cb in range(18):
                wc, wc_tk, wc_sem = wc_rot.next()
                bc, bc_tk, bc_sem = bc_rot.next()
                K.dma(SP, out=wc[:], in_=wc_view[:, :, cb * 512:(cb + 1) * 512], reads=[], writes=[wc_tk], sem=wc_sem)
                K.dma(SP, out=bc[:], in_=bcond_d[:, cb * 512:(cb + 1) * 512], reads=[], writes=[bc_tk], sem=bc_sem)
                b, btk = ps.alloc(1)
                pv = ps.f32(b)
                K.group(PE, [(lambda k=k: nc.tensor.matmul(pv, lhsT=cab[:, k, :], rhs=wc[:, k, :],
                                                         start=(k == 0), stop=(k == 7))) for k in range(8)],
                        reads=[par_tk, wc_tk], writes=btk)
                K.op(DVE, lambda: nc.vector.tensor_tensor(out=modbc[:, cb * 512:(cb + 1) * 512], in0=pv, in1=bc[:], op=ALU.add),
                     reads=btk + [bc_tk, mod_tk], writes=[mod_tk])
                ps.free(b)
            for k in range(3):
                for which, v in (("sh", 3 * k), ("sc", 3 * k + 1)):
                    K.op(DVE, lambda: nc.vector.tensor_tensor(out=dg[:], in0=modbc[:, v * D:(v + 1) * D].rearrange("p (j q) -> p j q", q=128),
                                                              in1=identf[:].unsqueeze(1).broadcast_to([128, 8, 128]), op=ALU.mult),
                         reads=[mod_tk, setup_tk, par_tk], writes=[par_tk])
                    dstc = sh_col if which == "sh" else a_col
                    K.op(DVE, lambda: nc.vector.tensor_reduce(out=dstc[:, k, :], in_=dg[:], axis=AX.X, op=ALU.add),
                         reads=[par_tk], writes=[par_tk])
                K.op(DVE, lambda: nc.vector.scalar_tensor_tensor(out=a_col[:, k, :], in0=a_col[:, k, :], scalar=1.0, in1=npre[:, k, :],
                                                                 op0=ALU.add, op1=ALU.mult),
                     reads=[par_tk, setup_tk], writes=[par_tk])
                resw = 1.0 if k == 1 else 0.5
                K.op(DVE, lambda: nc.vector.scalar_tensor_tensor(out=G[:, k, :], in0=modbc[:, (3 * k + 2) * D:(3 * k + 3) * D], scalar=resw,
                                                                 in1=npost[:, k, :], op0=ALU.mult, op1=ALU.mult),
                     reads=[mod_tk, setup_tk, par_tk], writes=[par_tk])

            posf = K.sb("posf", [128, NT], F32, ss)
            ang = K.sb("ang", [128, NT, 32], F32, ss)
            rr = K.sb("rr", [128, NT, 32], F32, ss)
            rc = K.sb("rc", [128, NT, 32], F32, ss)
            kf = K.sb("kf", [128, NT, 32], F32, ss)
            ki = K.sb("ki", [128, NT, 32], I32, ss)
            mk = K.sb("mk", [128, NT, 32], F32, ss)
            rt = Tk("ropetmp")

            def V(fn, reads=(), writes=()):
                K.op(DVE, fn, reads=list(reads) + [rt, setup_tk], writes=list(writes) + [rt])

            V(lambda: nc.vector.tensor_copy(out=posf[:], in_=posi[:]))
            V(lambda: nc.vector.tensor_tensor(out=ang[:], in0=posf[:].unsqueeze(2).broadcast_to([128, NT, 32]),
                                              in1=invf[:].unsqueeze(1).broadcast_to([128, NT, 32]), op=ALU.mult))
            V(lambda: nc.vector.tensor_scalar(out=kf[:], in0=ang[:], scalar1=1.0 / TWO_PI, scalar2=None, op0=ALU.mult))
            V(lambda: nc.vector.tensor_copy(out=ki[:], in_=kf[:]))
            V(lambda: nc.vector.tensor_copy(out=kf[:], in_=ki[:]))
            C1 = 6.28125
            C2 = TWO_PI - 6.28125
            V(lambda: nc.vector.scalar_tensor_tensor(out=rr[:], in0=kf[:], scalar=-C1, in1=ang[:], op0=ALU.mult, op1=ALU.add))
            V(lambda: nc.vector.scalar_tensor_tensor(out=rr[:], in0=kf[:], scalar=-C2, in1=rr[:], op0=ALU.mult, op1=ALU.add))

            def fold(t):
                V(lambda: nc.vector.tensor_scalar(out=mk[:], in0=t[:], scalar1=PI, scalar2=-TWO_PI, op0=ALU.is_gt, op1=ALU.mult))
                V(lambda: nc.vector.tensor_tensor(out=t[:], in0=t[:], in1=mk[:], op=ALU.add))
                V(lambda: nc.vector.tensor_scalar(out=mk[:], in0=t[:], scalar1=-PI, scalar2=TWO_PI, op0=ALU.is_lt, op1=ALU.mult))
                V(lambda: nc.vector.tensor_tensor(out=t[:], in0=t[:], in1=mk[:], op=ALU.add))
                V(lambda: nc.vector.tensor_scalar(out=t[:], in0=t[:], scalar1=PI, scalar2=-PI, op0=ALU.min, op1=ALU.max))

            fold(rr)
            V(lambda: nc.vector.tensor_scalar(out=rc[:], in0=rr[:], scalar1=PI / 2, scalar2=None, op0=ALU.add))
            fold(rc)
            K.op(ACT, lambda: nc.scalar.activation(out=sinT[:], in_=rr[:], func=AF.Sin), reads=[rt], writes=[rope_tk])
            K.op(ACT, lambda: nc.scalar.activation(out=cosT[:], in_=rc[:], func=AF.Sin), reads=[rt], writes=[rope_tk])
            rsem = K.newsem("ropest")
            K.dma(SP, out=rope_d[:, 0, :], in_=cosT[:].rearrange("p t f -> p (t f)"), reads=[rope_tk], writes=[dtk("rope", 0)], sem=rsem)
            K.dma(SP, out=rope_d[:, 1, :], in_=sinT[:].rearrange("p t f -> p (t f)"), reads=[rope_tk], writes=[dtk("rope", 0)], sem=rsem)
            K.barrier()

        def issue_cast_dma(dst, src, tk, sem):
            K.dma(POOL, out=dst, in_=src, reads=[], writes=[tk], sem=sem)

        eps_t = K.sb("eps_t", [128, 2], F32)
        const2_tk = Tk("const2")
        K.op(DVE, lambda: nc.vector.memset(eps_t[:, 0:1], EPS), writes=[const2_tk])
        K.op(DVE, lambda: nc.vector.memset(eps_t[:, 1:2], 0.0), writes=[const2_tk])

        def rstd_ops(st, st_tk, n, lnexp):
            if lnexp:
                K.op(ACT, lambda: nc.scalar.activation(out=st[:, 1:2], in_=st[:, 0:1], func=AF.Ln, scale=1.0 / n, bias=eps_t[:, 0:1]),
                     reads=[st_tk, const2_tk], writes=[st_tk])
                K.op(ACT, lambda: nc.scalar.activation(out=st[:, 2:3], in_=st[:, 1:2], func=AF.Exp, scale=-0.5),
                     reads=[st_tk], writes=[st_tk])
            else:
                K.op(ACT, lambda: nc.scalar.activation(out=st[:, 1:2], in_=st[:, 0:1], func=AF.Sqrt, scale=1.0 / n, bias=eps_t[:, 0:1]),
                     reads=[st_tk, const2_tk], writes=[st_tk])
                K.op(DVE, lambda: nc.vector.reciprocal(out=st[:, 2:3], in_=st[:, 1:2]), reads=[st_tk], writes=[st_tk])

        class Pro:
            def __init__(self, es_, src, srcname, kidx, lnexp, tag, nxin=2):
                self.src, self.srcname, self.kidx, self.lnexp = src, srcname, kidx, lnexp
                self.xin_rot = Rot(K, es_, tag + "xin", [128, D], F32, nxin, dma=True)
                self.xn_rot = Rot(K, es_, tag + "xn", [128, D], BF16, 2)
                self.st_rot = Rot(K, es_, tag + "st", [128, 4], F32, 4)
                self.tmpf = K.sb(tag + "tmpf", [128, 8, 128], F32, es_)
                self.tmpf_tk = Tk(tag + "tmpf")

            def elem(self, tile):
                xin, xin_tk, xin_sem = self.xin_rot.next()
                K.dma(SP, out=xin[:], in_=self.src[tile * 128:(tile + 1) * 128, :], reads=[dtk(self.srcname, tile)],
                      writes=[xin_tk], sem=xin_sem)
                st, st_tk, _ = self.st_rot.next()
                xn, xn_tk, _ = self.xn_rot.next()
                K.op(ACT, lambda: nc.scalar.activation(out=xn[:], in_=xin[:], func=AF.Square, accum_out=st[:, 0:1]),
                     reads=[xin_tk], writes=[xn_tk, st_tk])
                rstd_ops(st, st_tk, D, self.lnexp)
                K.op(DVE, lambda: nc.vector.tensor_scalar(out=xn[:], in0=xin[:], scalar1=st[:, 2:3], scalar2=None, op0=ALU.mult),
                     reads=[xin_tk, st_tk], writes=[xn_tk])
                return xn, xn_tk

            def pe(self, xn, xn_tk, h, h_tks, tt):
                b, tks = ps.alloc(1)
                tv = ps.bf(b)
                K.group(PE, [(lambda dc=dc: nc.tensor.transpose(tv[:, dc * 128:(dc + 1) * 128], xn[:, dc * 128:(dc + 1) * 128], ident[:]))
                             for dc in range(8)], reads=[xn_tk, const_tk], writes=tks)
                kidx = self.kidx
                K.op(DVE, lambda: nc.vector.tensor_tensor(out=self.tmpf[:], in0=tv.rearrange("p (c t) -> p c t", t=128),
                                                          in1=a_col[:, kidx, :].unsqueeze(2).broadcast_to([128, 8, 128]), op=ALU.mult),
                     reads=tks + [par_tk], writes=[self.tmpf_tk])
                ps.free(b)
                K.op(DVE, lambda: nc.vector.tensor_tensor(out=h[:, :, tt * 128:(tt + 1) * 128], in0=self.tmpf[:],
                                                          in1=sh_col[:, kidx, :].unsqueeze(2).broadcast_to([128, 8, 128]), op=ALU.add),
                     reads=[self.tmpf_tk, par_tk], writes=h_tks)

        class Epi:
            def __init__(self, es_, src, srcname, dst, dstname, kidx, lnexp, tag, junk):
                self.src, self.srcname, self.dst, self.dstname, self.kidx, self.lnexp = src, srcname, dst, dstname, kidx, lnexp
                self.junk = junk
                self.xre_rot = Rot(K, es_, tag + "xre", [128, D], F32, 1, dma=True)
                self.ob_rot = Rot(K, es_, tag + "ob", [128, D], F32, 1, dma=True)
                self.st_rot = Rot(K, es_, tag + "est", [128, 4], F32, 4)

            def run(self, tile, yb, ytk):
                yv = ps.f32(yb, 2)
                kidx = self.kidx
                xre, xre_tk, xre_sem = self.xre_rot.next()
                K.dma(SP, out=xre[:], in_=self.src[tile * 128:(tile + 1) * 128, :], reads=[dtk(self.srcname, tile)], writes=[xre_tk],
                      sem=xre_sem)
                st, st_tk, _ = self.st_rot.next()
                ob, ob_tk, ob_sem = self.ob_rot.next()
                jk, jk_tk, _ = self.junk.next()
                K.op(ACT, lambda: nc.scalar.activation(out=jk[:], in_=yv, func=AF.Square, accum_out=st[:, 0:1]),
                     reads=ytk, writes=[jk_tk, st_tk])
                rstd_ops(st, st_tk, D, self.lnexp)
                K.op(DVE, lambda: nc.vector.scalar_tensor_tensor(out=ob[:], in0=yv, scalar=st[:, 2:3], in1=G[:, kidx, :],
                                                                 op0=ALU.mult, op1=ALU.mult),
                     reads=ytk + [st_tk, par_tk], writes=[ob_tk])
                ps.free(yb, 2)
                K.op(POOL, lambda: nc.gpsimd.tensor_tensor(out=ob[:], in0=ob[:], in1=xre[:], op=ALU.add),
                     reads=[ob_tk, xre_tk], writes=[ob_tk])
                K.dma(POOL, out=self.dst[tile * 128:(tile + 1) * 128, :], in_=ob[:], reads=[ob_tk], writes=[dtk(self.dstname, tile)],
                      sem=ob_sem)

        def interleave(*gens):
            gens = list(gens)
            while gens:
                for g in list(gens):
                    try:
                        next(g)
                    except StopIteration:
                        gens.remove(g)

        def ffn_phase(fi, kidx, src, srcname, dst, dstname):
            with ExitStack() as pe_:
                Win = K.sb("Win", [128, 8, 2 * DFF], BF16, pe_)
                Wout = K.sb("Wout", [128, NF, D], BF16, pe_)
                groups = [(0, 6), (6, 12), (12, 17), (17, 22)]
                win_tk = [Tk("win%d" % g) for g in range(4)]
                wout_tk = [Tk("wout%d" % g) for g in range(2)]
                if fi == 0:
                    wv = win_d[fi].rearrange("(k p) n -> p k n", p=128)
                    wov = wout_d[fi].rearrange("(f p) n -> p f n", p=128)
                    for g, (f0, f1) in enumerate(groups):
                        sem = K.newsem("win%d_%d" % (fi, g))
                        for off in (0, DFF):
                            issue_cast_dma(Win[:, :, off + f0 * 128:off + f1 * 128], wv[:, :, off + f0 * 128:off + f1 * 128],
                                           win_tk[g], sem)
                    for g, (f0, f1) in enumerate([(0, 11), (11, 22)]):
                        sem = K.newsem("wout%d_%d" % (fi, g))
                        issue_cast_dma(Wout[:, f0:f1, :], wov[:, f0:f1, :], wout_tk[g], sem)

                    pcs = K.newsem("precast")

                    def precast(blk):
                        def cp(dst, src_, name):
                            K.dma(POOL, out=dst, in_=src_, reads=[], writes=[dtk(name, 0)], sem=pcs)
                        if blk == 2:
                            cp(wmi_s, wmi_d, "wmi_s")
                        elif blk == 3:
                            cp(wmo_s, wmo_d, "wmo_s")
                            cp(w2out_s[0:1408, :], wout_d[1][0:1408, :], "w2out_s")
                        elif blk == 4:
                            cp(w2out_s[1408:2816, :], wout_d[1][1408:2816, :], "w2out_s")
                            cp(w2in_s[0:256, :], win_d[1][0:256, :], "w2in_s")
                        elif blk == 5:
                            cp(w2in_s[256:512, :], win_d[1][256:512, :], "w2in_s")
                            cp(w2in_s[512:768, :], win_d[1][512:768, :], "w2in_s")
                        elif blk == 6:
                            cp(w2in_s[768:1024, :], win_d[1][768:1024, :], "w2in_s")
                else:
                    wv = w2in_s.rearrange("(k p) n -> p k n", p=128)
                    wov = w2out_s.rearrange("(f p) n -> p f n", p=128)
                    qi = 0
                    for g, (f0, f1) in enumerate(groups):
                        sem = K.newsem("win%d_%d" % (fi, g))
                        for off in (0, DFF):
                            K.dma(ACT, out=Win[:, :, off + f0 * 128:off + f1 * 128],
                                  in_=wv[:, :, off + f0 * 128:off + f1 * 128], reads=[dtk("w2in_s", 0)], writes=[win_tk[g]], sem=sem)
                            qi += 1
                    for g, (f0, f1) in enumerate([(0, 11), (11, 22)]):
                        sem = K.newsem("wout%d_%d" % (fi, g))
                        K.dma(ACT, out=Wout[:, f0:f1, :], in_=wov[:, f0:f1, :], reads=[dtk("w2out_s", 0)],
                              writes=[wout_tk[g]], sem=sem)
                        qi += 1

                def gof(f):
                    for g, (f0, f1) in enumerate(groups):
                        if f0 <= f < f1:
                            return g

                pro = Pro(pe_, src, srcname, kidx, False, "f")
                epi = Epi(pe_, src, srcname, dst, dstname, kidx, False, "f", pro.xn_rot)
                hs = [K.sb("h%d" % i, [128, 8, TB], BF16, pe_) for i in range(2)]
                h_tkss = [[Tk("h%d" % i)] for i in range(2)]
                act = K.sb("act", [128, NF, TB], BF16, pe_)
                act_tk = [Tk("act%d" % f) for f in range(NF)]
                sg_rot = Rot(K, pe_, "sg", [128, TB], F32, 1)

                for tt in range(4):
                    xn, xn_tk = pro.elem(tt)
                    pro.pe(xn, xn_tk, hs[0], h_tkss[0], tt)
                for blk in range(NB):
                    if fi == 0:
                        precast(blk)
                    h = hs[blk % 2]
                    h_tks = h_tkss[blk % 2]
                    hn = hs[(blk + 1) % 2]
                    hn_tks = h_tkss[(blk + 1) % 2]
                    pend = {}
                    for f in range(NF):
                        bg, tg = ps.alloc(1)
                        bu, tu = ps.alloc(1)
                        pg = ps.f32(bg)
                        pu = ps.f32(bu)
                        fns = []
                        for k in range(8):
                            fns.append(lambda k=k: nc.tensor.matmul(pg, lhsT=Win[:, k, f * 128:(f + 1) * 128], rhs=h[:, k, :],
                                                                    start=(k == 0), stop=(k == 7)))
                        for k in range(8):
                            fns.append(lambda k=k: nc.tensor.matmul(pu, lhsT=Win[:, k, DFF + f * 128:DFF + (f + 1) * 128],
                                                                    rhs=h[:, k, :], start=(k == 0), stop=(k == 7)))
                        K.group(PE, fns, reads=h_tks + [win_tk[gof(f)]], writes=tg + tu)
                        sg, sg_tk, _ = sg_rot.next()
                        K.op(ACT, lambda: nc.scalar.activation(out=sg[:], in_=pg, func=AF.Silu), reads=tg, writes=[sg_tk])
                        K.op(DVE, lambda: nc.vector.tensor_tensor(out=act[:, f, :], in0=sg[:], in1=pu, op=ALU.mult),
                             reads=[sg_tk] + tu, writes=[act_tk[f]])
                        ps.free(bg)
                        ps.free(bu)
                        if blk + 1 < NB:
                            if f % 4 == 0 and f // 4 < 4:
                                tt = f // 4
                                pend[tt] = pro.elem((blk + 1) * 4 + tt)
                            if f >= 7 and (f - 7) % 4 == 0 and (f - 7) // 4 < 4:
                                tt = (f - 7) // 4
                                pro.pe(pend[tt][0], pend[tt][1], hn, hn_tks, tt)
                    for tt in range(4):
                        yb, ytk = ps.alloc(2)
                        fns = []
                        for n in range(2):
                            o = ps.f32(yb + n)
                            for f in range(NF):
                                fns.append(lambda o=o, n=n, f=f: nc.tensor.matmul(
                                    o, lhsT=act[:, f, tt * 128:(tt + 1) * 128], rhs=Wout[:, f, n * 512:(n + 1) * 512],
                                    start=(f == 0), stop=(f == NF - 1)))
                        K.group(PE, fns, reads=act_tk + wout_tk, writes=ytk)
                        epi.run(blk * 4 + tt, yb, ytk)
                K.barrier()

        def mixer_phase(src, srcname, dst, dstname):
            kidx = 1
            with ExitStack() as pe_:
                Wmi = K.sb("Wmi", [128, 8, 2816], BF16, pe_)
                Wmo = K.sb("Wmo", [128, 8, D], BF16, pe_)
                wmi_tk = Tk("wmi")
                wmo_tk = Tk("wmo")
                wmiv = wmi_s.rearrange("(k p) n -> p k n", p=128)
                wmov = wmo_s.rearrange("(k p) n -> p k n", p=128)
                s1 = K.newsem("wmi")
                K.dma(ACT, out=Wmi[:, :, 0:1408], in_=wmiv[:, :, 0:1408], reads=[dtk("wmi_s", 0)], writes=[wmi_tk], sem=s1)
                K.dma(ACT, out=Wmi[:, :, 1408:2816], in_=wmiv[:, :, 1408:2816], reads=[dtk("wmi_s", 0)], writes=[wmi_tk], sem=s1)
                s2 = K.newsem("wmo")
                K.dma(ACT, out=Wmo[:], in_=wmov, reads=[dtk("wmo_s", 0)], writes=[wmo_tk], sem=s2)
                amask = K.sb("amask", [128, 2, 2, 512], BF16, pe_)
                hmask = K.sb("hmask", [128, 4, 128], BF16, pe_)
                rmask = K.sb("rmask", [128, 512], F32, pe_)
                cs_rot = Rot(K, pe_, "mcs", [128, 2, 128], F32, 2, dma=True)
                mc_tk = Tk("mconst")
                mcs = K.newsem("mconst")
                for (t_, d_) in ((amask[:], amask_d), (hmask[:], hmask_d), (rmask[:], rmask_d)):
                    K.dma(SP, out=t_, in_=d_, reads=[], writes=[mc_tk], sem=mcs)

                pro = Pro(pe_, src, srcname, kidx, True, "m", nxin=1)
                epi = Epi(pe_, src, srcname, dst, dstname, kidx, True, "m", pro.xn_rot)
                h = K.sb("mh", [128, 8, TB], BF16, pe_)
                h_tks = [Tk("mh")]
                q2 = K.sb("q2", [128, 4, TB], F32, pe_)
                ff = K.sb("ff", [128, 4, TB], F32, pe_)
                bb = K.sb("bb", [128, 4, TB], F32, pe_)
                q2_tk = [Tk("q2_%d" % c) for c in range(4)]
                ff_tk = [Tk("ff_%d" % c) for c in range(4)]
                bb_tk = [Tk("bb_%d" % c) for c in range(4)]
                tA = Rot(K, pe_, "tA", [128, TB], F32, 2)
                tB = Rot(K, pe_, "tB", [128, TB], F32, 2)
                qt = K.sb("qt", [128, 4, TB], BF16, pe_)
                kt = K.sb("kt", [128, 4, TB], BF16, pe_)
                k2T = K.sb("k2T", [128, 4, TB], BF16, pe_)
                gs = K.sb("gs", [128, 4, TB], BF16, pe_)
                dec = K.sb("dec", [128, 4, 8], F32, pe_)
                hb_tk = [Tk("hgblk%d" % c) for c in range(4)]
                ktw_tk = [Tk("ktw%d" % c) for c in range(4)]
                k2w_tk = [Tk("k2w%d" % c) for c in range(4)]
                catT = K.sb("catT", [128, 8, TB], BF16, pe_)
                cat_tk = [Tk("cat%d" % c) for c in range(8)]
                NCH = 2
                RING = 5
                qkr_r = Rot(K, pe_, "qkr", [128, 640], BF16, 1)
                vtm = K.sb("vtm", [128, 4, 512], BF16, pe_)
                vtm_tk = [Tk("vtm%d" % i) for i in range(4)]
                vbe = K.sb("vbe", [128, RING, 2, 65], BF16, pe_)
                vb_tk = [Tk("vb%d" % i) for i in range(RING)]
                qT_r = Rot(K, pe_, "qT", [128, 8, 128], BF16, 4)
                kTa = K.sb("kTa", [128, 2, RING * 128], BF16, pe_)
                kT_tk = [Tk("kT%d" % i) for i in range(RING)]
                PT_r = Rot(K, pe_, "PTs", [128, 2, 2, 512], BF16, NCH)
                atm_r = Rot(K, pe_, "atm", [128, 512], BF16, NCH)
                ra_r = Rot(K, pe_, "ra", [128, 10, 32], F32, 1)
                rb_r = Rot(K, pe_, "rb", [128, 10, 32], F32, 1)
                dn_r = Rot(K, pe_, "dn", [128, 2, 8], F32, NCH)
                esink = K.sb("esink", [128, 8], F32, pe_)
                negc = K.sb("negc", [128, 1], F32, pe_)
                SHIFT = 16.0
                k2A = K.sb("k2A", [128, 4, 128], BF16, pe_)
                k2B = K.sb("k2B", [128, 4, 128], BF16, pe_)
                k2A_tk = Tk("k2A")
                k2B_tk = Tk("k2B")
                scm = K.sb("scm", [128, 4, 128], BF16, pe_)
                scm_tk = Tk("scm")
                st32 = K.sb("st32", [128, 4, 128], F32, pe_)
                st32_tk = [Tk("st32")]
                stb_rot = Rot(K, pe_, "stb", [128, 4, 128], BF16, 2)
                sqr = K.sb("sqr", [128, 512], BF16, pe_)
                sqr_tk = Tk("sqr")
                rsb = K.sb("rsb", [128, 512], F32, pe_)
                rsb_tk = Tk("rsb")
                t1 = K.sb("t1", [128, 512], F32, pe_)
                t1_tk = Tk("t1")

                for (qT_, qT_tk_, _) in qT_r.items:
                    K.op(POOL, lambda: nc.gpsimd.memset(qT_[:], 0.0), writes=[qT_tk_])
                K.op(POOL, lambda: nc.gpsimd.memset(kTa[:], 0.0), writes=kT_tk)
                K.op(POOL, lambda: nc.gpsimd.memset(vbe[:], 1.0), writes=vb_tk)
                K.op(DVE, lambda: nc.vector.memset(negc[:], -SHIFT), writes=[mc_tk])
                K.op(DVE, lambda: nc.vector.tensor_scalar(out=esink[:], in0=sinks[:], scalar1=-SHIFT, scalar2=None, op0=ALU.add),
                     reads=[const_tk, mc_tk], writes=[mc_tk])
                K.op(ACT, lambda: nc.scalar.activation(out=esink[:], in_=esink[:], func=AF.Exp), reads=[mc_tk], writes=[mc_tk])
                K.op(POOL, lambda: nc.gpsimd.memset(st32[:], 0.0), writes=st32_tk)
                stb0, stb0_tk, _ = stb_rot.next()
                K.op(POOL, lambda: nc.gpsimd.memset(stb0[:], 0.0), writes=[stb0_tk])
                state = {"stb": (stb0, stb0_tk)}
                QSC = float(0.5 * 128 ** -0.5)

                def galloc(n):
                    tries = 0
                    while True:
                        r = ps.try_alloc(n)
                        if r is not None:
                            return r
                        tries += 1
                        assert tries < 10000, "PSUM livelock"
                        yield

                def fm_proj(col0, b, tks):
                    pv = ps.f32(b)
                    K.group(PE, [(lambda k=k: nc.tensor.matmul(pv, lhsT=Wmi[:, k, col0:col0 + 128], rhs=h[:, k, :],
                                                             start=(k == 0), stop=(k == 7))) for k in range(8)],
                            reads=h_tks + [wmi_tk], writes=tks)
                    return pv

                def gen_prologue(blk):
                    for tt in range(4):
                        xn, xn_tk = pro.elem(blk * 4 + tt)
                        yield
                        pro.pe(xn, xn_tk, h, h_tks, tt)
                        yield

                def gen_P(blk):
                    for tt in range(4):
                        cs = slice(tt * 128, (tt + 1) * 128)
                        hb_, htk = yield from galloc(1)
                        K.group(PE, [(lambda k=k: nc.tensor.matmul(ps.f32(hb_), lhsT=h[:, k, cs], rhs=Wmi[:, k, 1792:2304],
                                                                 start=(k == 0), stop=(k == 7))) for k in range(8)],
                                reads=h_tks + [wmi_tk], writes=htk)
                        K.op(ACT, lambda: nc.scalar.activation(out=vtm[:, tt, :], in_=ps.f32(hb_), func=AF.Copy), reads=htk, writes=[vtm_tk[tt]])
                        ps.free(hb_)
                        yield
                    for c in range(4):
                        b, tks = yield from galloc(1)
                        pv = fm_proj(1280 + c * 128, b, tks)
                        a_, a_tk, _ = tA.next()
                        K.op(ACT, lambda: nc.scalar.activation(out=a_[:], in_=pv, func=AF.Tanh, scale=0.5), reads=tks, writes=[a_tk])
                        K.op(DVE, lambda: nc.vector.tensor_scalar(out=ff[:, c, :], in0=a_[:], scalar1=c01[:, 1, c:c + 1], scalar2=c01[:, 0, c:c + 1],
                                                                  op0=ALU.mult, op1=ALU.add), reads=[a_tk, par_tk], writes=[ff_tk[c]])
                        ps.free(b)
                        yield
                        b, tks = yield from galloc(1)
                        pv = fm_proj(768 + c * 128, b, tks)
                        a_, a_tk, _ = tA.next()
                        K.op(ACT, lambda: nc.scalar.activation(out=a_[:], in_=pv, func=AF.Tanh, scale=0.5), reads=tks, writes=[a_tk])
                        K.op(DVE, lambda: nc.vector.scalar_tensor_tensor(out=q2[:, c, :], in0=a_[:], scalar=1.0, in1=pv, op0=ALU.add, op1=ALU.mult),
                             reads=[a_tk] + tks, writes=[q2_tk[c]])
                        ps.free(b)
                        yield
                        b, tks = yield from galloc(1)
                        pv = fm_proj(2304 + c * 128, b, tks)
                        a_, a_tk, _ = tA.next()
                        K.op(ACT, lambda: nc.scalar.activation(out=a_[:], in_=pv, func=AF.Tanh, scale=0.5), reads=tks, writes=[a_tk])
                        K.op(DVE, lambda: nc.vector.scalar_tensor_tensor(out=gs[:, c, :], in0=a_[:], scalar=1.0, in1=pv, op0=ALU.add, op1=ALU.mult),
                             reads=[a_tk] + tks, writes=[hb_tk[c]])
                        ps.free(b)
                        yield
                    for c in range(4):
                        wtk = [hb_tk[c]]
                        a_, a_tk, _ = tA.next()
                        K.op(ACT, lambda: nc.scalar.activation(out=a_[:], in_=ff[:, c, :], func=AF.Ln), reads=[ff_tk[c]], writes=[a_tk])
                        K.op(DVE, lambda: nc.vector.tensor_tensor_scan(out=bb[:, c, :], data0=rmask[:], data1=a_[:], initial=0.0,
                                                                       op0=ALU.mult, op1=ALU.add),
                             reads=[a_tk, mc_tk], writes=[bb_tk[c]])
                        K.op(DVE, lambda: nc.vector.tensor_scalar(out=ff[:, c, :], in0=ff[:, c, :], scalar1=-1.0, scalar2=1.0, op0=ALU.mult, op1=ALU.add),
                             reads=[ff_tk[c], a_tk], writes=[ff_tk[c]])
                        yield
                        e_, e_tk, _ = tB.next()
                        K.op(ACT, lambda: nc.scalar.activation(out=e_[:], in_=bb[:, c, :], func=AF.Exp), reads=[bb_tk[c]], writes=[e_tk])
                        K.op(DVE, lambda: nc.vector.scalar_tensor_tensor(out=qt[:, c, :], in0=q2[:, c, :], scalar=QSC, in1=e_[:],
                                                                         op0=ALU.mult, op1=ALU.mult),
                             reads=[q2_tk[c], e_tk], writes=wtk)
                        K.op(DVE, lambda: nc.vector.tensor_copy(out=dec[:, c, :],
                                                                in_=e_[:].rearrange("p (n t) -> p n t", t=64)[:, :, 63]),
                             reads=[e_tk], writes=wtk)
                        yield
                        x_, x_tk, _ = tB.next()
                        K.op(ACT, lambda: nc.scalar.activation(out=x_[:], in_=bb[:, c, :], func=AF.Exp, scale=-1.0), reads=[bb_tk[c]], writes=[x_tk])
                        K.op(DVE, lambda: nc.vector.tensor_tensor(out=kt[:, c, :], in0=ff[:, c, :], in1=x_[:], op=ALU.mult),
                             reads=[ff_tk[c], x_tk], writes=[ktw_tk[c]])
                        yield
                        d_, d_tk, _ = tA.next()
                        b3 = bb[:, c, :].rearrange("p (n t) -> p n t", t=64)
                        K.op(DVE, lambda: nc.vector.tensor_tensor(out=d_[:].rearrange("p (n t) -> p n t", t=64),
                                                                  in0=b3[:, :, 63:64].broadcast_to([128, 8, 64]), in1=b3, op=ALU.subtract),
                             reads=[bb_tk[c]], writes=[d_tk])
                        K.op(ACT, lambda: nc.scalar.activation(out=d_[:], in_=d_[:], func=AF.Exp), reads=[d_tk], writes=[d_tk])
                        K.op(DVE, lambda: nc.vector.tensor_tensor(out=k2T[:, c, :], in0=ff[:, c, :], in1=d_[:], op=ALU.mult),
                             reads=[ff_tk[c], d_tk], writes=[k2w_tk[c]])
                        yield

                pre_done = {}
                blkst = {}

                def gen_Apre(blk, tts):
                    cs_t, cs_tk, cs_sem = cs_rot.next()
                    K.dma(SP, out=cs_t[:, 0, :], in_=rope_d[:, 0, blk * 128:(blk + 1) * 128], reads=[dtk("rope", 0)], writes=[cs_tk], sem=cs_sem)
                    K.dma(SP, out=cs_t[:, 1, :], in_=rope_d[:, 1, blk * 128:(blk + 1) * 128], reads=[dtk("rope", 0)], writes=[cs_tk], sem=cs_sem)
                    for tt in tts:
                        tile = blk * 4 + tt
                        cs = slice(tt * 128, (tt + 1) * 128)
                        slot = tile % RING
                        qkr, qkr_tk, _ = qkr_r.next()
                        qT, qT_tk, _ = qT_r.items[tt]
                        ra, ra_tk, _ = ra_r.next()
                        rb, rb_tk, _ = rb_r.next()
                        pb, ptk = yield from galloc(2)
                        fns = []
                        for k in range(8):
                            fns.append(lambda k=k: nc.tensor.matmul(ps.f32(pb), lhsT=h[:, k, cs], rhs=Wmi[:, k, 0:512],
                                                                    start=(k == 0), stop=(k == 7)))
                        for k in range(8):
                            fns.append(lambda k=k: nc.tensor.matmul(ps.f32(pb + 1)[:, 0:256], lhsT=h[:, k, cs], rhs=Wmi[:, k, 512:768],
                                                                    start=(k == 0), stop=(k == 7)))
                        K.group(PE, fns, reads=h_tks + [wmi_tk], writes=ptk)
                        yield
                        K.op(ACT, lambda: nc.scalar.activation(out=vbe[:, slot, :, 0:64],
                                                               in_=ps.f32(pb + 1)[:, 128:256].rearrange("p (g d) -> p g d", d=64), func=AF.Copy),
                             reads=ptk, writes=[vb_tk[slot]])
                        qk3 = ps.f32(pb, 2)[:, 0:640].rearrange("p (h d) -> p h d", d=64)
                        cb_ = cs_t[:, 0, tt * 32:(tt + 1) * 32].unsqueeze(1).broadcast_to([128, 10, 32])
                        sb_ = cs_t[:, 1, tt * 32:(tt + 1) * 32].unsqueeze(1).broadcast_to([128, 10, 32])
                        o3 = qkr[:].rearrange("p (h d) -> p h d", d=64)
                        K.op(DVE, lambda: nc.vector.tensor_tensor(out=ra[:], in0=qk3[:, :, 0:32], in1=cb_, op=ALU.mult), reads=ptk + [cs_tk], writes=[ra_tk])
                        K.op(DVE, lambda: nc.vector.tensor_tensor(out=rb[:], in0=qk3[:, :, 32:64], in1=sb_, op=ALU.mult), reads=ptk + [cs_tk], writes=[rb_tk])
                        K.op(DVE, lambda: nc.vector.tensor_tensor(out=o3[:, :, 0:32], in0=ra[:], in1=rb[:], op=ALU.subtract),
                             reads=[ra_tk, rb_tk], writes=[qkr_tk])
                        yield
                        K.op(DVE, lambda: nc.vector.tensor_tensor(out=ra[:], in0=qk3[:, :, 32:64], in1=cb_, op=ALU.mult), reads=ptk + [cs_tk], writes=[ra_tk])
                        K.op(DVE, lambda: nc.vector.tensor_tensor(out=rb[:], in0=qk3[:, :, 0:32], in1=sb_, op=ALU.mult), reads=ptk + [cs_tk], writes=[rb_tk])
                        K.op(DVE, lambda: nc.vector.tensor_tensor(out=o3[:, :, 32:64], in0=ra[:], in1=rb[:], op=ALU.add),
                             reads=[ra_tk, rb_tk], writes=[qkr_tk])
                        ps.free(pb, 2)
                        yield
                        tb2, ttk = yield from galloc(1)
                        tv = ps.bf(tb2)
                        fns = [(lambda j=j: nc.tensor.transpose(tv[:, j * 128:(j + 1) * 128], qkr[:, j * 128:(j + 1) * 128], ident[:]))
                               for j in range(5)]
                        K.group(PE, fns, reads=[qkr_tk, const_tk], writes=ttk)
                        qv = tv[:, 0:512].rearrange("p (j t) -> p j t", t=128)
                        qT4 = qT[:].rearrange("p (j two) t -> p j two t", two=2)
                        K.op(DVE, lambda: nc.vector.tensor_copy(out=qT4[0:64, :, 0, :], in_=qv[0:64]), reads=ttk, writes=[qT_tk])
                        K.op(DVE, lambda: nc.vector.tensor_copy(out=qT4[0:64, :, 1, :], in_=qv[64:128]), reads=ttk, writes=[qT_tk])
                        K.op(ACT, lambda: nc.scalar.activation(out=kTa[0:64, 0, slot * 128:(slot + 1) * 128], in_=tv[0:64, 512:640], func=AF.Copy),
                             reads=ttk, writes=[kT_tk[slot]])
                        K.op(ACT, lambda: nc.scalar.activation(out=kTa[0:64, 1, slot * 128:(slot + 1) * 128], in_=tv[64:128, 512:640],
                                                               func=AF.Copy), reads=ttk, writes=[kT_tk[slot]])
                        ps.free(tb2)
                        pre_done[tile] = True
                        yield

                def gen_Amain(blk, tt):
                    if True:
                        tile = blk * 4 + tt
                        cs = slice(tt * 128, (tt + 1) * 128)
                        slot = tile % RING
                        pslot = (tile - 1) % RING
                        tries = 0
                        while not pre_done.get(tile, False):
                            tries += 1
                            assert tries < 10000, "attention main chain never released"
                            yield
                        qT, qT_tk, _ = qT_r.items[tt]
                        PTs, PT_tk, _ = PT_r.next()
                        atm, atm_tk, _ = atm_r.next()
                        dn, dn_tk, _ = dn_r.next()
                        kbs = [(1, slot)] if tile == 0 else [(0, pslot), (1, slot)]
                        var = 0
                        for g in range(2):
                            sb2, stk = yield from galloc(2)
                            fns = []
                            for (kb, sl) in kbs:
                                o = ps.f32(sb2 + kb)
                                fns.append(lambda o=o, g=g, sl=sl: nc.tensor.matmul(
                                    o, lhsT=kTa[:, g, sl * 128:(sl + 1) * 128], rhs=qT[:, g * 4:(g + 1) * 4, :].rearrange("p h t -> p (h t)"),
                                    start=True, stop=False))
                                fns.append(lambda o=o, kb=kb: nc.tensor.matmul(o, lhsT=ident[:], rhs=amask[:, var, kb, :], start=False, stop=True))
                            K.group(PE, fns, reads=[qT_tk, const_tk, mc_tk] + kT_tk, writes=stk)
                            for (kb, sl) in kbs:
                                K.op(ACT, lambda: nc.scalar.activation(out=PTs[:, g, kb, :], in_=ps.f32(sb2 + kb), func=AF.Exp, scale=0.125,
                                                                       bias=negc[:, 0:1]),
                                     reads=[stk[kb], mc_tk], writes=[PT_tk])
                            ps.free(sb2, 2)
                            yield
                        ob2, otk = yield from galloc(2)
                        fns = []
                        for hh in range(8):
                            g, hq = hh // 4, hh % 4
                            for i, (kb, sl) in enumerate(kbs):
                                fns.append(lambda g=g, hq=hq, kb=kb, sl=sl, i=i: nc.tensor.matmul(
                                    ps.f32(ob2 + g)[:, hq * 65:(hq + 1) * 65], lhsT=PTs[:, g, kb, hq * 128:(hq + 1) * 128],
                                    rhs=vbe[:, sl, g, :], start=(i == 0), stop=(i == len(kbs) - 1)))
                        K.group(PE, fns, reads=[PT_tk] + vb_tk, writes=otk)
                        yield
                        O4 = ps.f32(ob2, 2).rearrange("p (b c) -> p b c", c=512)[:, :, 0:260].rearrange("p b (h e) -> p b h e", e=65)
                        K.op(DVE, lambda: nc.vector.tensor_tensor(out=dn[:, 0, :].rearrange("p (b h) -> p b h", h=4), in0=O4[:, :, :, 64],
                                                                  in1=esink[:].rearrange("p (b h) -> p b h", h=4), op=ALU.add),
                             reads=otk + [mc_tk], writes=[dn_tk])
                        K.op(DVE, lambda: nc.vector.reciprocal(out=dn[:, 1, :], in_=dn[:, 0, :]), reads=[dn_tk], writes=[dn_tk])
                        K.op(DVE, lambda: nc.vector.tensor_tensor(out=atm[:].rearrange("p (b h d) -> p b h d", h=4, d=64), in0=O4[:, :, :, 0:64],
                                                                  in1=dn[:, 1, :].rearrange("p (b h) -> p b h", h=4).unsqueeze(3).broadcast_to([128, 2, 4, 64]),
                                                                  op=ALU.mult),
                             reads=otk + [dn_tk], writes=[atm_tk])
                        ps.free(ob2, 2)
                        yield
                        ab_, atk = yield from galloc(1)
                        av = ps.bf(ab_)
                        K.group(PE, [(lambda j=j: nc.tensor.transpose(av[:, j * 128:(j + 1) * 128], atm[:, j * 128:(j + 1) * 128], ident[:]))
                                     for j in range(4)], reads=[atm_tk, const_tk], writes=atk)
                        K.op(ACT, lambda: nc.scalar.activation(out=catT[:, 0:4, cs], in_=av[:, 0:512].rearrange("p (j t) -> p j t", t=128),
                                                               func=AF.Copy), reads=atk, writes=cat_tk[0:4])
                        ps.free(ab_)
                        yield

                def state_step(n, ubank, utk, sdst, sdst_tk):
                    K.op(DVE, lambda: nc.vector.tensor_tensor(out=st32[:], in0=st32[:], in1=dec[:, :, n:n + 1].broadcast_to([128, 4, 128]),
                                                              op=ALU.mult), reads=st32_tk + hb_tk, writes=st32_tk)
                    K.op(DVE, lambda: nc.vector.tensor_tensor(out=st32[:], in0=st32[:], in1=ps.f32(ubank).rearrange("p (c e) -> p c e", e=128),
                                                              op=ALU.add), reads=st32_tk + utk, writes=st32_tk)
                    K.op(ACT, lambda: nc.scalar.activation(out=sdst[:], in_=st32[:], func=AF.Copy), reads=st32_tk, writes=[sdst_tk])

                def gen_H(blk):
                    for tt in range(4):
                        cs = slice(tt * 128, (tt + 1) * 128)
                        vt = vtm[:, tt, :]
                        vt_tk = vtm_tk[tt]
                        kb_, ktk = yield from galloc(1)
                        kv = ps.bf(kb_)
                        K.group(PE, [(lambda c=c: nc.tensor.transpose(kv[:, c * 128:(c + 1) * 128], k2T[:, c, cs], ident[:])) for c in range(4)],
                                reads=hb_tk + k2w_tk + [const_tk], writes=ktk)
                        kv3 = kv[:, 0:512].rearrange("p (c d) -> p c d", d=128)
                        K.op(DVE, lambda: nc.vector.tensor_scalar(out=k2A[:], in0=kv3, scalar1=rowm[:, 0:1], scalar2=None, op0=ALU.mult),
                             reads=ktk + [const_tk], writes=[k2A_tk])
                        K.op(DVE, lambda: nc.vector.tensor_scalar(out=k2B[:], in0=kv3, scalar1=rowm[:, 1:2], scalar2=None, op0=ALU.mult),
                             reads=ktk + [const_tk], writes=[k2B_tk])
                        ps.free(kb_)
                        sb_2, sctk = yield from galloc(1)
                        scv = ps.f32(sb_2)
                        K.group(PE, [(lambda c=c: nc.tensor.matmul(scv[:, c * 128:(c + 1) * 128], lhsT=kt[:, c, cs], rhs=qt[:, c, cs],
                                                                 start=True, stop=True)) for c in range(4)],
                                reads=hb_tk + ktw_tk, writes=sctk)
                        yield
                        K.op(DVE, lambda: nc.vector.tensor_tensor(out=scm[:], in0=scv.rearrange("p (c t) -> p c t", t=128), in1=hmask[:], op=ALU.mult),
                             reads=sctk + [mc_tk], writes=[scm_tk])
                        ps.free(sb_2)
                        ua, uatk = yield from galloc(1)
                        ub, ubtk = yield from galloc(1)
                        K.group(PE, [(lambda c=c: nc.tensor.matmul(ps.f32(ua)[:, c * 128:(c + 1) * 128], lhsT=k2A[:, c, :],
                                                                 rhs=vt[:, c * 128:(c + 1) * 128], start=True, stop=True)) for c in range(4)],
                                reads=[k2A_tk, vt_tk], writes=uatk)
                        K.group(PE, [(lambda c=c: nc.tensor.matmul(ps.f32(ub)[:, c * 128:(c + 1) * 128], lhsT=k2B[:, c, :],
                                                                 rhs=vt[:, c * 128:(c + 1) * 128], start=True, stop=True)) for c in range(4)],
                                reads=[k2B_tk, vt_tk], writes=ubtk)
                        yield
                        oo, ootk = yield from galloc(1)
                        oov = ps.f32(oo)
                        stA, stA_tk = state["stb"]
                        fns = []
                        for c in range(4):
                            fns.append(lambda c=c: nc.tensor.matmul(oov[:, c * 128:(c + 1) * 128], lhsT=vt[:, c * 128:(c + 1) * 128],
                                                                    rhs=scm[:, c, :], start=(c == 0), stop=False, skip_group_check=True))
                        for c in range(4):
                            fns.append(lambda c=c: nc.tensor.matmul(oov[:, c * 128:c * 128 + 64], lhsT=stA[:, c, :],
                                                                    rhs=qt[:, c, tt * 128:tt * 128 + 64], start=False, stop=False,
                                                                    skip_group_check=True))
                        K.group(PE, fns, reads=[vt_tk, scm_tk, stA_tk] + hb_tk, writes=ootk)
                        stB, stB_tk, _ = stb_rot.next()
                        state_step(2 * tt, ua, uatk, stB, stB_tk)
                        ps.free(ua)
                        yield
                        K.group(PE, [(lambda c=c: nc.tensor.matmul(oov[:, c * 128 + 64:(c + 1) * 128], lhsT=stB[:, c, :],
                                                                 rhs=qt[:, c, tt * 128 + 64:(tt + 1) * 128], start=False, stop=True,
                                                                 skip_group_check=True)) for c in range(4)],
                                reads=[stB_tk] + hb_tk, writes=ootk)
                        stC, stC_tk, _ = stb_rot.next()
                        state_step(2 * tt + 1, ub, ubtk, stC, stC_tk)
                        ps.free(ub)
                        state["stb"] = (stC, stC_tk)
                        yield
                        K.op(ACT, lambda: nc.scalar.activation(out=sqr[:], in_=oov, func=AF.Square), reads=ootk, writes=[sqr_tk])
                        nb_, ntk = yield from galloc(1)
                        K.group(PE, [lambda: nc.tensor.matmul(ps.f32(nb_), lhsT=onesb[:], rhs=sqr[:], start=True, stop=True)],
                                reads=[sqr_tk, const_tk], writes=ntk)
                        yield
                        K.op(ACT, lambda: nc.scalar.activation(out=rsb[:], in_=ps.f32(nb_), func=AF.Ln, scale=1.0 / 128, bias=eps_t[:, 0:1]),
                             reads=ntk + [const2_tk], writes=[rsb_tk])
                        ps.free(nb_)
                        K.op(ACT, lambda: nc.scalar.activation(out=rsb[:], in_=rsb[:], func=AF.Exp, scale=-0.5), reads=[rsb_tk], writes=[rsb_tk])
                        K.op(DVE, lambda: nc.vector.scalar_tensor_tensor(out=t1[:], in0=oov, scalar=gnh[:, 0:1], in1=rsb[:], op0=ALU.mult, op1=ALU.mult),
                             reads=ootk + [rsb_tk, par_tk], writes=[t1_tk])
                        ps.free(oo)
                        K.op(DVE, lambda: nc.vector.tensor_tensor(out=catT[:, 4:8, cs], in0=t1[:].rearrange("p (c t) -> p c t", t=128),
                                                                  in1=gs[:, :, cs], op=ALU.mult),
                             reads=[t1_tk] + hb_tk, writes=cat_tk[4:8])
                        yield

                def gen_O(blk):
                    for tt in range(4):
                        yb, ytk = yield from galloc(2)
                        fns = []
                        for n in range(2):
                            o = ps.f32(yb + n)
                            for kc in range(8):
                                fns.append(lambda o=o, n=n, kc=kc: nc.tensor.matmul(
                                    o, lhsT=catT[:, kc, tt * 128:(tt + 1) * 128], rhs=Wmo[:, kc, n * 512:(n + 1) * 512],
                                    start=(kc == 0), stop=(kc == 7)))
                        K.group(PE, fns, reads=cat_tk + [wmo_tk], writes=ytk)
                        yield
                        epi.run(blk * 4 + tt, yb, ytk)
                        yield

                interleave(gen_prologue(0))
                for blk in range(NB):
                    gens = [gen_P(blk), gen_Apre(blk, [0, 1, 2, 3]), gen_Amain(blk, 0), gen_Amain(blk, 1)]
                    if blk > 0:
                        gens.insert(0, gen_O(blk - 1))
                    interleave(*gens)
                    gens = [gen_H(blk), gen_Amain(blk, 2), gen_Amain(blk, 3)]
                    if blk + 1 < NB:
                        gens.append(gen_prologue(blk + 1))
                    interleave(*gens)
                interleave(gen_O(NB - 1))
                K.barrier()

        dsts = {1: out_d if stop_after == 1 else x1_d, 2: out_d if stop_after == 2 else x2_d, 3: out_d}
        names = {1: "out" if stop_after == 1 else "x1", 2: "out" if stop_after == 2 else "x2", 3: "out"}
        if stop_after == 0:
            dbg = K.sb("dbg", [128, D], F32)
            dbg_tk = Tk("dbg")
            dsem = K.newsem("dbg")
            K.op(DVE, lambda: nc.vector.tensor_copy(out=dbg[:], in_=G[:, 0, :]), reads=[par_tk], writes=[dbg_tk])
            K.dma(SP, out=out_d[0:128, :], in_=dbg[:], reads=[dbg_tk], writes=[], sem=dsem)
            K.op(DVE, lambda: nc.vector.tensor_copy(out=dbg[:, 0:24], in_=a_col[:].rearrange("p a b -> p (a b)")), reads=[par_tk, dbg_tk], writes=[dbg_tk])
            K.op(DVE, lambda: nc.vector.tensor_copy(out=dbg[:, 24:48], in_=sh_col[:].rearrange("p a b -> p (a b)")), reads=[par_tk, dbg_tk], writes=[dbg_tk])
            K.op(DVE, lambda: nc.vector.tensor_copy(out=dbg[:, 48:52], in_=lb[:]), reads=[par_tk, dbg_tk], writes=[dbg_tk])
            K.dma(SP, out=out_d[128:256, :], in_=dbg[:], reads=[dbg_tk], writes=[], sem=dsem)
            K.barrier()
            return nc
        ffn_phase(0, 0, x_d, "x", dsts[1], names[1])
        if stop_after >= 2:
            mixer_phase(x1_d, "x1", dsts[2], names[2])
        if stop_after >= 3:
            ffn_phase(1, 2, x2_d, "x2", out_d, "out")
        K.barrier()
        print("instructions:", K.n_inst, "sems:", len(K.sems))
    return nc


def _consts():
    bf = ml_dtypes.bfloat16
    ident = np.eye(128, dtype=np.float32).astype(bf)
    onesb = np.ones((128, 128), np.float32).astype(bf)
    NEG = -1e30
    am = np.zeros((128, 2, 256), np.float32)
    am[0:64, 0, 192:256] = NEG
    am[64:128, 0, 0:64] = NEG
    am[0:64, 1, 64:128] = NEG
    am[64:128, 1, 128:192] = NEG
    s = np.arange(128)[:, None]
    t = np.arange(128)[None, :]
    hm = ((s // 64 == t // 64) & (s <= t)).astype(np.float32)
    hmask = np.broadcast_to(hm[:, None, :], (128, 4, 128)).copy()
    rmask = np.ones((128, 512), np.float32)
    rmask[:, ::64] = 0.0
    inv_freq = (1.0 / (np.float32(10000.0) ** (np.arange(0, 64, 2, dtype=np.float32) / np.float32(64)))).astype(np.float32)
    invf = np.broadcast_to(inv_freq[None, :], (128, 32)).copy()
    rowm = np.zeros((128, 2), np.float32)
    rowm[0:64, 0] = 1.0
    rowm[64:128, 1] = 1.0
    amT = np.zeros((128, 2, 2, 4, 128), np.float32)
    for v in range(2):
        for kb in range(2):
            amT[:, v, kb, :, :] = am[:, v, kb * 128:(kb + 1) * 128].T[:, None, :]
    amT = amT.reshape(128, 2, 2, 512)
    return dict(ident=ident, onesb=onesb, amask=amT.astype(bf), hmask=hmask.astype(bf), rmask=rmask, invf=invf, rowm=rowm)


def make_in_maps(x, c, positions, w_cond, b_cond, norm_pre, norm_post, ffn_w_in, ffn_w_out,
                 w_mix_in, w_mix_out, attn_sinks, hgrn_lb_logits, hgrn_gnorm):
    f = np.float32
    cst = _consts()
    shared = dict(
        w_cond=np.ascontiguousarray(w_cond[0], f), b_cond=np.ascontiguousarray(np.broadcast_to(np.asarray(b_cond[0], f)[None, :], (128, 9 * D))),
        npre=np.ascontiguousarray(np.asarray(norm_pre[0], f).reshape(3, 8, 128).transpose(2, 0, 1)),
        npost=np.ascontiguousarray(np.broadcast_to(np.asarray(norm_post[0], f)[None], (128, 3, D))),
        w_in=np.ascontiguousarray(ffn_w_in[0], f), w_out=np.ascontiguousarray(ffn_w_out[0], f),
        w_mi=np.ascontiguousarray(w_mix_in[0], f), w_mo=np.ascontiguousarray(w_mix_out[0], f),
        sinks=np.ascontiguousarray(np.broadcast_to(np.asarray(attn_sinks[0], f)[None], (128, 8))),
        lbl=np.ascontiguousarray(np.asarray(hgrn_lb_logits, f).reshape(2, 4, 128).transpose(2, 0, 1)),
        gn=np.ascontiguousarray(np.asarray(hgrn_gnorm[0], f)[:, None]),
        **cst)
    maps = []
    for b in range(8):
        m = dict(shared)
        m["x"] = np.ascontiguousarray(x[b], f)
        m["ccol"] = np.ascontiguousarray(np.asarray(c[b], f).reshape(8, 128).T)
        m["pos"] = np.ascontiguousarray(np.asarray(positions[b], np.int32).reshape(NT, 128).T)
        maps.append(m)
    return maps


def kernel(**inputs):
    nc = build(3)
    maps = make_in_maps(**inputs)
    res = run_bass_kernel_spmd(nc, maps, core_ids=list(range(8)))
    return np.stack([np.asarray(r["out"], np.float32) for r in res.results], axis=0)
```

```python
import numpy as np
from contextlib import ExitStack
import ml_dtypes
import concourse.bass as bass
import concourse.mybir as mybir
from concourse.bass_utils import run_bass_kernel_spmd

F32 = mybir.dt.float32
BF16 = mybir.dt.bfloat16
I32 = mybir.dt.int32
AF = mybir.ActivationFunctionType
ALU = mybir.AluOpType
AX = mybir.AxisListType

S = 4096
D = 1024
DFF = 2816
NF = DFF // 128
TB = 512
NB = S // TB
NT = S // 128
EPS = 1e-6
PI = float(np.pi)
TWO_PI = float(2 * np.pi)


class Tk:
    __slots__ = ("w", "r", "name", "excl")

    def __init__(self, name="", excl=False):
        self.w = None
        self.r = {}
        self.name = name
        self.excl = excl


class Sem:
    def __init__(self, h, name):
        self.h = h
        self.cnt = 0
        self.name = name


class Eng:
    def __init__(self, name, h, sem):
        self.name = name
        self.h = h
        self.sem = sem
        self.seen = {}


class Kern:
    def __init__(self, nc, es):
        self.nc = nc
        self.es = es
        self.sems = []
        self.PE = Eng("pe", nc.tensor, self.newsem("pe"))
        self.ACT = Eng("act", nc.scalar, self.newsem("act"))
        self.DVE = Eng("dve", nc.vector, self.newsem("dve"))
        self.POOL = Eng("pool", nc.gpsimd, self.newsem("pool"))
        self.SP = Eng("sp", nc.sync, self.newsem("sp"))
        self.engs = [self.PE, self.ACT, self.DVE, self.POOL, self.SP]
        self.n_inst = 0

    def newsem(self, name):
        self.sid = getattr(self, "sid", 0) + 1
        s = Sem(self.es.enter_context(self.nc.semaphore("m%d_%s" % (self.sid, name))), name)
        self.sems.append(s)
        return s

    def sb(self, name, shape, dt, es=None):
        self.uid = getattr(self, "uid", 0) + 1
        return (es or self.es).enter_context(self.nc.sbuf_tensor("s%d_%s" % (self.uid, name), shape, dt))

    def _wait(self, E, toks):
        for (s, v) in toks:
            if E.seen.get(id(s), 0) >= v:
                continue
            E.h.wait_ge(s.h, v)
            E.seen[id(s)] = v

    def _deps(self, E, reads, writes):
        toks = []
        for t in reads:
            if t.w is not None:
                toks.append(t.w)
            if t.excl:
                for tok in t.r.values():
                    if tok[0] is not E.sem:
                        toks.append(tok)
        for t in writes:
            if t.w is not None and t.w[0] is not E.sem:
                toks.append(t.w)
            for tok in t.r.values():
                if tok[0] is not E.sem or E is not self.PE:
                    toks.append(tok)
        return toks

    def _mark(self, tok, reads, writes):
        for t in writes:
            t.w = tok
            t.r = {}
        for t in reads:
            t.r[id(tok[0])] = tok

    def op(self, E, fn, reads=(), writes=()):
        self._wait(E, self._deps(E, reads, writes))
        ins = fn()
        ins.then_inc(E.sem.h, 1)
        E.sem.cnt += 1
        self._mark((E.sem, E.sem.cnt), reads, writes)
        self.n_inst += 1
        return ins

    def group(self, E, fns, reads=(), writes=()):
        self._wait(E, self._deps(E, reads, writes))
        ins = None
        for fn in fns:
            ins = fn()
            self.n_inst += 1
        ins.then_inc(E.sem.h, 1)
        E.sem.cnt += 1
        self._mark((E.sem, E.sem.cnt), reads, writes)

    def dma(self, E, out, in_, reads, writes, sem, **kw):
        self._wait(E, self._deps(E, reads, writes))
        E.h.dma_start(out=out, in_=in_, **kw).then_inc(sem.h, 16)
        sem.cnt += 16
        self._mark((sem, sem.cnt), reads, writes)
        self.n_inst += 1

    def barrier(self):
        toks = [(s, s.cnt) for s in self.sems if s.cnt > 0]
        for E in self.engs:
            self._wait(E, [t for t in toks if t[0] is not E.sem])


class Psum:
    def __init__(self, K):
        nc = K.nc
        self.t = K.es.enter_context(nc.psum_tensor("psum_all", [128, 8 * 512], F32))
        self.tb = self.t.bitcast(BF16)
        self.tk = [Tk("bank%d" % i, excl=True) for i in range(8)]
        self.p = 0
        self.held = [False] * 8

    def try_alloc(self, n=1):
        for i in range(8):
            b = (self.p + i) % 8
            if b % n or b + n > 8:
                continue
            if any(self.held[b:b + n]):
                continue
            for j in range(b, b + n):
                self.held[j] = True
            self.p = (b + n) % 8
            return b, self.tk[b:b + n]
        return None

    def alloc(self, n=1):
        r = self.try_alloc(n)
        assert r is not None, "PSUM exhausted (straight-line code must free before allocating)"
        return r

    def free(self, b, n=1):
        for j in range(b, b + n):
            assert self.held[j]
            self.held[j] = False

    def f32(self, b, n=1):
        return self.t[:, b * 512:(b + n) * 512]

    def bf(self, b, n=1):
        return self.tb[:, b * 1024:(b + n) * 1024]


class Rot:
    def __init__(self, K, es, name, shape, dt, n, dma=False):
        self.items = []
        for i in range(n):
            t = K.sb("%s%d" % (name, i), shape, dt, es)
            self.items.append((t, Tk("%s%d" % (name, i)), K.newsem("%s%d" % (name, i)) if dma else None))
        self.i = 0

    def next(self):
        it = self.items[self.i % len(self.items)]
        self.i += 1
        return it


def build(stop_after=3):
    nc = bass.Bass("TRN2", target_bir_lowering=False)

    def din(name, shape, dt=F32):
        return nc.dram_tensor(name, shape, dt, kind="ExternalInput").ap()

    x_d = din("x", [S, D])
    ccol_d = din("ccol", [128, 8])
    pos_d = din("pos", [128, NT], I32)
    wcond_d = din("w_cond", [D, 9 * D])
    bcond_d = din("b_cond", [128, 9 * D])
    npre_d = din("npre", [128, 3, 8])
    npost_d = din("npost", [128, 3, D])
    win_d = din("w_in", [2, D, 2 * DFF])
    wout_d = din("w_out", [2, DFF, D])
    wmi_d = din("w_mi", [D, 2816])
    wmo_d = din("w_mo", [D, D])
    sinks_d = din("sinks", [128, 8])
    lbl_d = din("lbl", [128, 2, 4])
    gn_d = din("gn", [128, 1])
    ident_d = din("ident", [128, 128], BF16)
    onesb_d = din("onesb", [128, 128], BF16)
    amask_d = din("amask", [128, 2, 2, 512], BF16)
    hmask_d = din("hmask", [128, 4, 128], BF16)
    rmask_d = din("rmask", [128, 512])
    invf_d = din("invf", [128, 32])
    rowm_d = din("rowm", [128, 2])
    out_d = nc.dram_tensor("out", [S, D], F32, kind="ExternalOutput").ap()
    x1_d = nc.dram_tensor("x1s", [S, D], F32).ap()
    x2_d = nc.dram_tensor("x2s", [S, D], F32).ap()
    rope_d = nc.dram_tensor("rope_s", [128, 2, NT * 32], F32).ap()
    w2in_s = nc.dram_tensor("w2in_s", [D, 2 * DFF], BF16).ap()
    w2out_s = nc.dram_tensor("w2out_s", [DFF, D], BF16).ap()
    wmi_s = nc.dram_tensor("wmi_s", [D, 2816], BF16).ap()
    wmo_s = nc.dram_tensor("wmo_s", [D, D], BF16).ap()

    with ExitStack() as es:
        K = Kern(nc, es)
        PE, ACT, DVE, POOL, SP = K.PE, K.ACT, K.DVE, K.POOL, K.SP
        ps = Psum(K)
        dram_tk = {}

        def dtk(name, tile):
            key = (name, tile)
            if key not in dram_tk:
                dram_tk[key] = Tk("%s_%d" % key)
            return dram_tk[key]

        ident = K.sb("ident", [128, 128], BF16)
        onesb = K.sb("onesb", [128, 128], BF16)
        rowm = K.sb("rowm", [128, 2], F32)
        sinks = K.sb("sinks", [128, 8], F32)
        gn = K.sb("gn", [128, 1], F32)
        lb = K.sb("lb", [128, 4], F32)
        oml = K.sb("oml", [128, 4], F32)
        a_col = K.sb("a_col", [128, 3, 8], F32)
        sh_col = K.sb("sh_col", [128, 3, 8], F32)
        G = K.sb("G", [128, 3, D], BF16)
        c01 = K.sb("c01", [128, 2, 4], F32)
        gnh = K.sb("gnh", [128, 1], F32)
        const_tk = Tk("consts")
        par_tk = Tk("params")
        rope_tk = Tk("rope")

        with ExitStack() as ss:
            ld = K.newsem("setup_ld")
            ccol = K.sb("ccol", [128, 8], F32, ss)
            ca = K.sb("ca", [128, 8], F32, ss)
            posi = K.sb("posi", [128, NT], I32, ss)
            npre = K.sb("npre", [128, 3, 8], F32, ss)
            npost = K.sb("npost", [128, 3, D], F32, ss)
            lbl = K.sb("lbl", [128, 2, 4], F32, ss)
            invf = K.sb("invf", [128, 32], F32, ss)
            setup_tk = Tk("setup_in")
            cosT = K.sb("cosT", [128, NT, 32], F32, ss)
            sinT = K.sb("sinT", [128, NT, 32], F32, ss)
            loads = [(ident, ident_d), (onesb, onesb_d),
                     (rowm, rowm_d), (sinks, sinks_d), (gn, gn_d), (ccol, ccol_d), (posi, pos_d), (npre, npre_d),
                     (npost, npost_d), (lbl, lbl_d), (invf, invf_d)]
            for (t, d_) in loads:
                nc.sync.dma_start(out=t[:], in_=d_).then_inc(ld.h, 16)
                ld.cnt += 16
            setup_tk.w = (ld, ld.cnt)
            const_tk.w = (ld, ld.cnt)

            K.op(ACT, lambda: nc.scalar.activation(out=ca[:], in_=ccol[:], func=AF.Silu), reads=[setup_tk], writes=[par_tk])
            K.op(DVE, lambda: nc.vector.tensor_tensor(out=lb[:], in0=lbl[:, 0, :], in1=lbl[:, 1, :], op=ALU.subtract),
                 reads=[setup_tk], writes=[par_tk])
            K.op(ACT, lambda: nc.scalar.activation(out=lb[:], in_=lb[:], func=AF.Sigmoid), reads=[par_tk], writes=[par_tk])
            K.op(DVE, lambda: nc.vector.tensor_scalar(out=oml[:], in0=lb[:], scalar1=-1.0, scalar2=1.0, op0=ALU.mult, op1=ALU.add),
                 reads=[par_tk], writes=[par_tk])
            K.op(DVE, lambda: nc.vector.tensor_scalar(out=c01[:, 1, :], in0=oml[:], scalar1=0.5, scalar2=None, op0=ALU.mult),
                 reads=[par_tk], writes=[par_tk])
            K.op(DVE, lambda: nc.vector.tensor_tensor(out=c01[:, 0, :], in0=lb[:], in1=c01[:, 1, :], op=ALU.add),
                 reads=[par_tk], writes=[par_tk])
            K.op(DVE, lambda: nc.vector.tensor_scalar(out=gnh[:], in0=gn[:], scalar1=0.5, scalar2=None, op0=ALU.mult),
                 reads=[setup_tk, par_tk], writes=[par_tk])

            cab = K.sb("cab", [128, 8, 128], F32, ss)
            identf = K.sb("identf", [128, 128], F32, ss)
            modbc = K.sb("modbc", [128, 9 * D], F32, ss)
            dg = K.sb("dg", [128, 8, 128], F32, ss)
            K.op(DVE, lambda: nc.vector.tensor_copy(out=cab[:], in_=ca[:].unsqueeze(2).broadcast_to([128, 8, 128])),
                 reads=[par_tk], writes=[par_tk])
            K.op(DVE, lambda: nc.vector.tensor_copy(out=identf[:], in_=ident[:]), reads=[setup_tk], writes=[setup_tk])
            wc_rot = Rot(K, ss, "wc", [128, 8, 512], F32, 2, dma=True)
            bc_rot = Rot(K, ss, "bc", [128, 512], F32, 2, dma=True)
            wc_view = wcond_d.rearrange("(k p) n -> p k n", p=128)
            mod_tk = Tk("modbc")
            for cb in range(18):
                wc, wc_tk, wc_sem = wc_rot.next()
                bc, bc_tk, bc_sem = bc_rot.next()
                K.dma(SP, out=wc[:], in_=wc_view[:, :, cb * 512:(cb + 1) * 512], reads=[], writes=[wc_tk], sem=wc_sem)
                K.dma(SP, out=bc[:], in_=bcond_d[:, cb * 512:(cb + 1) * 512], reads=[], writes=[bc_tk], sem=bc_sem)
                b, btk = ps.alloc(1)
                pv = ps.f32(b)
                K.group(PE, [(lambda k=k: nc.tensor.matmul(pv, lhsT=cab[:, k, :], rhs=wc[:, k, :],
                                                         start=(k == 0), stop=(k == 7))) for k in range(8)],
                        reads=[par_tk, wc_tk], writes=btk)
                K.op(DVE, lambda: nc.vector.tensor_tensor(out=modbc[:, cb * 512:(cb + 1) * 512], in0=pv, in1=bc[:], op=ALU.add),
                     reads=btk + [bc_tk, mod_tk], writes=[mod_tk])
                ps.free(b)
            for k in range(3):
                for which, v in (("sh", 3 * k), ("sc", 3 * k + 1)):
                    K.op(DVE, lambda: nc.vector.tensor_tensor(out=dg[:], in0=modbc[:, v * D:(v + 1) * D].rearrange("p (j q) -> p j q", q=128),
                                                              in1=identf[:].unsqueeze(1).broadcast_to([128, 8, 128]), op=ALU.mult),
                         reads=[mod_tk, setup_tk, par_tk], writes=[par_tk])
                    dstc = sh_col if which == "sh" else a_col
                    K.op(DVE, lambda: nc.vector.tensor_reduce(out=dstc[:, k, :], in_=dg[:], axis=AX.X, op=ALU.add),
                         reads=[par_tk], writes=[par_tk])
                K.op(DVE, lambda: nc.vector.scalar_tensor_tensor(out=a_col[:, k, :], in0=a_col[:, k, :], scalar=1.0, in1=npre[:, k, :],
                                                                 op0=ALU.add, op1=ALU.mult),
                     reads=[par_tk, setup_tk], writes=[par_tk])
                resw = 1.0 if k == 1 else 0.5
                K.op(DVE, lambda: nc.vector.scalar_tensor_tensor(out=G[:, k, :], in0=modbc[:, (3 * k + 2) * D:(3 * k + 3) * D], scalar=resw,
                                                                 in1=npost[:, k, :], op0=ALU.mult, op1=ALU.mult),
                     reads=[mod_tk, setup_tk, par_tk], writes=[par_tk])

            posf = K.sb("posf", [128, NT], F32, ss)
            ang = K.sb("ang", [128, NT, 32], F32, ss)
            rr = K.sb("rr", [128, NT, 32], F32, ss)
            rc = K.sb("rc", [128, NT, 32], F32, ss)
            kf = K.sb("kf", [128, NT, 32], F32, ss)
            ki = K.sb("ki", [128, NT, 32], I32, ss)
            mk = K.sb("mk", [128, NT, 32], F32, ss)
            rt = Tk("ropetmp")

            def V(fn, reads=(), writes=()):
                K.op(DVE, fn, reads=list(reads) + [rt, setup_tk], writes=list(writes) + [rt])

            V(lambda: nc.vector.tensor_copy(out=posf[:], in_=posi[:]))
            V(lambda: nc.vector.tensor_tensor(out=ang[:], in0=posf[:].unsqueeze(2).broadcast_to([128, NT, 32]),
                                              in1=invf[:].unsqueeze(1).broadcast_to([128, NT, 32]), op=ALU.mult))
            V(lambda: nc.vector.tensor_scalar(out=kf[:], in0=ang[:], scalar1=1.0 / TWO_PI, scalar2=None, op0=ALU.mult))
            V(lambda: nc.vector.tensor_copy(out=ki[:], in_=kf[:]))
            V(lambda: nc.vector.tensor_copy(out=kf[:], in_=ki[:]))
            C1 = 6.28125
            C2 = TWO_PI - 6.28125
            V(lambda: nc.vector.scalar_tensor_tensor(out=rr[:], in0=kf[:], scalar=-C1, in1=ang[:], op0=ALU.mult, op1=ALU.add))
            V(lambda: nc.vector.scalar_tensor_tensor(out=rr[:], in0=kf[:], scalar=-C2, in1=rr[:], op0=ALU.mult, op1=ALU.add))

            def fold(t):
                V(lambda: nc.vector.tensor_scalar(out=mk[:], in0=t[:], scalar1=PI, scalar2=-TWO_PI, op0=ALU.is_gt, op1=ALU.mult))
                V(lambda: nc.vector.tensor_tensor(out=t[:], in0=t[:], in1=mk[:], op=ALU.add))
                V(lambda: nc.vector.tensor_scalar(out=mk[:], in0=t[:], scalar1=-PI, scalar2=TWO_PI, op0=ALU.is_lt, op1=ALU.mult))
                V(lambda: nc.vector.tensor_tensor(out=t[:], in0=t[:], in1=mk[:], op=ALU.add))
                V(lambda: nc.vector.tensor_scalar(out=t[:], in0=t[:], scalar1=PI, scalar2=-PI, op0=ALU.min, op1=ALU.max))

            fold(rr)
            V(lambda: nc.vector.tensor_scalar(out=rc[:], in0=rr[:], scalar1=PI / 2, scalar2=None, op0=ALU.add))
            fold(rc)
            K.op(ACT, lambda: nc.scalar.activation(out=sinT[:], in_=rr[:], func=AF.Sin), reads=[rt], writes=[rope_tk])
            K.op(ACT, lambda: nc.scalar.activation(out=cosT[:], in_=rc[:], func=AF.Sin), reads=[rt], writes=[rope_tk])
            rsem = K.newsem("ropest")
            K.dma(SP, out=rope_d[:, 0, :], in_=cosT[:].rearrange("p t f -> p (t f)"), reads=[rope_tk], writes=[dtk("rope", 0)], sem=rsem)
            K.dma(SP, out=rope_d[:, 1, :], in_=sinT[:].rearrange("p t f -> p (t f)"), reads=[rope_tk], writes=[dtk("rope", 0)], sem=rsem)
            K.barrier()

        def issue_cast_dma(dst, src, tk, sem):
            K.dma(POOL, out=dst, in_=src, reads=[], writes=[tk], sem=sem)

        eps_t = K.sb("eps_t", [128, 2], F32)
        const2_tk = Tk("const2")
        K.op(DVE, lambda: nc.vector.memset(eps_t[:, 0:1], EPS), writes=[const2_tk])
        K.op(DVE, lambda: nc.vector.memset(eps_t[:, 1:2], 0.0), writes=[const2_tk])

        def rstd_ops(st, st_tk, n, lnexp):
            if lnexp:
                K.op(ACT, lambda: nc.scalar.activation(out=st[:, 1:2], in_=st[:, 0:1], func=AF.Ln, scale=1.0 / n, bias=eps_t[:, 0:1]),
                     reads=[st_tk, const2_tk], writes=[st_tk])
                K.op(ACT, lambda: nc.scalar.activation(out=st[:, 2:3], in_=st[:, 1:2], func=AF.Exp, scale=-0.5),
                     reads=[st_tk], writes=[st_tk])
            else:
                K.op(ACT, lambda: nc.scalar.activation(out=st[:, 1:2], in_=st[:, 0:1], func=AF.Sqrt, scale=1.0 / n, bias=eps_t[:, 0:1]),
                     reads=[st_tk, const2_tk], writes=[st_tk])
                K.op(DVE, lambda: nc.vector.reciprocal(out=st[:, 2:3], in_=st[:, 1:2]), reads=[st_tk], writes=[st_tk])

        class Pro:
            def __init__(self, es_, src, srcname, kidx, lnexp, tag, nxin=2):
                self.src, self.srcname, self.kidx, self.lnexp = src, srcname, kidx, lnexp
                self.xin_rot = Rot(K, es_, tag + "xin", [128, D], F32, nxin, dma=True)
                self.xn_rot = Rot(K, es_, tag + "xn", [128, D], BF16, 2)
                self.st_rot = Rot(K, es_, tag + "st", [128, 4], F32, 4)
                self.tmpf = K.sb(tag + "tmpf", [128, 8, 128], F32, es_)
                self.tmpf_tk = Tk(tag + "tmpf")

            def elem(self, tile):
                xin, xin_tk, xin_sem = self.xin_rot.next()
                K.dma(SP, out=xin[:], in_=self.src[tile * 128:(tile + 1) * 128, :], reads=[dtk(self.srcname, tile)],
                      writes=[xin_tk], sem=xin_sem)
                st, st_tk, _ = self.st_rot.next()
                xn, xn_tk, _ = self.xn_rot.next()
                K.op(ACT, lambda: nc.scalar.activation(out=xn[:], in_=xin[:], func=AF.Square, accum_out=st[:, 0:1]),
                     reads=[xin_tk], writes=[xn_tk, st_tk])
                rstd_ops(st, st_tk, D, self.lnexp)
                K.op(DVE, lambda: nc.vector.tensor_scalar(out=xn[:], in0=xin[:], scalar1=st[:, 2:3], scalar2=None, op0=ALU.mult),
                     reads=[xin_tk, st_tk], writes=[xn_tk])
                return xn, xn_tk

            def pe(self, xn, xn_tk, h, h_tks, tt):
                b, tks = ps.alloc(1)
                tv = ps.bf(b)
                K.group(PE, [(lambda dc=dc: nc.tensor.transpose(tv[:, dc * 128:(dc + 1) * 128], xn[:, dc * 128:(dc + 1) * 128], ident[:]))
                             for dc in range(8)], reads=[xn_tk, const_tk], writes=tks)
                kidx = self.kidx
                K.op(DVE, lambda: nc.vector.tensor_tensor(out=self.tmpf[:], in0=tv.rearrange("p (c t) -> p c t", t=128),
                                                          in1=a_col[:, kidx, :].unsqueeze(2).broadcast_to([128, 8, 128]), op=ALU.mult),
                     reads=tks + [par_tk], writes=[self.tmpf_tk])
                ps.free(b)
                K.op(DVE, lambda: nc.vector.tensor_tensor(out=h[:, :, tt * 128:(tt + 1) * 128], in0=self.tmpf[:],
                                                          in1=sh_col[:, kidx, :].unsqueeze(2).broadcast_to([128, 8, 128]), op=ALU.add),
                     reads=[self.tmpf_tk, par_tk], writes=h_tks)

        class Epi:
            def __init__(self, es_, src, srcname, dst, dstname, kidx, lnexp, tag, junk):
                self.src, self.srcname, self.dst, self.dstname, self.kidx, self.lnexp = src, srcname, dst, dstname, kidx, lnexp
                self.junk = junk
                self.xre_rot = Rot(K, es_, tag + "xre", [128, D], F32, 1, dma=True)
                self.ob_rot = Rot(K, es_, tag + "ob", [128, D], F32, 1, dma=True)
                self.st_rot = Rot(K, es_, tag + "est", [128, 4], F32, 4)

            def run(self, tile, yb, ytk):
                yv = ps.f32(yb, 2)
                kidx = self.kidx
                xre, xre_tk, xre_sem = self.xre_rot.next()
                K.dma(SP, out=xre[:], in_=self.src[tile * 128:(tile + 1) * 128, :], reads=[dtk(self.srcname, tile)], writes=[xre_tk],
                      sem=xre_sem)
                st, st_tk, _ = self.st_rot.next()
                ob, ob_tk, ob_sem = self.ob_rot.next()
                jk, jk_tk, _ = self.junk.next()
                K.op(ACT, lambda: nc.scalar.activation(out=jk[:], in_=yv, func=AF.Square, accum_out=st[:, 0:1]),
                     reads=ytk, writes=[jk_tk, st_tk])
                rstd_ops(st, st_tk, D, self.lnexp)
                K.op(DVE, lambda: nc.vector.scalar_tensor_tensor(out=ob[:], in0=yv, scalar=st[:, 2:3], in1=G[:, kidx, :],
                                                                 op0=ALU.mult, op1=ALU.mult),
                     reads=ytk + [st_tk, par_tk], writes=[ob_tk])
                ps.free(yb, 2)
                K.op(POOL, lambda: nc.gpsimd.tensor_tensor(out=ob[:], in0=ob[:], in1=xre[:], op=ALU.add),
                     reads=[ob_tk, xre_tk], writes=[ob_tk])
                K.dma(POOL, out=self.dst[tile * 128:(tile + 1) * 128, :], in_=ob[:], reads=[ob_tk], writes=[dtk(self.dstname, tile)],
                      sem=ob_sem)

        def interleave(*gens):
            gens = list(gens)
            while gens:
                for g in list(gens):
                    try:
                        next(g)
                    except StopIteration:
                        gens.remove(g)

        def ffn_phase(fi, kidx, src, srcname, dst, dstname):
            with ExitStack() as pe_:
                Win = K.sb("Win", [128, 8, 2 * DFF], BF16, pe_)
                Wout = K.sb("Wout", [128, NF, D], BF16, pe_)
                groups = [(0, 6), (6, 12), (12, 17), (17, 22)]
                win_tk = [Tk("win%d" % g) for g in range(4)]
                wout_tk = [Tk("wout%d" % g) for g in range(2)]
                if fi == 0:
                    wv = win_d[fi].rearrange("(k p) n -> p k n", p=128)
                    wov = wout_d[fi].rearrange("(f p) n -> p f n", p=128)
                    for g, (f0, f1) in enumerate(groups):
                        sem = K.newsem("win%d_%d" % (fi, g))
                        for off in (0, DFF):
                            issue_cast_dma(Win[:, :, off + f0 * 128:off + f1 * 128], wv[:, :, off + f0 * 128:off + f1 * 128],
                                           win_tk[g], sem)
                    for g, (f0, f1) in enumerate([(0, 11), (11, 22)]):
                        sem = K.newsem("wout%d_%d" % (fi, g))
                        issue_cast_dma(Wout[:, f0:f1, :], wov[:, f0:f1, :], wout_tk[g], sem)

                    pcs = K.newsem("precast")

                    def precast(blk):
                        def cp(dst, src_, name):
                            K.dma(POOL, out=dst, in_=src_, reads=[], writes=[dtk(name, 0)], sem=pcs)
                        if blk == 2:
                            cp(wmi_s, wmi_d, "wmi_s")
                        elif blk == 3:
                            cp(wmo_s, wmo_d, "wmo_s")
                            cp(w2out_s[0:1408, :], wout_d[1][0:1408, :], "w2out_s")
                        elif blk == 4:
                            cp(w2out_s[1408:2816, :], wout_d[1][1408:2816, :], "w2out_s")
                            cp(w2in_s[0:256, :], win_d[1][0:256, :], "w2in_s")
                        elif blk == 5:
                            cp(w2in_s[256:512, :], win_d[1][256:512, :], "w2in_s")
                            cp(w2in_s[512:768, :], win_d[1][512:768, :], "w2in_s")
                        elif blk == 6:
                            cp(w2in_s[768:1024, :], win_d[1][768:1024, :], "w2in_s")
                else:
                    wv = w2in_s.rearrange("(k p) n -> p k n", p=128)
                    wov = w2out_s.rearrange("(f p) n -> p f n", p=128)
                    qi = 0
                    for g, (f0, f1) in enumerate(groups):
                        sem = K.newsem("win%d_%d" % (fi, g))
                        for off in (0, DFF):
                            K.dma(POOL, out=Win[:, :, off + f0 * 128:off + f1 * 128],
                                  in_=wv[:, :, off + f0 * 128:off + f1 * 128], reads=[dtk("w2in_s", 0)], writes=[win_tk[g]], sem=sem)
                            qi += 1
                    for g, (f0, f1) in enumerate([(0, 11), (11, 22)]):
                        sem = K.newsem("wout%d_%d" % (fi, g))
                        K.dma(POOL, out=Wout[:, f0:f1, :], in_=wov[:, f0:f1, :], reads=[dtk("w2out_s", 0)],
                              writes=[wout_tk[g]], sem=sem)
                        qi += 1

                def gof(f):
                    for g, (f0, f1) in enumerate(groups):
                        if f0 <= f < f1:
                            return g

                pro = Pro(pe_, src, srcname, kidx, False, "f")
                epi = Epi(pe_, src, srcname, dst, dstname, kidx, False, "f", pro.xn_rot)
                hs = [K.sb("h%d" % i, [128, 8, TB], BF16, pe_) for i in range(2)]
                h_tkss = [[Tk("h%d" % i)] for i in range(2)]
                act = K.sb("act", [128, NF, TB], BF16, pe_)
                act_tk = [Tk("act%d" % f) for f in range(NF)]
                sg_rot = Rot(K, pe_, "sg", [128, TB], F32, 1)

                for tt in range(4):
                    xn, xn_tk = pro.elem(tt)
                    pro.pe(xn, xn_tk, hs[0], h_tkss[0], tt)
                for blk in range(NB):
                    if fi == 0:
                        precast(blk)
                    h = hs[blk % 2]
                    h_tks = h_tkss[blk % 2]
                    hn = hs[(blk + 1) % 2]
                    hn_tks = h_tkss[(blk + 1) % 2]
                    pend = {}
                    for f in range(NF):
                        bg, tg = ps.alloc(1)
                        bu, tu = ps.alloc(1)
                        pg = ps.f32(bg)
                        pu = ps.f32(bu)
                        fns = []
                        for k in range(8):
                            fns.append(lambda k=k: nc.tensor.matmul(pg, lhsT=Win[:, k, f * 128:(f + 1) * 128], rhs=h[:, k, :],
                                                                    start=(k == 0), stop=(k == 7)))
                        for k in range(8):
                            fns.append(lambda k=k: nc.tensor.matmul(pu, lhsT=Win[:, k, DFF + f * 128:DFF + (f + 1) * 128],
                                                                    rhs=h[:, k, :], start=(k == 0), stop=(k == 7)))
                        K.group(PE, fns, reads=h_tks + [win_tk[gof(f)]], writes=tg + tu)
                        sg, sg_tk, _ = sg_rot.next()
                        K.op(ACT, lambda: nc.scalar.activation(out=sg[:], in_=pg, func=AF.Silu), reads=tg, writes=[sg_tk])
                        K.op(DVE, lambda: nc.vector.tensor_tensor(out=act[:, f, :], in0=sg[:], in1=pu, op=ALU.mult),
                             reads=[sg_tk] + tu, writes=[act_tk[f]])
                        ps.free(bg)
                        ps.free(bu)
                        if blk + 1 < NB:
                            if f % 4 == 0 and f // 4 < 4:
                                tt = f // 4
                                pend[tt] = pro.elem((blk + 1) * 4 + tt)
                            if f >= 7 and (f - 7) % 4 == 0 and (f - 7) // 4 < 4:
                                tt = (f - 7) // 4
                                pro.pe(pend[tt][0], pend[tt][1], hn, hn_tks, tt)
                    for tt in range(4):
                        yb, ytk = ps.alloc(2)
                        fns = []
                        for n in range(2):
                            o = ps.f32(yb + n)
                            for f in range(NF):
                                fns.append(lambda o=o, n=n, f=f: nc.tensor.matmul(
                                    o, lhsT=act[:, f, tt * 128:(tt + 1) * 128], rhs=Wout[:, f, n * 512:(n + 1) * 512],
                                    start=(f == 0), stop=(f == NF - 1)))
                        K.group(PE, fns, reads=act_tk + wout_tk, writes=ytk)
                        epi.run(blk * 4 + tt, yb, ytk)
                K.barrier()

        def mixer_phase(src, srcname, dst, dstname):
            kidx = 1
            with ExitStack() as pe_:
                Wmi = K.sb("Wmi", [128, 8, 2816], BF16, pe_)
                Wmo = K.sb("Wmo", [128, 8, D], BF16, pe_)
                wmi_tk = Tk("wmi")
                wmo_tk = Tk("wmo")
                wmiv = wmi_s.rearrange("(k p) n -> p k n", p=128)
                wmov = wmo_s.rearrange("(k p) n -> p k n", p=128)
                s1 = K.newsem("wmi")
                K.dma(POOL, out=Wmi[:, :, 0:1408], in_=wmiv[:, :, 0:1408], reads=[dtk("wmi_s", 0)], writes=[wmi_tk], sem=s1)
                K.dma(POOL, out=Wmi[:, :, 1408:2816], in_=wmiv[:, :, 1408:2816], reads=[dtk("wmi_s", 0)], writes=[wmi_tk], sem=s1)
                s2 = K.newsem("wmo")
                K.dma(POOL, out=Wmo[:], in_=wmov, reads=[dtk("wmo_s", 0)], writes=[wmo_tk], sem=s2)
                amask = K.sb("amask", [128, 2, 2, 512], BF16, pe_)
                hmask = K.sb("hmask", [128, 4, 128], BF16, pe_)
                rmask = K.sb("rmask", [128, 512], F32, pe_)
                cs_rot = Rot(K, pe_, "mcs", [128, 2, 128], F32, 2, dma=True)
                mc_tk = Tk("mconst")
                mcs = K.newsem("mconst")
                for (t_, d_) in ((amask[:], amask_d), (hmask[:], hmask_d), (rmask[:], rmask_d)):
                    K.dma(SP, out=t_, in_=d_, reads=[], writes=[mc_tk], sem=mcs)

                pro = Pro(pe_, src, srcname, kidx, True, "m", nxin=1)
                epi = Epi(pe_, src, srcname, dst, dstname, kidx, True, "m", pro.xn_rot)
                h = K.sb("mh", [128, 8, TB], BF16, pe_)
                h_tks = [Tk("mh")]
                q2 = K.sb("q2", [128, 4, TB], F32, pe_)
                ff = K.sb("ff", [128, 4, TB], F32, pe_)
                bb = K.sb("bb", [128, 4, TB], F32, pe_)
                q2_tk = [Tk("q2_%d" % c) for c in range(4)]
                ff_tk = [Tk("ff_%d" % c) for c in range(4)]
                bb_tk = [Tk("bb_%d" % c) for c in range(4)]
                tA = Rot(K, pe_, "tA", [128, TB], F32, 2)
                tB = Rot(K, pe_, "tB", [128, TB], F32, 2)
                qt = K.sb("qt", [128, 4, TB], BF16, pe_)
                kt = K.sb("kt", [128, 4, TB], BF16, pe_)
                k2T = K.sb("k2T", [128, 4, TB], BF16, pe_)
                gs = K.sb("gs", [128, 4, TB], BF16, pe_)
                dec = K.sb("dec", [128, 4, 8], F32, pe_)
                hb_tk = [Tk("hgblk%d" % c) for c in range(4)]
                ktw_tk = [Tk("ktw%d" % c) for c in range(4)]
                k2w_tk = [Tk("k2w%d" % c) for c in range(4)]
                catT = K.sb("catT", [128, 8, TB], BF16, pe_)
                cat_tk = [Tk("cat%d" % c) for c in range(8)]
                NCH = 2
                RING = 5
                qkr_r = Rot(K, pe_, "qkr", [128, 640], BF16, 1)
                vtm = K.sb("vtm", [128, 4, 512], BF16, pe_)
                vtm_tk = [Tk("vtm%d" % i) for i in range(4)]
                vbe = K.sb("vbe", [128, RING, 2, 65], BF16, pe_)
                vb_tk = [Tk("vb%d" % i) for i in range(RING)]
                qT_r = Rot(K, pe_, "qT", [128, 8, 128], BF16, 4)
                kTa = K.sb("kTa", [128, 2, RING * 128], BF16, pe_)
                kT_tk = [Tk("kT%d" % i) for i in range(RING)]
                PT_r = Rot(K, pe_, "PTs", [128, 2, 2, 512], BF16, NCH)
                atm_r = Rot(K, pe_, "atm", [128, 512], BF16, NCH)
                ra_r = Rot(K, pe_, "ra", [128, 10, 32], F32, 1)
                rb_r = Rot(K, pe_, "rb", [128, 10, 32], F32, 1)
                dn_r = Rot(K, pe_, "dn", [128, 2, 8], F32, NCH)
                esink = K.sb("esink", [128, 8], F32, pe_)
                negc = K.sb("negc", [128, 1], F32, pe_)
                SHIFT = 16.0
                st32 = K.sb("st32", [128, 4, 128], F32, pe_)
                st32_tk = [Tk("st32")]
                stb_rot = Rot(K, pe_, "stb", [128, 4, 128], BF16, 2)
                sqr = K.sb("sqr", [128, 512], BF16, pe_)
                sqr_tk = Tk("sqr")
                rsb = K.sb("rsb", [128, 512], F32, pe_)
                rsb_tk = Tk("rsb")
                t1 = K.sb("t1", [128, 512], F32, pe_)
                t1_tk = Tk("t1")

                for (qT_, qT_tk_, _) in qT_r.items:
                    K.op(POOL, lambda: nc.gpsimd.memset(qT_[:], 0.0), writes=[qT_tk_])
                K.op(POOL, lambda: nc.gpsimd.memset(kTa[:], 0.0), writes=kT_tk)
                K.op(POOL, lambda: nc.gpsimd.memset(vbe[:], 1.0), writes=vb_tk)
                K.op(DVE, lambda: nc.vector.memset(negc[:], -SHIFT), writes=[mc_tk])
                K.op(DVE, lambda: nc.vector.tensor_scalar(out=esink[:], in0=sinks[:], scalar1=-SHIFT, scalar2=None, op0=ALU.add),
                     reads=[const_tk, mc_tk], writes=[mc_tk])
                K.op(ACT, lambda: nc.scalar.activation(out=esink[:], in_=esink[:], func=AF.Exp), reads=[mc_tk], writes=[mc_tk])
                K.op(POOL, lambda: nc.gpsimd.memset(st32[:], 0.0), writes=st32_tk)
                stb0, stb0_tk, _ = stb_rot.next()
                K.op(POOL, lambda: nc.gpsimd.memset(stb0[:], 0.0), writes=[stb0_tk])
                state = {"stb": (stb0, stb0_tk)}
                QSC = float(0.5 * 128 ** -0.5)

                def galloc(n):
                    tries = 0
                    while True:
                        r = ps.try_alloc(n)
                        if r is not None:
                            return r
                        tries += 1
                        assert tries < 10000, "PSUM livelock"
                        yield

                def fm_proj(col0, b, tks):
                    pv = ps.f32(b)
                    K.group(PE, [(lambda k=k: nc.tensor.matmul(pv, lhsT=Wmi[:, k, col0:col0 + 128], rhs=h[:, k, :],
                                                             start=(k == 0), stop=(k == 7))) for k in range(8)],
                            reads=h_tks + [wmi_tk], writes=tks)
                    return pv

                def gen_prologue(blk):
                    for tt in range(4):
                        xn, xn_tk = pro.elem(blk * 4 + tt)
                        yield
                        pro.pe(xn, xn_tk, h, h_tks, tt)
                        yield

                def gen_P(blk):
                    for tt in range(4):
                        cs = slice(tt * 128, (tt + 1) * 128)
                        hb_, htk = yield from galloc(1)
                        K.group(PE, [(lambda k=k: nc.tensor.matmul(ps.f32(hb_), lhsT=h[:, k, cs], rhs=Wmi[:, k, 1792:2304],
                                                                 start=(k == 0), stop=(k == 7))) for k in range(8)],
                                reads=h_tks + [wmi_tk], writes=htk)
                        K.op(ACT, lambda: nc.scalar.activation(out=vtm[:, tt, :], in_=ps.f32(hb_), func=AF.Copy), reads=htk, writes=[vtm_tk[tt]])
                        ps.free(hb_)
                        yield
                    for c in range(4):
                        b, tks = yield from galloc(1)
                        pv = fm_proj(1280 + c * 128, b, tks)
                        a_, a_tk, _ = tA.next()
                        K.op(ACT, lambda: nc.scalar.activation(out=a_[:], in_=pv, func=AF.Tanh, scale=0.5), reads=tks, writes=[a_tk])
                        K.op(DVE, lambda: nc.vector.tensor_scalar(out=ff[:, c, :], in0=a_[:], scalar1=c01[:, 1, c:c + 1], scalar2=c01[:, 0, c:c + 1],
                                                                  op0=ALU.mult, op1=ALU.add), reads=[a_tk, par_tk], writes=[ff_tk[c]])
                        ps.free(b)
                        yield
                        b, tks = yield from galloc(1)
                        pv = fm_proj(768 + c * 128, b, tks)
                        a_, a_tk, _ = tA.next()
                        K.op(ACT, lambda: nc.scalar.activation(out=a_[:], in_=pv, func=AF.Tanh, scale=0.5), reads=tks, writes=[a_tk])
                        K.op(DVE, lambda: nc.vector.scalar_tensor_tensor(out=q2[:, c, :], in0=a_[:], scalar=1.0, in1=pv, op0=ALU.add, op1=ALU.mult),
                             reads=[a_tk] + tks, writes=[q2_tk[c]])
                        ps.free(b)
                        yield
                        b, tks = yield from galloc(1)
                        pv = fm_proj(2304 + c * 128, b, tks)
                        a_, a_tk, _ = tA.next()
                        K.op(ACT, lambda: nc.scalar.activation(out=a_[:], in_=pv, func=AF.Tanh, scale=0.5), reads=tks, writes=[a_tk])
                        K.op(DVE, lambda: nc.vector.scalar_tensor_tensor(out=gs[:, c, :], in0=a_[:], scalar=1.0, in1=pv, op0=ALU.add, op1=ALU.mult),
                             reads=[a_tk] + tks, writes=[hb_tk[c]])
                        ps.free(b)
                        yield
                    for c in range(4):
                        wtk = [hb_tk[c]]
                        a_, a_tk, _ = tA.next()
                        K.op(ACT, lambda: nc.scalar.activation(out=a_[:], in_=ff[:, c, :], func=AF.Ln), reads=[ff_tk[c]], writes=[a_tk])
                        K.op(DVE, lambda: nc.vector.tensor_tensor_scan(out=bb[:, c, :], data0=rmask[:], data1=a_[:], initial=0.0,
                                                                       op0=ALU.mult, op1=ALU.add),
                             reads=[a_tk, mc_tk], writes=[bb_tk[c]])
                        K.op(DVE, lambda: nc.vector.tensor_scalar(out=ff[:, c, :], in0=ff[:, c, :], scalar1=-1.0, scalar2=1.0, op0=ALU.mult, op1=ALU.add),
                             reads=[ff_tk[c], a_tk], writes=[ff_tk[c]])
                        yield
                        e_, e_tk, _ = tB.next()
                        K.op(ACT, lambda: nc.scalar.activation(out=e_[:], in_=bb[:, c, :], func=AF.Exp), reads=[bb_tk[c]], writes=[e_tk])
                        K.op(DVE, lambda: nc.vector.scalar_tensor_tensor(out=qt[:, c, :], in0=q2[:, c, :], scalar=QSC, in1=e_[:],
                                                                         op0=ALU.mult, op1=ALU.mult),
                             reads=[q2_tk[c], e_tk], writes=wtk)
                        K.op(DVE, lambda: nc.vector.tensor_copy(out=dec[:, c, :],
                                                                in_=e_[:].rearrange("p (n t) -> p n t", t=64)[:, :, 63]),
                             reads=[e_tk], writes=wtk)
                        yield
                        x_, x_tk, _ = tB.next()
                        K.op(ACT, lambda: nc.scalar.activation(out=x_[:], in_=bb[:, c, :], func=AF.Exp, scale=-1.0), reads=[bb_tk[c]], writes=[x_tk])
                        K.op(DVE, lambda: nc.vector.tensor_tensor(out=kt[:, c, :], in0=ff[:, c, :], in1=x_[:], op=ALU.mult),
                             reads=[ff_tk[c], x_tk], writes=[ktw_tk[c]])
                        yield
                        d_, d_tk, _ = tA.next()
                        b3 = bb[:, c, :].rearrange("p (n t) -> p n t", t=64)
                        K.op(DVE, lambda: nc.vector.tensor_tensor(out=d_[:].rearrange("p (n t) -> p n t", t=64),
                                                                  in0=b3[:, :, 63:64].broadcast_to([128, 8, 64]), in1=b3, op=ALU.subtract),
                             reads=[bb_tk[c]], writes=[d_tk])
                        K.op(ACT, lambda: nc.scalar.activation(out=d_[:], in_=d_[:], func=AF.Exp), reads=[d_tk], writes=[d_tk])
                        K.op(DVE, lambda: nc.vector.tensor_tensor(out=k2T[:, c, :], in0=ff[:, c, :], in1=d_[:], op=ALU.mult),
                             reads=[ff_tk[c], d_tk], writes=[k2w_tk[c]])
                        yield

                pre_done = {}
                blkst = {}

                def gen_Apre(blk, tts):
                    cs_t, cs_tk, cs_sem = cs_rot.next()
                    K.dma(SP, out=cs_t[:, 0, :], in_=rope_d[:, 0, blk * 128:(blk + 1) * 128], reads=[dtk("rope", 0)], writes=[cs_tk], sem=cs_sem)
                    K.dma(SP, out=cs_t[:, 1, :], in_=rope_d[:, 1, blk * 128:(blk + 1) * 128], reads=[dtk("rope", 0)], writes=[cs_tk], sem=cs_sem)
                    for tt in tts:
                        tile = blk * 4 + tt
                        cs = slice(tt * 128, (tt + 1) * 128)
                        slot = tile % RING
                        qkr, qkr_tk, _ = qkr_r.next()
                        qT, qT_tk, _ = qT_r.items[tt]
                        ra, ra_tk, _ = ra_r.next()
                        rb, rb_tk, _ = rb_r.next()
                        pb, ptk = yield from galloc(2)
                        fns = []
                        for k in range(8):
                            fns.append(lambda k=k: nc.tensor.matmul(ps.f32(pb), lhsT=h[:, k, cs], rhs=Wmi[:, k, 0:512],
                                                                    start=(k == 0), stop=(k == 7)))
                        for k in range(8):
                            fns.append(lambda k=k: nc.tensor.matmul(ps.f32(pb + 1)[:, 0:256], lhsT=h[:, k, cs], rhs=Wmi[:, k, 512:768],
                                                                    start=(k == 0), stop=(k == 7)))
                        K.group(PE, fns, reads=h_tks + [wmi_tk], writes=ptk)
                        yield
                        K.op(ACT, lambda: nc.scalar.activation(out=vbe[:, slot, :, 0:64],
                                                               in_=ps.f32(pb + 1)[:, 128:256].rearrange("p (g d) -> p g d", d=64), func=AF.Copy),
                             reads=ptk, writes=[vb_tk[slot]])
                        qk3 = ps.f32(pb, 2)[:, 0:640].rearrange("p (h d) -> p h d", d=64)
                        cb_ = cs_t[:, 0, tt * 32:(tt + 1) * 32].unsqueeze(1).broadcast_to([128, 10, 32])
                        sb_ = cs_t[:, 1, tt * 32:(tt + 1) * 32].unsqueeze(1).broadcast_to([128, 10, 32])
                        o3 = qkr[:].rearrange("p (h d) -> p h d", d=64)
                        K.op(DVE, lambda: nc.vector.tensor_tensor(out=ra[:], in0=qk3[:, :, 0:32], in1=cb_, op=ALU.mult), reads=ptk + [cs_tk], writes=[ra_tk])
                        K.op(DVE, lambda: nc.vector.tensor_tensor(out=rb[:], in0=qk3[:, :, 32:64], in1=sb_, op=ALU.mult), reads=ptk + [cs_tk], writes=[rb_tk])
                        K.op(DVE, lambda: nc.vector.tensor_tensor(out=o3[:, :, 0:32], in0=ra[:], in1=rb[:], op=ALU.subtract),
                             reads=[ra_tk, rb_tk], writes=[qkr_tk])
                        yield
                        K.op(DVE, lambda: nc.vector.tensor_tensor(out=ra[:], in0=qk3[:, :, 32:64], in1=cb_, op=ALU.mult), reads=ptk + [cs_tk], writes=[ra_tk])
                        K.op(DVE, lambda: nc.vector.tensor_tensor(out=rb[:], in0=qk3[:, :, 0:32], in1=sb_, op=ALU.mult), reads=ptk + [cs_tk], writes=[rb_tk])
                        K.op(DVE, lambda: nc.vector.tensor_tensor(out=o3[:, :, 32:64], in0=ra[:], in1=rb[:], op=ALU.add),
                             reads=[ra_tk, rb_tk], writes=[qkr_tk])
                        ps.free(pb, 2)
                        yield
                        tb2, ttk = yield from galloc(1)
                        tv = ps.bf(tb2)
                        fns = [(lambda j=j: nc.tensor.transpose(tv[:, j * 128:(j + 1) * 128], qkr[:, j * 128:(j + 1) * 128], ident[:]))
                               for j in range(5)]
                        K.group(PE, fns, reads=[qkr_tk, const_tk], writes=ttk)
                        qv = tv[:, 0:512].rearrange("p (j t) -> p j t", t=128)
                        qT4 = qT[:].rearrange("p (j two) t -> p j two t", two=2)
                        K.op(DVE, lambda: nc.vector.tensor_copy(out=qT4[0:64, :, 0, :], in_=qv[0:64]), reads=ttk, writes=[qT_tk])
                        K.op(DVE, lambda: nc.vector.tensor_copy(out=qT4[0:64, :, 1, :], in_=qv[64:128]), reads=ttk, writes=[qT_tk])
                        K.op(ACT, lambda: nc.scalar.activation(out=kTa[0:64, 0, slot * 128:(slot + 1) * 128], in_=tv[0:64, 512:640], func=AF.Copy),
                             reads=ttk, writes=[kT_tk[slot]])
                        K.op(ACT, lambda: nc.scalar.activation(out=kTa[0:64, 1, slot * 128:(slot + 1) * 128], in_=tv[64:128, 512:640],
                                                               func=AF.Copy), reads=ttk, writes=[kT_tk[slot]])
                        ps.free(tb2)
                        pre_done[tile] = True
                        yield

                def gen_Amain(blk, tt):
                    if True:
                        tile = blk * 4 + tt
                        cs = slice(tt * 128, (tt + 1) * 128)
                        slot = tile % RING
                        pslot = (tile - 1) % RING
                        tries = 0
                        while not pre_done.get(tile, False):
                            tries += 1
                            assert tries < 10000, "attention main chain never released"
                            yield
                        qT, qT_tk, _ = qT_r.items[tt]
                        PTs, PT_tk, _ = PT_r.next()
                        atm, atm_tk, _ = atm_r.next()
                        dn, dn_tk, _ = dn_r.next()
                        kbs = [(1, slot)] if tile == 0 else [(0, pslot), (1, slot)]
                        var = 0
                        for g in range(2):
                            sb2, stk = yield from galloc(2)
                            fns = []
                            for (kb, sl) in kbs:
                                o = ps.f32(sb2 + kb)
                                fns.append(lambda o=o, g=g, sl=sl: nc.tensor.matmul(
                                    o, lhsT=kTa[:, g, sl * 128:(sl + 1) * 128], rhs=qT[:, g * 4:(g + 1) * 4, :].rearrange("p h t -> p (h t)"),
                                    start=True, stop=False))
                                fns.append(lambda o=o, kb=kb: nc.tensor.matmul(o, lhsT=ident[:], rhs=amask[:, var, kb, :], start=False, stop=True))
                            K.group(PE, fns, reads=[qT_tk, const_tk, mc_tk] + kT_tk, writes=stk)
                            for (kb, sl) in kbs:
                                K.op(ACT, lambda: nc.scalar.activation(out=PTs[:, g, kb, :], in_=ps.f32(sb2 + kb), func=AF.Exp, scale=0.125,
                                                                       bias=negc[:, 0:1]),
                                     reads=[stk[kb], mc_tk], writes=[PT_tk])
                            ps.free(sb2, 2)
                            yield
                        ob2, otk = yield from galloc(2)
                        fns = []
                        for hh in range(8):
                            g, hq = hh // 4, hh % 4
                            for i, (kb, sl) in enumerate(kbs):
                                fns.append(lambda g=g, hq=hq, kb=kb, sl=sl, i=i: nc.tensor.matmul(
                                    ps.f32(ob2 + g)[:, hq * 65:(hq + 1) * 65], lhsT=PTs[:, g, kb, hq * 128:(hq + 1) * 128],
                                    rhs=vbe[:, sl, g, :], start=(i == 0), stop=(i == len(kbs) - 1)))
                        K.group(PE, fns, reads=[PT_tk] + vb_tk, writes=otk)
                        yield
                        O4 = ps.f32(ob2, 2).rearrange("p (b c) -> p b c", c=512)[:, :, 0:260].rearrange("p b (h e) -> p b h e", e=65)
                        K.op(DVE, lambda: nc.vector.tensor_tensor(out=dn[:, 0, :].rearrange("p (b h) -> p b h", h=4), in0=O4[:, :, :, 64],
                                                                  in1=esink[:].rearrange("p (b h) -> p b h", h=4), op=ALU.add),
                             reads=otk + [mc_tk], writes=[dn_tk])
                        K.op(DVE, lambda: nc.vector.reciprocal(out=dn[:, 1, :], in_=dn[:, 0, :]), reads=[dn_tk], writes=[dn_tk])
                        K.op(DVE, lambda: nc.vector.tensor_tensor(out=atm[:].rearrange("p (b h d) -> p b h d", h=4, d=64), in0=O4[:, :, :, 0:64],
                                                                  in1=dn[:, 1, :].rearrange("p (b h) -> p b h", h=4).unsqueeze(3).broadcast_to([128, 2, 4, 64]),
                                                                  op=ALU.mult),
                             reads=otk + [dn_tk], writes=[atm_tk])
                        ps.free(ob2, 2)
                        yield
                        ab_, atk = yield from galloc(1)
                        av = ps.bf(ab_)
                        K.group(PE, [(lambda j=j: nc.tensor.transpose(av[:, j * 128:(j + 1) * 128], atm[:, j * 128:(j + 1) * 128], ident[:]))
                                     for j in range(4)], reads=[atm_tk, const_tk], writes=atk)
                        K.op(ACT, lambda: nc.scalar.activation(out=catT[:, 0:4, cs], in_=av[:, 0:512].rearrange("p (j t) -> p j t", t=128),
                                                               func=AF.Copy), reads=atk, writes=cat_tk[0:4])
                        ps.free(ab_)
                        yield

                def state_step(n, ubank, utk, sdst, sdst_tk):
                    K.op(DVE, lambda: nc.vector.tensor_tensor(out=st32[:], in0=st32[:], in1=dec[:, :, n:n + 1].broadcast_to([128, 4, 128]),
                                                              op=ALU.mult), reads=st32_tk + hb_tk, writes=st32_tk)
                    K.op(DVE, lambda: nc.vector.tensor_tensor(out=st32[:], in0=st32[:], in1=ps.f32(ubank).rearrange("p (c e) -> p c e", e=128),
                                                              op=ALU.add), reads=st32_tk + utk, writes=st32_tk)
                    K.op(ACT, lambda: nc.scalar.activation(out=sdst[:], in_=st32[:], func=AF.Copy), reads=st32_tk, writes=[sdst_tk])

                k2_r = Rot(K, pe_, "k2AB", [128, 2, 4, 128], BF16, 2)
                scm_r = Rot(K, pe_, "scm2", [128, 4, 128], BF16, 2)

                def gen_H(blk):
                    pre = {}
                    post = {}

                    def h_pre(tt):
                        cs = slice(tt * 128, (tt + 1) * 128)
                        vt = vtm[:, tt, :]
                        vt_tk = vtm_tk[tt]
                        k2, k2_tk, _ = k2_r.next()
                        sc_, sc_tk, _ = scm_r.next()
                        kb_, ktk = yield from galloc(1)
                        kv = ps.bf(kb_)
                        K.group(PE, [(lambda c=c: nc.tensor.transpose(kv[:, c * 128:(c + 1) * 128], k2T[:, c, cs], ident[:])) for c in range(4)],
                                reads=hb_tk + k2w_tk + [const_tk], writes=ktk)
                        kv3 = kv[:, 0:512].rearrange("p (c d) -> p c d", d=128)
                        K.op(DVE, lambda: nc.vector.tensor_scalar(out=k2[:, 0], in0=kv3, scalar1=rowm[:, 0:1], scalar2=None, op0=ALU.mult),
                             reads=ktk + [const_tk], writes=[k2_tk])
                        K.op(DVE, lambda: nc.vector.tensor_scalar(out=k2[:, 1], in0=kv3, scalar1=rowm[:, 1:2], scalar2=None, op0=ALU.mult),
                             reads=ktk + [const_tk], writes=[k2_tk])
                        ps.free(kb_)
                        sb_2, sctk = yield from galloc(1)
                        scv = ps.f32(sb_2)
                        K.group(PE, [(lambda c=c: nc.tensor.matmul(scv[:, c * 128:(c + 1) * 128], lhsT=kt[:, c, cs], rhs=qt[:, c, cs],
                                                                 start=True, stop=True)) for c in range(4)],
                                reads=hb_tk + ktw_tk, writes=sctk)
                        yield
                        K.op(DVE, lambda: nc.vector.tensor_tensor(out=sc_[:], in0=scv.rearrange("p (c t) -> p c t", t=128), in1=hmask[:], op=ALU.mult),
                             reads=sctk + [mc_tk], writes=[sc_tk])
                        ps.free(sb_2)
                        ua, uatk = yield from galloc(1)
                        ub, ubtk = yield from galloc(1)
                        K.group(PE, [(lambda c=c: nc.tensor.matmul(ps.f32(ua)[:, c * 128:(c + 1) * 128], lhsT=k2[:, 0, c, :],
                                                                 rhs=vt[:, c * 128:(c + 1) * 128], start=True, stop=True)) for c in range(4)],
                                reads=[k2_tk, vt_tk], writes=uatk)
                        K.group(PE, [(lambda c=c: nc.tensor.matmul(ps.f32(ub)[:, c * 128:(c + 1) * 128], lhsT=k2[:, 1, c, :],
                                                                 rhs=vt[:, c * 128:(c + 1) * 128], start=True, stop=True)) for c in range(4)],
                                reads=[k2_tk, vt_tk], writes=ubtk)
                        pre[tt] = (sc_, sc_tk, ua, uatk, ub, ubtk)
                        yield

                    def h_chain(tt):
                        vt = vtm[:, tt, :]
                        vt_tk = vtm_tk[tt]
                        sc_, sc_tk, ua, uatk, ub, ubtk = pre[tt]
                        oo, ootk = yield from galloc(1)
                        oov = ps.f32(oo)
                        stA, stA_tk = state["stb"]
                        fns = []
                        for c in range(4):
                            fns.append(lambda c=c: nc.tensor.matmul(oov[:, c * 128:(c + 1) * 128], lhsT=vt[:, c * 128:(c + 1) * 128],
                                                                    rhs=sc_[:, c, :], start=(c == 0), stop=False, skip_group_check=True))
                        for c in range(4):
                            fns.append(lambda c=c: nc.tensor.matmul(oov[:, c * 128:c * 128 + 64], lhsT=stA[:, c, :],
                                                                    rhs=qt[:, c, tt * 128:tt * 128 + 64], start=False, stop=False,
                                                                    skip_group_check=True))
                        K.group(PE, fns, reads=[vt_tk, sc_tk, stA_tk] + hb_tk, writes=ootk)
                        stB, stB_tk, _ = stb_rot.next()
                        state_step(2 * tt, ua, uatk, stB, stB_tk)
                        ps.free(ua)
                        yield
                        K.group(PE, [(lambda c=c: nc.tensor.matmul(oov[:, c * 128 + 64:(c + 1) * 128], lhsT=stB[:, c, :],
                                                                 rhs=qt[:, c, tt * 128 + 64:(tt + 1) * 128], start=False, stop=True,
                                                                 skip_group_check=True)) for c in range(4)],
                                reads=[stB_tk] + hb_tk, writes=ootk)
                        stC, stC_tk, _ = stb_rot.next()
                        state_step(2 * tt + 1, ub, ubtk, stC, stC_tk)
                        ps.free(ub)
                        state["stb"] = (stC, stC_tk)
                        post[tt] = (oo, ootk)
                        yield

                    def h_post(tt):
                        cs = slice(tt * 128, (tt + 1) * 128)
                        oo, ootk = post[tt]
                        oov = ps.f32(oo)
                        K.op(ACT, lambda: nc.scalar.activation(out=sqr[:], in_=oov, func=AF.Square), reads=ootk, writes=[sqr_tk])
                        nb_, ntk = yield from galloc(1)
                        K.group(PE, [lambda: nc.tensor.matmul(ps.f32(nb_), lhsT=onesb[:], rhs=sqr[:], start=True, stop=True)],
                                reads=[sqr_tk, const_tk], writes=ntk)
                        yield
                        K.op(ACT, lambda: nc.scalar.activation(out=rsb[:], in_=ps.f32(nb_), func=AF.Ln, scale=1.0 / 128, bias=eps_t[:, 0:1]),
                             reads=ntk + [const2_tk], writes=[rsb_tk])
                        ps.free(nb_)
                        K.op(ACT, lambda: nc.scalar.activation(out=rsb[:], in_=rsb[:], func=AF.Exp, scale=-0.5), reads=[rsb_tk], writes=[rsb_tk])
                        K.op(DVE, lambda: nc.vector.scalar_tensor_tensor(out=t1[:], in0=oov, scalar=gnh[:, 0:1], in1=rsb[:], op0=ALU.mult, op1=ALU.mult),
                             reads=ootk + [rsb_tk, par_tk], writes=[t1_tk])
                        ps.free(oo)
                        K.op(DVE, lambda: nc.vector.tensor_tensor(out=catT[:, 4:8, cs], in0=t1[:].rearrange("p (c t) -> p c t", t=128),
                                                                  in1=gs[:, :, cs], op=ALU.mult),
                             reads=[t1_tk] + hb_tk, writes=cat_tk[4:8])
                        yield

                    yield from h_pre(0)
                    for tt in range(4):
                        yield from h_chain(tt)
                        if tt + 1 < 4:
                            yield from h_pre(tt + 1)
                        yield from h_post(tt)

                def gen_O(blk):
                    for tt in range(4):
                        yb, ytk = yield from galloc(2)
                        fns = []
                        for n in range(2):
                            o = ps.f32(yb + n)
                            for kc in range(8):
                                fns.append(lambda o=o, n=n, kc=kc: nc.tensor.matmul(
                                    o, lhsT=catT[:, kc, tt * 128:(tt + 1) * 128], rhs=Wmo[:, kc, n * 512:(n + 1) * 512],
                                    start=(kc == 0), stop=(kc == 7)))
                        K.group(PE, fns, reads=cat_tk + [wmo_tk], writes=ytk)
                        yield
                        epi.run(blk * 4 + tt, yb, ytk)
                        yield

                interleave(gen_prologue(0))
                for blk in range(NB):
                    gens = [gen_P(blk), gen_Apre(blk, [0, 1, 2, 3]), gen_Amain(blk, 0), gen_Amain(blk, 1)]
                    if blk > 0:
                        gens.insert(0, gen_O(blk - 1))
                    interleave(*gens)
                    gens = [gen_H(blk), gen_Amain(blk, 2), gen_Amain(blk, 3)]
                    if blk + 1 < NB:
                        gens.append(gen_prologue(blk + 1))
                    interleave(*gens)
                interleave(gen_O(NB - 1))
                K.barrier()

        dsts = {1: out_d if stop_after == 1 else x1_d, 2: out_d if stop_after == 2 else x2_d, 3: out_d}
        names = {1: "out" if stop_after == 1 else "x1", 2: "out" if stop_after == 2 else "x2", 3: "out"}
        if stop_after == 0:
            dbg = K.sb("dbg", [128, D], F32)
            dbg_tk = Tk("dbg")
            dsem = K.newsem("dbg")
            K.op(DVE, lambda: nc.vector.tensor_copy(out=dbg[:], in_=G[:, 0, :]), reads=[par_tk], writes=[dbg_tk])
            K.dma(SP, out=out_d[0:128, :], in_=dbg[:], reads=[dbg_tk], writes=[], sem=dsem)
            K.op(DVE, lambda: nc.vector.tensor_copy(out=dbg[:, 0:24], in_=a_col[:].rearrange("p a b -> p (a b)")), reads=[par_tk, dbg_tk], writes=[dbg_tk])
            K.op(DVE, lambda: nc.vector.tensor_copy(out=dbg[:, 24:48], in_=sh_col[:].rearrange("p a b -> p (a b)")), reads=[par_tk, dbg_tk], writes=[dbg_tk])
            K.op(DVE, lambda: nc.vector.tensor_copy(out=dbg[:, 48:52], in_=lb[:]), reads=[par_tk, dbg_tk], writes=[dbg_tk])
            K.dma(SP, out=out_d[128:256, :], in_=dbg[:], reads=[dbg_tk], writes=[], sem=dsem)
            K.barrier()
            return nc
        ffn_phase(0, 0, x_d, "x", dsts[1], names[1])
        if stop_after >= 2:
            mixer_phase(x1_d, "x1", dsts[2], names[2])
        if stop_after >= 3:
            ffn_phase(1, 2, x2_d, "x2", out_d, "out")
        K.barrier()
        print("instructions:", K.n_inst, "sems:", len(K.sems))
    return nc


def _consts():
    bf = ml_dtypes.bfloat16
    ident = np.eye(128, dtype=np.float32).astype(bf)
    onesb = np.ones((128, 128), np.float32).astype(bf)
    NEG = -1e30
    am = np.zeros((128, 2, 256), np.float32)
    am[0:64, 0, 192:256] = NEG
    am[64:128, 0, 0:64] = NEG
    am[0:64, 1, 64:128] = NEG
    am[64:128, 1, 128:192] = NEG
    s = np.arange(128)[:, None]
    t = np.arange(128)[None, :]
    hm = ((s // 64 == t // 64) & (s <= t)).astype(np.float32)
    hmask = np.broadcast_to(hm[:, None, :], (128, 4, 128)).copy()
    rmask = np.ones((128, 512), np.float32)
    rmask[:, ::64] = 0.0
    inv_freq = (1.0 / (np.float32(10000.0) ** (np.arange(0, 64, 2, dtype=np.float32) / np.float32(64)))).astype(np.float32)
    invf = np.broadcast_to(inv_freq[None, :], (128, 32)).copy()
    rowm = np.zeros((128, 2), np.float32)
    rowm[0:64, 0] = 1.0
    rowm[64:128, 1] = 1.0
    amT = np.zeros((128, 2, 2, 4, 128), np.float32)
    for v in range(2):
        for kb in range(2):
            amT[:, v, kb, :, :] = am[:, v, kb * 128:(kb + 1) * 128].T[:, None, :]
    amT = amT.reshape(128, 2, 2, 512)
    return dict(ident=ident, onesb=onesb, amask=amT.astype(bf), hmask=hmask.astype(bf), rmask=rmask, invf=invf, rowm=rowm)


def make_in_maps(x, c, positions, w_cond, b_cond, norm_pre, norm_post, ffn_w_in, ffn_w_out,
                 w_mix_in, w_mix_out, attn_sinks, hgrn_lb_logits, hgrn_gnorm):
    f = np.float32
    cst = _consts()
    shared = dict(
        w_cond=np.ascontiguousarray(w_cond[0], f), b_cond=np.ascontiguousarray(np.broadcast_to(np.asarray(b_cond[0], f)[None, :], (128, 9 * D))),
        npre=np.ascontiguousarray(np.asarray(norm_pre[0], f).reshape(3, 8, 128).transpose(2, 0, 1)),
        npost=np.ascontiguousarray(np.broadcast_to(np.asarray(norm_post[0], f)[None], (128, 3, D))),
        w_in=np.ascontiguousarray(ffn_w_in[0], f), w_out=np.ascontiguousarray(ffn_w_out[0], f),
        w_mi=np.ascontiguousarray(w_mix_in[0], f), w_mo=np.ascontiguousarray(w_mix_out[0], f),
        sinks=np.ascontiguousarray(np.broadcast_to(np.asarray(attn_sinks[0], f)[None], (128, 8))),
        lbl=np.ascontiguousarray(np.asarray(hgrn_lb_logits, f).reshape(2, 4, 128).transpose(2, 0, 1)),
        gn=np.ascontiguousarray(np.asarray(hgrn_gnorm[0], f)[:, None]),
        **cst)
    maps = []
    for b in range(8):
        m = dict(shared)
        m["x"] = np.ascontiguousarray(x[b], f)
        m["ccol"] = np.ascontiguousarray(np.asarray(c[b], f).reshape(8, 128).T)
        m["pos"] = np.ascontiguousarray(np.asarray(positions[b], np.int32).reshape(NT, 128).T)
        maps.append(m)
    return maps


def kernel(**inputs):
    nc = build(3)
    maps = make_in_maps(**inputs)
    res = run_bass_kernel_spmd(nc, maps, core_ids=list(range(8)))
    return np.stack([np.asarray(r["out"], np.float32) for r in res.results], axis=0)
```

```python
import numpy as np
from contextlib import ExitStack
import ml_dtypes
import concourse.bass as bass
import concourse.mybir as mybir
from concourse.bass_utils import run_bass_kernel_spmd

F32 = mybir.dt.float32
BF16 = mybir.dt.bfloat16
I32 = mybir.dt.int32
AF = mybir.ActivationFunctionType
ALU = mybir.AluOpType
AX = mybir.AxisListType

S = 4096
D = 1024
DFF = 2816
NF = DFF // 128
TB = 512
NB = S // TB
NT = S // 128
EPS = 1e-6
PI = float(np.pi)
TWO_PI = float(2 * np.pi)


class Tk:
    __slots__ = ("w", "r", "name", "excl")

    def __init__(self, name="", excl=False):
        self.w = None
        self.r = {}
        self.name = name
        self.excl = excl


class Sem:
    def __init__(self, h, name):
        self.h = h
        self.cnt = 0
        self.name = name


class Eng:
    def __init__(self, name, h, sem):
        self.name = name
        self.h = h
        self.sem = sem
        self.seen = {}


class Kern:
    def __init__(self, nc, es):
        self.nc = nc
        self.es = es
        self.sems = []
        self.PE = Eng("pe", nc.tensor, self.newsem("pe"))
        self.ACT = Eng("act", nc.scalar, self.newsem("act"))
        self.DVE = Eng("dve", nc.vector, self.newsem("dve"))
        self.POOL = Eng("pool", nc.gpsimd, self.newsem("pool"))
        self.SP = Eng("sp", nc.sync, self.newsem("sp"))
        self.engs = [self.PE, self.ACT, self.DVE, self.POOL, self.SP]
        self.n_inst = 0

    def newsem(self, name):
        self.sid = getattr(self, "sid", 0) + 1
        s = Sem(self.es.enter_context(self.nc.semaphore("m%d_%s" % (self.sid, name))), name)
        self.sems.append(s)
        return s

    def sb(self, name, shape, dt, es=None):
        self.uid = getattr(self, "uid", 0) + 1
        return (es or self.es).enter_context(self.nc.sbuf_tensor("s%d_%s" % (self.uid, name), shape, dt))

    def _wait(self, E, toks):
        for (s, v) in toks:
            if E.seen.get(id(s), 0) >= v:
                continue
            E.h.wait_ge(s.h, v)
            E.seen[id(s)] = v

    def _deps(self, E, reads, writes):
        toks = []
        for t in reads:
            if t.w is not None:
                toks.append(t.w)
            if t.excl:
                for tok in t.r.values():
                    if tok[0] is not E.sem:
                        toks.append(tok)
        for t in writes:
            if t.w is not None and t.w[0] is not E.sem:
                toks.append(t.w)
            for tok in t.r.values():
                if tok[0] is not E.sem or E is not self.PE:
                    toks.append(tok)
        return toks

    def _mark(self, tok, reads, writes):
        for t in writes:
            t.w = tok
            t.r = {}
        for t in reads:
            t.r[id(tok[0])] = tok

    def op(self, E, fn, reads=(), writes=()):
        self._wait(E, self._deps(E, reads, writes))
        ins = fn()
        ins.then_inc(E.sem.h, 1)
        E.sem.cnt += 1
        self._mark((E.sem, E.sem.cnt), reads, writes)
        self.n_inst += 1
        return ins

    def group(self, E, fns, reads=(), writes=()):
        self._wait(E, self._deps(E, reads, writes))
        ins = None
        for fn in fns:
            ins = fn()
            self.n_inst += 1
        ins.then_inc(E.sem.h, 1)
        E.sem.cnt += 1
        self._mark((E.sem, E.sem.cnt), reads, writes)

    def dma(self, E, out, in_, reads, writes, sem, **kw):
        self._wait(E, self._deps(E, reads, writes))
        E.h.dma_start(out=out, in_=in_, **kw).then_inc(sem.h, 16)
        sem.cnt += 16
        self._mark((sem, sem.cnt), reads, writes)
        self.n_inst += 1

    def barrier(self):
        toks = [(s, s.cnt) for s in self.sems if s.cnt > 0]
        for E in self.engs:
            self._wait(E, [t for t in toks if t[0] is not E.sem])


class Psum:
    def __init__(self, K):
        nc = K.nc
        self.t = K.es.enter_context(nc.psum_tensor("psum_all", [128, 8 * 512], F32))
        self.tb = self.t.bitcast(BF16)
        self.tk = [Tk("bank%d" % i, excl=True) for i in range(8)]
        self.p = 0
        self.held = [False] * 8

    def try_alloc(self, n=1):
        for i in range(8):
            b = (self.p + i) % 8
            if b % n or b + n > 8:
                continue
            if any(self.held[b:b + n]):
                continue
            for j in range(b, b + n):
                self.held[j] = True
            self.p = (b + n) % 8
            return b, self.tk[b:b + n]
        return None

    def alloc(self, n=1):
        r = self.try_alloc(n)
        assert r is not None, "PSUM exhausted (straight-line code must free before allocating)"
        return r

    def free(self, b, n=1):
        for j in range(b, b + n):
            assert self.held[j]
            self.held[j] = False

    def f32(self, b, n=1):
        return self.t[:, b * 512:(b + n) * 512]

    def bf(self, b, n=1):
        return self.tb[:, b * 1024:(b + n) * 1024]


class Rot:
    def __init__(self, K, es, name, shape, dt, n, dma=False):
        self.items = []
        for i in range(n):
            t = K.sb("%s%d" % (name, i), shape, dt, es)
            self.items.append((t, Tk("%s%d" % (name, i)), K.newsem("%s%d" % (name, i)) if dma else None))
        self.i = 0

    def next(self):
        it = self.items[self.i % len(self.items)]
        self.i += 1
        return it


def build(stop_after=3):
    nc = bass.Bass("TRN2", target_bir_lowering=False)

    def din(name, shape, dt=F32):
        return nc.dram_tensor(name, shape, dt, kind="ExternalInput").ap()

    x_d = din("x", [S, D])
    ccol_d = din("ccol", [128, 8])
    pos_d = din("pos", [128, NT], I32)
    wcond_d = din("w_cond", [D, 9 * D])
    bcond_d = din("b_cond", [128, 9 * D])
    npre_d = din("npre", [128, 3, 8])
    npost_d = din("npost", [128, 3, D])
    win_d = din("w_in", [2, D, 2 * DFF])
    wout_d = din("w_out", [2, DFF, D])
    wmi_d = din("w_mi", [D, 2816])
    wmo_d = din("w_mo", [D, D])
    sinks_d = din("sinks", [128, 8])
    lbl_d = din("lbl", [128, 2, 4])
    gn_d = din("gn", [128, 1])
    ident_d = din("ident", [128, 128], BF16)
    onesb_d = din("onesb", [128, 128], BF16)
    amask_d = din("amask", [128, 2, 2, 512], BF16)
    hmask_d = din("hmask", [128, 4, 128], BF16)
    rmask_d = din("rmask", [128, 512])
    invf_d = din("invf", [128, 32])
    rowm_d = din("rowm", [128, 2])
    out_d = nc.dram_tensor("out", [S, D], F32, kind="ExternalOutput").ap()
    x1_d = nc.dram_tensor("x1s", [S, D], F32).ap()
    x2_d = nc.dram_tensor("x2s", [S, D], F32).ap()
    rope_d = nc.dram_tensor("rope_s", [128, 2, NT * 32], F32).ap()
    w2in_s = nc.dram_tensor("w2in_s", [D, 2 * DFF], BF16).ap()
    w2out_s = nc.dram_tensor("w2out_s", [DFF, D], BF16).ap()
    wmi_s = nc.dram_tensor("wmi_s", [D, 2816], BF16).ap()
    wmo_s = nc.dram_tensor("wmo_s", [D, D], BF16).ap()

    with ExitStack() as es:
        K = Kern(nc, es)
        PE, ACT, DVE, POOL, SP = K.PE, K.ACT, K.DVE, K.POOL, K.SP
        ps = Psum(K)
        dram_tk = {}

        def dtk(name, tile):
            key = (name, tile)
            if key not in dram_tk:
                dram_tk[key] = Tk("%s_%d" % key)
            return dram_tk[key]

        ident = K.sb("ident", [128, 128], BF16)
        onesb = K.sb("onesb", [128, 128], BF16)
        rowm = K.sb("rowm", [128, 2], F32)
        sinks = K.sb("sinks", [128, 8], F32)
        gn = K.sb("gn", [128, 1], F32)
        lb = K.sb("lb", [128, 4], F32)
        oml = K.sb("oml", [128, 4], F32)
        a_col = K.sb("a_col", [128, 3, 8], F32)
        sh_col = K.sb("sh_col", [128, 3, 8], F32)
        G = K.sb("G", [128, 3, D], BF16)
        c01 = K.sb("c01", [128, 2, 4], F32)
        gnh = K.sb("gnh", [128, 1], F32)
        const_tk = Tk("consts")
        par_tk = Tk("params")
        rope_tk = Tk("rope")

        with ExitStack() as ss:
            ld = K.newsem("setup_ld")
            ccol = K.sb("ccol", [128, 8], F32, ss)
            ca = K.sb("ca", [128, 8], F32, ss)
            posi = K.sb("posi", [128, NT], I32, ss)
            npre = K.sb("npre", [128, 3, 8], F32, ss)
            npost = K.sb("npost", [128, 3, D], F32, ss)
            lbl = K.sb("lbl", [128, 2, 4], F32, ss)
            invf = K.sb("invf", [128, 32], F32, ss)
            setup_tk = Tk("setup_in")
            cosT = K.sb("cosT", [128, NT, 32], F32, ss)
            sinT = K.sb("sinT", [128, NT, 32], F32, ss)
            loads = [(ident, ident_d), (onesb, onesb_d),
                     (rowm, rowm_d), (sinks, sinks_d), (gn, gn_d), (ccol, ccol_d), (posi, pos_d), (npre, npre_d),
                     (npost, npost_d), (lbl, lbl_d), (invf, invf_d)]
            for (t, d_) in loads:
                nc.sync.dma_start(out=t[:], in_=d_).then_inc(ld.h, 16)
                ld.cnt += 16
            setup_tk.w = (ld, ld.cnt)
            const_tk.w = (ld, ld.cnt)

            K.op(ACT, lambda: nc.scalar.activation(out=ca[:], in_=ccol[:], func=AF.Silu), reads=[setup_tk], writes=[par_tk])
            K.op(DVE, lambda: nc.vector.tensor_tensor(out=lb[:], in0=lbl[:, 0, :], in1=lbl[:, 1, :], op=ALU.subtract),
                 reads=[setup_tk], writes=[par_tk])
            K.op(ACT, lambda: nc.scalar.activation(out=lb[:], in_=lb[:], func=AF.Sigmoid), reads=[par_tk], writes=[par_tk])
            K.op(DVE, lambda: nc.vector.tensor_scalar(out=oml[:], in0=lb[:], scalar1=-1.0, scalar2=1.0, op0=ALU.mult, op1=ALU.add),
                 reads=[par_tk], writes=[par_tk])
            K.op(DVE, lambda: nc.vector.tensor_scalar(out=c01[:, 1, :], in0=oml[:], scalar1=0.5, scalar2=None, op0=ALU.mult),
                 reads=[par_tk], writes=[par_tk])
            K.op(DVE, lambda: nc.vector.tensor_tensor(out=c01[:, 0, :], in0=lb[:], in1=c01[:, 1, :], op=ALU.add),
                 reads=[par_tk], writes=[par_tk])
            K.op(DVE, lambda: nc.vector.tensor_scalar(out=gnh[:], in0=gn[:], scalar1=0.5, scalar2=None, op0=ALU.mult),
                 reads=[setup_tk, par_tk], writes=[par_tk])

            cab = K.sb("cab", [128, 8, 128], F32, ss)
            identf = K.sb("identf", [128, 128], F32, ss)
            modbc = K.sb("modbc", [128, 9 * D], F32, ss)
            dg = K.sb("dg", [128, 8, 128], F32, ss)
            K.op(DVE, lambda: nc.vector.tensor_copy(out=cab[:], in_=ca[:].unsqueeze(2).broadcast_to([128, 8, 128])),
                 reads=[par_tk], writes=[par_tk])
            K.op(DVE, lambda: nc.vector.tensor_copy(out=identf[:], in_=ident[:]), reads=[setup_tk], writes=[setup_tk])
            wc_rot = Rot(K, ss, "wc", [128, 8, 512], F32, 2, dma=True)
            bc_rot = Rot(K, ss, "bc", [128, 512], F32, 2, dma=True)
            wc_view = wcond_d.rearrange("(k p) n -> p k n", p=128)
            mod_tk = Tk("modbc")
            for cb in range(18):
                wc, wc_tk, wc_sem = wc_rot.next()
                bc, bc_tk, bc_sem = bc_rot.next()
                K.dma(SP, out=wc[:], in_=wc_view[:, :, cb * 512:(cb + 1) * 512], reads=[], writes=[wc_tk], sem=wc_sem)
                K.dma(SP, out=bc[:], in_=bcond_d[:, cb * 512:(cb + 1) * 512], reads=[], writes=[bc_tk], sem=bc_sem)
                b, btk = ps.alloc(1)
                pv = ps.f32(b)
                K.group(PE, [(lambda k=k: nc.tensor.matmul(pv, lhsT=cab[:, k, :], rhs=wc[:, k, :],
                                                         start=(k == 0), stop=(k == 7))) for k in range(8)],
                        reads=[par_tk, wc_tk], writes=btk)
                K.op(DVE, lambda: nc.vector.tensor_tensor(out=modbc[:, cb * 512:(cb + 1) * 512], in0=pv, in1=bc[:], op=ALU.add),
                     reads=btk + [bc_tk, mod_tk], writes=[mod_tk])
                ps.free(b)
            for k in range(3):
                for which, v in (("sh", 3 * k), ("sc", 3 * k + 1)):
                    K.op(DVE, lambda: nc.vector.tensor_tensor(out=dg[:], in0=modbc[:, v * D:(v + 1) * D].rearrange("p (j q) -> p j q", q=128),
                                                              in1=identf[:].unsqueeze(1).broadcast_to([128, 8, 128]), op=ALU.mult),
                         reads=[mod_tk, setup_tk, par_tk], writes=[par_tk])
                    dstc = sh_col if which == "sh" else a_col
                    K.op(DVE, lambda: nc.vector.tensor_reduce(out=dstc[:, k, :], in_=dg[:], axis=AX.X, op=ALU.add),
                         reads=[par_tk], writes=[par_tk])
                K.op(DVE, lambda: nc.vector.scalar_tensor_tensor(out=a_col[:, k, :], in0=a_col[:, k, :], scalar=1.0, in1=npre[:, k, :],
                                                                 op0=ALU.add, op1=ALU.mult),
                     reads=[par_tk, setup_tk], writes=[par_tk])
                resw = 1.0 if k == 1 else 0.5
                K.op(DVE, lambda: nc.vector.scalar_tensor_tensor(out=G[:, k, :], in0=modbc[:, (3 * k + 2) * D:(3 * k + 3) * D], scalar=resw,
                                                                 in1=npost[:, k, :], op0=ALU.mult, op1=ALU.mult),
                     reads=[mod_tk, setup_tk, par_tk], writes=[par_tk])

            posf = K.sb("posf", [128, NT], F32, ss)
            ang = K.sb("ang", [128, NT, 32], F32, ss)
            rr = K.sb("rr", [128, NT, 32], F32, ss)
            rc = K.sb("rc", [128, NT, 32], F32, ss)
            kf = K.sb("kf", [128, NT, 32], F32, ss)
            ki = K.sb("ki", [128, NT, 32], I32, ss)
            mk = K.sb("mk", [128, NT, 32], F32, ss)
            rt = Tk("ropetmp")

            def V(fn, reads=(), writes=()):
                K.op(DVE, fn, reads=list(reads) + [rt, setup_tk], writes=list(writes) + [rt])

            V(lambda: nc.vector.tensor_copy(out=posf[:], in_=posi[:]))
            V(lambda: nc.vector.tensor_tensor(out=ang[:], in0=posf[:].unsqueeze(2).broadcast_to([128, NT, 32]),
                                              in1=invf[:].unsqueeze(1).broadcast_to([128, NT, 32]), op=ALU.mult))
            V(lambda: nc.vector.tensor_scalar(out=kf[:], in0=ang[:], scalar1=1.0 / TWO_PI, scalar2=None, op0=ALU.mult))
            V(lambda: nc.vector.tensor_copy(out=ki[:], in_=kf[:]))
            V(lambda: nc.vector.tensor_copy(out=kf[:], in_=ki[:]))
            C1 = 6.28125
            C2 = TWO_PI - 6.28125
            V(lambda: nc.vector.scalar_tensor_tensor(out=rr[:], in0=kf[:], scalar=-C1, in1=ang[:], op0=ALU.mult, op1=ALU.add))
            V(lambda: nc.vector.scalar_tensor_tensor(out=rr[:], in0=kf[:], scalar=-C2, in1=rr[:], op0=ALU.mult, op1=ALU.add))

            def fold(t):
                V(lambda: nc.vector.tensor_scalar(out=mk[:], in0=t[:], scalar1=PI, scalar2=-TWO_PI, op0=ALU.is_gt, op1=ALU.mult))
                V(lambda: nc.vector.tensor_tensor(out=t[:], in0=t[:], in1=mk[:], op=ALU.add))
                V(lambda: nc.vector.tensor_scalar(out=mk[:], in0=t[:], scalar1=-PI, scalar2=TWO_PI, op0=ALU.is_lt, op1=ALU.mult))
                V(lambda: nc.vector.tensor_tensor(out=t[:], in0=t[:], in1=mk[:], op=ALU.add))
                V(lambda: nc.vector.tensor_scalar(out=t[:], in0=t[:], scalar1=PI, scalar2=-PI, op0=ALU.min, op1=ALU.max))

            fold(rr)
            V(lambda: nc.vector.tensor_scalar(out=rc[:], in0=rr[:], scalar1=PI / 2, scalar2=None, op0=ALU.add))
            fold(rc)
            K.op(ACT, lambda: nc.scalar.activation(out=sinT[:], in_=rr[:], func=AF.Sin), reads=[rt], writes=[rope_tk])
            K.op(ACT, lambda: nc.scalar.activation(out=cosT[:], in_=rc[:], func=AF.Sin), reads=[rt], writes=[rope_tk])
            rsem = K.newsem("ropest")
            K.dma(SP, out=rope_d[:, 0, :], in_=cosT[:].rearrange("p t f -> p (t f)"), reads=[rope_tk], writes=[dtk("rope", 0)], sem=rsem)
            K.dma(SP, out=rope_d[:, 1, :], in_=sinT[:].rearrange("p t f -> p (t f)"), reads=[rope_tk], writes=[dtk("rope", 0)], sem=rsem)
            K.barrier()

        def issue_cast_dma(dst, src, tk, sem):
            K.dma(POOL, out=dst, in_=src, reads=[], writes=[tk], sem=sem)

        eps_t = K.sb("eps_t", [128, 2], F32)
        const2_tk = Tk("const2")
        K.op(DVE, lambda: nc.vector.memset(eps_t[:, 0:1], EPS), writes=[const2_tk])
        K.op(DVE, lambda: nc.vector.memset(eps_t[:, 1:2], 0.0), writes=[const2_tk])

        def rstd_ops(st, st_tk, n, lnexp):
            if lnexp:
                K.op(ACT, lambda: nc.scalar.activation(out=st[:, 1:2], in_=st[:, 0:1], func=AF.Ln, scale=1.0 / n, bias=eps_t[:, 0:1]),
                     reads=[st_tk, const2_tk], writes=[st_tk])
                K.op(ACT, lambda: nc.scalar.activation(out=st[:, 2:3], in_=st[:, 1:2], func=AF.Exp, scale=-0.5),
                     reads=[st_tk], writes=[st_tk])
            else:
                K.op(ACT, lambda: nc.scalar.activation(out=st[:, 1:2], in_=st[:, 0:1], func=AF.Sqrt, scale=1.0 / n, bias=eps_t[:, 0:1]),
                     reads=[st_tk, const2_tk], writes=[st_tk])
                K.op(DVE, lambda: nc.vector.reciprocal(out=st[:, 2:3], in_=st[:, 1:2]), reads=[st_tk], writes=[st_tk])

        class Pro:
            def __init__(self, es_, src, srcname, kidx, lnexp, tag, nxin=2, evac_act=False):
                self.src, self.srcname, self.kidx, self.lnexp = src, srcname, kidx, lnexp
                self.evac_act = evac_act
                self.xin_rot = Rot(K, es_, tag + "xin", [128, D], F32, nxin, dma=True)
                self.xn_rot = Rot(K, es_, tag + "xn", [128, D], BF16, 2)
                self.st_rot = Rot(K, es_, tag + "st", [128, 4], F32, 4)
                self.tmpf = K.sb(tag + "tmpf", [128, 8, 128], F32, es_)
                self.tmpf_tk = Tk(tag + "tmpf")

            def elem(self, tile):
                xin, xin_tk, xin_sem = self.xin_rot.next()
                K.dma(SP, out=xin[:], in_=self.src[tile * 128:(tile + 1) * 128, :], reads=[dtk(self.srcname, tile)],
                      writes=[xin_tk], sem=xin_sem)
                st, st_tk, _ = self.st_rot.next()
                xn, xn_tk, _ = self.xn_rot.next()
                K.op(ACT, lambda: nc.scalar.activation(out=xn[:], in_=xin[:], func=AF.Square, accum_out=st[:, 0:1]),
                     reads=[xin_tk], writes=[xn_tk, st_tk])
                rstd_ops(st, st_tk, D, self.lnexp)
                K.op(DVE, lambda: nc.vector.tensor_scalar(out=xn[:], in0=xin[:], scalar1=st[:, 2:3], scalar2=None, op0=ALU.mult),
                     reads=[xin_tk, st_tk], writes=[xn_tk])
                return xn, xn_tk

            def pe(self, xn, xn_tk, h, h_tks, tt):
                b, tks = ps.alloc(1)
                tv = ps.bf(b)
                K.group(PE, [(lambda dc=dc: nc.tensor.transpose(tv[:, dc * 128:(dc + 1) * 128], xn[:, dc * 128:(dc + 1) * 128], ident[:]))
                             for dc in range(8)], reads=[xn_tk, const_tk], writes=tks)
                kidx = self.kidx
                if self.evac_act:
                    for dc in range(8):
                        K.op(ACT, lambda: nc.scalar.activation(out=h[:, dc, tt * 128:(tt + 1) * 128], in_=tv[:, dc * 128:(dc + 1) * 128],
                                                               func=AF.Identity, scale=a_col[:, kidx, dc:dc + 1], bias=sh_col[:, kidx, dc:dc + 1]),
                             reads=tks + [par_tk], writes=h_tks)
                    ps.free(b)
                    return
                K.op(DVE, lambda: nc.vector.tensor_tensor(out=self.tmpf[:], in0=tv.rearrange("p (c t) -> p c t", t=128),
                                                          in1=a_col[:, kidx, :].unsqueeze(2).broadcast_to([128, 8, 128]), op=ALU.mult),
                     reads=tks + [par_tk], writes=[self.tmpf_tk])
                ps.free(b)
                K.op(DVE, lambda: nc.vector.tensor_tensor(out=h[:, :, tt * 128:(tt + 1) * 128], in0=self.tmpf[:],
                                                          in1=sh_col[:, kidx, :].unsqueeze(2).broadcast_to([128, 8, 128]), op=ALU.add),
                     reads=[self.tmpf_tk, par_tk], writes=h_tks)

        class Epi:
            def __init__(self, es_, src, srcname, dst, dstname, kidx, lnexp, tag, junk):
                self.src, self.srcname, self.dst, self.dstname, self.kidx, self.lnexp = src, srcname, dst, dstname, kidx, lnexp
                self.junk = junk
                self.xre_rot = Rot(K, es_, tag + "xre", [128, D], F32, 1, dma=True)
                self.ob_rot = Rot(K, es_, tag + "ob", [128, D], F32, 1, dma=True)
                self.st_rot = Rot(K, es_, tag + "est", [128, 4], F32, 4)

            def run(self, tile, yb, ytk):
                yv = ps.f32(yb, 2)
                kidx = self.kidx
                xre, xre_tk, xre_sem = self.xre_rot.next()
                K.dma(SP, out=xre[:], in_=self.src[tile * 128:(tile + 1) * 128, :], reads=[dtk(self.srcname, tile)], writes=[xre_tk],
                      sem=xre_sem)
                st, st_tk, _ = self.st_rot.next()
                ob, ob_tk, ob_sem = self.ob_rot.next()
                jk, jk_tk, _ = self.junk.next()
                K.op(ACT, lambda: nc.scalar.activation(out=jk[:], in_=yv, func=AF.Square, accum_out=st[:, 0:1]),
                     reads=ytk, writes=[jk_tk, st_tk])
                rstd_ops(st, st_tk, D, self.lnexp)
                K.op(DVE, lambda: nc.vector.scalar_tensor_tensor(out=ob[:], in0=yv, scalar=st[:, 2:3], in1=G[:, kidx, :],
                                                                 op0=ALU.mult, op1=ALU.mult),
                     reads=ytk + [st_tk, par_tk], writes=[ob_tk])
                ps.free(yb, 2)
                K.op(POOL, lambda: nc.gpsimd.tensor_tensor(out=ob[:], in0=ob[:], in1=xre[:], op=ALU.add),
                     reads=[ob_tk, xre_tk], writes=[ob_tk])
                K.dma(POOL, out=self.dst[tile * 128:(tile + 1) * 128, :], in_=ob[:], reads=[ob_tk], writes=[dtk(self.dstname, tile)],
                      sem=ob_sem)

        def interleave(*gens):
            gens = list(gens)
            while gens:
                for g in list(gens):
                    try:
                        next(g)
                    except StopIteration:
                        gens.remove(g)

        def ffn_phase(fi, kidx, src, srcname, dst, dstname):
            with ExitStack() as pe_:
                Win = K.sb("Win", [128, 8, 2 * DFF], BF16, pe_)
                Wout = K.sb("Wout", [128, NF, D], BF16, pe_)
                groups = [(0, 6), (6, 12), (12, 17), (17, 22)]
                win_tk = [Tk("win%d" % g) for g in range(4)]
                wout_tk = [Tk("wout%d" % g) for g in range(2)]
                if fi == 0:
                    wv = win_d[fi].rearrange("(k p) n -> p k n", p=128)
                    wov = wout_d[fi].rearrange("(f p) n -> p f n", p=128)
                    for g, (f0, f1) in enumerate(groups):
                        sem = K.newsem("win%d_%d" % (fi, g))
                        for off in (0, DFF):
                            issue_cast_dma(Win[:, :, off + f0 * 128:off + f1 * 128], wv[:, :, off + f0 * 128:off + f1 * 128],
                                           win_tk[g], sem)
                    for g, (f0, f1) in enumerate([(0, 11), (11, 22)]):
                        sem = K.newsem("wout%d_%d" % (fi, g))
                        issue_cast_dma(Wout[:, f0:f1, :], wov[:, f0:f1, :], wout_tk[g], sem)

                    pcs = K.newsem("precast")

                    def precast(blk):
                        def cp(dst, src_, name):
                            K.dma(POOL, out=dst, in_=src_, reads=[], writes=[dtk(name, 0)], sem=pcs)
                        if blk == 2:
                            cp(wmi_s, wmi_d, "wmi_s")
                        elif blk == 3:
                            cp(wmo_s, wmo_d, "wmo_s")
                            cp(w2out_s[0:1408, :], wout_d[1][0:1408, :], "w2out_s")
                        elif blk == 4:
                            cp(w2out_s[1408:2816, :], wout_d[1][1408:2816, :], "w2out_s")
                            cp(w2in_s[0:256, :], win_d[1][0:256, :], "w2in_s")
                        elif blk == 5:
                            cp(w2in_s[256:512, :], win_d[1][256:512, :], "w2in_s")
                            cp(w2in_s[512:768, :], win_d[1][512:768, :], "w2in_s")
                        elif blk == 6:
                            cp(w2in_s[768:1024, :], win_d[1][768:1024, :], "w2in_s")
                else:
                    wv = w2in_s.rearrange("(k p) n -> p k n", p=128)
                    wov = w2out_s.rearrange("(f p) n -> p f n", p=128)
                    qi = 0
                    for g, (f0, f1) in enumerate(groups):
                        sem = K.newsem("win%d_%d" % (fi, g))
                        for off in (0, DFF):
                            K.dma(POOL, out=Win[:, :, off + f0 * 128:off + f1 * 128],
                                  in_=wv[:, :, off + f0 * 128:off + f1 * 128], reads=[dtk("w2in_s", 0)], writes=[win_tk[g]], sem=sem)
                            qi += 1
                    for g, (f0, f1) in enumerate([(0, 11), (11, 22)]):
                        sem = K.newsem("wout%d_%d" % (fi, g))
                        K.dma(POOL, out=Wout[:, f0:f1, :], in_=wov[:, f0:f1, :], reads=[dtk("w2out_s", 0)],
                              writes=[wout_tk[g]], sem=sem)
                        qi += 1

                def gof(f):
                    for g, (f0, f1) in enumerate(groups):
                        if f0 <= f < f1:
                            return g

                pro = Pro(pe_, src, srcname, kidx, False, "f")
                epi = Epi(pe_, src, srcname, dst, dstname, kidx, False, "f", pro.xn_rot)
                hs = [K.sb("h%d" % i, [128, 8, TB], BF16, pe_) for i in range(2)]
                h_tkss = [[Tk("h%d" % i)] for i in range(2)]
                act = K.sb("act", [128, NF, TB], BF16, pe_)
                act_tk = [Tk("act%d" % f) for f in range(NF)]
                sg_rot = Rot(K, pe_, "sg", [128, TB], F32, 1)

                for tt in range(4):
                    xn, xn_tk = pro.elem(tt)
                    pro.pe(xn, xn_tk, hs[0], h_tkss[0], tt)
                for blk in range(NB):
                    if fi == 0:
                        precast(blk)
                    h = hs[blk % 2]
                    h_tks = h_tkss[blk % 2]
                    hn = hs[(blk + 1) % 2]
                    hn_tks = h_tkss[(blk + 1) % 2]
                    pend = {}
                    for f in range(NF):
                        bg, tg = ps.alloc(1)
                        bu, tu = ps.alloc(1)
                        pg = ps.f32(bg)
                        pu = ps.f32(bu)
                        fns = []
                        for k in range(8):
                            fns.append(lambda k=k: nc.tensor.matmul(pg, lhsT=Win[:, k, f * 128:(f + 1) * 128], rhs=h[:, k, :],
                                                                    start=(k == 0), stop=(k == 7)))
                        for k in range(8):
                            fns.append(lambda k=k: nc.tensor.matmul(pu, lhsT=Win[:, k, DFF + f * 128:DFF + (f + 1) * 128],
                                                                    rhs=h[:, k, :], start=(k == 0), stop=(k == 7)))
                        K.group(PE, fns, reads=h_tks + [win_tk[gof(f)]], writes=tg + tu)
                        sg, sg_tk, _ = sg_rot.next()
                        K.op(ACT, lambda: nc.scalar.activation(out=sg[:], in_=pg, func=AF.Silu), reads=tg, writes=[sg_tk])
                        K.op(DVE, lambda: nc.vector.tensor_tensor(out=act[:, f, :], in0=sg[:], in1=pu, op=ALU.mult),
                             reads=[sg_tk] + tu, writes=[act_tk[f]])
                        ps.free(bg)
                        ps.free(bu)
                        if blk + 1 < NB:
                            if f % 4 == 0 and f // 4 < 4:
                                tt = f // 4
                                pend[tt] = pro.elem((blk + 1) * 4 + tt)
                            if f >= 7 and (f - 7) % 4 == 0 and (f - 7) // 4 < 4:
                                tt = (f - 7) // 4
                                pro.pe(pend[tt][0], pend[tt][1], hn, hn_tks, tt)
                    for tt in range(4):
                        yb, ytk = ps.alloc(2)
                        fns = []
                        for n in range(2):
                            o = ps.f32(yb + n)
                            for f in range(NF):
                                fns.append(lambda o=o, n=n, f=f: nc.tensor.matmul(
                                    o, lhsT=act[:, f, tt * 128:(tt + 1) * 128], rhs=Wout[:, f, n * 512:(n + 1) * 512],
                                    start=(f == 0), stop=(f == NF - 1)))
                        K.group(PE, fns, reads=act_tk + wout_tk, writes=ytk)
                        epi.run(blk * 4 + tt, yb, ytk)
                K.barrier()

        def mixer_phase(src, srcname, dst, dstname):
            kidx = 1
            with ExitStack() as pe_:
                Wmi = K.sb("Wmi", [128, 8, 2816], BF16, pe_)
                Wmo = K.sb("Wmo", [128, 8, D], BF16, pe_)
                wmi_tk = Tk("wmi")
                wmo_tk = Tk("wmo")
                wmiv = wmi_s.rearrange("(k p) n -> p k n", p=128)
                wmov = wmo_s.rearrange("(k p) n -> p k n", p=128)
                s1 = K.newsem("wmi")
                K.dma(POOL, out=Wmi[:, :, 0:1408], in_=wmiv[:, :, 0:1408], reads=[dtk("wmi_s", 0)], writes=[wmi_tk], sem=s1)
                K.dma(POOL, out=Wmi[:, :, 1408:2816], in_=wmiv[:, :, 1408:2816], reads=[dtk("wmi_s", 0)], writes=[wmi_tk], sem=s1)
                s2 = K.newsem("wmo")
                K.dma(POOL, out=Wmo[:], in_=wmov, reads=[dtk("wmo_s", 0)], writes=[wmo_tk], sem=s2)
                amask = K.sb("amask", [128, 2, 2, 512], BF16, pe_)
                hmask = K.sb("hmask", [128, 4, 128], BF16, pe_)
                rmask = K.sb("rmask", [128, 512], F32, pe_)
                cs_rot = Rot(K, pe_, "mcs", [128, 2, 128], F32, 2, dma=True)
                mc_tk = Tk("mconst")
                mcs = K.newsem("mconst")
                for (t_, d_) in ((amask[:], amask_d), (hmask[:], hmask_d), (rmask[:], rmask_d)):
                    K.dma(SP, out=t_, in_=d_, reads=[], writes=[mc_tk], sem=mcs)

                pro = Pro(pe_, src, srcname, kidx, True, "m", nxin=1, evac_act=True)
                epi = Epi(pe_, src, srcname, dst, dstname, kidx, True, "m", pro.xn_rot)
                h = K.sb("mh", [128, 8, TB], BF16, pe_)
                h_tks = [Tk("mh")]
                q2 = K.sb("q2", [128, 4, TB], F32, pe_)
                ff = K.sb("ff", [128, 4, TB], F32, pe_)
                bb = K.sb("bb", [128, 4, TB], F32, pe_)
                q2_tk = [Tk("q2_%d" % c) for c in range(4)]
                ff_tk = [Tk("ff_%d" % c) for c in range(4)]
                bb_tk = [Tk("bb_%d" % c) for c in range(4)]
                tA = Rot(K, pe_, "tA", [128, TB], F32, 2)
                tB = Rot(K, pe_, "tB", [128, TB], F32, 2)
                qt = K.sb("qt", [128, 4, TB], BF16, pe_)
                kt = K.sb("kt", [128, 4, TB], BF16, pe_)
                k2T = K.sb("k2T", [128, 4, TB], BF16, pe_)
                gs = K.sb("gs", [128, 4, TB], BF16, pe_)
                dec = K.sb("dec", [128, 4, 8], F32, pe_)
                hb_tk = [Tk("hgblk%d" % c) for c in range(4)]
                ktw_tk = [Tk("ktw%d" % c) for c in range(4)]
                k2w_tk = [Tk("k2w%d" % c) for c in range(4)]
                catT = K.sb("catT", [128, 8, TB], BF16, pe_)
                cat_tk = [Tk("cat%d" % c) for c in range(8)]
                NCH = 2
                RING = 5
                qkr_r = Rot(K, pe_, "qkr", [128, 640], BF16, 1)
                vtm = K.sb("vtm", [128, 4, 512], BF16, pe_)
                vtm_tk = [Tk("vtm%d" % i) for i in range(4)]
                vbe = K.sb("vbe", [128, RING, 2, 65], BF16, pe_)
                vb_tk = [Tk("vb%d" % i) for i in range(RING)]
                qT_r = Rot(K, pe_, "qT", [128, 8, 128], BF16, 4)
                kTa = K.sb("kTa", [128, 2, RING * 128], BF16, pe_)
                kT_tk = [Tk("kT%d" % i) for i in range(RING)]
                PT_r = Rot(K, pe_, "PTs", [128, 2, 2, 512], BF16, NCH)
                atm_r = Rot(K, pe_, "atm", [128, 512], BF16, NCH)
                ra_r = Rot(K, pe_, "ra", [128, 10, 32], F32, 1)
                rb_r = Rot(K, pe_, "rb", [128, 10, 32], F32, 1)
                dn_r = Rot(K, pe_, "dn", [128, 2, 8], F32, NCH)
                esink = K.sb("esink", [128, 8], F32, pe_)
                negc = K.sb("negc", [128, 1], F32, pe_)
                SHIFT = 16.0
                st32 = K.sb("st32", [128, 4, 128], F32, pe_)
                st32_tk = [Tk("st32")]
                stb_rot = Rot(K, pe_, "stb", [128, 4, 128], BF16, 2)
                sqr = K.sb("sqr", [128, 512], BF16, pe_)
                sqr_tk = Tk("sqr")
                rsb = K.sb("rsb", [128, 512], F32, pe_)
                rsb_tk = Tk("rsb")
                t1 = K.sb("t1", [128, 512], F32, pe_)
                t1_tk = Tk("t1")

                for (qT_, qT_tk_, _) in qT_r.items:
                    K.op(POOL, lambda: nc.gpsimd.memset(qT_[:], 0.0), writes=[qT_tk_])
                K.op(POOL, lambda: nc.gpsimd.memset(kTa[:], 0.0), writes=kT_tk)
                K.op(POOL, lambda: nc.gpsimd.memset(vbe[:], 1.0), writes=vb_tk)
                K.op(DVE, lambda: nc.vector.memset(negc[:], -SHIFT), writes=[mc_tk])
                K.op(DVE, lambda: nc.vector.tensor_scalar(out=esink[:], in0=sinks[:], scalar1=-SHIFT, scalar2=None, op0=ALU.add),
                     reads=[const_tk, mc_tk], writes=[mc_tk])
                K.op(ACT, lambda: nc.scalar.activation(out=esink[:], in_=esink[:], func=AF.Exp), reads=[mc_tk], writes=[mc_tk])
                K.op(POOL, lambda: nc.gpsimd.memset(st32[:], 0.0), writes=st32_tk)
                stb0, stb0_tk, _ = stb_rot.next()
                K.op(POOL, lambda: nc.gpsimd.memset(stb0[:], 0.0), writes=[stb0_tk])
                state = {"stb": (stb0, stb0_tk)}
                QSC = float(0.5 * 128 ** -0.5)

                def galloc(n):
                    tries = 0
                    while True:
                        r = ps.try_alloc(n)
                        if r is not None:
                            return r
                        tries += 1
                        assert tries < 10000, "PSUM livelock"
                        yield

                def fm_proj(col0, b, tks):
                    pv = ps.f32(b)
                    K.group(PE, [(lambda k=k: nc.tensor.matmul(pv, lhsT=Wmi[:, k, col0:col0 + 128], rhs=h[:, k, :],
                                                             start=(k == 0), stop=(k == 7))) for k in range(8)],
                            reads=h_tks + [wmi_tk], writes=tks)
                    return pv

                def gen_prologue(blk):
                    for tt in range(4):
                        xn, xn_tk = pro.elem(blk * 4 + tt)
                        yield
                        pro.pe(xn, xn_tk, h, h_tks, tt)
                        yield

                def gen_P(blk):
                    for tt in range(4):
                        cs = slice(tt * 128, (tt + 1) * 128)
                        hb_, htk = yield from galloc(1)
                        K.group(PE, [(lambda k=k: nc.tensor.matmul(ps.f32(hb_), lhsT=h[:, k, cs], rhs=Wmi[:, k, 1792:2304],
                                                                 start=(k == 0), stop=(k == 7))) for k in range(8)],
                                reads=h_tks + [wmi_tk], writes=htk)
                        K.op(ACT, lambda: nc.scalar.activation(out=vtm[:, tt, :], in_=ps.f32(hb_), func=AF.Copy), reads=htk, writes=[vtm_tk[tt]])
                        ps.free(hb_)
                        yield
                    for c in range(4):
                        b, tks = yield from galloc(1)
                        pv = fm_proj(1280 + c * 128, b, tks)
                        a_, a_tk, _ = tA.next()
                        K.op(ACT, lambda: nc.scalar.activation(out=a_[:], in_=pv, func=AF.Tanh, scale=0.5), reads=tks, writes=[a_tk])
                        K.op(DVE, lambda: nc.vector.tensor_scalar(out=ff[:, c, :], in0=a_[:], scalar1=c01[:, 1, c:c + 1], scalar2=c01[:, 0, c:c + 1],
                                                                  op0=ALU.mult, op1=ALU.add), reads=[a_tk, par_tk], writes=[ff_tk[c]])
                        ps.free(b)
                        yield
                        b, tks = yield from galloc(1)
                        pv = fm_proj(768 + c * 128, b, tks)
                        a_, a_tk, _ = tA.next()
                        K.op(ACT, lambda: nc.scalar.activation(out=a_[:], in_=pv, func=AF.Tanh, scale=0.5), reads=tks, writes=[a_tk])
                        K.op(DVE, lambda: nc.vector.scalar_tensor_tensor(out=q2[:, c, :], in0=a_[:], scalar=1.0, in1=pv, op0=ALU.add, op1=ALU.mult),
                             reads=[a_tk] + tks, writes=[q2_tk[c]])
                        ps.free(b)
                        yield
                        b, tks = yield from galloc(1)
                        pv = fm_proj(2304 + c * 128, b, tks)
                        a_, a_tk, _ = tA.next()
                        K.op(ACT, lambda: nc.scalar.activation(out=a_[:], in_=pv, func=AF.Tanh, scale=0.5), reads=tks, writes=[a_tk])
                        K.op(DVE, lambda: nc.vector.scalar_tensor_tensor(out=gs[:, c, :], in0=a_[:], scalar=1.0, in1=pv, op0=ALU.add, op1=ALU.mult),
                             reads=[a_tk] + tks, writes=[hb_tk[c]])
                        ps.free(b)
                        yield

                def gen_Pln(cs_):
                    for c in cs_:
                        wtk = [hb_tk[c]]
                        a_, a_tk, _ = tA.next()
                        K.op(ACT, lambda: nc.scalar.activation(out=a_[:], in_=ff[:, c, :], func=AF.Ln), reads=[ff_tk[c]], writes=[a_tk])
                        K.op(DVE, lambda: nc.vector.tensor_tensor_scan(out=bb[:, c, :], data0=rmask[:], data1=a_[:], initial=0.0,
                                                                       op0=ALU.mult, op1=ALU.add),
                             reads=[a_tk, mc_tk], writes=[bb_tk[c]])
                        yield
                        e_, e_tk, _ = tB.next()
                        K.op(ACT, lambda: nc.scalar.activation(out=e_[:], in_=bb[:, c, :], func=AF.Exp), reads=[bb_tk[c]], writes=[e_tk])
                        K.op(DVE, lambda: nc.vector.scalar_tensor_tensor(out=qt[:, c, :], in0=q2[:, c, :], scalar=-QSC, in1=e_[:],
                                                                         op0=ALU.mult, op1=ALU.mult),
                             reads=[q2_tk[c], e_tk], writes=wtk)
                        K.op(DVE, lambda: nc.vector.tensor_copy(out=dec[:, c, :],
                                                                in_=e_[:].rearrange("p (n t) -> p n t", t=64)[:, :, 63]),
                             reads=[e_tk], writes=wtk)
                        yield
                        x_, x_tk, _ = tB.next()
                        K.op(ACT, lambda: nc.scalar.activation(out=x_[:], in_=bb[:, c, :], func=AF.Exp, scale=-1.0), reads=[bb_tk[c]], writes=[x_tk])
                        K.op(DVE, lambda: nc.vector.scalar_tensor_tensor(out=kt[:, c, :], in0=ff[:, c, :], scalar=1.0, in1=x_[:],
                                                                         op0=ALU.subtract, op1=ALU.mult),
                             reads=[ff_tk[c], x_tk], writes=[ktw_tk[c]])
                        yield
                        d_, d_tk, _ = tA.next()
                        b3 = bb[:, c, :].rearrange("p (n t) -> p n t", t=64)
                        K.op(DVE, lambda: nc.vector.tensor_tensor(out=d_[:].rearrange("p (n t) -> p n t", t=64),
                                                                  in0=b3[:, :, 63:64].broadcast_to([128, 8, 64]), in1=b3, op=ALU.subtract),
                             reads=[bb_tk[c]], writes=[d_tk])
                        K.op(ACT, lambda: nc.scalar.activation(out=d_[:], in_=d_[:], func=AF.Exp), reads=[d_tk], writes=[d_tk])
                        K.op(DVE, lambda: nc.vector.scalar_tensor_tensor(out=k2T[:, c, :], in0=ff[:, c, :], scalar=1.0, in1=d_[:],
                                                                         op0=ALU.subtract, op1=ALU.mult),
                             reads=[ff_tk[c], d_tk], writes=[k2w_tk[c]])
                        yield

                pre_done = {}
                blkst = {}

                def gen_Apre(blk, tts):
                    cs_t, cs_tk, cs_sem = cs_rot.next()
                    K.dma(SP, out=cs_t[:, 0, :], in_=rope_d[:, 0, blk * 128:(blk + 1) * 128], reads=[dtk("rope", 0)], writes=[cs_tk], sem=cs_sem)
                    K.dma(SP, out=cs_t[:, 1, :], in_=rope_d[:, 1, blk * 128:(blk + 1) * 128], reads=[dtk("rope", 0)], writes=[cs_tk], sem=cs_sem)
                    for tt in tts:
                        tile = blk * 4 + tt
                        cs = slice(tt * 128, (tt + 1) * 128)
                        slot = tile % RING
                        qkr, qkr_tk, _ = qkr_r.next()
                        qT, qT_tk, _ = qT_r.items[tt]
                        ra, ra_tk, _ = ra_r.next()
                        rb, rb_tk, _ = rb_r.next()
                        pb, ptk = yield from galloc(2)
                        fns = []
                        for k in range(8):
                            fns.append(lambda k=k: nc.tensor.matmul(ps.f32(pb), lhsT=h[:, k, cs], rhs=Wmi[:, k, 0:512],
                                                                    start=(k == 0), stop=(k == 7)))
                        for k in range(8):
                            fns.append(lambda k=k: nc.tensor.matmul(ps.f32(pb + 1)[:, 0:256], lhsT=h[:, k, cs], rhs=Wmi[:, k, 512:768],
                                                                    start=(k == 0), stop=(k == 7)))
                        K.group(PE, fns, reads=h_tks + [wmi_tk], writes=ptk)
                        yield
                        K.op(ACT, lambda: nc.scalar.activation(out=vbe[:, slot, :, 0:64],
                                                               in_=ps.f32(pb + 1)[:, 128:256].rearrange("p (g d) -> p g d", d=64), func=AF.Copy),
                             reads=ptk, writes=[vb_tk[slot]])
                        qk3 = ps.f32(pb, 2)[:, 0:640].rearrange("p (h d) -> p h d", d=64)
                        cb_ = cs_t[:, 0, tt * 32:(tt + 1) * 32].unsqueeze(1).broadcast_to([128, 10, 32])
                        sb_ = cs_t[:, 1, tt * 32:(tt + 1) * 32].unsqueeze(1).broadcast_to([128, 10, 32])
                        o3 = qkr[:].rearrange("p (h d) -> p h d", d=64)
                        K.op(DVE, lambda: nc.vector.tensor_tensor(out=ra[:], in0=qk3[:, :, 0:32], in1=cb_, op=ALU.mult), reads=ptk + [cs_tk], writes=[ra_tk])
                        K.op(DVE, lambda: nc.vector.tensor_tensor(out=rb[:], in0=qk3[:, :, 32:64], in1=sb_, op=ALU.mult), reads=ptk + [cs_tk], writes=[rb_tk])
                        K.op(DVE, lambda: nc.vector.tensor_tensor(out=o3[:, :, 0:32], in0=ra[:], in1=rb[:], op=ALU.subtract),
                             reads=[ra_tk, rb_tk], writes=[qkr_tk])
                        yield
                        K.op(DVE, lambda: nc.vector.tensor_tensor(out=ra[:], in0=qk3[:, :, 32:64], in1=cb_, op=ALU.mult), reads=ptk + [cs_tk], writes=[ra_tk])
                        K.op(DVE, lambda: nc.vector.tensor_tensor(out=rb[:], in0=qk3[:, :, 0:32], in1=sb_, op=ALU.mult), reads=ptk + [cs_tk], writes=[rb_tk])
                        K.op(DVE, lambda: nc.vector.tensor_tensor(out=o3[:, :, 32:64], in0=ra[:], in1=rb[:], op=ALU.add),
                             reads=[ra_tk, rb_tk], writes=[qkr_tk])
                        ps.free(pb, 2)
                        yield
                        tb2, ttk = yield from galloc(1)
                        tv = ps.bf(tb2)
                        fns = [(lambda j=j: nc.tensor.transpose(tv[:, j * 128:(j + 1) * 128], qkr[:, j * 128:(j + 1) * 128], ident[:]))
                               for j in range(5)]
                        K.group(PE, fns, reads=[qkr_tk, const_tk], writes=ttk)
                        qv = tv[:, 0:512].rearrange("p (j t) -> p j t", t=128)
                        qT4 = qT[:].rearrange("p (j two) t -> p j two t", two=2)
                        K.op(DVE, lambda: nc.vector.tensor_copy(out=qT4[0:64, :, 0, :], in_=qv[0:64]), reads=ttk, writes=[qT_tk])
                        K.op(DVE, lambda: nc.vector.tensor_copy(out=qT4[0:64, :, 1, :], in_=qv[64:128]), reads=ttk, writes=[qT_tk])
                        K.op(ACT, lambda: nc.scalar.activation(out=kTa[0:64, 0, slot * 128:(slot + 1) * 128], in_=tv[0:64, 512:640], func=AF.Copy),
                             reads=ttk, writes=[kT_tk[slot]])
                        K.op(ACT, lambda: nc.scalar.activation(out=kTa[0:64, 1, slot * 128:(slot + 1) * 128], in_=tv[64:128, 512:640],
                                                               func=AF.Copy), reads=ttk, writes=[kT_tk[slot]])
                        ps.free(tb2)
                        pre_done[tile] = True
                        yield

                def gen_Amain(blk, tt):
                    if True:
                        tile = blk * 4 + tt
                        cs = slice(tt * 128, (tt + 1) * 128)
                        slot = tile % RING
                        pslot = (tile - 1) % RING
                        tries = 0
                        while not pre_done.get(tile, False):
                            tries += 1
                            assert tries < 10000, "attention main chain never released"
                            yield
                        qT, qT_tk, _ = qT_r.items[tt]
                        PTs, PT_tk, _ = PT_r.next()
                        atm, atm_tk, _ = atm_r.next()
                        dn, dn_tk, _ = dn_r.next()
                        kbs = [(1, slot)] if tile == 0 else [(0, pslot), (1, slot)]
                        var = 0
                        for g in range(2):
                            sb2, stk = yield from galloc(2)
                            fns = []
                            for (kb, sl) in kbs:
                                o = ps.f32(sb2 + kb)
                                fns.append(lambda o=o, g=g, sl=sl: nc.tensor.matmul(
                                    o, lhsT=kTa[:, g, sl * 128:(sl + 1) * 128], rhs=qT[:, g * 4:(g + 1) * 4, :].rearrange("p h t -> p (h t)"),
                                    start=True, stop=False))
                                fns.append(lambda o=o, kb=kb: nc.tensor.matmul(o, lhsT=ident[:], rhs=amask[:, var, kb, :], start=False, stop=True))
                            K.group(PE, fns, reads=[qT_tk, const_tk, mc_tk] + kT_tk, writes=stk)
                            for (kb, sl) in kbs:
                                K.op(ACT, lambda: nc.scalar.activation(out=PTs[:, g, kb, :], in_=ps.f32(sb2 + kb), func=AF.Exp, scale=0.125,
                                                                       bias=negc[:, 0:1]),
                                     reads=[stk[kb], mc_tk], writes=[PT_tk])
                            ps.free(sb2, 2)
                            yield
                        ob2, otk = yield from galloc(2)
                        fns = []
                        for hh in range(8):
                            g, hq = hh // 4, hh % 4
                            for i, (kb, sl) in enumerate(kbs):
                                fns.append(lambda g=g, hq=hq, kb=kb, sl=sl, i=i: nc.tensor.matmul(
                                    ps.f32(ob2 + g)[:, hq * 65:(hq + 1) * 65], lhsT=PTs[:, g, kb, hq * 128:(hq + 1) * 128],
                                    rhs=vbe[:, sl, g, :], start=(i == 0), stop=(i == len(kbs) - 1)))
                        K.group(PE, fns, reads=[PT_tk] + vb_tk, writes=otk)
                        yield
                        O4 = ps.f32(ob2, 2).rearrange("p (b c) -> p b c", c=512)[:, :, 0:260].rearrange("p b (h e) -> p b h e", e=65)
                        K.op(DVE, lambda: nc.vector.tensor_tensor(out=dn[:, 0, :].rearrange("p (b h) -> p b h", h=4), in0=O4[:, :, :, 64],
                                                                  in1=esink[:].rearrange("p (b h) -> p b h", h=4), op=ALU.add),
                             reads=otk + [mc_tk], writes=[dn_tk])
                        K.op(DVE, lambda: nc.vector.reciprocal(out=dn[:, 1, :], in_=dn[:, 0, :]), reads=[dn_tk], writes=[dn_tk])
                        K.op(DVE, lambda: nc.vector.tensor_tensor(out=atm[:].rearrange("p (b h d) -> p b h d", h=4, d=64), in0=O4[:, :, :, 0:64],
                                                                  in1=dn[:, 1, :].rearrange("p (b h) -> p b h", h=4).unsqueeze(3).broadcast_to([128, 2, 4, 64]),
                                                                  op=ALU.mult),
                             reads=otk + [dn_tk], writes=[atm_tk])
                        ps.free(ob2, 2)
                        yield
                        ab_, atk = yield from galloc(1)
                        av = ps.bf(ab_)
                        K.group(PE, [(lambda j=j: nc.tensor.transpose(av[:, j * 128:(j + 1) * 128], atm[:, j * 128:(j + 1) * 128], ident[:]))
                                     for j in range(4)], reads=[atm_tk, const_tk], writes=atk)
                        K.op(ACT, lambda: nc.scalar.activation(out=catT[:, 0:4, cs], in_=av[:, 0:512].rearrange("p (j t) -> p j t", t=128),
                                                               func=AF.Copy), reads=atk, writes=cat_tk[0:4])
                        ps.free(ab_)
                        yield

                def state_step(n, ubank, utk, sdst, sdst_tk):
                    K.op(DVE, lambda: nc.vector.tensor_tensor(out=st32[:], in0=st32[:], in1=dec[:, :, n:n + 1].broadcast_to([128, 4, 128]),
                                                              op=ALU.mult), reads=st32_tk + hb_tk, writes=st32_tk)
                    K.op(DVE, lambda: nc.vector.tensor_tensor(out=st32[:], in0=st32[:], in1=ps.f32(ubank).rearrange("p (c e) -> p c e", e=128),
                                                              op=ALU.add), reads=st32_tk + utk, writes=st32_tk)
                    K.op(ACT, lambda: nc.scalar.activation(out=sdst[:], in_=st32[:], func=AF.Copy), reads=st32_tk, writes=[sdst_tk])

                k2_r = Rot(K, pe_, "k2AB", [128, 2, 4, 128], BF16, 2)
                scm_r = Rot(K, pe_, "scm2", [128, 4, 128], BF16, 2)

                def gen_H(blk):
                    pre = {}
                    post = {}

                    def h_pre(tt):
                        cs = slice(tt * 128, (tt + 1) * 128)
                        vt = vtm[:, tt, :]
                        vt_tk = vtm_tk[tt]
                        k2, k2_tk, _ = k2_r.next()
                        sc_, sc_tk, _ = scm_r.next()
                        kb_, ktk = yield from galloc(1)
                        kv = ps.bf(kb_)
                        K.group(PE, [(lambda c=c: nc.tensor.transpose(kv[:, c * 128:(c + 1) * 128], k2T[:, c, cs], ident[:])) for c in range(4)],
                                reads=hb_tk + k2w_tk + [const_tk], writes=ktk)
                        kv3 = kv[:, 0:512].rearrange("p (c d) -> p c d", d=128)
                        K.op(DVE, lambda: nc.vector.tensor_scalar(out=k2[:, 0], in0=kv3, scalar1=rowm[:, 0:1], scalar2=None, op0=ALU.mult),
                             reads=ktk + [const_tk], writes=[k2_tk])
                        K.op(DVE, lambda: nc.vector.tensor_scalar(out=k2[:, 1], in0=kv3, scalar1=rowm[:, 1:2], scalar2=None, op0=ALU.mult),
                             reads=ktk + [const_tk], writes=[k2_tk])
                        ps.free(kb_)
                        sb_2, sctk = yield from galloc(1)
                        scv = ps.f32(sb_2)
                        K.group(PE, [(lambda c=c: nc.tensor.matmul(scv[:, c * 128:(c + 1) * 128], lhsT=kt[:, c, cs], rhs=qt[:, c, cs],
                                                                 start=True, stop=True)) for c in range(4)],
                                reads=hb_tk + ktw_tk, writes=sctk)
                        yield
                        K.op(DVE, lambda: nc.vector.tensor_tensor(out=sc_[:], in0=scv.rearrange("p (c t) -> p c t", t=128), in1=hmask[:], op=ALU.mult),
                             reads=sctk + [mc_tk], writes=[sc_tk])
                        ps.free(sb_2)
                        ua, uatk = yield from galloc(1)
                        ub, ubtk = yield from galloc(1)
                        K.group(PE, [(lambda c=c: nc.tensor.matmul(ps.f32(ua)[:, c * 128:(c + 1) * 128], lhsT=k2[:, 0, c, :],
                                                                 rhs=vt[:, c * 128:(c + 1) * 128], start=True, stop=True)) for c in range(4)],
                                reads=[k2_tk, vt_tk], writes=uatk)
                        K.group(PE, [(lambda c=c: nc.tensor.matmul(ps.f32(ub)[:, c * 128:(c + 1) * 128], lhsT=k2[:, 1, c, :],
                                                                 rhs=vt[:, c * 128:(c + 1) * 128], start=True, stop=True)) for c in range(4)],
                                reads=[k2_tk, vt_tk], writes=ubtk)
                        pre[tt] = (sc_, sc_tk, ua, uatk, ub, ubtk)
                        yield

                    def h_chain(tt):
                        vt = vtm[:, tt, :]
                        vt_tk = vtm_tk[tt]
                        sc_, sc_tk, ua, uatk, ub, ubtk = pre[tt]
                        oo, ootk = yield from galloc(1)
                        oov = ps.f32(oo)
                        stA, stA_tk = state["stb"]
                        fns = []
                        for c in range(4):
                            fns.append(lambda c=c: nc.tensor.matmul(oov[:, c * 128:(c + 1) * 128], lhsT=vt[:, c * 128:(c + 1) * 128],
                                                                    rhs=sc_[:, c, :], start=(c == 0), stop=False, skip_group_check=True))
                        for c in range(4):
                            fns.append(lambda c=c: nc.tensor.matmul(oov[:, c * 128:c * 128 + 64], lhsT=stA[:, c, :],
                                                                    rhs=qt[:, c, tt * 128:tt * 128 + 64], start=False, stop=False,
                                                                    skip_group_check=True))
                        K.group(PE, fns, reads=[vt_tk, sc_tk, stA_tk] + hb_tk, writes=ootk)
                        stB, stB_tk, _ = stb_rot.next()
                        state_step(2 * tt, ua, uatk, stB, stB_tk)
                        ps.free(ua)
                        yield
                        K.group(PE, [(lambda c=c: nc.tensor.matmul(oov[:, c * 128 + 64:(c + 1) * 128], lhsT=stB[:, c, :],
                                                                 rhs=qt[:, c, tt * 128 + 64:(tt + 1) * 128], start=False, stop=True,
                                                                 skip_group_check=True)) for c in range(4)],
                                reads=[stB_tk] + hb_tk, writes=ootk)
                        stC, stC_tk, _ = stb_rot.next()
                        state_step(2 * tt + 1, ub, ubtk, stC, stC_tk)
                        ps.free(ub)
                        state["stb"] = (stC, stC_tk)
                        post[tt] = (oo, ootk)
                        yield

                    def h_post(tt):
                        cs = slice(tt * 128, (tt + 1) * 128)
                        oo, ootk = post[tt]
                        oov = ps.f32(oo)
                        K.op(ACT, lambda: nc.scalar.activation(out=sqr[:], in_=oov, func=AF.Square), reads=ootk, writes=[sqr_tk])
                        nb_, ntk = yield from galloc(1)
                        K.group(PE, [lambda: nc.tensor.matmul(ps.f32(nb_), lhsT=onesb[:], rhs=sqr[:], start=True, stop=True)],
                                reads=[sqr_tk, const_tk], writes=ntk)
                        yield
                        K.op(ACT, lambda: nc.scalar.activation(out=rsb[:], in_=ps.f32(nb_), func=AF.Ln, scale=1.0 / 128, bias=eps_t[:, 0:1]),
                             reads=ntk + [const2_tk], writes=[rsb_tk])
                        ps.free(nb_)
                        K.op(ACT, lambda: nc.scalar.activation(out=rsb[:], in_=rsb[:], func=AF.Exp, scale=-0.5), reads=[rsb_tk], writes=[rsb_tk])
                        K.op(DVE, lambda: nc.vector.scalar_tensor_tensor(out=t1[:], in0=oov, scalar=gnh[:, 0:1], in1=rsb[:], op0=ALU.mult, op1=ALU.mult),
                             reads=ootk + [rsb_tk, par_tk], writes=[t1_tk])
                        ps.free(oo)
                        K.op(DVE, lambda: nc.vector.tensor_tensor(out=catT[:, 4:8, cs], in0=t1[:].rearrange("p (c t) -> p c t", t=128),
                                                                  in1=gs[:, :, cs], op=ALU.mult),
                             reads=[t1_tk] + hb_tk, writes=cat_tk[4:8])
                        yield

                    yield from h_pre(0)
                    for tt in range(4):
                        yield from h_chain(tt)
                        if tt + 1 < 4:
                            yield from h_pre(tt + 1)
                        yield from h_post(tt)

                def gen_O(blk):
                    for tt in range(4):
                        yb, ytk = yield from galloc(2)
                        fns = []
                        for n in range(2):
                            o = ps.f32(yb + n)
                            for kc in range(8):
                                fns.append(lambda o=o, n=n, kc=kc: nc.tensor.matmul(
                                    o, lhsT=catT[:, kc, tt * 128:(tt + 1) * 128], rhs=Wmo[:, kc, n * 512:(n + 1) * 512],
                                    start=(kc == 0), stop=(kc == 7)))
                        K.group(PE, fns, reads=cat_tk + [wmo_tk], writes=ytk)
                        yield
                        epi.run(blk * 4 + tt, yb, ytk)
                        yield

                interleave(gen_prologue(0))
                for blk in range(NB):
                    def gen_Pall(blk=blk):
                        yield from gen_P(blk)
                        g1, g2 = gen_Pln([0, 2]), gen_Pln([1, 3])
                        live = [g1, g2]
                        while live:
                            for g in list(live):
                                try:
                                    next(g)
                                except StopIteration:
                                    live.remove(g)
                            yield

                    gens = [gen_Pall(), gen_Apre(blk, [0, 1, 2, 3]), gen_Amain(blk, 0), gen_Amain(blk, 1)]
                    if blk > 0:
                        gens.insert(0, gen_O(blk - 1))
                    interleave(*gens)
                    gens = [gen_H(blk), gen_Amain(blk, 2), gen_Amain(blk, 3)]
                    if blk + 1 < NB:
                        gens.append(gen_prologue(blk + 1))
                    interleave(*gens)
                interleave(gen_O(NB - 1))
                K.barrier()

        dsts = {1: out_d if stop_after == 1 else x1_d, 2: out_d if stop_after == 2 else x2_d, 3: out_d}
        names = {1: "out" if stop_after == 1 else "x1", 2: "out" if stop_after == 2 else "x2", 3: "out"}
        if stop_after == 0:
            dbg = K.sb("dbg", [128, D], F32)
            dbg_tk = Tk("dbg")
            dsem = K.newsem("dbg")
            K.op(DVE, lambda: nc.vector.tensor_copy(out=dbg[:], in_=G[:, 0, :]), reads=[par_tk], writes=[dbg_tk])
            K.dma(SP, out=out_d[0:128, :], in_=dbg[:], reads=[dbg_tk], writes=[], sem=dsem)
            K.op(DVE, lambda: nc.vector.tensor_copy(out=dbg[:, 0:24], in_=a_col[:].rearrange("p a b -> p (a b)")), reads=[par_tk, dbg_tk], writes=[dbg_tk])
            K.op(DVE, lambda: nc.vector.tensor_copy(out=dbg[:, 24:48], in_=sh_col[:].rearrange("p a b -> p (a b)")), reads=[par_tk, dbg_tk], writes=[dbg_tk])
            K.op(DVE, lambda: nc.vector.tensor_copy(out=dbg[:, 48:52], in_=lb[:]), reads=[par_tk, dbg_tk], writes=[dbg_tk])
            K.dma(SP, out=out_d[128:256, :], in_=dbg[:], reads=[dbg_tk], writes=[], sem=dsem)
            K.barrier()
            return nc
        ffn_phase(0, 0, x_d, "x", dsts[1], names[1])
        if stop_after >= 2:
            mixer_phase(x1_d, "x1", dsts[2], names[2])
        if stop_after >= 3:
            ffn_phase(1, 2, x2_d, "x2", out_d, "out")
        K.barrier()
        print("instructions:", K.n_inst, "sems:", len(K.sems))
    return nc


def _consts():
    bf = ml_dtypes.bfloat16
    ident = np.eye(128, dtype=np.float32).astype(bf)
    onesb = np.ones((128, 128), np.float32).astype(bf)
    NEG = -1e30
    am = np.zeros((128, 2, 256), np.float32)
    am[0:64, 0, 192:256] = NEG
    am[64:128, 0, 0:64] = NEG
    am[0:64, 1, 64:128] = NEG
    am[64:128, 1, 128:192] = NEG
    s = np.arange(128)[:, None]
    t = np.arange(128)[None, :]
    hm = ((s // 64 == t // 64) & (s <= t)).astype(np.float32)
    hmask = np.broadcast_to(hm[:, None, :], (128, 4, 128)).copy()
    rmask = np.ones((128, 512), np.float32)
    rmask[:, ::64] = 0.0
    inv_freq = (1.0 / (np.float32(10000.0) ** (np.arange(0, 64, 2, dtype=np.float32) / np.float32(64)))).astype(np.float32)
    invf = np.broadcast_to(inv_freq[None, :], (128, 32)).copy()
    rowm = np.zeros((128, 2), np.float32)
    rowm[0:64, 0] = 1.0
    rowm[64:128, 1] = 1.0
    amT = np.zeros((128, 2, 2, 4, 128), np.float32)
    for v in range(2):
        for kb in range(2):
            amT[:, v, kb, :, :] = am[:, v, kb * 128:(kb + 1) * 128].T[:, None, :]
    amT = amT.reshape(128, 2, 2, 512)
    return dict(ident=ident, onesb=onesb, amask=amT.astype(bf), hmask=hmask.astype(bf), rmask=rmask, invf=invf, rowm=rowm)


def make_in_maps(x, c, positions, w_cond, b_cond, norm_pre, norm_post, ffn_w_in, ffn_w_out,
                 w_mix_in, w_mix_out, attn_sinks, hgrn_lb_logits, hgrn_gnorm):
    f = np.float32
    cst = _consts()
    shared = dict(
        w_cond=np.ascontiguousarray(w_cond[0], f), b_cond=np.ascontiguousarray(np.broadcast_to(np.asarray(b_cond[0], f)[None, :], (128, 9 * D))),
        npre=np.ascontiguousarray(np.asarray(norm_pre[0], f).reshape(3, 8, 128).transpose(2, 0, 1)),
        npost=np.ascontiguousarray(np.broadcast_to(np.asarray(norm_post[0], f)[None], (128, 3, D))),
        w_in=np.ascontiguousarray(ffn_w_in[0], f), w_out=np.ascontiguousarray(ffn_w_out[0], f),
        w_mi=np.ascontiguousarray(w_mix_in[0], f), w_mo=np.ascontiguousarray(w_mix_out[0], f),
        sinks=np.ascontiguousarray(np.broadcast_to(np.asarray(attn_sinks[0], f)[None], (128, 8))),
        lbl=np.ascontiguousarray(np.asarray(hgrn_lb_logits, f).reshape(2, 4, 128).transpose(2, 0, 1)),
        gn=np.ascontiguousarray(np.asarray(hgrn_gnorm[0], f)[:, None]),
        **cst)
    maps = []
    for b in range(8):
        m = dict(shared)
        m["x"] = np.ascontiguousarray(x[b], f)
        m["ccol"] = np.ascontiguousarray(np.asarray(c[b], f).reshape(8, 128).T)
        m["pos"] = np.ascontiguousarray(np.asarray(positions[b], np.int32).reshape(NT, 128).T)
        maps.append(m)
    return maps


def kernel(**inputs):
    nc = build(3)
    maps = make_in_maps(**inputs)
    res = run_bass_kernel_spmd(nc, maps, core_ids=list(range(8)))
    return np.stack([np.asarray(r["out"], np.float32) for r in res.results], axis=0)
```

```python
import numpy as np
from contextlib import ExitStack
import ml_dtypes
import concourse.bass as bass
import concourse.mybir as mybir
from concourse.bass_utils import run_bass_kernel_spmd

F32 = mybir.dt.float32
BF16 = mybir.dt.bfloat16
I32 = mybir.dt.int32
AF = mybir.ActivationFunctionType
ALU = mybir.AluOpType
AX = mybir.AxisListType

S = 4096
D = 1024
DFF = 2816
NF = DFF // 128
TB = 512
NB = S // TB
NT = S // 128
EPS = 1e-6
PI = float(np.pi)
TWO_PI = float(2 * np.pi)


class Tk:
    __slots__ = ("w", "r", "name", "excl")

    def __init__(self, name="", excl=False):
        self.w = None
        self.r = {}
        self.name = name
        self.excl = excl


class Sem:
    def __init__(self, h, name):
        self.h = h
        self.cnt = 0
        self.name = name


class Eng:
    def __init__(self, name, h, sem):
        self.name = name
        self.h = h
        self.sem = sem
        self.seen = {}


class Kern:
    def __init__(self, nc, es):
        self.nc = nc
        self.es = es
        self.sems = []
        self.PE = Eng("pe", nc.tensor, self.newsem("pe"))
        self.ACT = Eng("act", nc.scalar, self.newsem("act"))
        self.DVE = Eng("dve", nc.vector, self.newsem("dve"))
        self.POOL = Eng("pool", nc.gpsimd, self.newsem("pool"))
        self.SP = Eng("sp", nc.sync, self.newsem("sp"))
        self.engs = [self.PE, self.ACT, self.DVE, self.POOL, self.SP]
        self.n_inst = 0

    def newsem(self, name):
        self.sid = getattr(self, "sid", 0) + 1
        s = Sem(self.es.enter_context(self.nc.semaphore("m%d_%s" % (self.sid, name))), name)
        self.sems.append(s)
        return s

    def sb(self, name, shape, dt, es=None):
        self.uid = getattr(self, "uid", 0) + 1
        return (es or self.es).enter_context(self.nc.sbuf_tensor("s%d_%s" % (self.uid, name), shape, dt))

    def _wait(self, E, toks):
        for (s, v) in toks:
            if E.seen.get(id(s), 0) >= v:
                continue
            E.h.wait_ge(s.h, v)
            E.seen[id(s)] = v

    def _deps(self, E, reads, writes):
        toks = []
        for t in reads:
            if t.w is not None:
                toks.append(t.w)
            if t.excl:
                for tok in t.r.values():
                    if tok[0] is not E.sem:
                        toks.append(tok)
        for t in writes:
            if t.w is not None and t.w[0] is not E.sem:
                toks.append(t.w)
            for tok in t.r.values():
                if tok[0] is not E.sem or E is not self.PE:
                    toks.append(tok)
        return toks

    def _mark(self, tok, reads, writes):
        for t in writes:
            t.w = tok
            t.r = {}
        for t in reads:
            t.r[id(tok[0])] = tok

    def op(self, E, fn, reads=(), writes=()):
        self._wait(E, self._deps(E, reads, writes))
        ins = fn()
        ins.then_inc(E.sem.h, 1)
        E.sem.cnt += 1
        self._mark((E.sem, E.sem.cnt), reads, writes)
        self.n_inst += 1
        return ins

    def group(self, E, fns, reads=(), writes=()):
        self._wait(E, self._deps(E, reads, writes))
        ins = None
        for fn in fns:
            ins = fn()
            self.n_inst += 1
        ins.then_inc(E.sem.h, 1)
        E.sem.cnt += 1
        self._mark((E.sem, E.sem.cnt), reads, writes)

    def dma(self, E, out, in_, reads, writes, sem, **kw):
        self._wait(E, self._deps(E, reads, writes))
        E.h.dma_start(out=out, in_=in_, **kw).then_inc(sem.h, 16)
        sem.cnt += 16
        self._mark((sem, sem.cnt), reads, writes)
        self.n_inst += 1

    def barrier(self):
        toks = [(s, s.cnt) for s in self.sems if s.cnt > 0]
        for E in self.engs:
            self._wait(E, [t for t in toks if t[0] is not E.sem])


class Psum:
    def __init__(self, K):
        nc = K.nc
        self.t = K.es.enter_context(nc.psum_tensor("psum_all", [128, 8 * 512], F32))
        self.tb = self.t.bitcast(BF16)
        self.tk = [Tk("bank%d" % i, excl=True) for i in range(8)]
        self.p = 0
        self.held = [False] * 8

    def try_alloc(self, n=1):
        for i in range(8):
            b = (self.p + i) % 8
            if b % n or b + n > 8:
                continue
            if any(self.held[b:b + n]):
                continue
            for j in range(b, b + n):
                self.held[j] = True
            self.p = (b + n) % 8
            return b, self.tk[b:b + n]
        return None

    def alloc(self, n=1):
        r = self.try_alloc(n)
        assert r is not None, "PSUM exhausted (straight-line code must free before allocating)"
        return r

    def free(self, b, n=1):
        for j in range(b, b + n):
            assert self.held[j]
            self.held[j] = False

    def f32(self, b, n=1):
        return self.t[:, b * 512:(b + n) * 512]

    def bf(self, b, n=1):
        return self.tb[:, b * 1024:(b + n) * 1024]


class Rot:
    def __init__(self, K, es, name, shape, dt, n, dma=False):
        self.items = []
        for i in range(n):
            t = K.sb("%s%d" % (name, i), shape, dt, es)
            self.items.append((t, Tk("%s%d" % (name, i)), K.newsem("%s%d" % (name, i)) if dma else None))
        self.i = 0

    def next(self):
        it = self.items[self.i % len(self.items)]
        self.i += 1
        return it


def build(stop_after=3):
    nc = bass.Bass("TRN2", target_bir_lowering=False)

    def din(name, shape, dt=F32):
        return nc.dram_tensor(name, shape, dt, kind="ExternalInput").ap()

    x_d = din("x", [S, D])
    ccol_d = din("ccol", [128, 8])
    pos_d = din("pos", [128, NT], I32)
    wcond_d = din("w_cond", [D, 9 * D])
    bcond_d = din("b_cond", [128, 9 * D])
    npre_d = din("npre", [128, 3, 8])
    npost_d = din("npost", [128, 3, D])
    win_d = din("w_in", [2, D, 2 * DFF])
    wout_d = din("w_out", [2, DFF, D])
    wmi_d = din("w_mi", [D, 2816])
    wmo_d = din("w_mo", [D, D])
    sinks_d = din("sinks", [128, 8])
    lbl_d = din("lbl", [128, 2, 4])
    gn_d = din("gn", [128, 1])
    ident_d = din("ident", [128, 128], BF16)
    onesb_d = din("onesb", [128, 128], BF16)
    amask_d = din("amask", [128, 2, 2, 512], BF16)
    hmask_d = din("hmask", [128, 4, 128], BF16)
    rmask_d = din("rmask", [128, 512])
    invf_d = din("invf", [128, 32])
    rowm_d = din("rowm", [128, 2])
    out_d = nc.dram_tensor("out", [S, D], F32, kind="ExternalOutput").ap()
    x1_d = nc.dram_tensor("x1s", [S, D], F32).ap()
    x2_d = nc.dram_tensor("x2s", [S, D], F32).ap()
    rope_d = nc.dram_tensor("rope_s", [128, 2, NT * 32], F32).ap()
    w2in_s = nc.dram_tensor("w2in_s", [D, 2 * DFF], BF16).ap()
    w2out_s = nc.dram_tensor("w2out_s", [DFF, D], BF16).ap()
    wmi_s = nc.dram_tensor("wmi_s", [D, 2816], BF16).ap()
    wmo_s = nc.dram_tensor("wmo_s", [D, D], BF16).ap()

    with ExitStack() as es:
        K = Kern(nc, es)
        PE, ACT, DVE, POOL, SP = K.PE, K.ACT, K.DVE, K.POOL, K.SP
        ps = Psum(K)
        dram_tk = {}

        def dtk(name, tile):
            key = (name, tile)
            if key not in dram_tk:
                dram_tk[key] = Tk("%s_%d" % key)
            return dram_tk[key]

        ident = K.sb("ident", [128, 128], BF16)
        onesb = K.sb("onesb", [128, 128], BF16)
        rowm = K.sb("rowm", [128, 2], F32)
        sinks = K.sb("sinks", [128, 8], F32)
        gn = K.sb("gn", [128, 1], F32)
        lb = K.sb("lb", [128, 4], F32)
        oml = K.sb("oml", [128, 4], F32)
        a_col = K.sb("a_col", [128, 3, 8], F32)
        sh_col = K.sb("sh_col", [128, 3, 8], F32)
        G = K.sb("G", [128, 3, D], BF16)
        c01 = K.sb("c01", [128, 2, 4], F32)
        gnh = K.sb("gnh", [128, 1], F32)
        const_tk = Tk("consts")
        par_tk = Tk("params")
        rope_tk = Tk("rope")

        with ExitStack() as ss:
            ld = K.newsem("setup_ld")
            ccol = K.sb("ccol", [128, 8], F32, ss)
            ca = K.sb("ca", [128, 8], F32, ss)
            posi = K.sb("posi", [128, NT], I32, ss)
            npre = K.sb("npre", [128, 3, 8], F32, ss)
            npost = K.sb("npost", [128, 3, D], F32, ss)
            lbl = K.sb("lbl", [128, 2, 4], F32, ss)
            invf = K.sb("invf", [128, 32], F32, ss)
            setup_tk = Tk("setup_in")
            cosT = K.sb("cosT", [128, NT, 32], F32, ss)
            sinT = K.sb("sinT", [128, NT, 32], F32, ss)
            loads = [(ident, ident_d), (onesb, onesb_d),
                     (rowm, rowm_d), (sinks, sinks_d), (gn, gn_d), (ccol, ccol_d), (posi, pos_d), (npre, npre_d),
                     (npost, npost_d), (lbl, lbl_d), (invf, invf_d)]
            for (t, d_) in loads:
                nc.sync.dma_start(out=t[:], in_=d_).then_inc(ld.h, 16)
                ld.cnt += 16
            setup_tk.w = (ld, ld.cnt)
            const_tk.w = (ld, ld.cnt)

            K.op(ACT, lambda: nc.scalar.activation(out=ca[:], in_=ccol[:], func=AF.Silu), reads=[setup_tk], writes=[par_tk])
            K.op(DVE, lambda: nc.vector.tensor_tensor(out=lb[:], in0=lbl[:, 0, :], in1=lbl[:, 1, :], op=ALU.subtract),
                 reads=[setup_tk], writes=[par_tk])
            K.op(ACT, lambda: nc.scalar.activation(out=lb[:], in_=lb[:], func=AF.Sigmoid), reads=[par_tk], writes=[par_tk])
            K.op(DVE, lambda: nc.vector.tensor_scalar(out=oml[:], in0=lb[:], scalar1=-1.0, scalar2=1.0, op0=ALU.mult, op1=ALU.add),
                 reads=[par_tk], writes=[par_tk])
            K.op(DVE, lambda: nc.vector.tensor_scalar(out=c01[:, 1, :], in0=oml[:], scalar1=0.5, scalar2=None, op0=ALU.mult),
                 reads=[par_tk], writes=[par_tk])
            K.op(DVE, lambda: nc.vector.tensor_tensor(out=c01[:, 0, :], in0=lb[:], in1=c01[:, 1, :], op=ALU.add),
                 reads=[par_tk], writes=[par_tk])
            K.op(DVE, lambda: nc.vector.tensor_scalar(out=gnh[:], in0=gn[:], scalar1=0.5, scalar2=None, op0=ALU.mult),
                 reads=[setup_tk, par_tk], writes=[par_tk])

            posf = K.sb("posf", [128, NT], F32, ss)
            ang = K.sb("ang", [128, NT, 32], F32, ss)
            rr = K.sb("rr", [128, NT, 32], F32, ss)
            rc = K.sb("rc", [128, NT, 32], F32, ss)
            kf = K.sb("kf", [128, NT, 32], F32, ss)
            ki = K.sb("ki", [128, NT, 32], I32, ss)
            mk = K.sb("mk", [128, NT, 32], F32, ss)
            rt = Tk("ropetmp")

            def V(fn, reads=(), writes=()):
                K.op(DVE, fn, reads=list(reads) + [rt, setup_tk], writes=list(writes) + [rt])

            V(lambda: nc.vector.tensor_copy(out=posf[:], in_=posi[:]))
            V(lambda: nc.vector.tensor_tensor(out=ang[:], in0=posf[:].unsqueeze(2).broadcast_to([128, NT, 32]),
                                              in1=invf[:].unsqueeze(1).broadcast_to([128, NT, 32]), op=ALU.mult))
            V(lambda: nc.vector.tensor_scalar(out=kf[:], in0=ang[:], scalar1=1.0 / TWO_PI, scalar2=None, op0=ALU.mult))
            V(lambda: nc.vector.tensor_copy(out=ki[:], in_=kf[:]))
            V(lambda: nc.vector.tensor_copy(out=kf[:], in_=ki[:]))
            C1 = 6.28125
            C2 = TWO_PI - 6.28125
            V(lambda: nc.vector.scalar_tensor_tensor(out=rr[:], in0=kf[:], scalar=-C1, in1=ang[:], op0=ALU.mult, op1=ALU.add))
            V(lambda: nc.vector.scalar_tensor_tensor(out=rr[:], in0=kf[:], scalar=-C2, in1=rr[:], op0=ALU.mult, op1=ALU.add))

            def fold(t):
                V(lambda: nc.vector.tensor_scalar(out=mk[:], in0=t[:], scalar1=PI, scalar2=-TWO_PI, op0=ALU.is_gt, op1=ALU.mult))
                V(lambda: nc.vector.tensor_tensor(out=t[:], in0=t[:], in1=mk[:], op=ALU.add))
                V(lambda: nc.vector.tensor_scalar(out=mk[:], in0=t[:], scalar1=-PI, scalar2=TWO_PI, op0=ALU.is_lt, op1=ALU.mult))
                V(lambda: nc.vector.tensor_tensor(out=t[:], in0=t[:], in1=mk[:], op=ALU.add))
                V(lambda: nc.vector.tensor_scalar(out=t[:], in0=t[:], scalar1=PI, scalar2=-PI, op0=ALU.min, op1=ALU.max))

            fold(rr)
            V(lambda: nc.vector.tensor_scalar(out=rc[:], in0=rr[:], scalar1=PI / 2, scalar2=None, op0=ALU.add))
            fold(rc)
            K.op(ACT, lambda: nc.scalar.activation(out=sinT[:], in_=rr[:], func=AF.Sin), reads=[rt], writes=[rope_tk])
            K.op(ACT, lambda: nc.scalar.activation(out=cosT[:], in_=rc[:], func=AF.Sin), reads=[rt], writes=[rope_tk])
            cab = K.sb("cab", [128, 8, 128], F32, ss)
            identf = K.sb("identf", [128, 128], F32, ss)
            modbc = K.sb("modbc", [128, 9 * D], F32, ss)
            dg = K.sb("dg", [128, 8, 128], F32, ss)
            K.op(DVE, lambda: nc.vector.tensor_copy(out=cab[:], in_=ca[:].unsqueeze(2).broadcast_to([128, 8, 128])),
                 reads=[par_tk], writes=[par_tk])
            K.op(DVE, lambda: nc.vector.tensor_copy(out=identf[:], in_=ident[:]), reads=[setup_tk], writes=[setup_tk])
            wc_rot = Rot(K, ss, "wc", [128, 8, 512], F32, 2, dma=True)
            bc_rot = Rot(K, ss, "bc", [128, 512], F32, 2, dma=True)
            wc_view = wcond_d.rearrange("(k p) n -> p k n", p=128)
            mod_tk = Tk("modbc")
            for cb in range(18):
                wc, wc_tk, wc_sem = wc_rot.next()
                bc, bc_tk, bc_sem = bc_rot.next()
                K.dma(SP, out=wc[:], in_=wc_view[:, :, cb * 512:(cb + 1) * 512], reads=[], writes=[wc_tk], sem=wc_sem)
                K.dma(SP, out=bc[:], in_=bcond_d[:, cb * 512:(cb + 1) * 512], reads=[], writes=[bc_tk], sem=bc_sem)
                b, btk = ps.alloc(1)
                pv = ps.f32(b)
                K.group(PE, [(lambda k=k: nc.tensor.matmul(pv, lhsT=cab[:, k, :], rhs=wc[:, k, :],
                                                         start=(k == 0), stop=(k == 7))) for k in range(8)],
                        reads=[par_tk, wc_tk], writes=btk)
                K.op(DVE, lambda: nc.vector.tensor_tensor(out=modbc[:, cb * 512:(cb + 1) * 512], in0=pv, in1=bc[:], op=ALU.add),
                     reads=btk + [bc_tk, mod_tk], writes=[mod_tk])
                ps.free(b)
            for k in range(3):
                for which, v in (("sh", 3 * k), ("sc", 3 * k + 1)):
                    K.op(DVE, lambda: nc.vector.tensor_tensor(out=dg[:], in0=modbc[:, v * D:(v + 1) * D].rearrange("p (j q) -> p j q", q=128),
                                                              in1=identf[:].unsqueeze(1).broadcast_to([128, 8, 128]), op=ALU.mult),
                         reads=[mod_tk, setup_tk, par_tk], writes=[par_tk])
                    dstc = sh_col if which == "sh" else a_col
                    K.op(DVE, lambda: nc.vector.tensor_reduce(out=dstc[:, k, :], in_=dg[:], axis=AX.X, op=ALU.add),
                         reads=[par_tk], writes=[par_tk])
                K.op(DVE, lambda: nc.vector.scalar_tensor_tensor(out=a_col[:, k, :], in0=a_col[:, k, :], scalar=1.0, in1=npre[:, k, :],
                                                                 op0=ALU.add, op1=ALU.mult),
                     reads=[par_tk, setup_tk], writes=[par_tk])
                resw = 1.0 if k == 1 else 0.5
                K.op(DVE, lambda: nc.vector.scalar_tensor_tensor(out=G[:, k, :], in0=modbc[:, (3 * k + 2) * D:(3 * k + 3) * D], scalar=resw,
                                                                 in1=npost[:, k, :], op0=ALU.mult, op1=ALU.mult),
                     reads=[mod_tk, setup_tk, par_tk], writes=[par_tk])

            rsem = K.newsem("ropest")
            K.dma(SP, out=rope_d[:, 0, :], in_=cosT[:].rearrange("p t f -> p (t f)"), reads=[rope_tk], writes=[dtk("rope", 0)], sem=rsem)
            K.dma(SP, out=rope_d[:, 1, :], in_=sinT[:].rearrange("p t f -> p (t f)"), reads=[rope_tk], writes=[dtk("rope", 0)], sem=rsem)
            K.barrier()

        def issue_cast_dma(dst, src, tk, sem):
            K.dma(POOL, out=dst, in_=src, reads=[], writes=[tk], sem=sem)

        eps_t = K.sb("eps_t", [128, 2], F32)
        const2_tk = Tk("const2")
        K.op(DVE, lambda: nc.vector.memset(eps_t[:, 0:1], EPS), writes=[const2_tk])
        K.op(DVE, lambda: nc.vector.memset(eps_t[:, 1:2], 0.0), writes=[const2_tk])

        def rstd_ops(st, st_tk, n, lnexp):
            if lnexp:
                K.op(ACT, lambda: nc.scalar.activation(out=st[:, 1:2], in_=st[:, 0:1], func=AF.Ln, scale=1.0 / n, bias=eps_t[:, 0:1]),
                     reads=[st_tk, const2_tk], writes=[st_tk])
                K.op(ACT, lambda: nc.scalar.activation(out=st[:, 2:3], in_=st[:, 1:2], func=AF.Exp, scale=-0.5),
                     reads=[st_tk], writes=[st_tk])
            else:
                K.op(ACT, lambda: nc.scalar.activation(out=st[:, 1:2], in_=st[:, 0:1], func=AF.Sqrt, scale=1.0 / n, bias=eps_t[:, 0:1]),
                     reads=[st_tk, const2_tk], writes=[st_tk])
                K.op(DVE, lambda: nc.vector.reciprocal(out=st[:, 2:3], in_=st[:, 1:2]), reads=[st_tk], writes=[st_tk])

        class Pro:
            def __init__(self, es_, src, srcname, kidx, lnexp, tag, nxin=2, evac_act=False):
                self.src, self.srcname, self.kidx, self.lnexp = src, srcname, kidx, lnexp
                self.evac_act = evac_act
                self.xin_rot = Rot(K, es_, tag + "xin", [128, D], F32, nxin, dma=True)
                self.xn_rot = Rot(K, es_, tag + "xn", [128, D], BF16, 2)
                self.st_rot = Rot(K, es_, tag + "st", [128, 4], F32, 4)
                self.tmpf = K.sb(tag + "tmpf", [128, 8, 128], F32, es_)
                self.tmpf_tk = Tk(tag + "tmpf")

            def elem(self, tile):
                xin, xin_tk, xin_sem = self.xin_rot.next()
                K.dma(SP, out=xin[:], in_=self.src[tile * 128:(tile + 1) * 128, :], reads=[dtk(self.srcname, tile)],
                      writes=[xin_tk], sem=xin_sem)
                st, st_tk, _ = self.st_rot.next()
                xn, xn_tk, _ = self.xn_rot.next()
                K.op(ACT, lambda: nc.scalar.activation(out=xn[:], in_=xin[:], func=AF.Square, accum_out=st[:, 0:1]),
                     reads=[xin_tk], writes=[xn_tk, st_tk])
                rstd_ops(st, st_tk, D, self.lnexp)
                K.op(DVE, lambda: nc.vector.tensor_scalar(out=xn[:], in0=xin[:], scalar1=st[:, 2:3], scalar2=None, op0=ALU.mult),
                     reads=[xin_tk, st_tk], writes=[xn_tk])
                return xn, xn_tk

            def pe(self, xn, xn_tk, h, h_tks, tt):
                b, tks = ps.alloc(1)
                tv = ps.bf(b)
                K.group(PE, [(lambda dc=dc: nc.tensor.transpose(tv[:, dc * 128:(dc + 1) * 128], xn[:, dc * 128:(dc + 1) * 128], ident[:]))
                             for dc in range(8)], reads=[xn_tk, const_tk], writes=tks)
                kidx = self.kidx
                if self.evac_act:
                    for dc in range(8):
                        K.op(ACT, lambda: nc.scalar.activation(out=h[:, dc, tt * 128:(tt + 1) * 128], in_=tv[:, dc * 128:(dc + 1) * 128],
                                                               func=AF.Identity, scale=a_col[:, kidx, dc:dc + 1], bias=sh_col[:, kidx, dc:dc + 1]),
                             reads=tks + [par_tk], writes=h_tks)
                    ps.free(b)
                    return
                K.op(DVE, lambda: nc.vector.tensor_tensor(out=self.tmpf[:], in0=tv.rearrange("p (c t) -> p c t", t=128),
                                                          in1=a_col[:, kidx, :].unsqueeze(2).broadcast_to([128, 8, 128]), op=ALU.mult),
                     reads=tks + [par_tk], writes=[self.tmpf_tk])
                ps.free(b)
                K.op(DVE, lambda: nc.vector.tensor_tensor(out=h[:, :, tt * 128:(tt + 1) * 128], in0=self.tmpf[:],
                                                          in1=sh_col[:, kidx, :].unsqueeze(2).broadcast_to([128, 8, 128]), op=ALU.add),
                     reads=[self.tmpf_tk, par_tk], writes=h_tks)

        class Epi:
            def __init__(self, es_, src, srcname, dst, dstname, kidx, lnexp, tag, junk):
                self.src, self.srcname, self.dst, self.dstname, self.kidx, self.lnexp = src, srcname, dst, dstname, kidx, lnexp
                self.junk = junk
                self.xre_rot = Rot(K, es_, tag + "xre", [128, D], F32, 1, dma=True)
                self.ob_rot = Rot(K, es_, tag + "ob", [128, D], F32, 1, dma=True)
                self.st_rot = Rot(K, es_, tag + "est", [128, 4], F32, 4)

            def run(self, tile, yb, ytk):
                yv = ps.f32(yb, 2)
                kidx = self.kidx
                xre, xre_tk, xre_sem = self.xre_rot.next()
                K.dma(SP, out=xre[:], in_=self.src[tile * 128:(tile + 1) * 128, :], reads=[dtk(self.srcname, tile)], writes=[xre_tk],
                      sem=xre_sem)
                st, st_tk, _ = self.st_rot.next()
                ob, ob_tk, ob_sem = self.ob_rot.next()
                jk, jk_tk, _ = self.junk.next()
                K.op(ACT, lambda: nc.scalar.activation(out=jk[:], in_=yv, func=AF.Square, accum_out=st[:, 0:1]),
                     reads=ytk, writes=[jk_tk, st_tk])
                rstd_ops(st, st_tk, D, self.lnexp)
                K.op(DVE, lambda: nc.vector.scalar_tensor_tensor(out=ob[:], in0=yv, scalar=st[:, 2:3], in1=G[:, kidx, :],
                                                                 op0=ALU.mult, op1=ALU.mult),
                     reads=ytk + [st_tk, par_tk], writes=[ob_tk])
                ps.free(yb, 2)
                K.op(POOL, lambda: nc.gpsimd.tensor_tensor(out=ob[:], in0=ob[:], in1=xre[:], op=ALU.add),
                     reads=[ob_tk, xre_tk], writes=[ob_tk])
                K.dma(POOL, out=self.dst[tile * 128:(tile + 1) * 128, :], in_=ob[:], reads=[ob_tk], writes=[dtk(self.dstname, tile)],
                      sem=ob_sem)

        def interleave(*gens):
            gens = list(gens)
            while gens:
                for g in list(gens):
                    try:
                        next(g)
                    except StopIteration:
                        gens.remove(g)

        def ffn_phase(fi, kidx, src, srcname, dst, dstname):
            with ExitStack() as pe_:
                Win = K.sb("Win", [128, 8, 2 * DFF], BF16, pe_)
                Wout = K.sb("Wout", [128, NF, D], BF16, pe_)
                groups = [(0, 6), (6, 12), (12, 17), (17, 22)]
                win_tk = [Tk("win%d" % g) for g in range(4)]
                wout_tk = [Tk("wout%d" % g) for g in range(2)]
                if fi == 0:
                    wv = win_d[fi].rearrange("(k p) n -> p k n", p=128)
                    wov = wout_d[fi].rearrange("(f p) n -> p f n", p=128)
                    for g, (f0, f1) in enumerate(groups):
                        sem = K.newsem("win%d_%d" % (fi, g))
                        for off in (0, DFF):
                            issue_cast_dma(Win[:, :, off + f0 * 128:off + f1 * 128], wv[:, :, off + f0 * 128:off + f1 * 128],
                                           win_tk[g], sem)
                    for g, (f0, f1) in enumerate([(0, 11), (11, 22)]):
                        sem = K.newsem("wout%d_%d" % (fi, g))
                        issue_cast_dma(Wout[:, f0:f1, :], wov[:, f0:f1, :], wout_tk[g], sem)

                    pcs = K.newsem("precast")

                    def precast(blk):
                        def cp(dst, src_, name):
                            K.dma(POOL, out=dst, in_=src_, reads=[], writes=[dtk(name, 0)], sem=pcs)
                        if blk == 2:
                            cp(wmi_s, wmi_d, "wmi_s")
                        elif blk == 3:
                            cp(wmo_s, wmo_d, "wmo_s")
                            cp(w2out_s[0:1408, :], wout_d[1][0:1408, :], "w2out_s")
                        elif blk == 4:
                            cp(w2out_s[1408:2816, :], wout_d[1][1408:2816, :], "w2out_s")
                            cp(w2in_s[0:256, :], win_d[1][0:256, :], "w2in_s")
                        elif blk == 5:
                            cp(w2in_s[256:512, :], win_d[1][256:512, :], "w2in_s")
                            cp(w2in_s[512:768, :], win_d[1][512:768, :], "w2in_s")
                        elif blk == 6:
                            cp(w2in_s[768:1024, :], win_d[1][768:1024, :], "w2in_s")
                else:
                    wv = w2in_s.rearrange("(k p) n -> p k n", p=128)
                    wov = w2out_s.rearrange("(f p) n -> p f n", p=128)
                    qi = 0
                    for g, (f0, f1) in enumerate(groups):
                        sem = K.newsem("win%d_%d" % (fi, g))
                        for off in (0, DFF):
                            K.dma(POOL, out=Win[:, :, off + f0 * 128:off + f1 * 128],
                                  in_=wv[:, :, off + f0 * 128:off + f1 * 128], reads=[dtk("w2in_s", 0)], writes=[win_tk[g]], sem=sem)
                            qi += 1
                    for g, (f0, f1) in enumerate([(0, 11), (11, 22)]):
                        sem = K.newsem("wout%d_%d" % (fi, g))
                        K.dma(POOL, out=Wout[:, f0:f1, :], in_=wov[:, f0:f1, :], reads=[dtk("w2out_s", 0)],
                              writes=[wout_tk[g]], sem=sem)
                        qi += 1

                def gof(f):
                    for g, (f0, f1) in enumerate(groups):
                        if f0 <= f < f1:
                            return g

                pro = Pro(pe_, src, srcname, kidx, False, "f")
                epi = Epi(pe_, src, srcname, dst, dstname, kidx, False, "f", pro.xn_rot)
                hs = [K.sb("h%d" % i, [128, 8, TB], BF16, pe_) for i in range(2)]
                h_tkss = [[Tk("h%d" % i)] for i in range(2)]
                act = K.sb("act", [128, NF, TB], BF16, pe_)
                act_tk = [Tk("act%d" % f) for f in range(NF)]
                sg_rot = Rot(K, pe_, "sg", [128, TB], F32, 1)

                for tt in range(4):
                    xn, xn_tk = pro.elem(tt)
                    pro.pe(xn, xn_tk, hs[0], h_tkss[0], tt)
                for blk in range(NB):
                    if fi == 0:
                        precast(blk)
                    h = hs[blk % 2]
                    h_tks = h_tkss[blk % 2]
                    hn = hs[(blk + 1) % 2]
                    hn_tks = h_tkss[(blk + 1) % 2]
                    pend = {}
                    for f in range(NF):
                        bg, tg = ps.alloc(1)
                        bu, tu = ps.alloc(1)
                        pg = ps.f32(bg)
                        pu = ps.f32(bu)
                        fns = []
                        for k in range(8):
                            fns.append(lambda k=k: nc.tensor.matmul(pg, lhsT=Win[:, k, f * 128:(f + 1) * 128], rhs=h[:, k, :],
                                                                    start=(k == 0), stop=(k == 7)))
                        for k in range(8):
                            fns.append(lambda k=k: nc.tensor.matmul(pu, lhsT=Win[:, k, DFF + f * 128:DFF + (f + 1) * 128],
                                                                    rhs=h[:, k, :], start=(k == 0), stop=(k == 7)))
                        K.group(PE, fns, reads=h_tks + [win_tk[gof(f)]], writes=tg + tu)
                        sg, sg_tk, _ = sg_rot.next()
                        K.op(ACT, lambda: nc.scalar.activation(out=sg[:], in_=pg, func=AF.Silu), reads=tg, writes=[sg_tk])
                        K.op(DVE, lambda: nc.vector.tensor_tensor(out=act[:, f, :], in0=sg[:], in1=pu, op=ALU.mult),
                             reads=[sg_tk] + tu, writes=[act_tk[f]])
                        ps.free(bg)
                        ps.free(bu)
                        if blk + 1 < NB:
                            if f % 4 == 0 and f // 4 < 4:
                                tt = f // 4
                                pend[tt] = pro.elem((blk + 1) * 4 + tt)
                            if f >= 7 and (f - 7) % 4 == 0 and (f - 7) // 4 < 4:
                                tt = (f - 7) // 4
                                pro.pe(pend[tt][0], pend[tt][1], hn, hn_tks, tt)
                    for tt in range(4):
                        yb, ytk = ps.alloc(2)
                        fns = []
                        for n in range(2):
                            o = ps.f32(yb + n)
                            for f in range(NF):
                                fns.append(lambda o=o, n=n, f=f: nc.tensor.matmul(
                                    o, lhsT=act[:, f, tt * 128:(tt + 1) * 128], rhs=Wout[:, f, n * 512:(n + 1) * 512],
                                    start=(f == 0), stop=(f == NF - 1)))
                        K.group(PE, fns, reads=act_tk + wout_tk, writes=ytk)
                        epi.run(blk * 4 + tt, yb, ytk)
                K.barrier()

        def mixer_phase(src, srcname, dst, dstname):
            kidx = 1
            with ExitStack() as pe_:
                Wmi = K.sb("Wmi", [128, 8, 2816], BF16, pe_)
                Wmo = K.sb("Wmo", [128, 8, D], BF16, pe_)
                wmi_tk = Tk("wmi")
                wmo_tk = Tk("wmo")
                wmiv = wmi_s.rearrange("(k p) n -> p k n", p=128)
                wmov = wmo_s.rearrange("(k p) n -> p k n", p=128)
                s1 = K.newsem("wmi")
                K.dma(POOL, out=Wmi[:, :, 0:1408], in_=wmiv[:, :, 0:1408], reads=[dtk("wmi_s", 0)], writes=[wmi_tk], sem=s1)
                K.dma(POOL, out=Wmi[:, :, 1408:2816], in_=wmiv[:, :, 1408:2816], reads=[dtk("wmi_s", 0)], writes=[wmi_tk], sem=s1)
                s2 = K.newsem("wmo")
                K.dma(POOL, out=Wmo[:], in_=wmov, reads=[dtk("wmo_s", 0)], writes=[wmo_tk], sem=s2)
                amask = K.sb("amask", [128, 2, 2, 512], BF16, pe_)
                hmask = K.sb("hmask", [128, 4, 128], BF16, pe_)
                rmask = K.sb("rmask", [128, 512], F32, pe_)
                cs_rot = Rot(K, pe_, "mcs", [128, 2, 128], F32, 2, dma=True)
                mc_tk = Tk("mconst")
                mcs = K.newsem("mconst")
                for (t_, d_) in ((amask[:], amask_d), (hmask[:], hmask_d), (rmask[:], rmask_d)):
                    K.dma(SP, out=t_, in_=d_, reads=[], writes=[mc_tk], sem=mcs)

                pro = Pro(pe_, src, srcname, kidx, True, "m", nxin=1, evac_act=True)
                epi = Epi(pe_, src, srcname, dst, dstname, kidx, True, "m", pro.xn_rot)
                h = K.sb("mh", [128, 8, TB], BF16, pe_)
                h_tks = [Tk("mh")]
                q2 = K.sb("q2", [128, 4, TB], F32, pe_)
                ff = K.sb("ff", [128, 4, TB], F32, pe_)
                bb = K.sb("bb", [128, 4, TB], F32, pe_)
                q2_tk = [Tk("q2_%d" % c) for c in range(4)]
                ff_tk = [Tk("ff_%d" % c) for c in range(4)]
                bb_tk = [Tk("bb_%d" % c) for c in range(4)]
                tA = Rot(K, pe_, "tA", [128, TB], F32, 2)
                tB = Rot(K, pe_, "tB", [128, TB], F32, 2)
                qt = K.sb("qt", [128, 4, TB], BF16, pe_)
                kt = K.sb("kt", [128, 4, TB], BF16, pe_)
                k2T = K.sb("k2T", [128, 4, TB], BF16, pe_)
                gs = K.sb("gs", [128, 4, TB], BF16, pe_)
                dec = K.sb("dec", [128, 4, 8], F32, pe_)
                hb_tk = [Tk("hgblk%d" % c) for c in range(4)]
                ktw_tk = [Tk("ktw%d" % c) for c in range(4)]
                k2w_tk = [Tk("k2w%d" % c) for c in range(4)]
                catT = K.sb("catT", [128, 8, TB], BF16, pe_)
                cat_tk = [Tk("cat%d" % c) for c in range(8)]
                NCH = 2
                RING = 5
                qkr_r = Rot(K, pe_, "qkr", [128, 640], BF16, 1)
                vtm = K.sb("vtm", [128, 4, 512], BF16, pe_)
                vtm_tk = [Tk("vtm%d" % i) for i in range(4)]
                vbe = K.sb("vbe", [128, RING, 2, 65], BF16, pe_)
                vb_tk = [Tk("vb%d" % i) for i in range(RING)]
                qT_r = Rot(K, pe_, "qT", [128, 8, 128], BF16, 4)
                kTa = K.sb("kTa", [128, 2, RING * 128], BF16, pe_)
                kT_tk = [Tk("kT%d" % i) for i in range(RING)]
                PT_r = Rot(K, pe_, "PTs", [128, 2, 2, 512], BF16, NCH)
                atm_r = Rot(K, pe_, "atm", [128, 512], BF16, NCH)
                ra_r = Rot(K, pe_, "ra", [128, 10, 32], F32, 1)
                rb_r = Rot(K, pe_, "rb", [128, 10, 32], F32, 1)
                dn_r = Rot(K, pe_, "dn", [128, 2, 8], F32, NCH)
                esink = K.sb("esink", [128, 8], F32, pe_)
                negc = K.sb("negc", [128, 1], F32, pe_)
                SHIFT = 16.0
                st32 = K.sb("st32", [128, 4, 128], F32, pe_)
                st32_tk = [Tk("st32")]
                stb_rot = Rot(K, pe_, "stb", [128, 4, 128], BF16, 2)
                sqr = K.sb("sqr", [128, 512], BF16, pe_)
                sqr_tk = Tk("sqr")
                rsb = K.sb("rsb", [128, 512], F32, pe_)
                rsb_tk = Tk("rsb")
                t1 = K.sb("t1", [128, 512], F32, pe_)
                t1_tk = Tk("t1")

                for (qT_, qT_tk_, _) in qT_r.items:
                    K.op(POOL, lambda: nc.gpsimd.memset(qT_[:], 0.0), writes=[qT_tk_])
                K.op(POOL, lambda: nc.gpsimd.memset(kTa[:], 0.0), writes=kT_tk)
                K.op(POOL, lambda: nc.gpsimd.memset(vbe[:], 1.0), writes=vb_tk)
                K.op(DVE, lambda: nc.vector.memset(negc[:], -SHIFT), writes=[mc_tk])
                K.op(DVE, lambda: nc.vector.tensor_scalar(out=esink[:], in0=sinks[:], scalar1=-SHIFT, scalar2=None, op0=ALU.add),
                     reads=[const_tk, mc_tk], writes=[mc_tk])
                K.op(ACT, lambda: nc.scalar.activation(out=esink[:], in_=esink[:], func=AF.Exp), reads=[mc_tk], writes=[mc_tk])
                K.op(POOL, lambda: nc.gpsimd.memset(st32[:], 0.0), writes=st32_tk)
                stb0, stb0_tk, _ = stb_rot.next()
                K.op(POOL, lambda: nc.gpsimd.memset(stb0[:], 0.0), writes=[stb0_tk])
                state = {"stb": (stb0, stb0_tk)}
                QSC = float(0.5 * 128 ** -0.5)

                def galloc(n):
                    tries = 0
                    while True:
                        r = ps.try_alloc(n)
                        if r is not None:
                            return r
                        tries += 1
                        assert tries < 10000, "PSUM livelock"
                        yield

                def fm_proj(col0, b, tks):
                    pv = ps.f32(b)
                    K.group(PE, [(lambda k=k: nc.tensor.matmul(pv, lhsT=Wmi[:, k, col0:col0 + 128], rhs=h[:, k, :],
                                                             start=(k == 0), stop=(k == 7))) for k in range(8)],
                            reads=h_tks + [wmi_tk], writes=tks)
                    return pv

                def gen_prologue(blk):
                    for tt in range(4):
                        xn, xn_tk = pro.elem(blk * 4 + tt)
                        yield
                        pro.pe(xn, xn_tk, h, h_tks, tt)
                        yield

                def gen_P(blk):
                    for tt in range(4):
                        cs = slice(tt * 128, (tt + 1) * 128)
                        hb_, htk = yield from galloc(1)
                        K.group(PE, [(lambda k=k: nc.tensor.matmul(ps.f32(hb_), lhsT=h[:, k, cs], rhs=Wmi[:, k, 1792:2304],
                                                                 start=(k == 0), stop=(k == 7))) for k in range(8)],
                                reads=h_tks + [wmi_tk], writes=htk)
                        K.op(ACT, lambda: nc.scalar.activation(out=vtm[:, tt, :], in_=ps.f32(hb_), func=AF.Copy), reads=htk, writes=[vtm_tk[tt]])
                        ps.free(hb_)
                        yield
                    for c in range(4):
                        b, tks = yield from galloc(1)
                        pv = fm_proj(1280 + c * 128, b, tks)
                        a_, a_tk, _ = tA.next()
                        K.op(ACT, lambda: nc.scalar.activation(out=a_[:], in_=pv, func=AF.Tanh, scale=0.5), reads=tks, writes=[a_tk])
                        K.op(DVE, lambda: nc.vector.tensor_scalar(out=ff[:, c, :], in0=a_[:], scalar1=c01[:, 1, c:c + 1], scalar2=c01[:, 0, c:c + 1],
                                                                  op0=ALU.mult, op1=ALU.add), reads=[a_tk, par_tk], writes=[ff_tk[c]])
                        ps.free(b)
                        yield
                        b, tks = yield from galloc(1)
                        pv = fm_proj(768 + c * 128, b, tks)
                        a_, a_tk, _ = tA.next()
                        K.op(ACT, lambda: nc.scalar.activation(out=a_[:], in_=pv, func=AF.Tanh, scale=0.5), reads=tks, writes=[a_tk])
                        K.op(DVE, lambda: nc.vector.scalar_tensor_tensor(out=q2[:, c, :], in0=a_[:], scalar=1.0, in1=pv, op0=ALU.add, op1=ALU.mult),
                             reads=[a_tk] + tks, writes=[q2_tk[c]])
                        ps.free(b)
                        yield
                        b, tks = yield from galloc(1)
                        pv = fm_proj(2304 + c * 128, b, tks)
                        a_, a_tk, _ = tA.next()
                        K.op(ACT, lambda: nc.scalar.activation(out=a_[:], in_=pv, func=AF.Tanh, scale=0.5), reads=tks, writes=[a_tk])
                        K.op(DVE, lambda: nc.vector.scalar_tensor_tensor(out=gs[:, c, :], in0=a_[:], scalar=1.0, in1=pv, op0=ALU.add, op1=ALU.mult),
                             reads=[a_tk] + tks, writes=[hb_tk[c]])
                        ps.free(b)
                        yield

                def gen_Pln(cs_):
                    for c in cs_:
                        wtk = [hb_tk[c]]
                        a_, a_tk, _ = tA.next()
                        K.op(ACT, lambda: nc.scalar.activation(out=a_[:], in_=ff[:, c, :], func=AF.Ln), reads=[ff_tk[c]], writes=[a_tk])
                        K.op(DVE, lambda: nc.vector.tensor_tensor_scan(out=bb[:, c, :], data0=rmask[:], data1=a_[:], initial=0.0,
                                                                       op0=ALU.mult, op1=ALU.add),
                             reads=[a_tk, mc_tk], writes=[bb_tk[c]])
                        yield
                        e_, e_tk, _ = tB.next()
                        K.op(ACT, lambda: nc.scalar.activation(out=e_[:], in_=bb[:, c, :], func=AF.Exp), reads=[bb_tk[c]], writes=[e_tk])
                        K.op(DVE, lambda: nc.vector.scalar_tensor_tensor(out=qt[:, c, :], in0=q2[:, c, :], scalar=-QSC, in1=e_[:],
                                                                         op0=ALU.mult, op1=ALU.mult),
                             reads=[q2_tk[c], e_tk], writes=wtk)
                        K.op(DVE, lambda: nc.vector.tensor_copy(out=dec[:, c, :],
                                                                in_=e_[:].rearrange("p (n t) -> p n t", t=64)[:, :, 63]),
                             reads=[e_tk], writes=wtk)
                        yield
                        x_, x_tk, _ = tB.next()
                        K.op(ACT, lambda: nc.scalar.activation(out=x_[:], in_=bb[:, c, :], func=AF.Exp, scale=-1.0), reads=[bb_tk[c]], writes=[x_tk])
                        K.op(DVE, lambda: nc.vector.scalar_tensor_tensor(out=kt[:, c, :], in0=ff[:, c, :], scalar=1.0, in1=x_[:],
                                                                         op0=ALU.subtract, op1=ALU.mult),
                             reads=[ff_tk[c], x_tk], writes=[ktw_tk[c]])
                        yield
                        d_, d_tk, _ = tA.next()
                        b3 = bb[:, c, :].rearrange("p (n t) -> p n t", t=64)
                        K.op(DVE, lambda: nc.vector.tensor_tensor(out=d_[:].rearrange("p (n t) -> p n t", t=64),
                                                                  in0=b3[:, :, 63:64].broadcast_to([128, 8, 64]), in1=b3, op=ALU.subtract),
                             reads=[bb_tk[c]], writes=[d_tk])
                        K.op(ACT, lambda: nc.scalar.activation(out=d_[:], in_=d_[:], func=AF.Exp), reads=[d_tk], writes=[d_tk])
                        K.op(DVE, lambda: nc.vector.scalar_tensor_tensor(out=k2T[:, c, :], in0=ff[:, c, :], scalar=1.0, in1=d_[:],
                                                                         op0=ALU.subtract, op1=ALU.mult),
                             reads=[ff_tk[c], d_tk], writes=[k2w_tk[c]])
                        yield

                pre_done = {}
                o_emitted = {}
                blkst = {}

                def gen_Apre(blk, tts):
                    cs_t, cs_tk, cs_sem = cs_rot.next()
                    K.dma(SP, out=cs_t[:, 0, :], in_=rope_d[:, 0, blk * 128:(blk + 1) * 128], reads=[dtk("rope", 0)], writes=[cs_tk], sem=cs_sem)
                    K.dma(SP, out=cs_t[:, 1, :], in_=rope_d[:, 1, blk * 128:(blk + 1) * 128], reads=[dtk("rope", 0)], writes=[cs_tk], sem=cs_sem)
                    for tt in tts:
                        tile = blk * 4 + tt
                        cs = slice(tt * 128, (tt + 1) * 128)
                        slot = tile % RING
                        qkr, qkr_tk, _ = qkr_r.next()
                        qT, qT_tk, _ = qT_r.items[tt]
                        ra, ra_tk, _ = ra_r.next()
                        rb, rb_tk, _ = rb_r.next()
                        pb, ptk = yield from galloc(2)
                        fns = []
                        for k in range(8):
                            fns.append(lambda k=k: nc.tensor.matmul(ps.f32(pb), lhsT=h[:, k, cs], rhs=Wmi[:, k, 0:512],
                                                                    start=(k == 0), stop=(k == 7)))
                        for k in range(8):
                            fns.append(lambda k=k: nc.tensor.matmul(ps.f32(pb + 1)[:, 0:256], lhsT=h[:, k, cs], rhs=Wmi[:, k, 512:768],
                                                                    start=(k == 0), stop=(k == 7)))
                        K.group(PE, fns, reads=h_tks + [wmi_tk], writes=ptk)
                        yield
                        K.op(ACT, lambda: nc.scalar.activation(out=vbe[:, slot, :, 0:64],
                                                               in_=ps.f32(pb + 1)[:, 128:256].rearrange("p (g d) -> p g d", d=64), func=AF.Copy),
                             reads=ptk, writes=[vb_tk[slot]])
                        qk3 = ps.f32(pb, 2)[:, 0:640].rearrange("p (h d) -> p h d", d=64)
                        cb_ = cs_t[:, 0, tt * 32:(tt + 1) * 32].unsqueeze(1).broadcast_to([128, 10, 32])
                        sb_ = cs_t[:, 1, tt * 32:(tt + 1) * 32].unsqueeze(1).broadcast_to([128, 10, 32])
                        o3 = qkr[:].rearrange("p (h d) -> p h d", d=64)
                        K.op(DVE, lambda: nc.vector.tensor_tensor(out=ra[:], in0=qk3[:, :, 0:32], in1=cb_, op=ALU.mult), reads=ptk + [cs_tk], writes=[ra_tk])
                        K.op(DVE, lambda: nc.vector.tensor_tensor(out=rb[:], in0=qk3[:, :, 32:64], in1=sb_, op=ALU.mult), reads=ptk + [cs_tk], writes=[rb_tk])
                        K.op(DVE, lambda: nc.vector.tensor_tensor(out=o3[:, :, 0:32], in0=ra[:], in1=rb[:], op=ALU.subtract),
                             reads=[ra_tk, rb_tk], writes=[qkr_tk])
                        yield
                        K.op(DVE, lambda: nc.vector.tensor_tensor(out=ra[:], in0=qk3[:, :, 32:64], in1=cb_, op=ALU.mult), reads=ptk + [cs_tk], writes=[ra_tk])
                        K.op(DVE, lambda: nc.vector.tensor_tensor(out=rb[:], in0=qk3[:, :, 0:32], in1=sb_, op=ALU.mult), reads=ptk + [cs_tk], writes=[rb_tk])
                        K.op(DVE, lambda: nc.vector.tensor_tensor(out=o3[:, :, 32:64], in0=ra[:], in1=rb[:], op=ALU.add),
                             reads=[ra_tk, rb_tk], writes=[qkr_tk])
                        ps.free(pb, 2)
                        yield
                        tb2, ttk = yield from galloc(1)
                        tv = ps.bf(tb2)
                        fns = [(lambda j=j: nc.tensor.transpose(tv[:, j * 128:(j + 1) * 128], qkr[:, j * 128:(j + 1) * 128], ident[:]))
                               for j in range(5)]
                        K.group(PE, fns, reads=[qkr_tk, const_tk], writes=ttk)
                        qv = tv[:, 0:512].rearrange("p (j t) -> p j t", t=128)
                        qT4 = qT[:].rearrange("p (j two) t -> p j two t", two=2)
                        K.op(DVE, lambda: nc.vector.tensor_copy(out=qT4[0:64, :, 0, :], in_=qv[0:64]), reads=ttk, writes=[qT_tk])
                        K.op(DVE, lambda: nc.vector.tensor_copy(out=qT4[0:64, :, 1, :], in_=qv[64:128]), reads=ttk, writes=[qT_tk])
                        K.op(ACT, lambda: nc.scalar.activation(out=kTa[0:64, 0, slot * 128:(slot + 1) * 128], in_=tv[0:64, 512:640], func=AF.Copy),
                             reads=ttk, writes=[kT_tk[slot]])
                        K.op(ACT, lambda: nc.scalar.activation(out=kTa[0:64, 1, slot * 128:(slot + 1) * 128], in_=tv[64:128, 512:640],
                                                               func=AF.Copy), reads=ttk, writes=[kT_tk[slot]])
                        ps.free(tb2)
                        pre_done[tile] = True
                        yield

                def gen_Amain(blk, tt):
                    if True:
                        tile = blk * 4 + tt
                        cs = slice(tt * 128, (tt + 1) * 128)
                        slot = tile % RING
                        pslot = (tile - 1) % RING
                        tries = 0
                        while not pre_done.get(tile, False):
                            tries += 1
                            assert tries < 10000, "attention main chain never released"
                            yield
                        qT, qT_tk, _ = qT_r.items[tt]
                        PTs, PT_tk, _ = PT_r.next()
                        atm, atm_tk, _ = atm_r.next()
                        dn, dn_tk, _ = dn_r.next()
                        kbs = [(1, slot)] if tile == 0 else [(0, pslot), (1, slot)]
                        var = 0
                        for g in range(2):
                            sb2, stk = yield from galloc(2)
                            fns = []
                            for (kb, sl) in kbs:
                                o = ps.f32(sb2 + kb)
                                fns.append(lambda o=o, g=g, sl=sl: nc.tensor.matmul(
                                    o, lhsT=kTa[:, g, sl * 128:(sl + 1) * 128], rhs=qT[:, g * 4:(g + 1) * 4, :].rearrange("p h t -> p (h t)"),
                                    start=True, stop=False))
                                fns.append(lambda o=o, kb=kb: nc.tensor.matmul(o, lhsT=ident[:], rhs=amask[:, var, kb, :], start=False, stop=True))
                            K.group(PE, fns, reads=[qT_tk, const_tk, mc_tk] + kT_tk, writes=stk)
                            for (kb, sl) in kbs:
                                K.op(ACT, lambda: nc.scalar.activation(out=PTs[:, g, kb, :], in_=ps.f32(sb2 + kb), func=AF.Exp, scale=0.125,
                                                                       bias=negc[:, 0:1]),
                                     reads=[stk[kb], mc_tk], writes=[PT_tk])
                            ps.free(sb2, 2)
                            yield
                        ob2, otk = yield from galloc(2)
                        fns = []
                        for hh in range(8):
                            g, hq = hh // 4, hh % 4
                            for i, (kb, sl) in enumerate(kbs):
                                fns.append(lambda g=g, hq=hq, kb=kb, sl=sl, i=i: nc.tensor.matmul(
                                    ps.f32(ob2 + g)[:, hq * 65:(hq + 1) * 65], lhsT=PTs[:, g, kb, hq * 128:(hq + 1) * 128],
                                    rhs=vbe[:, sl, g, :], start=(i == 0), stop=(i == len(kbs) - 1)))
                        K.group(PE, fns, reads=[PT_tk] + vb_tk, writes=otk)
                        yield
                        O4 = ps.f32(ob2, 2).rearrange("p (b c) -> p b c", c=512)[:, :, 0:260].rearrange("p b (h e) -> p b h e", e=65)
                        K.op(DVE, lambda: nc.vector.tensor_tensor(out=dn[:, 0, :].rearrange("p (b h) -> p b h", h=4), in0=O4[:, :, :, 64],
                                                                  in1=esink[:].rearrange("p (b h) -> p b h", h=4), op=ALU.add),
                             reads=otk + [mc_tk], writes=[dn_tk])
                        K.op(DVE, lambda: nc.vector.reciprocal(out=dn[:, 1, :], in_=dn[:, 0, :]), reads=[dn_tk], writes=[dn_tk])
                        K.op(DVE, lambda: nc.vector.tensor_tensor(out=atm[:].rearrange("p (b h d) -> p b h d", h=4, d=64), in0=O4[:, :, :, 0:64],
                                                                  in1=dn[:, 1, :].rearrange("p (b h) -> p b h", h=4).unsqueeze(3).broadcast_to([128, 2, 4, 64]),
                                                                  op=ALU.mult),
                             reads=otk + [dn_tk], writes=[atm_tk])
                        ps.free(ob2, 2)
                        yield
                        tries = 0
                        while blk > 0 and not o_emitted.get((blk - 1, tt), False):
                            tries += 1
                            assert tries < 10000
                            yield
                        ab_, atk = yield from galloc(1)
                        av = ps.bf(ab_)
                        K.group(PE, [(lambda j=j: nc.tensor.transpose(av[:, j * 128:(j + 1) * 128], atm[:, j * 128:(j + 1) * 128], ident[:]))
                                     for j in range(4)], reads=[atm_tk, const_tk], writes=atk)
                        K.op(ACT, lambda: nc.scalar.activation(out=catT[:, 0:4, cs], in_=av[:, 0:512].rearrange("p (j t) -> p j t", t=128),
                                                               func=AF.Copy), reads=atk, writes=cat_tk[0:4])
                        ps.free(ab_)
                        yield

                def state_step(n, ubank, utk, sdst, sdst_tk):
                    K.op(DVE, lambda: nc.vector.tensor_tensor(out=st32[:], in0=st32[:], in1=dec[:, :, n:n + 1].broadcast_to([128, 4, 128]),
                                                              op=ALU.mult), reads=st32_tk + hb_tk, writes=st32_tk)
                    K.op(DVE, lambda: nc.vector.tensor_tensor(out=st32[:], in0=st32[:], in1=ps.f32(ubank).rearrange("p (c e) -> p c e", e=128),
                                                              op=ALU.add), reads=st32_tk + utk, writes=st32_tk)
                    K.op(ACT, lambda: nc.scalar.activation(out=sdst[:], in_=st32[:], func=AF.Copy), reads=st32_tk, writes=[sdst_tk])

                k2_r = Rot(K, pe_, "k2AB", [128, 2, 4, 128], BF16, 2)
                scm_r = Rot(K, pe_, "scm2", [128, 4, 128], BF16, 2)

                def gen_H(blk):
                    pre = {}
                    post = {}

                    def h_pre(tt):
                        cs = slice(tt * 128, (tt + 1) * 128)
                        vt = vtm[:, tt, :]
                        vt_tk = vtm_tk[tt]
                        k2, k2_tk, _ = k2_r.next()
                        sc_, sc_tk, _ = scm_r.next()
                        kb_, ktk = yield from galloc(1)
                        kv = ps.bf(kb_)
                        K.group(PE, [(lambda c=c: nc.tensor.transpose(kv[:, c * 128:(c + 1) * 128], k2T[:, c, cs], ident[:])) for c in range(4)],
                                reads=hb_tk + k2w_tk + [const_tk], writes=ktk)
                        kv3 = kv[:, 0:512].rearrange("p (c d) -> p c d", d=128)
                        K.op(DVE, lambda: nc.vector.tensor_scalar(out=k2[:, 0], in0=kv3, scalar1=rowm[:, 0:1], scalar2=None, op0=ALU.mult),
                             reads=ktk + [const_tk], writes=[k2_tk])
                        K.op(DVE, lambda: nc.vector.tensor_scalar(out=k2[:, 1], in0=kv3, scalar1=rowm[:, 1:2], scalar2=None, op0=ALU.mult),
                             reads=ktk + [const_tk], writes=[k2_tk])
                        ps.free(kb_)
                        sb_2, sctk = yield from galloc(1)
                        scv = ps.f32(sb_2)
                        K.group(PE, [(lambda c=c: nc.tensor.matmul(scv[:, c * 128:(c + 1) * 128], lhsT=kt[:, c, cs], rhs=qt[:, c, cs],
                                                                 start=True, stop=True)) for c in range(4)],
                                reads=hb_tk + ktw_tk, writes=sctk)
                        yield
                        K.op(DVE, lambda: nc.vector.tensor_tensor(out=sc_[:], in0=scv.rearrange("p (c t) -> p c t", t=128), in1=hmask[:], op=ALU.mult),
                             reads=sctk + [mc_tk], writes=[sc_tk])
                        ps.free(sb_2)
                        ua, uatk = yield from galloc(1)
                        ub, ubtk = yield from galloc(1)
                        K.group(PE, [(lambda c=c: nc.tensor.matmul(ps.f32(ua)[:, c * 128:(c + 1) * 128], lhsT=k2[:, 0, c, :],
                                                                 rhs=vt[:, c * 128:(c + 1) * 128], start=True, stop=True)) for c in range(4)],
                                reads=[k2_tk, vt_tk], writes=uatk)
                        K.group(PE, [(lambda c=c: nc.tensor.matmul(ps.f32(ub)[:, c * 128:(c + 1) * 128], lhsT=k2[:, 1, c, :],
                                                                 rhs=vt[:, c * 128:(c + 1) * 128], start=True, stop=True)) for c in range(4)],
                                reads=[k2_tk, vt_tk], writes=ubtk)
                        pre[tt] = (sc_, sc_tk, ua, uatk, ub, ubtk)
                        yield

                    def h_chain(tt):
                        vt = vtm[:, tt, :]
                        vt_tk = vtm_tk[tt]
                        sc_, sc_tk, ua, uatk, ub, ubtk = pre[tt]
                        oo, ootk = yield from galloc(1)
                        oov = ps.f32(oo)
                        stA, stA_tk = state["stb"]
                        fns = []
                        for c in range(4):
                            fns.append(lambda c=c: nc.tensor.matmul(oov[:, c * 128:(c + 1) * 128], lhsT=vt[:, c * 128:(c + 1) * 128],
                                                                    rhs=sc_[:, c, :], start=(c == 0), stop=False, skip_group_check=True))
                        for c in range(4):
                            fns.append(lambda c=c: nc.tensor.matmul(oov[:, c * 128:c * 128 + 64], lhsT=stA[:, c, :],
                                                                    rhs=qt[:, c, tt * 128:tt * 128 + 64], start=False, stop=False,
                                                                    skip_group_check=True))
                        K.group(PE, fns, reads=[vt_tk, sc_tk, stA_tk] + hb_tk, writes=ootk)
                        stB, stB_tk, _ = stb_rot.next()
                        state_step(2 * tt, ua, uatk, stB, stB_tk)
                        ps.free(ua)
                        yield
                        K.group(PE, [(lambda c=c: nc.tensor.matmul(oov[:, c * 128 + 64:(c + 1) * 128], lhsT=stB[:, c, :],
                                                                 rhs=qt[:, c, tt * 128 + 64:(tt + 1) * 128], start=False, stop=True,
                                                                 skip_group_check=True)) for c in range(4)],
                                reads=[stB_tk] + hb_tk, writes=ootk)
                        stC, stC_tk, _ = stb_rot.next()
                        state_step(2 * tt + 1, ub, ubtk, stC, stC_tk)
                        ps.free(ub)
                        state["stb"] = (stC, stC_tk)
                        post[tt] = (oo, ootk)
                        yield

                    def h_post(tt):
                        cs = slice(tt * 128, (tt + 1) * 128)
                        oo, ootk = post[tt]
                        oov = ps.f32(oo)
                        K.op(ACT, lambda: nc.scalar.activation(out=sqr[:], in_=oov, func=AF.Square), reads=ootk, writes=[sqr_tk])
                        nb_, ntk = yield from galloc(1)
                        K.group(PE, [lambda: nc.tensor.matmul(ps.f32(nb_), lhsT=onesb[:], rhs=sqr[:], start=True, stop=True)],
                                reads=[sqr_tk, const_tk], writes=ntk)
                        yield
                        K.op(ACT, lambda: nc.scalar.activation(out=rsb[:], in_=ps.f32(nb_), func=AF.Ln, scale=1.0 / 128, bias=eps_t[:, 0:1]),
                             reads=ntk + [const2_tk], writes=[rsb_tk])
                        ps.free(nb_)
                        K.op(ACT, lambda: nc.scalar.activation(out=rsb[:], in_=rsb[:], func=AF.Exp, scale=-0.5), reads=[rsb_tk], writes=[rsb_tk])
                        K.op(DVE, lambda: nc.vector.scalar_tensor_tensor(out=t1[:], in0=oov, scalar=gnh[:, 0:1], in1=rsb[:], op0=ALU.mult, op1=ALU.mult),
                             reads=ootk + [rsb_tk, par_tk], writes=[t1_tk])
                        ps.free(oo)
                        K.op(DVE, lambda: nc.vector.tensor_tensor(out=catT[:, 4:8, cs], in0=t1[:].rearrange("p (c t) -> p c t", t=128),
                                                                  in1=gs[:, :, cs], op=ALU.mult),
                             reads=[t1_tk] + hb_tk, writes=cat_tk[4:8])
                        yield

                    yield from h_pre(0)
                    for tt in range(4):
                        yield from h_chain(tt)
                        if tt + 1 < 4:
                            yield from h_pre(tt + 1)
                        yield from h_post(tt)

                def gen_O(blk):
                    for tt in range(4):
                        yb, ytk = yield from galloc(2)
                        fns = []
                        for n in range(2):
                            o = ps.f32(yb + n)
                            for kc in range(8):
                                fns.append(lambda o=o, n=n, kc=kc: nc.tensor.matmul(
                                    o, lhsT=catT[:, kc, tt * 128:(tt + 1) * 128], rhs=Wmo[:, kc, n * 512:(n + 1) * 512],
                                    start=(kc == 0), stop=(kc == 7)))
                        K.group(PE, fns, reads=cat_tk + [wmo_tk], writes=ytk)
                        o_emitted[(blk, tt)] = True
                        yield
                        epi.run(blk * 4 + tt, yb, ytk)
                        yield

                interleave(gen_prologue(0))
                for blk in range(NB):
                    ptanh = {"done": False}

                    def gen_Pall(blk=blk, ptanh=ptanh):
                        yield from gen_P(blk)
                        ptanh["done"] = True
                        g1, g2 = gen_Pln([0, 2]), gen_Pln([1, 3])
                        live = [g1, g2]
                        while live:
                            for g in list(live):
                                try:
                                    next(g)
                                except StopIteration:
                                    live.remove(g)
                            yield

                    def gen_Olate(blk=blk, ptanh=ptanh):
                        tries = 0
                        while not ptanh["done"]:
                            tries += 1
                            assert tries < 10000
                            yield
                        yield from gen_O(blk - 1)

                    gens = [gen_Pall(), gen_Apre(blk, [0, 1, 2, 3]), gen_Amain(blk, 0), gen_Amain(blk, 1)]
                    if blk > 0:
                        gens.append(gen_Olate())
                    interleave(*gens)
                    gens = [gen_H(blk), gen_Amain(blk, 2), gen_Amain(blk, 3)]
                    if blk + 1 < NB:
                        gens.append(gen_prologue(blk + 1))
                    interleave(*gens)
                interleave(gen_O(NB - 1))
                K.barrier()

        dsts = {1: out_d if stop_after == 1 else x1_d, 2: out_d if stop_after == 2 else x2_d, 3: out_d}
        names = {1: "out" if stop_after == 1 else "x1", 2: "out" if stop_after == 2 else "x2", 3: "out"}
        if stop_after == 0:
            dbg = K.sb("dbg", [128, D], F32)
            dbg_tk = Tk("dbg")
            dsem = K.newsem("dbg")
            K.op(DVE, lambda: nc.vector.tensor_copy(out=dbg[:], in_=G[:, 0, :]), reads=[par_tk], writes=[dbg_tk])
            K.dma(SP, out=out_d[0:128, :], in_=dbg[:], reads=[dbg_tk], writes=[], sem=dsem)
            K.op(DVE, lambda: nc.vector.tensor_copy(out=dbg[:, 0:24], in_=a_col[:].rearrange("p a b -> p (a b)")), reads=[par_tk, dbg_tk], writes=[dbg_tk])
            K.op(DVE, lambda: nc.vector.tensor_copy(out=dbg[:, 24:48], in_=sh_col[:].rearrange("p a b -> p (a b)")), reads=[par_tk, dbg_tk], writes=[dbg_tk])
            K.op(DVE, lambda: nc.vector.tensor_copy(out=dbg[:, 48:52], in_=lb[:]), reads=[par_tk, dbg_tk], writes=[dbg_tk])
            K.dma(SP, out=out_d[128:256, :], in_=dbg[:], reads=[dbg_tk], writes=[], sem=dsem)
            K.barrier()
            return nc
        ffn_phase(0, 0, x_d, "x", dsts[1], names[1])
        if stop_after >= 2:
            mixer_phase(x1_d, "x1", dsts[2], names[2])
        if stop_after >= 3:
            ffn_phase(1, 2, x2_d, "x2", out_d, "out")
        K.barrier()
        print("instructions:", K.n_inst, "sems:", len(K.sems))
    return nc


def _consts():
    bf = ml_dtypes.bfloat16
    ident = np.eye(128, dtype=np.float32).astype(bf)
    onesb = np.ones((128, 128), np.float32).astype(bf)
    NEG = -1e30
    am = np.zeros((128, 2, 256), np.float32)
    am[0:64, 0, 192:256] = NEG
    am[64:128, 0, 0:64] = NEG
    am[0:64, 1, 64:128] = NEG
    am[64:128, 1, 128:192] = NEG
    s = np.arange(128)[:, None]
    t = np.arange(128)[None, :]
    hm = ((s // 64 == t // 64) & (s <= t)).astype(np.float32)
    hmask = np.broadcast_to(hm[:, None, :], (128, 4, 128)).copy()
    rmask = np.ones((128, 512), np.float32)
    rmask[:, ::64] = 0.0
    inv_freq = (1.0 / (np.float32(10000.0) ** (np.arange(0, 64, 2, dtype=np.float32) / np.float32(64)))).astype(np.float32)
    invf = np.broadcast_to(inv_freq[None, :], (128, 32)).copy()
    rowm = np.zeros((128, 2), np.float32)
    rowm[0:64, 0] = 1.0
    rowm[64:128, 1] = 1.0
    amT = np.zeros((128, 2, 2, 4, 128), np.float32)
    for v in range(2):
        for kb in range(2):
            amT[:, v, kb, :, :] = am[:, v, kb * 128:(kb + 1) * 128].T[:, None, :]
    amT = amT.reshape(128, 2, 2, 512)
    return dict(ident=ident, onesb=onesb, amask=amT.astype(bf), hmask=hmask.astype(bf), rmask=rmask, invf=invf, rowm=rowm)


def make_in_maps(x, c, positions, w_cond, b_cond, norm_pre, norm_post, ffn_w_in, ffn_w_out,
                 w_mix_in, w_mix_out, attn_sinks, hgrn_lb_logits, hgrn_gnorm):
    f = np.float32
    cst = _consts()
    shared = dict(
        w_cond=np.ascontiguousarray(w_cond[0], f), b_cond=np.ascontiguousarray(np.broadcast_to(np.asarray(b_cond[0], f)[None, :], (128, 9 * D))),
        npre=np.ascontiguousarray(np.asarray(norm_pre[0], f).reshape(3, 8, 128).transpose(2, 0, 1)),
        npost=np.ascontiguousarray(np.broadcast_to(np.asarray(norm_post[0], f)[None], (128, 3, D))),
        w_in=np.ascontiguousarray(ffn_w_in[0], f), w_out=np.ascontiguousarray(ffn_w_out[0], f),
        w_mi=np.ascontiguousarray(w_mix_in[0], f), w_mo=np.ascontiguousarray(w_mix_out[0], f),
        sinks=np.ascontiguousarray(np.broadcast_to(np.asarray(attn_sinks[0], f)[None], (128, 8))),
        lbl=np.ascontiguousarray(np.asarray(hgrn_lb_logits, f).reshape(2, 4, 128).transpose(2, 0, 1)),
        gn=np.ascontiguousarray(np.asarray(hgrn_gnorm[0], f)[:, None]),
        **cst)
    maps = []
    for b in range(8):
        m = dict(shared)
        m["x"] = np.ascontiguousarray(x[b], f)
        m["ccol"] = np.ascontiguousarray(np.asarray(c[b], f).reshape(8, 128).T)
        m["pos"] = np.ascontiguousarray(np.asarray(positions[b], np.int32).reshape(NT, 128).T)
        maps.append(m)
    return maps


def kernel(**inputs):
    nc = build(3)
    maps = make_in_maps(**inputs)
    res = run_bass_kernel_spmd(nc, maps, core_ids=list(range(8)))
    return np.stack([np.asarray(r["out"], np.float32) for r in res.results], axis=0)
```

```python
import numpy as np
from contextlib import ExitStack
import ml_dtypes
import concourse.bass as bass
import concourse.mybir as mybir
from concourse.bass_utils import run_bass_kernel_spmd

F32 = mybir.dt.float32
BF16 = mybir.dt.bfloat16
I32 = mybir.dt.int32
AF = mybir.ActivationFunctionType
ALU = mybir.AluOpType
AX = mybir.AxisListType

S = 4096
D = 1024
DFF = 2816
NF = DFF // 128
TB = 512
NB = S // TB
NT = S // 128
EPS = 1e-6
PI = float(np.pi)
TWO_PI = float(2 * np.pi)


class Tk:
    __slots__ = ("w", "r", "name", "excl")

    def __init__(self, name="", excl=False):
        self.w = None
        self.r = {}
        self.name = name
        self.excl = excl


class Sem:
    def __init__(self, h, name):
        self.h = h
        self.cnt = 0
        self.name = name


class Eng:
    def __init__(self, name, h, sem):
        self.name = name
        self.h = h
        self.sem = sem
        self.seen = {}


class Kern:
    def __init__(self, nc, es):
        self.nc = nc
        self.es = es
        self.sems = []
        self.PE = Eng("pe", nc.tensor, self.newsem("pe"))
        self.ACT = Eng("act", nc.scalar, self.newsem("act"))
        self.DVE = Eng("dve", nc.vector, self.newsem("dve"))
        self.POOL = Eng("pool", nc.gpsimd, self.newsem("pool"))
        self.SP = Eng("sp", nc.sync, self.newsem("sp"))
        self.engs = [self.PE, self.ACT, self.DVE, self.POOL, self.SP]
        self.n_inst = 0

    def newsem(self, name):
        self.sid = getattr(self, "sid", 0) + 1
        s = Sem(self.es.enter_context(self.nc.semaphore("m%d_%s" % (self.sid, name))), name)
        self.sems.append(s)
        return s

    def sb(self, name, shape, dt, es=None):
        self.uid = getattr(self, "uid", 0) + 1
        return (es or self.es).enter_context(self.nc.sbuf_tensor("s%d_%s" % (self.uid, name), shape, dt))

    def _wait(self, E, toks):
        for (s, v) in toks:
            if E.seen.get(id(s), 0) >= v:
                continue
            E.h.wait_ge(s.h, v)
            E.seen[id(s)] = v

    def _deps(self, E, reads, writes):
        toks = []
        for t in reads:
            if t.w is not None:
                toks.append(t.w)
            if t.excl:
                for tok in t.r.values():
                    if tok[0] is not E.sem:
                        toks.append(tok)
        for t in writes:
            if t.w is not None and t.w[0] is not E.sem:
                toks.append(t.w)
            for tok in t.r.values():
                if tok[0] is not E.sem or E is not self.PE:
                    toks.append(tok)
        return toks

    def _mark(self, tok, reads, writes):
        for t in writes:
            t.w = tok
            t.r = {}
        for t in reads:
            t.r[id(tok[0])] = tok

    def op(self, E, fn, reads=(), writes=()):
        self._wait(E, self._deps(E, reads, writes))
        ins = fn()
        ins.then_inc(E.sem.h, 1)
        E.sem.cnt += 1
        self._mark((E.sem, E.sem.cnt), reads, writes)
        self.n_inst += 1
        return ins

    def group(self, E, fns, reads=(), writes=()):
        self._wait(E, self._deps(E, reads, writes))
        ins = None
        for fn in fns:
            ins = fn()
            self.n_inst += 1
        ins.then_inc(E.sem.h, 1)
        E.sem.cnt += 1
        self._mark((E.sem, E.sem.cnt), reads, writes)

    def dma(self, E, out, in_, reads, writes, sem, **kw):
        self._wait(E, self._deps(E, reads, writes))
        E.h.dma_start(out=out, in_=in_, **kw).then_inc(sem.h, 16)
        sem.cnt += 16
        self._mark((sem, sem.cnt), reads, writes)
        self.n_inst += 1

    def barrier(self):
        toks = [(s, s.cnt) for s in self.sems if s.cnt > 0]
        for E in self.engs:
            self._wait(E, [t for t in toks if t[0] is not E.sem])


class Psum:
    def __init__(self, K):
        nc = K.nc
        self.t = K.es.enter_context(nc.psum_tensor("psum_all", [128, 8 * 512], F32))
        self.tb = self.t.bitcast(BF16)
        self.tk = [Tk("bank%d" % i, excl=True) for i in range(8)]
        self.p = 0
        self.held = [False] * 8

    def try_alloc(self, n=1):
        for i in range(8):
            b = (self.p + i) % 8
            if b % n or b + n > 8:
                continue
            if any(self.held[b:b + n]):
                continue
            for j in range(b, b + n):
                self.held[j] = True
            self.p = (b + n) % 8
            return b, self.tk[b:b + n]
        return None

    def alloc(self, n=1):
        r = self.try_alloc(n)
        assert r is not None, "PSUM exhausted (straight-line code must free before allocating)"
        return r

    def free(self, b, n=1):
        for j in range(b, b + n):
            assert self.held[j]
            self.held[j] = False

    def f32(self, b, n=1):
        return self.t[:, b * 512:(b + n) * 512]

    def bf(self, b, n=1):
        return self.tb[:, b * 1024:(b + n) * 1024]


class Rot:
    def __init__(self, K, es, name, shape, dt, n, dma=False):
        self.items = []
        for i in range(n):
            t = K.sb("%s%d" % (name, i), shape, dt, es)
            self.items.append((t, Tk("%s%d" % (name, i)), K.newsem("%s%d" % (name, i)) if dma else None))
        self.i = 0

    def next(self):
        it = self.items[self.i % len(self.items)]
        self.i += 1
        return it


def build(stop_after=3):
    nc = bass.Bass("TRN2", target_bir_lowering=False)

    def din(name, shape, dt=F32):
        return nc.dram_tensor(name, shape, dt, kind="ExternalInput").ap()

    x_d = din("x", [S, D])
    ccol_d = din("ccol", [128, 8])
    pos_d = din("pos", [128, NT], I32)
    wcond_d = din("w_cond", [D, 9 * D])
    bcond_d = din("b_cond", [128, 9 * D])
    npre_d = din("npre", [128, 3, 8])
    npost_d = din("npost", [128, 3, D])
    win_d = din("w_in", [2, D, 2 * DFF])
    wout_d = din("w_out", [2, DFF, D])
    wmi_d = din("w_mi", [D, 2816])
    wmo_d = din("w_mo", [D, D])
    sinks_d = din("sinks", [128, 8])
    lbl_d = din("lbl", [128, 2, 4])
    gn_d = din("gn", [128, 1])
    ident_d = din("ident", [128, 128], BF16)
    onesb_d = din("onesb", [128, 128], BF16)
    amask_d = din("amask", [128, 2, 2, 512], BF16)
    hmask_d = din("hmask", [128, 4, 128], BF16)
    rmask_d = din("rmask", [128, 512])
    invf_d = din("invf", [128, 32])
    rowm_d = din("rowm", [128, 2])
    out_d = nc.dram_tensor("out", [S, D], F32, kind="ExternalOutput").ap()
    x1_d = nc.dram_tensor("x1s", [S, D], F32).ap()
    x2_d = nc.dram_tensor("x2s", [S, D], F32).ap()
    rope_d = nc.dram_tensor("rope_s", [128, 2, NT * 32], F32).ap()
    w2in_s = nc.dram_tensor("w2in_s", [D, 2 * DFF], BF16).ap()
    w2out_s = nc.dram_tensor("w2out_s", [DFF, D], BF16).ap()
    wmi_s = nc.dram_tensor("wmi_s", [D, 2816], BF16).ap()
    wmo_s = nc.dram_tensor("wmo_s", [D, D], BF16).ap()

    with ExitStack() as es:
        K = Kern(nc, es)
        PE, ACT, DVE, POOL, SP = K.PE, K.ACT, K.DVE, K.POOL, K.SP
        ps = Psum(K)
        dram_tk = {}

        def dtk(name, tile):
            key = (name, tile)
            if key not in dram_tk:
                dram_tk[key] = Tk("%s_%d" % key)
            return dram_tk[key]

        ident = K.sb("ident", [128, 128], BF16)
        onesb = K.sb("onesb", [128, 128], BF16)
        rowm = K.sb("rowm", [128, 2], F32)
        sinks = K.sb("sinks", [128, 8], F32)
        gn = K.sb("gn", [128, 1], F32)
        lb = K.sb("lb", [128, 4], F32)
        oml = K.sb("oml", [128, 4], F32)
        a_col = K.sb("a_col", [128, 3, 8], F32)
        sh_col = K.sb("sh_col", [128, 3, 8], F32)
        G = K.sb("G", [128, 3, D], BF16)
        c01 = K.sb("c01", [128, 2, 4], F32)
        gnh = K.sb("gnh", [128, 1], F32)
        const_tk = Tk("consts")
        par_tk = Tk("params")
        rope_tk = Tk("rope")

        with ExitStack() as ss:
            ld = K.newsem("setup_ld")
            ccol = K.sb("ccol", [128, 8], F32, ss)
            ca = K.sb("ca", [128, 8], F32, ss)
            posi = K.sb("posi", [128, NT], I32, ss)
            npre = K.sb("npre", [128, 3, 8], F32, ss)
            npost = K.sb("npost", [128, 3, D], F32, ss)
            lbl = K.sb("lbl", [128, 2, 4], F32, ss)
            invf = K.sb("invf", [128, 32], F32, ss)
            setup_tk = Tk("setup_in")
            cosT = K.sb("cosT", [128, NT, 32], F32, ss)
            sinT = K.sb("sinT", [128, NT, 32], F32, ss)
            loads = [(ident, ident_d), (onesb, onesb_d),
                     (rowm, rowm_d), (sinks, sinks_d), (gn, gn_d), (ccol, ccol_d), (posi, pos_d), (npre, npre_d),
                     (npost, npost_d), (lbl, lbl_d), (invf, invf_d)]
            for (t, d_) in loads:
                nc.sync.dma_start(out=t[:], in_=d_).then_inc(ld.h, 16)
                ld.cnt += 16
            setup_tk.w = (ld, ld.cnt)
            const_tk.w = (ld, ld.cnt)

            K.op(ACT, lambda: nc.scalar.activation(out=ca[:], in_=ccol[:], func=AF.Silu), reads=[setup_tk], writes=[par_tk])
            K.op(DVE, lambda: nc.vector.tensor_tensor(out=lb[:], in0=lbl[:, 0, :], in1=lbl[:, 1, :], op=ALU.subtract),
                 reads=[setup_tk], writes=[par_tk])
            K.op(ACT, lambda: nc.scalar.activation(out=lb[:], in_=lb[:], func=AF.Sigmoid), reads=[par_tk], writes=[par_tk])
            K.op(DVE, lambda: nc.vector.tensor_scalar(out=oml[:], in0=lb[:], scalar1=-1.0, scalar2=1.0, op0=ALU.mult, op1=ALU.add),
                 reads=[par_tk], writes=[par_tk])
            K.op(DVE, lambda: nc.vector.tensor_scalar(out=c01[:, 1, :], in0=oml[:], scalar1=0.5, scalar2=None, op0=ALU.mult),
                 reads=[par_tk], writes=[par_tk])
            K.op(DVE, lambda: nc.vector.tensor_tensor(out=c01[:, 0, :], in0=lb[:], in1=c01[:, 1, :], op=ALU.add),
                 reads=[par_tk], writes=[par_tk])
            K.op(DVE, lambda: nc.vector.tensor_scalar(out=gnh[:], in0=gn[:], scalar1=0.5, scalar2=None, op0=ALU.mult),
                 reads=[setup_tk, par_tk], writes=[par_tk])

            posf = K.sb("posf", [128, NT], F32, ss)
            ang = K.sb("ang", [128, NT, 32], F32, ss)
            rr = K.sb("rr", [128, NT, 32], F32, ss)
            rc = K.sb("rc", [128, NT, 32], F32, ss)
            kf = K.sb("kf", [128, NT, 32], F32, ss)
            ki = K.sb("ki", [128, NT, 32], I32, ss)
            mk = K.sb("mk", [128, NT, 32], F32, ss)
            rt = Tk("ropetmp")

            def V(fn, reads=(), writes=()):
                K.op(DVE, fn, reads=list(reads) + [rt, setup_tk], writes=list(writes) + [rt])

            V(lambda: nc.vector.tensor_copy(out=posf[:], in_=posi[:]))
            V(lambda: nc.vector.tensor_tensor(out=ang[:], in0=posf[:].unsqueeze(2).broadcast_to([128, NT, 32]),
                                              in1=invf[:].unsqueeze(1).broadcast_to([128, NT, 32]), op=ALU.mult))
            V(lambda: nc.vector.tensor_scalar(out=kf[:], in0=ang[:], scalar1=1.0 / TWO_PI, scalar2=None, op0=ALU.mult))
            V(lambda: nc.vector.tensor_copy(out=ki[:], in_=kf[:]))
            V(lambda: nc.vector.tensor_copy(out=kf[:], in_=ki[:]))
            C1 = 6.28125
            C2 = TWO_PI - 6.28125
            V(lambda: nc.vector.scalar_tensor_tensor(out=rr[:], in0=kf[:], scalar=-C1, in1=ang[:], op0=ALU.mult, op1=ALU.add))
            V(lambda: nc.vector.scalar_tensor_tensor(out=rr[:], in0=kf[:], scalar=-C2, in1=rr[:], op0=ALU.mult, op1=ALU.add))

            def fold(t):
                V(lambda: nc.vector.tensor_scalar(out=mk[:], in0=t[:], scalar1=PI, scalar2=-TWO_PI, op0=ALU.is_gt, op1=ALU.mult))
                V(lambda: nc.vector.tensor_tensor(out=t[:], in0=t[:], in1=mk[:], op=ALU.add))
                V(lambda: nc.vector.tensor_scalar(out=mk[:], in0=t[:], scalar1=-PI, scalar2=TWO_PI, op0=ALU.is_lt, op1=ALU.mult))
                V(lambda: nc.vector.tensor_tensor(out=t[:], in0=t[:], in1=mk[:], op=ALU.add))
                V(lambda: nc.vector.tensor_scalar(out=t[:], in0=t[:], scalar1=PI, scalar2=-PI, op0=ALU.min, op1=ALU.max))

            fold(rr)
            V(lambda: nc.vector.tensor_scalar(out=rc[:], in0=rr[:], scalar1=PI / 2, scalar2=None, op0=ALU.add))
            fold(rc)
            K.op(ACT, lambda: nc.scalar.activation(out=sinT[:], in_=rr[:], func=AF.Sin), reads=[rt], writes=[rope_tk])
            K.op(ACT, lambda: nc.scalar.activation(out=cosT[:], in_=rc[:], func=AF.Sin), reads=[rt], writes=[rope_tk])
            cab = K.sb("cab", [128, 8, 128], F32, ss)
            identf = K.sb("identf", [128, 128], F32, ss)
            modbc = K.sb("modbc", [128, 9 * D], F32, ss)
            dg = K.sb("dg", [128, 8, 128], F32, ss)
            K.op(DVE, lambda: nc.vector.tensor_copy(out=cab[:], in_=ca[:].unsqueeze(2).broadcast_to([128, 8, 128])),
                 reads=[par_tk], writes=[par_tk])
            K.op(DVE, lambda: nc.vector.tensor_copy(out=identf[:], in_=ident[:]), reads=[setup_tk], writes=[setup_tk])
            wc_rot = Rot(K, ss, "wc", [128, 8, 512], F32, 2, dma=True)
            bc_rot = Rot(K, ss, "bc", [128, 512], F32, 2, dma=True)
            wc_view = wcond_d.rearrange("(k p) n -> p k n", p=128)
            mod_tk = Tk("modbc")
            for cb in range(18):
                wc, wc_tk, wc_sem = wc_rot.next()
                bc, bc_tk, bc_sem = bc_rot.next()
                K.dma(SP, out=wc[:], in_=wc_view[:, :, cb * 512:(cb + 1) * 512], reads=[], writes=[wc_tk], sem=wc_sem)
                K.dma(SP, out=bc[:], in_=bcond_d[:, cb * 512:(cb + 1) * 512], reads=[], writes=[bc_tk], sem=bc_sem)
                b, btk = ps.alloc(1)
                pv = ps.f32(b)
                K.group(PE, [(lambda k=k: nc.tensor.matmul(pv, lhsT=cab[:, k, :], rhs=wc[:, k, :],
                                                         start=(k == 0), stop=(k == 7))) for k in range(8)],
                        reads=[par_tk, wc_tk], writes=btk)
                K.op(DVE, lambda: nc.vector.tensor_tensor(out=modbc[:, cb * 512:(cb + 1) * 512], in0=pv, in1=bc[:], op=ALU.add),
                     reads=btk + [bc_tk, mod_tk], writes=[mod_tk])
                ps.free(b)
            for k in range(3):
                for which, v in (("sh", 3 * k), ("sc", 3 * k + 1)):
                    K.op(DVE, lambda: nc.vector.tensor_tensor(out=dg[:], in0=modbc[:, v * D:(v + 1) * D].rearrange("p (j q) -> p j q", q=128),
                                                              in1=identf[:].unsqueeze(1).broadcast_to([128, 8, 128]), op=ALU.mult),
                         reads=[mod_tk, setup_tk, par_tk], writes=[par_tk])
                    dstc = sh_col if which == "sh" else a_col
                    K.op(DVE, lambda: nc.vector.tensor_reduce(out=dstc[:, k, :], in_=dg[:], axis=AX.X, op=ALU.add),
                         reads=[par_tk], writes=[par_tk])
                K.op(DVE, lambda: nc.vector.scalar_tensor_tensor(out=a_col[:, k, :], in0=a_col[:, k, :], scalar=1.0, in1=npre[:, k, :],
                                                                 op0=ALU.add, op1=ALU.mult),
                     reads=[par_tk, setup_tk], writes=[par_tk])
                resw = 1.0 if k == 1 else 0.5
                K.op(DVE, lambda: nc.vector.scalar_tensor_tensor(out=G[:, k, :], in0=modbc[:, (3 * k + 2) * D:(3 * k + 3) * D], scalar=resw,
                                                                 in1=npost[:, k, :], op0=ALU.mult, op1=ALU.mult),
                     reads=[mod_tk, setup_tk, par_tk], writes=[par_tk])

            rsem = K.newsem("ropest")
            K.dma(SP, out=rope_d[:, 0, :], in_=cosT[:].rearrange("p t f -> p (t f)"), reads=[rope_tk], writes=[dtk("rope", 0)], sem=rsem)
            K.dma(SP, out=rope_d[:, 1, :], in_=sinT[:].rearrange("p t f -> p (t f)"), reads=[rope_tk], writes=[dtk("rope", 0)], sem=rsem)
            K.barrier()

        def issue_cast_dma(dst, src, tk, sem):
            K.dma(POOL, out=dst, in_=src, reads=[], writes=[tk], sem=sem)

        eps_t = K.sb("eps_t", [128, 2], F32)
        const2_tk = Tk("const2")
        K.op(DVE, lambda: nc.vector.memset(eps_t[:, 0:1], EPS), writes=[const2_tk])
        K.op(DVE, lambda: nc.vector.memset(eps_t[:, 1:2], 0.0), writes=[const2_tk])

        def rstd_ops(st, st_tk, n, lnexp):
            if lnexp:
                K.op(ACT, lambda: nc.scalar.activation(out=st[:, 1:2], in_=st[:, 0:1], func=AF.Ln, scale=1.0 / n, bias=eps_t[:, 0:1]),
                     reads=[st_tk, const2_tk], writes=[st_tk])
                K.op(ACT, lambda: nc.scalar.activation(out=st[:, 2:3], in_=st[:, 1:2], func=AF.Exp, scale=-0.5),
                     reads=[st_tk], writes=[st_tk])
            else:
                K.op(ACT, lambda: nc.scalar.activation(out=st[:, 1:2], in_=st[:, 0:1], func=AF.Sqrt, scale=1.0 / n, bias=eps_t[:, 0:1]),
                     reads=[st_tk, const2_tk], writes=[st_tk])
                K.op(DVE, lambda: nc.vector.reciprocal(out=st[:, 2:3], in_=st[:, 1:2]), reads=[st_tk], writes=[st_tk])

        class Pro:
            def __init__(self, es_, src, srcname, kidx, lnexp, tag, nxin=2, evac_act=False):
                self.src, self.srcname, self.kidx, self.lnexp = src, srcname, kidx, lnexp
                self.evac_act = evac_act
                self.xin_rot = Rot(K, es_, tag + "xin", [128, D], F32, nxin, dma=True)
                self.xn_rot = Rot(K, es_, tag + "xn", [128, D], BF16, 2)
                self.st_rot = Rot(K, es_, tag + "st", [128, 4], F32, 4)
                self.tmpf = K.sb(tag + "tmpf", [128, 8, 128], F32, es_)
                self.tmpf_tk = Tk(tag + "tmpf")

            def elem(self, tile):
                xin, xin_tk, xin_sem = self.xin_rot.next()
                K.dma(SP, out=xin[:], in_=self.src[tile * 128:(tile + 1) * 128, :], reads=[dtk(self.srcname, tile)],
                      writes=[xin_tk], sem=xin_sem)
                st, st_tk, _ = self.st_rot.next()
                xn, xn_tk, _ = self.xn_rot.next()
                K.op(ACT, lambda: nc.scalar.activation(out=xn[:], in_=xin[:], func=AF.Square, accum_out=st[:, 0:1]),
                     reads=[xin_tk], writes=[xn_tk, st_tk])
                rstd_ops(st, st_tk, D, self.lnexp)
                K.op(DVE, lambda: nc.vector.tensor_scalar(out=xn[:], in0=xin[:], scalar1=st[:, 2:3], scalar2=None, op0=ALU.mult),
                     reads=[xin_tk, st_tk], writes=[xn_tk])
                return xn, xn_tk

            def pe(self, xn, xn_tk, h, h_tks, tt):
                b, tks = ps.alloc(1)
                tv = ps.bf(b)
                K.group(PE, [(lambda dc=dc: nc.tensor.transpose(tv[:, dc * 128:(dc + 1) * 128], xn[:, dc * 128:(dc + 1) * 128], ident[:]))
                             for dc in range(8)], reads=[xn_tk, const_tk], writes=tks)
                kidx = self.kidx
                if self.evac_act:
                    for dc in range(4, 8):
                        K.op(ACT, lambda: nc.scalar.activation(out=h[:, dc, tt * 128:(tt + 1) * 128], in_=tv[:, dc * 128:(dc + 1) * 128],
                                                               func=AF.Identity, scale=a_col[:, kidx, dc:dc + 1], bias=sh_col[:, kidx, dc:dc + 1]),
                             reads=tks + [par_tk], writes=h_tks)
                    K.op(DVE, lambda: nc.vector.tensor_tensor(out=self.tmpf[:, 0:4, :], in0=tv[:, 0:512].rearrange("p (c t) -> p c t", t=128),
                                                              in1=a_col[:, kidx, 0:4].unsqueeze(2).broadcast_to([128, 4, 128]), op=ALU.mult),
                         reads=tks + [par_tk], writes=[self.tmpf_tk])
                    ps.free(b)
                    K.op(DVE, lambda: nc.vector.tensor_tensor(out=h[:, 0:4, tt * 128:(tt + 1) * 128], in0=self.tmpf[:, 0:4, :],
                                                              in1=sh_col[:, kidx, 0:4].unsqueeze(2).broadcast_to([128, 4, 128]), op=ALU.add),
                         reads=[self.tmpf_tk, par_tk], writes=h_tks)
                    return
                K.op(DVE, lambda: nc.vector.tensor_tensor(out=self.tmpf[:], in0=tv.rearrange("p (c t) -> p c t", t=128),
                                                          in1=a_col[:, kidx, :].unsqueeze(2).broadcast_to([128, 8, 128]), op=ALU.mult),
                     reads=tks + [par_tk], writes=[self.tmpf_tk])
                ps.free(b)
                K.op(DVE, lambda: nc.vector.tensor_tensor(out=h[:, :, tt * 128:(tt + 1) * 128], in0=self.tmpf[:],
                                                          in1=sh_col[:, kidx, :].unsqueeze(2).broadcast_to([128, 8, 128]), op=ALU.add),
                     reads=[self.tmpf_tk, par_tk], writes=h_tks)

        class Epi:
            def __init__(self, es_, src, srcname, dst, dstname, kidx, lnexp, tag, junk):
                self.src, self.srcname, self.dst, self.dstname, self.kidx, self.lnexp = src, srcname, dst, dstname, kidx, lnexp
                self.junk = junk
                self.xre_rot = Rot(K, es_, tag + "xre", [128, D], F32, 1, dma=True)
                self.ob_rot = Rot(K, es_, tag + "ob", [128, D], F32, 1, dma=True)
                self.st_rot = Rot(K, es_, tag + "est", [128, 4], F32, 4)

            def run(self, tile, yb, ytk):
                yv = ps.f32(yb, 2)
                kidx = self.kidx
                xre, xre_tk, xre_sem = self.xre_rot.next()
                K.dma(SP, out=xre[:], in_=self.src[tile * 128:(tile + 1) * 128, :], reads=[dtk(self.srcname, tile)], writes=[xre_tk],
                      sem=xre_sem)
                st, st_tk, _ = self.st_rot.next()
                ob, ob_tk, ob_sem = self.ob_rot.next()
                jk, jk_tk, _ = self.junk.next()
                K.op(ACT, lambda: nc.scalar.activation(out=jk[:], in_=yv, func=AF.Square, accum_out=st[:, 0:1]),
                     reads=ytk, writes=[jk_tk, st_tk])
                rstd_ops(st, st_tk, D, self.lnexp)
                K.op(DVE, lambda: nc.vector.scalar_tensor_tensor(out=ob[:], in0=yv, scalar=st[:, 2:3], in1=G[:, kidx, :],
                                                                 op0=ALU.mult, op1=ALU.mult),
                     reads=ytk + [st_tk, par_tk], writes=[ob_tk])
                ps.free(yb, 2)
                K.op(POOL, lambda: nc.gpsimd.tensor_tensor(out=ob[:], in0=ob[:], in1=xre[:], op=ALU.add),
                     reads=[ob_tk, xre_tk], writes=[ob_tk])
                K.dma(POOL, out=self.dst[tile * 128:(tile + 1) * 128, :], in_=ob[:], reads=[ob_tk], writes=[dtk(self.dstname, tile)],
                      sem=ob_sem)

        def interleave(*gens):
            gens = list(gens)
            while gens:
                for g in list(gens):
                    try:
                        next(g)
                    except StopIteration:
                        gens.remove(g)

        def ffn_phase(fi, kidx, src, srcname, dst, dstname):
            with ExitStack() as pe_:
                Win = K.sb("Win", [128, 8, 2 * DFF], BF16, pe_)
                Wout = K.sb("Wout", [128, NF, D], BF16, pe_)
                groups = [(0, 6), (6, 12), (12, 17), (17, 22)]
                win_tk = [Tk("win%d" % g) for g in range(4)]
                wout_tk = [Tk("wout%d" % g) for g in range(2)]
                if fi == 0:
                    wv = win_d[fi].rearrange("(k p) n -> p k n", p=128)
                    wov = wout_d[fi].rearrange("(f p) n -> p f n", p=128)
                    for g, (f0, f1) in enumerate(groups):
                        sem = K.newsem("win%d_%d" % (fi, g))
                        for off in (0, DFF):
                            issue_cast_dma(Win[:, :, off + f0 * 128:off + f1 * 128], wv[:, :, off + f0 * 128:off + f1 * 128],
                                           win_tk[g], sem)
                    for g, (f0, f1) in enumerate([(0, 11), (11, 22)]):
                        sem = K.newsem("wout%d_%d" % (fi, g))
                        issue_cast_dma(Wout[:, f0:f1, :], wov[:, f0:f1, :], wout_tk[g], sem)

                    pcs = K.newsem("precast")

                    def precast(blk):
                        def cp(dst, src_, name):
                            K.dma(POOL, out=dst, in_=src_, reads=[], writes=[dtk(name, 0)], sem=pcs)
                        if blk == 2:
                            cp(wmi_s, wmi_d, "wmi_s")
                        elif blk == 3:
                            cp(wmo_s, wmo_d, "wmo_s")
                            cp(w2out_s[0:1408, :], wout_d[1][0:1408, :], "w2out_s")
                        elif blk == 4:
                            cp(w2out_s[1408:2816, :], wout_d[1][1408:2816, :], "w2out_s")
                            cp(w2in_s[0:256, :], win_d[1][0:256, :], "w2in_s")
                        elif blk == 5:
                            cp(w2in_s[256:512, :], win_d[1][256:512, :], "w2in_s")
                            cp(w2in_s[512:768, :], win_d[1][512:768, :], "w2in_s")
                        elif blk == 6:
                            cp(w2in_s[768:1024, :], win_d[1][768:1024, :], "w2in_s")
                else:
                    wv = w2in_s.rearrange("(k p) n -> p k n", p=128)
                    wov = w2out_s.rearrange("(f p) n -> p f n", p=128)
                    qi = 0
                    for g, (f0, f1) in enumerate(groups):
                        sem = K.newsem("win%d_%d" % (fi, g))
                        for off in (0, DFF):
                            K.dma(POOL, out=Win[:, :, off + f0 * 128:off + f1 * 128],
                                  in_=wv[:, :, off + f0 * 128:off + f1 * 128], reads=[dtk("w2in_s", 0)], writes=[win_tk[g]], sem=sem)
                            qi += 1
                    for g, (f0, f1) in enumerate([(0, 11), (11, 22)]):
                        sem = K.newsem("wout%d_%d" % (fi, g))
                        K.dma(POOL, out=Wout[:, f0:f1, :], in_=wov[:, f0:f1, :], reads=[dtk("w2out_s", 0)],
                              writes=[wout_tk[g]], sem=sem)
                        qi += 1

                def gof(f):
                    for g, (f0, f1) in enumerate(groups):
                        if f0 <= f < f1:
                            return g

                pro = Pro(pe_, src, srcname, kidx, False, "f")
                epi = Epi(pe_, src, srcname, dst, dstname, kidx, False, "f", pro.xn_rot)
                hs = [K.sb("h%d" % i, [128, 8, TB], BF16, pe_) for i in range(2)]
                h_tkss = [[Tk("h%d" % i)] for i in range(2)]
                act = K.sb("act", [128, NF, TB], BF16, pe_)
                act_tk = [Tk("act%d" % f) for f in range(NF)]
                sg_rot = Rot(K, pe_, "sg", [128, TB], F32, 1)

                for tt in range(4):
                    xn, xn_tk = pro.elem(tt)
                    pro.pe(xn, xn_tk, hs[0], h_tkss[0], tt)
                for blk in range(NB):
                    if fi == 0:
                        precast(blk)
                    h = hs[blk % 2]
                    h_tks = h_tkss[blk % 2]
                    hn = hs[(blk + 1) % 2]
                    hn_tks = h_tkss[(blk + 1) % 2]
                    pend = {}
                    for f in range(NF):
                        bg, tg = ps.alloc(1)
                        bu, tu = ps.alloc(1)
                        pg = ps.f32(bg)
                        pu = ps.f32(bu)
                        fns = []
                        for k in range(8):
                            fns.append(lambda k=k: nc.tensor.matmul(pg, lhsT=Win[:, k, f * 128:(f + 1) * 128], rhs=h[:, k, :],
                                                                    start=(k == 0), stop=(k == 7)))
                        for k in range(8):
                            fns.append(lambda k=k: nc.tensor.matmul(pu, lhsT=Win[:, k, DFF + f * 128:DFF + (f + 1) * 128],
                                                                    rhs=h[:, k, :], start=(k == 0), stop=(k == 7)))
                        K.group(PE, fns, reads=h_tks + [win_tk[gof(f)]], writes=tg + tu)
                        sg, sg_tk, _ = sg_rot.next()
                        K.op(ACT, lambda: nc.scalar.activation(out=sg[:], in_=pg, func=AF.Silu), reads=tg, writes=[sg_tk])
                        K.op(DVE, lambda: nc.vector.tensor_tensor(out=act[:, f, :], in0=sg[:], in1=pu, op=ALU.mult),
                             reads=[sg_tk] + tu, writes=[act_tk[f]])
                        ps.free(bg)
                        ps.free(bu)
                        if blk + 1 < NB:
                            if f % 4 == 0 and f // 4 < 4:
                                tt = f // 4
                                pend[tt] = pro.elem((blk + 1) * 4 + tt)
                            if f >= 7 and (f - 7) % 4 == 0 and (f - 7) // 4 < 4:
                                tt = (f - 7) // 4
                                pro.pe(pend[tt][0], pend[tt][1], hn, hn_tks, tt)
                    for tt in range(4):
                        yb, ytk = ps.alloc(2)
                        fns = []
                        for n in range(2):
                            o = ps.f32(yb + n)
                            for f in range(NF):
                                fns.append(lambda o=o, n=n, f=f: nc.tensor.matmul(
                                    o, lhsT=act[:, f, tt * 128:(tt + 1) * 128], rhs=Wout[:, f, n * 512:(n + 1) * 512],
                                    start=(f == 0), stop=(f == NF - 1)))
                        K.group(PE, fns, reads=act_tk + wout_tk, writes=ytk)
                        epi.run(blk * 4 + tt, yb, ytk)
                K.barrier()

        def mixer_phase(src, srcname, dst, dstname):
            kidx = 1
            with ExitStack() as pe_:
                Wmi = K.sb("Wmi", [128, 8, 2816], BF16, pe_)
                Wmo = K.sb("Wmo", [128, 8, D], BF16, pe_)
                wmi_tk = Tk("wmi")
                wmo_tk = Tk("wmo")
                wmiv = wmi_s.rearrange("(k p) n -> p k n", p=128)
                wmov = wmo_s.rearrange("(k p) n -> p k n", p=128)
                s1 = K.newsem("wmi")
                wmi_g = {}
                for (c0, c1) in ((1792, 2304), (1280, 1792), (768, 1280), (2304, 2816), (0, 768)):
                    gtk = Tk("wmi_%d" % c0)
                    gsem = K.newsem("wmi_%d" % c0)
                    K.dma(POOL, out=Wmi[:, :, c0:c1], in_=wmiv[:, :, c0:c1], reads=[dtk("wmi_s", 0)], writes=[gtk], sem=gsem)
                    wmi_g[c0] = gtk
                s2 = K.newsem("wmo")
                K.dma(POOL, out=Wmo[:], in_=wmov, reads=[dtk("wmo_s", 0)], writes=[wmo_tk], sem=s2)
                amask = K.sb("amask", [128, 2, 2, 512], BF16, pe_)
                hmask = K.sb("hmask", [128, 4, 128], BF16, pe_)
                rmask = K.sb("rmask", [128, 512], F32, pe_)
                cs_rot = Rot(K, pe_, "mcs", [128, 2, 128], F32, 2, dma=True)
                mc_tk = Tk("mconst")
                mcs = K.newsem("mconst")
                for (t_, d_) in ((amask[:], amask_d), (hmask[:], hmask_d), (rmask[:], rmask_d)):
                    K.dma(SP, out=t_, in_=d_, reads=[], writes=[mc_tk], sem=mcs)

                pro = Pro(pe_, src, srcname, kidx, True, "m", nxin=1, evac_act=True)
                epi = Epi(pe_, src, srcname, dst, dstname, kidx, True, "m", pro.xn_rot)
                h = K.sb("mh", [128, 8, TB], BF16, pe_)
                h_tks = [Tk("mh")]
                q2 = K.sb("q2", [128, 4, TB], F32, pe_)
                ff = K.sb("ff", [128, 4, TB], F32, pe_)
                bb = K.sb("bb", [128, 4, TB], F32, pe_)
                q2_tk = [Tk("q2_%d" % c) for c in range(4)]
                ff_tk = [Tk("ff_%d" % c) for c in range(4)]
                bb_tk = [Tk("bb_%d" % c) for c in range(4)]
                tA = Rot(K, pe_, "tA", [128, TB], F32, 2)
                tB = Rot(K, pe_, "tB", [128, TB], F32, 2)
                qt = K.sb("qt", [128, 4, TB], BF16, pe_)
                kt = K.sb("kt", [128, 4, TB], BF16, pe_)
                k2T = K.sb("k2T", [128, 4, TB], BF16, pe_)
                gs = K.sb("gs", [128, 4, TB], BF16, pe_)
                dec = K.sb("dec", [128, 4, 8], F32, pe_)
                hb_tk = [Tk("hgblk%d" % c) for c in range(4)]
                ktw_tk = [Tk("ktw%d" % c) for c in range(4)]
                k2w_tk = [Tk("k2w%d" % c) for c in range(4)]
                catT = K.sb("catT", [128, 8, TB], BF16, pe_)
                cat_tk = [Tk("cat%d" % c) for c in range(8)]
                NCH = 2
                RING = 5
                qkr_r = Rot(K, pe_, "qkr", [128, 640], BF16, 1)
                vtm = K.sb("vtm", [128, 4, 512], BF16, pe_)
                vtm_tk = [Tk("vtm%d" % i) for i in range(4)]
                vbe = K.sb("vbe", [128, RING, 2, 65], BF16, pe_)
                vb_tk = [Tk("vb%d" % i) for i in range(RING)]
                qT_r = Rot(K, pe_, "qT", [128, 8, 128], BF16, 4)
                kTa = K.sb("kTa", [128, 2, RING * 128], BF16, pe_)
                kT_tk = [Tk("kT%d" % i) for i in range(RING)]
                PT_r = Rot(K, pe_, "PTs", [128, 2, 2, 512], BF16, NCH)
                atm_r = Rot(K, pe_, "atm", [128, 512], BF16, NCH)
                ra_r = Rot(K, pe_, "ra", [128, 10, 32], F32, 1)
                rb_r = Rot(K, pe_, "rb", [128, 10, 32], F32, 1)
                dn_r = Rot(K, pe_, "dn", [128, 2, 8], F32, NCH)
                esink = K.sb("esink", [128, 8], F32, pe_)
                negc = K.sb("negc", [128, 1], F32, pe_)
                SHIFT = 16.0
                st32 = K.sb("st32", [128, 4, 128], F32, pe_)
                st32_tk = [Tk("st32")]
                stb_rot = Rot(K, pe_, "stb", [128, 4, 128], BF16, 2)
                sqr = K.sb("sqr", [128, 512], BF16, pe_)
                sqr_tk = Tk("sqr")
                rsb = K.sb("rsb", [128, 512], F32, pe_)
                rsb_tk = Tk("rsb")
                t1 = K.sb("t1", [128, 512], F32, pe_)
                t1_tk = Tk("t1")

                for (qT_, qT_tk_, _) in qT_r.items:
                    K.op(POOL, lambda: nc.gpsimd.memset(qT_[:], 0.0), writes=[qT_tk_])
                K.op(POOL, lambda: nc.gpsimd.memset(kTa[:], 0.0), writes=kT_tk)
                K.op(POOL, lambda: nc.gpsimd.memset(vbe[:], 1.0), writes=vb_tk)
                K.op(DVE, lambda: nc.vector.memset(negc[:], -SHIFT), writes=[mc_tk])
                K.op(DVE, lambda: nc.vector.tensor_scalar(out=esink[:], in0=sinks[:], scalar1=-SHIFT, scalar2=None, op0=ALU.add),
                     reads=[const_tk, mc_tk], writes=[mc_tk])
                K.op(ACT, lambda: nc.scalar.activation(out=esink[:], in_=esink[:], func=AF.Exp), reads=[mc_tk], writes=[mc_tk])
                K.op(POOL, lambda: nc.gpsimd.memset(st32[:], 0.0), writes=st32_tk)
                stb0, stb0_tk, _ = stb_rot.next()
                K.op(POOL, lambda: nc.gpsimd.memset(stb0[:], 0.0), writes=[stb0_tk])
                state = {"stb": (stb0, stb0_tk)}
                QSC = float(0.5 * 128 ** -0.5)

                def wg(col0):
                    for c0 in (2304, 1792, 1280, 768, 0):
                        if col0 >= c0:
                            return wmi_g[c0]

                def galloc(n):
                    tries = 0
                    while True:
                        r = ps.try_alloc(n)
                        if r is not None:
                            return r
                        tries += 1
                        assert tries < 10000, "PSUM livelock"
                        yield

                def fm_proj(col0, b, tks):
                    pv = ps.f32(b)
                    K.group(PE, [(lambda k=k: nc.tensor.matmul(pv, lhsT=Wmi[:, k, col0:col0 + 128], rhs=h[:, k, :],
                                                             start=(k == 0), stop=(k == 7))) for k in range(8)],
                            reads=h_tks + [wg(col0)], writes=tks)
                    return pv

                def gen_prologue(blk):
                    for tt in range(4):
                        xn, xn_tk = pro.elem(blk * 4 + tt)
                        yield
                        pro.pe(xn, xn_tk, h, h_tks, tt)
                        yield

                def gen_P(blk):
                    for tt in range(4):
                        cs = slice(tt * 128, (tt + 1) * 128)
                        hb_, htk = yield from galloc(1)
                        K.group(PE, [(lambda k=k: nc.tensor.matmul(ps.f32(hb_), lhsT=h[:, k, cs], rhs=Wmi[:, k, 1792:2304],
                                                                 start=(k == 0), stop=(k == 7))) for k in range(8)],
                                reads=h_tks + [wg(1792)], writes=htk)
                        K.op(ACT, lambda: nc.scalar.activation(out=vtm[:, tt, :], in_=ps.f32(hb_), func=AF.Copy), reads=htk, writes=[vtm_tk[tt]])
                        ps.free(hb_)
                        yield
                    for c in range(4):
                        b, tks = yield from galloc(1)
                        pv = fm_proj(1280 + c * 128, b, tks)
                        a_, a_tk, _ = tA.next()
                        K.op(ACT, lambda: nc.scalar.activation(out=a_[:], in_=pv, func=AF.Tanh, scale=0.5), reads=tks, writes=[a_tk])
                        K.op(DVE, lambda: nc.vector.tensor_scalar(out=ff[:, c, :], in0=a_[:], scalar1=c01[:, 1, c:c + 1], scalar2=c01[:, 0, c:c + 1],
                                                                  op0=ALU.mult, op1=ALU.add), reads=[a_tk, par_tk], writes=[ff_tk[c]])
                        ps.free(b)
                        yield
                        b, tks = yield from galloc(1)
                        pv = fm_proj(768 + c * 128, b, tks)
                        a_, a_tk, _ = tA.next()
                        K.op(ACT, lambda: nc.scalar.activation(out=a_[:], in_=pv, func=AF.Tanh, scale=0.5), reads=tks, writes=[a_tk])
                        K.op(DVE, lambda: nc.vector.scalar_tensor_tensor(out=q2[:, c, :], in0=a_[:], scalar=1.0, in1=pv, op0=ALU.add, op1=ALU.mult),
                             reads=[a_tk] + tks, writes=[q2_tk[c]])
                        ps.free(b)
                        yield
                        b, tks = yield from galloc(1)
                        pv = fm_proj(2304 + c * 128, b, tks)
                        a_, a_tk, _ = tA.next()
                        K.op(ACT, lambda: nc.scalar.activation(out=a_[:], in_=pv, func=AF.Tanh, scale=0.5), reads=tks, writes=[a_tk])
                        K.op(DVE, lambda: nc.vector.scalar_tensor_tensor(out=gs[:, c, :], in0=a_[:], scalar=1.0, in1=pv, op0=ALU.add, op1=ALU.mult),
                             reads=[a_tk] + tks, writes=[hb_tk[c]])
                        ps.free(b)
                        yield

                def gen_Pln(cs_):
                    for c in cs_:
                        wtk = [hb_tk[c]]
                        a_, a_tk, _ = tA.next()
                        K.op(ACT, lambda: nc.scalar.activation(out=a_[:], in_=ff[:, c, :], func=AF.Ln), reads=[ff_tk[c]], writes=[a_tk])
                        K.op(DVE, lambda: nc.vector.tensor_tensor_scan(out=bb[:, c, :], data0=rmask[:], data1=a_[:], initial=0.0,
                                                                       op0=ALU.mult, op1=ALU.add),
                             reads=[a_tk, mc_tk], writes=[bb_tk[c]])
                        yield
                        e_, e_tk, _ = tB.next()
                        K.op(ACT, lambda: nc.scalar.activation(out=e_[:], in_=bb[:, c, :], func=AF.Exp), reads=[bb_tk[c]], writes=[e_tk])
                        K.op(DVE, lambda: nc.vector.scalar_tensor_tensor(out=qt[:, c, :], in0=q2[:, c, :], scalar=-QSC, in1=e_[:],
                                                                         op0=ALU.mult, op1=ALU.mult),
                             reads=[q2_tk[c], e_tk], writes=wtk)
                        K.op(DVE, lambda: nc.vector.tensor_copy(out=dec[:, c, :],
                                                                in_=e_[:].rearrange("p (n t) -> p n t", t=64)[:, :, 63]),
                             reads=[e_tk], writes=wtk)
                        yield
                        x_, x_tk, _ = tB.next()
                        K.op(ACT, lambda: nc.scalar.activation(out=x_[:], in_=bb[:, c, :], func=AF.Exp, scale=-1.0), reads=[bb_tk[c]], writes=[x_tk])
                        K.op(DVE, lambda: nc.vector.scalar_tensor_tensor(out=kt[:, c, :], in0=ff[:, c, :], scalar=1.0, in1=x_[:],
                                                                         op0=ALU.subtract, op1=ALU.mult),
                             reads=[ff_tk[c], x_tk], writes=[ktw_tk[c]])
                        yield
                        d_, d_tk, _ = tA.next()
                        b3 = bb[:, c, :].rearrange("p (n t) -> p n t", t=64)
                        K.op(DVE, lambda: nc.vector.tensor_tensor(out=d_[:].rearrange("p (n t) -> p n t", t=64),
                                                                  in0=b3[:, :, 63:64].broadcast_to([128, 8, 64]), in1=b3, op=ALU.subtract),
                             reads=[bb_tk[c]], writes=[d_tk])
                        K.op(ACT, lambda: nc.scalar.activation(out=d_[:], in_=d_[:], func=AF.Exp), reads=[d_tk], writes=[d_tk])
                        K.op(DVE, lambda: nc.vector.scalar_tensor_tensor(out=k2T[:, c, :], in0=ff[:, c, :], scalar=1.0, in1=d_[:],
                                                                         op0=ALU.subtract, op1=ALU.mult),
                             reads=[ff_tk[c], d_tk], writes=[k2w_tk[c]])
                        yield

                pre_done = {}
                o_emitted = {}
                blkst = {}

                def gen_Apre(blk, tts):
                    cs_t, cs_tk, cs_sem = cs_rot.next()
                    K.dma(SP, out=cs_t[:, 0, :], in_=rope_d[:, 0, blk * 128:(blk + 1) * 128], reads=[dtk("rope", 0)], writes=[cs_tk], sem=cs_sem)
                    K.dma(SP, out=cs_t[:, 1, :], in_=rope_d[:, 1, blk * 128:(blk + 1) * 128], reads=[dtk("rope", 0)], writes=[cs_tk], sem=cs_sem)
                    for tt in tts:
                        tile = blk * 4 + tt
                        cs = slice(tt * 128, (tt + 1) * 128)
                        slot = tile % RING
                        qkr, qkr_tk, _ = qkr_r.next()
                        qT, qT_tk, _ = qT_r.items[tt]
                        ra, ra_tk, _ = ra_r.next()
                        rb, rb_tk, _ = rb_r.next()
                        pb, ptk = yield from galloc(2)
                        fns = []
                        for k in range(8):
                            fns.append(lambda k=k: nc.tensor.matmul(ps.f32(pb), lhsT=h[:, k, cs], rhs=Wmi[:, k, 0:512],
                                                                    start=(k == 0), stop=(k == 7)))
                        for k in range(8):
                            fns.append(lambda k=k: nc.tensor.matmul(ps.f32(pb + 1)[:, 0:256], lhsT=h[:, k, cs], rhs=Wmi[:, k, 512:768],
                                                                    start=(k == 0), stop=(k == 7)))
                        K.group(PE, fns, reads=h_tks + [wg(0)], writes=ptk)
                        yield
                        K.op(ACT, lambda: nc.scalar.activation(out=vbe[:, slot, :, 0:64],
                                                               in_=ps.f32(pb + 1)[:, 128:256].rearrange("p (g d) -> p g d", d=64), func=AF.Copy),
                             reads=ptk, writes=[vb_tk[slot]])
                        qk3 = ps.f32(pb, 2)[:, 0:640].rearrange("p (h d) -> p h d", d=64)
                        cb_ = cs_t[:, 0, tt * 32:(tt + 1) * 32].unsqueeze(1).broadcast_to([128, 10, 32])
                        sb_ = cs_t[:, 1, tt * 32:(tt + 1) * 32].unsqueeze(1).broadcast_to([128, 10, 32])
                        o3 = qkr[:].rearrange("p (h d) -> p h d", d=64)
                        K.op(DVE, lambda: nc.vector.tensor_tensor(out=ra[:], in0=qk3[:, :, 0:32], in1=cb_, op=ALU.mult), reads=ptk + [cs_tk], writes=[ra_tk])
                        K.op(DVE, lambda: nc.vector.tensor_tensor(out=rb[:], in0=qk3[:, :, 32:64], in1=sb_, op=ALU.mult), reads=ptk + [cs_tk], writes=[rb_tk])
                        K.op(DVE, lambda: nc.vector.tensor_tensor(out=o3[:, :, 0:32], in0=ra[:], in1=rb[:], op=ALU.subtract),
                             reads=[ra_tk, rb_tk], writes=[qkr_tk])
                        yield
                        K.op(DVE, lambda: nc.vector.tensor_tensor(out=ra[:], in0=qk3[:, :, 32:64], in1=cb_, op=ALU.mult), reads=ptk + [cs_tk], writes=[ra_tk])
                        K.op(DVE, lambda: nc.vector.tensor_tensor(out=rb[:], in0=qk3[:, :, 0:32], in1=sb_, op=ALU.mult), reads=ptk + [cs_tk], writes=[rb_tk])
                        K.op(DVE, lambda: nc.vector.tensor_tensor(out=o3[:, :, 32:64], in0=ra[:], in1=rb[:], op=ALU.add),
                             reads=[ra_tk, rb_tk], writes=[qkr_tk])
                        ps.free(pb, 2)
                        yield
                        tb2, ttk = yield from galloc(1)
                        tv = ps.bf(tb2)
                        fns = [(lambda j=j: nc.tensor.transpose(tv[:, j * 128:(j + 1) * 128], qkr[:, j * 128:(j + 1) * 128], ident[:]))
                               for j in range(5)]
                        K.group(PE, fns, reads=[qkr_tk, const_tk], writes=ttk)
                        qv = tv[:, 0:512].rearrange("p (j t) -> p j t", t=128)
                        qT4 = qT[:].rearrange("p (j two) t -> p j two t", two=2)
                        K.op(DVE, lambda: nc.vector.tensor_copy(out=qT4[0:64, :, 0, :], in_=qv[0:64]), reads=ttk, writes=[qT_tk])
                        K.op(DVE, lambda: nc.vector.tensor_copy(out=qT4[0:64, :, 1, :], in_=qv[64:128]), reads=ttk, writes=[qT_tk])
                        K.op(ACT, lambda: nc.scalar.activation(out=kTa[0:64, 0, slot * 128:(slot + 1) * 128], in_=tv[0:64, 512:640], func=AF.Copy),
                             reads=ttk, writes=[kT_tk[slot]])
                        K.op(ACT, lambda: nc.scalar.activation(out=kTa[0:64, 1, slot * 128:(slot + 1) * 128], in_=tv[64:128, 512:640],
                                                               func=AF.Copy), reads=ttk, writes=[kT_tk[slot]])
                        ps.free(tb2)
                        pre_done[tile] = True
                        yield

                def gen_Amain(blk, tt):
                    if True:
                        tile = blk * 4 + tt
                        cs = slice(tt * 128, (tt + 1) * 128)
                        slot = tile % RING
                        pslot = (tile - 1) % RING
                        tries = 0
                        while not pre_done.get(tile, False):
                            tries += 1
                            assert tries < 10000, "attention main chain never released"
                            yield
                        qT, qT_tk, _ = qT_r.items[tt]
                        PTs, PT_tk, _ = PT_r.next()
                        atm, atm_tk, _ = atm_r.next()
                        dn, dn_tk, _ = dn_r.next()
                        kbs = [(1, slot)] if tile == 0 else [(0, pslot), (1, slot)]
                        var = 0
                        for g in range(2):
                            sb2, stk = yield from galloc(2)
                            fns = []
                            for (kb, sl) in kbs:
                                o = ps.f32(sb2 + kb)
                                fns.append(lambda o=o, g=g, sl=sl: nc.tensor.matmul(
                                    o, lhsT=kTa[:, g, sl * 128:(sl + 1) * 128], rhs=qT[:, g * 4:(g + 1) * 4, :].rearrange("p h t -> p (h t)"),
                                    start=True, stop=False))
                                fns.append(lambda o=o, kb=kb: nc.tensor.matmul(o, lhsT=ident[:], rhs=amask[:, var, kb, :], start=False, stop=True))
                            K.group(PE, fns, reads=[qT_tk, const_tk, mc_tk] + kT_tk, writes=stk)
                            for (kb, sl) in kbs:
                                K.op(ACT, lambda: nc.scalar.activation(out=PTs[:, g, kb, :], in_=ps.f32(sb2 + kb), func=AF.Exp, scale=0.125,
                                                                       bias=negc[:, 0:1]),
                                     reads=[stk[kb], mc_tk], writes=[PT_tk])
                            ps.free(sb2, 2)
                            yield
                        ob2, otk = yield from galloc(2)
                        fns = []
                        for hh in range(8):
                            g, hq = hh // 4, hh % 4
                            for i, (kb, sl) in enumerate(kbs):
                                fns.append(lambda g=g, hq=hq, kb=kb, sl=sl, i=i: nc.tensor.matmul(
                                    ps.f32(ob2 + g)[:, hq * 65:(hq + 1) * 65], lhsT=PTs[:, g, kb, hq * 128:(hq + 1) * 128],
                                    rhs=vbe[:, sl, g, :], start=(i == 0), stop=(i == len(kbs) - 1)))
                        K.group(PE, fns, reads=[PT_tk] + vb_tk, writes=otk)
                        yield
                        O4 = ps.f32(ob2, 2).rearrange("p (b c) -> p b c", c=512)[:, :, 0:260].rearrange("p b (h e) -> p b h e", e=65)
                        K.op(DVE, lambda: nc.vector.tensor_tensor(out=dn[:, 0, :].rearrange("p (b h) -> p b h", h=4), in0=O4[:, :, :, 64],
                                                                  in1=esink[:].rearrange("p (b h) -> p b h", h=4), op=ALU.add),
                             reads=otk + [mc_tk], writes=[dn_tk])
                        K.op(DVE, lambda: nc.vector.reciprocal(out=dn[:, 1, :], in_=dn[:, 0, :]), reads=[dn_tk], writes=[dn_tk])
                        K.op(DVE, lambda: nc.vector.tensor_tensor(out=atm[:].rearrange("p (b h d) -> p b h d", h=4, d=64), in0=O4[:, :, :, 0:64],
                                                                  in1=dn[:, 1, :].rearrange("p (b h) -> p b h", h=4).unsqueeze(3).broadcast_to([128, 2, 4, 64]),
                                                                  op=ALU.mult),
                             reads=otk + [dn_tk], writes=[atm_tk])
                        ps.free(ob2, 2)
                        yield
                        tries = 0
                        while blk > 0 and not o_emitted.get((blk - 1, tt), False):
                            tries += 1
                            assert tries < 10000
                            yield
                        ab_, atk = yield from galloc(1)
                        av = ps.bf(ab_)
                        K.group(PE, [(lambda j=j: nc.tensor.transpose(av[:, j * 128:(j + 1) * 128], atm[:, j * 128:(j + 1) * 128], ident[:]))
                                     for j in range(4)], reads=[atm_tk, const_tk], writes=atk)
                        K.op(ACT, lambda: nc.scalar.activation(out=catT[:, 0:4, cs], in_=av[:, 0:512].rearrange("p (j t) -> p j t", t=128),
                                                               func=AF.Copy), reads=atk, writes=cat_tk[0:4])
                        ps.free(ab_)
                        yield

                def state_step(n, ubank, utk, sdst, sdst_tk):
                    K.op(DVE, lambda: nc.vector.tensor_tensor(out=st32[:], in0=st32[:], in1=dec[:, :, n:n + 1].broadcast_to([128, 4, 128]),
                                                              op=ALU.mult), reads=st32_tk + hb_tk, writes=st32_tk)
                    K.op(DVE, lambda: nc.vector.tensor_tensor(out=st32[:], in0=st32[:], in1=ps.f32(ubank).rearrange("p (c e) -> p c e", e=128),
                                                              op=ALU.add), reads=st32_tk + utk, writes=st32_tk)
                    K.op(DVE, lambda: nc.vector.tensor_copy(out=sdst[:], in_=st32[:]), reads=st32_tk, writes=[sdst_tk])

                k2_r = Rot(K, pe_, "k2AB", [128, 2, 4, 128], BF16, 2)
                scm_r = Rot(K, pe_, "scm2", [128, 4, 128], BF16, 2)

                def gen_H(blk):
                    pre = {}
                    post = {}

                    def h_pre(tt):
                        cs = slice(tt * 128, (tt + 1) * 128)
                        vt = vtm[:, tt, :]
                        vt_tk = vtm_tk[tt]
                        k2, k2_tk, _ = k2_r.next()
                        sc_, sc_tk, _ = scm_r.next()
                        kb_, ktk = yield from galloc(1)
                        kv = ps.bf(kb_)
                        K.group(PE, [(lambda c=c: nc.tensor.transpose(kv[:, c * 128:(c + 1) * 128], k2T[:, c, cs], ident[:])) for c in range(4)],
                                reads=hb_tk + k2w_tk + [const_tk], writes=ktk)
                        kv3 = kv[:, 0:512].rearrange("p (c d) -> p c d", d=128)
                        K.op(DVE, lambda: nc.vector.tensor_scalar(out=k2[:, 0], in0=kv3, scalar1=rowm[:, 0:1], scalar2=None, op0=ALU.mult),
                             reads=ktk + [const_tk], writes=[k2_tk])
                        K.op(DVE, lambda: nc.vector.tensor_scalar(out=k2[:, 1], in0=kv3, scalar1=rowm[:, 1:2], scalar2=None, op0=ALU.mult),
                             reads=ktk + [const_tk], writes=[k2_tk])
                        ps.free(kb_)
                        sb_2, sctk = yield from galloc(1)
                        scv = ps.f32(sb_2)
                        K.group(PE, [(lambda c=c: nc.tensor.matmul(scv[:, c * 128:(c + 1) * 128], lhsT=kt[:, c, cs], rhs=qt[:, c, cs],
                                                                 start=True, stop=True)) for c in range(4)],
                                reads=hb_tk + ktw_tk, writes=sctk)
                        yield
                        K.op(DVE, lambda: nc.vector.tensor_tensor(out=sc_[:], in0=scv.rearrange("p (c t) -> p c t", t=128), in1=hmask[:], op=ALU.mult),
                             reads=sctk + [mc_tk], writes=[sc_tk])
                        ps.free(sb_2)
                        ua, uatk = yield from galloc(1)
                        ub, ubtk = yield from galloc(1)
                        K.group(PE, [(lambda c=c: nc.tensor.matmul(ps.f32(ua)[:, c * 128:(c + 1) * 128], lhsT=k2[:, 0, c, :],
                                                                 rhs=vt[:, c * 128:(c + 1) * 128], start=True, stop=True)) for c in range(4)],
                                reads=[k2_tk, vt_tk], writes=uatk)
                        K.group(PE, [(lambda c=c: nc.tensor.matmul(ps.f32(ub)[:, c * 128:(c + 1) * 128], lhsT=k2[:, 1, c, :],
                                                                 rhs=vt[:, c * 128:(c + 1) * 128], start=True, stop=True)) for c in range(4)],
                                reads=[k2_tk, vt_tk], writes=ubtk)
                        pre[tt] = (sc_, sc_tk, ua, uatk, ub, ubtk)
                        yield

                    def h_chain(tt):
                        vt = vtm[:, tt, :]
                        vt_tk = vtm_tk[tt]
                        sc_, sc_tk, ua, uatk, ub, ubtk = pre[tt]
                        oo, ootk = yield from galloc(1)
                        oov = ps.f32(oo)
                        stA, stA_tk = state["stb"]
                        fns = []
                        for c in range(4):
                            fns.append(lambda c=c: nc.tensor.matmul(oov[:, c * 128:(c + 1) * 128], lhsT=vt[:, c * 128:(c + 1) * 128],
                                                                    rhs=sc_[:, c, :], start=(c == 0), stop=False, skip_group_check=True))
                        for c in range(4):
                            fns.append(lambda c=c: nc.tensor.matmul(oov[:, c * 128:c * 128 + 64], lhsT=stA[:, c, :],
                                                                    rhs=qt[:, c, tt * 128:tt * 128 + 64], start=False, stop=False,
                                                                    skip_group_check=True))
                        K.group(PE, fns, reads=[vt_tk, sc_tk, stA_tk] + hb_tk, writes=ootk)
                        stB, stB_tk, _ = stb_rot.next()
                        state_step(2 * tt, ua, uatk, stB, stB_tk)
                        ps.free(ua)
                        yield
                        K.group(PE, [(lambda c=c: nc.tensor.matmul(oov[:, c * 128 + 64:(c + 1) * 128], lhsT=stB[:, c, :],
                                                                 rhs=qt[:, c, tt * 128 + 64:(tt + 1) * 128], start=False, stop=True,
                                                                 skip_group_check=True)) for c in range(4)],
                                reads=[stB_tk] + hb_tk, writes=ootk)
                        stC, stC_tk, _ = stb_rot.next()
                        state_step(2 * tt + 1, ub, ubtk, stC, stC_tk)
                        ps.free(ub)
                        state["stb"] = (stC, stC_tk)
                        post[tt] = (oo, ootk)
                        yield

                    def h_post(tt):
                        cs = slice(tt * 128, (tt + 1) * 128)
                        oo, ootk = post[tt]
                        oov = ps.f32(oo)
                        tries = 0
                        while blk > 0 and not o_emitted.get((blk - 1, tt), False):
                            tries += 1
                            assert tries < 100000
                            yield
                        K.op(ACT, lambda: nc.scalar.activation(out=sqr[:], in_=oov, func=AF.Square), reads=ootk, writes=[sqr_tk])
                        nb_, ntk = yield from galloc(1)
                        K.group(PE, [lambda: nc.tensor.matmul(ps.f32(nb_), lhsT=onesb[:], rhs=sqr[:], start=True, stop=True)],
                                reads=[sqr_tk, const_tk], writes=ntk)
                        yield
                        K.op(ACT, lambda: nc.scalar.activation(out=rsb[:], in_=ps.f32(nb_), func=AF.Ln, scale=1.0 / 128, bias=eps_t[:, 0:1]),
                             reads=ntk + [const2_tk], writes=[rsb_tk])
                        ps.free(nb_)
                        K.op(ACT, lambda: nc.scalar.activation(out=rsb[:], in_=rsb[:], func=AF.Exp, scale=-0.5), reads=[rsb_tk], writes=[rsb_tk])
                        K.op(DVE, lambda: nc.vector.scalar_tensor_tensor(out=t1[:], in0=oov, scalar=gnh[:, 0:1], in1=rsb[:], op0=ALU.mult, op1=ALU.mult),
                             reads=ootk + [rsb_tk, par_tk], writes=[t1_tk])
                        ps.free(oo)
                        K.op(DVE, lambda: nc.vector.tensor_tensor(out=catT[:, 4:8, cs], in0=t1[:].rearrange("p (c t) -> p c t", t=128),
                                                                  in1=gs[:, :, cs], op=ALU.mult),
                             reads=[t1_tk] + hb_tk, writes=cat_tk[4:8])
                        yield

                    yield from h_pre(0)
                    for tt in range(4):
                        yield from h_chain(tt)
                        if tt + 1 < 4:
                            yield from h_pre(tt + 1)
                        yield from h_post(tt)

                def gen_O(blk):
                    for tt in range(4):
                        yb, ytk = yield from galloc(2)
                        fns = []
                        for n in range(2):
                            o = ps.f32(yb + n)
                            for kc in range(8):
                                fns.append(lambda o=o, n=n, kc=kc: nc.tensor.matmul(
                                    o, lhsT=catT[:, kc, tt * 128:(tt + 1) * 128], rhs=Wmo[:, kc, n * 512:(n + 1) * 512],
                                    start=(kc == 0), stop=(kc == 7)))
                        K.group(PE, fns, reads=cat_tk + [wmo_tk], writes=ytk)
                        o_emitted[(blk, tt)] = True
                        yield
                        epi.run(blk * 4 + tt, yb, ytk)
                        yield

                interleave(gen_prologue(0))
                for blk in range(NB):
                    fl = {"ptanh": False, "pln": False}

                    def gen_Pall(blk=blk, fl=fl):
                        yield from gen_P(blk)
                        fl["ptanh"] = True
                        live = [gen_Pln([0, 2]), gen_Pln([1, 3])]
                        while live:
                            for g in list(live):
                                try:
                                    next(g)
                                except StopIteration:
                                    live.remove(g)
                            yield
                        fl["pln"] = True

                    def gated(cond, gen):
                        tries = 0
                        while not cond():
                            tries += 1
                            assert tries < 100000, "gated chain never released"
                            yield
                        yield from gen

                    def gen_Aseq(tts, blk=blk):
                        for tt in tts:
                            yield from gen_Amain(blk, tt)

                    gens = [gen_Pall(), gen_Apre(blk, [0, 1, 2, 3]), gen_Aseq([0, 2]), gen_Aseq([1, 3]),
                            gated(lambda fl=fl: fl["pln"], gen_H(blk))]
                    if blk > 0:
                        gens.append(gated(lambda fl=fl: fl["ptanh"], gen_O(blk - 1)))
                    if blk + 1 < NB:
                        gens.append(gated(lambda fl=fl, blk=blk: fl["ptanh"] and all(pre_done.get(blk * 4 + t, False) for t in range(4)),
                                          gen_prologue(blk + 1)))
                    interleave(*gens)
                interleave(gen_O(NB - 1))
                K.barrier()

        dsts = {1: out_d if stop_after == 1 else x1_d, 2: out_d if stop_after == 2 else x2_d, 3: out_d}
        names = {1: "out" if stop_after == 1 else "x1", 2: "out" if stop_after == 2 else "x2", 3: "out"}
        if stop_after == 0:
            dbg = K.sb("dbg", [128, D], F32)
            dbg_tk = Tk("dbg")
            dsem = K.newsem("dbg")
            K.op(DVE, lambda: nc.vector.tensor_copy(out=dbg[:], in_=G[:, 0, :]), reads=[par_tk], writes=[dbg_tk])
            K.dma(SP, out=out_d[0:128, :], in_=dbg[:], reads=[dbg_tk], writes=[], sem=dsem)
            K.op(DVE, lambda: nc.vector.tensor_copy(out=dbg[:, 0:24], in_=a_col[:].rearrange("p a b -> p (a b)")), reads=[par_tk, dbg_tk], writes=[dbg_tk])
            K.op(DVE, lambda: nc.vector.tensor_copy(out=dbg[:, 24:48], in_=sh_col[:].rearrange("p a b -> p (a b)")), reads=[par_tk, dbg_tk], writes=[dbg_tk])
            K.op(DVE, lambda: nc.vector.tensor_copy(out=dbg[:, 48:52], in_=lb[:]), reads=[par_tk, dbg_tk], writes=[dbg_tk])
            K.dma(SP, out=out_d[128:256, :], in_=dbg[:], reads=[dbg_tk], writes=[], sem=dsem)
            K.barrier()
            return nc
        ffn_phase(0, 0, x_d, "x", dsts[1], names[1])
        if stop_after >= 2:
            mixer_phase(x1_d, "x1", dsts[2], names[2])
        if stop_after >= 3:
            ffn_phase(1, 2, x2_d, "x2", out_d, "out")
        K.barrier()
        print("instructions:", K.n_inst, "sems:", len(K.sems))
    return nc


def _consts():
    bf = ml_dtypes.bfloat16
    ident = np.eye(128, dtype=np.float32).astype(bf)
    onesb = np.ones((128, 128), np.float32).astype(bf)
    NEG = -1e30
    am = np.zeros((128, 2, 256), np.float32)
    am[0:64, 0, 192:256] = NEG
    am[64:128, 0, 0:64] = NEG
    am[0:64, 1, 64:128] = NEG
    am[64:128, 1, 128:192] = NEG
    s = np.arange(128)[:, None]
    t = np.arange(128)[None, :]
    hm = ((s // 64 == t // 64) & (s <= t)).astype(np.float32)
    hmask = np.broadcast_to(hm[:, None, :], (128, 4, 128)).copy()
    rmask = np.ones((128, 512), np.float32)
    rmask[:, ::64] = 0.0
    inv_freq = (1.0 / (np.float32(10000.0) ** (np.arange(0, 64, 2, dtype=np.float32) / np.float32(64)))).astype(np.float32)
    invf = np.broadcast_to(inv_freq[None, :], (128, 32)).copy()
    rowm = np.zeros((128, 2), np.float32)
    rowm[0:64, 0] = 1.0
    rowm[64:128, 1] = 1.0
    amT = np.zeros((128, 2, 2, 4, 128), np.float32)
    for v in range(2):
        for kb in range(2):
            amT[:, v, kb, :, :] = am[:, v, kb * 128:(kb + 1) * 128].T[:, None, :]
    amT = amT.reshape(128, 2, 2, 512)
    return dict(ident=ident, onesb=onesb, amask=amT.astype(bf), hmask=hmask.astype(bf), rmask=rmask, invf=invf, rowm=rowm)


def make_in_maps(x, c, positions, w_cond, b_cond, norm_pre, norm_post, ffn_w_in, ffn_w_out,
                 w_mix_in, w_mix_out, attn_sinks, hgrn_lb_logits, hgrn_gnorm):
    f = np.float32
    cst = _consts()
    shared = dict(
        w_cond=np.ascontiguousarray(w_cond[0], f), b_cond=np.ascontiguousarray(np.broadcast_to(np.asarray(b_cond[0], f)[None, :], (128, 9 * D))),
        npre=np.ascontiguousarray(np.asarray(norm_pre[0], f).reshape(3, 8, 128).transpose(2, 0, 1)),
        npost=np.ascontiguousarray(np.broadcast_to(np.asarray(norm_post[0], f)[None], (128, 3, D))),
        w_in=np.ascontiguousarray(ffn_w_in[0], f), w_out=np.ascontiguousarray(ffn_w_out[0], f),
        w_mi=np.ascontiguousarray(w_mix_in[0], f), w_mo=np.ascontiguousarray(w_mix_out[0], f),
        sinks=np.ascontiguousarray(np.broadcast_to(np.asarray(attn_sinks[0], f)[None], (128, 8))),
        lbl=np.ascontiguousarray(np.asarray(hgrn_lb_logits, f).reshape(2, 4, 128).transpose(2, 0, 1)),
        gn=np.ascontiguousarray(np.asarray(hgrn_gnorm[0], f)[:, None]),
        **cst)
    maps = []
    for b in range(8):
        m = dict(shared)
        m["x"] = np.ascontiguousarray(x[b], f)
        m["ccol"] = np.ascontiguousarray(np.asarray(c[b], f).reshape(8, 128).T)
        m["pos"] = np.ascontiguousarray(np.asarray(positions[b], np.int32).reshape(NT, 128).T)
        maps.append(m)
    return maps


def kernel(**inputs):
    nc = build(3)
    maps = make_in_maps(**inputs)
    res = run_bass_kernel_spmd(nc, maps, core_ids=list(range(8)))
    return np.stack([np.asarray(r["out"], np.float32) for r in res.results], axis=0)
```

```python
import numpy as np
from contextlib import ExitStack
import ml_dtypes
import concourse.bass as bass
import concourse.mybir as mybir
from concourse.bass_utils import run_bass_kernel_spmd

F32 = mybir.dt.float32
BF16 = mybir.dt.bfloat16
I32 = mybir.dt.int32
AF = mybir.ActivationFunctionType
ALU = mybir.AluOpType
AX = mybir.AxisListType

S = 4096
D = 1024
DFF = 2816
NF = DFF // 128
TB = 512
NB = S // TB
NT = S // 128
EPS = 1e-6
PI = float(np.pi)
TWO_PI = float(2 * np.pi)


class Tk:
    __slots__ = ("w", "r", "name", "excl")

    def __init__(self, name="", excl=False):
        self.w = None
        self.r = {}
        self.name = name
        self.excl = excl


class Sem:
    def __init__(self, h, name, idx, needed):
        self.h = h
        self.cnt = 0
        self.name = name
        self.idx = idx
        self.needed = needed
        self.waited = set()
        self.rank = {n: i + 1 for i, n in enumerate(needed)} if needed is not None else None

    def value(self, n):
        return n if self.rank is None else self.rank[n]

    def signals(self, n):
        return True if self.rank is None else (n in self.rank)


class Eng:
    def __init__(self, name, h, sem):
        self.name = name
        self.h = h
        self.sem = sem
        self.seen = {}


class Kern:
    def __init__(self, nc, es, plan=None):
        self.nc = nc
        self.es = es
        self.plan = plan
        self.sems = []
        self.PE = Eng("pe", nc.tensor, self.newsem("pe"))
        self.ACT = Eng("act", nc.scalar, self.newsem("act"))
        self.DVE = Eng("dve", nc.vector, self.newsem("dve"))
        self.POOL = Eng("pool", nc.gpsimd, self.newsem("pool"))
        self.SP = Eng("sp", nc.sync, self.newsem("sp"))
        self.engs = [self.PE, self.ACT, self.DVE, self.POOL, self.SP]
        self.n_inst = 0
        self.n_inc = 0
        self.eng_sem_idx = [E.sem.idx for E in self.engs]

    def newsem(self, name):
        self.sid = getattr(self, "sid", 0) + 1
        idx = self.sid
        needed = None
        if self.plan is not None and idx in self.plan:
            needed = self.plan[idx]
        s = Sem(self.es.enter_context(self.nc.semaphore("m%d_%s" % (self.sid, name))), name, idx, needed)
        self.sems.append(s)
        return s

    def sb(self, name, shape, dt, es=None):
        self.uid = getattr(self, "uid", 0) + 1
        return (es or self.es).enter_context(self.nc.sbuf_tensor("s%d_%s" % (self.uid, name), shape, dt))

    def _wait(self, E, toks):
        for (s, v) in toks:
            if E.seen.get(id(s), 0) >= v:
                continue
            s.waited.add(v)
            E.h.wait_ge(s.h, s.value(v))
            E.seen[id(s)] = v

    def _deps(self, E, reads, writes):
        toks = []
        for t in reads:
            if t.w is not None:
                toks.append(t.w)
            if t.excl:
                for tok in t.r.values():
                    if tok[0] is not E.sem:
                        toks.append(tok)
        for t in writes:
            if t.w is not None and t.w[0] is not E.sem:
                toks.append(t.w)
            for tok in t.r.values():
                if tok[0] is not E.sem or E is not self.PE:
                    toks.append(tok)
        return toks

    def _mark(self, tok, reads, writes):
        for t in writes:
            t.w = tok
            t.r = {}
        for t in reads:
            t.r[id(tok[0])] = tok

    def op(self, E, fn, reads=(), writes=()):
        self._wait(E, self._deps(E, reads, writes))
        ins = fn()
        E.sem.cnt += 1
        if E.sem.signals(E.sem.cnt):
            ins.then_inc(E.sem.h, 1)
            self.n_inc += 1
        self._mark((E.sem, E.sem.cnt), reads, writes)
        self.n_inst += 1
        return ins

    def group(self, E, fns, reads=(), writes=()):
        self._wait(E, self._deps(E, reads, writes))
        ins = None
        for fn in fns:
            ins = fn()
            self.n_inst += 1
        E.sem.cnt += 1
        if E.sem.signals(E.sem.cnt):
            ins.then_inc(E.sem.h, 1)
            self.n_inc += 1
        self._mark((E.sem, E.sem.cnt), reads, writes)

    def dma(self, E, out, in_, reads, writes, sem, **kw):
        self._wait(E, self._deps(E, reads, writes))
        E.h.dma_start(out=out, in_=in_, **kw).then_inc(sem.h, 16)
        sem.cnt += 16
        self._mark((sem, sem.cnt), reads, writes)
        self.n_inst += 1

    def barrier(self):
        toks = [(s, s.cnt) for s in self.sems if s.cnt > 0]
        for E in self.engs:
            self._wait(E, [t for t in toks if t[0] is not E.sem])


class Psum:
    def __init__(self, K):
        nc = K.nc
        self.t = K.es.enter_context(nc.psum_tensor("psum_all", [128, 8 * 512], F32))
        self.tb = self.t.bitcast(BF16)
        self.tk = [Tk("bank%d" % i, excl=True) for i in range(8)]
        self.p = 0
        self.held = [False] * 8

    def try_alloc(self, n=1):
        for i in range(8):
            b = (self.p + i) % 8
            if b % n or b + n > 8:
                continue
            if any(self.held[b:b + n]):
                continue
            for j in range(b, b + n):
                self.held[j] = True
            self.p = (b + n) % 8
            return b, self.tk[b:b + n]
        return None

    def alloc(self, n=1):
        r = self.try_alloc(n)
        assert r is not None, "PSUM exhausted (straight-line code must free before allocating)"
        return r

    def free(self, b, n=1):
        for j in range(b, b + n):
            assert self.held[j]
            self.held[j] = False

    def f32(self, b, n=1):
        return self.t[:, b * 512:(b + n) * 512]

    def bf(self, b, n=1):
        return self.tb[:, b * 1024:(b + n) * 1024]


class Rot:
    def __init__(self, K, es, name, shape, dt, n, dma=False):
        self.items = []
        for i in range(n):
            t = K.sb("%s%d" % (name, i), shape, dt, es)
            self.items.append((t, Tk("%s%d" % (name, i)), K.newsem("%s%d" % (name, i)) if dma else None))
        self.i = 0

    def next(self):
        it = self.items[self.i % len(self.items)]
        self.i += 1
        return it


def build(stop_after=3, plan=None, want_plan=False):
    nc = bass.Bass("TRN2", target_bir_lowering=False)

    def din(name, shape, dt=F32):
        return nc.dram_tensor(name, shape, dt, kind="ExternalInput").ap()

    x_d = din("x", [S, D])
    ccol_d = din("ccol", [128, 8])
    pos_d = din("pos", [128, NT], I32)
    wcond_d = din("w_cond", [D, 9 * D])
    bcond_d = din("b_cond", [128, 9 * D])
    npre_d = din("npre", [128, 3, 8])
    npost_d = din("npost", [128, 3, D])
    win_d = din("w_in", [2, D, 2 * DFF])
    wout_d = din("w_out", [2, DFF, D])
    wmi_d = din("w_mi", [D, 2816])
    wmo_d = din("w_mo", [D, D])
    sinks_d = din("sinks", [128, 8])
    lbl_d = din("lbl", [128, 2, 4])
    gn_d = din("gn", [128, 1])
    ident_d = din("ident", [128, 128], BF16)
    onesb_d = din("onesb", [128, 128], BF16)
    amask_d = din("amask", [128, 2, 2, 512], BF16)
    hmask_d = din("hmask", [128, 4, 128], BF16)
    rmask_d = din("rmask", [128, 512])
    invf_d = din("invf", [128, 32])
    rowm_d = din("rowm", [128, 2])
    out_d = nc.dram_tensor("out", [S, D], F32, kind="ExternalOutput").ap()
    x1_d = nc.dram_tensor("x1s", [S, D], F32).ap()
    x2_d = nc.dram_tensor("x2s", [S, D], F32).ap()
    rope_d = nc.dram_tensor("rope_s", [128, 2, NT * 32], F32).ap()
    w2in_s = nc.dram_tensor("w2in_s", [D, 2 * DFF], BF16).ap()
    w2out_s = nc.dram_tensor("w2out_s", [DFF, D], BF16).ap()
    wmi_s = nc.dram_tensor("wmi_s", [D, 2816], BF16).ap()
    wmo_s = nc.dram_tensor("wmo_s", [D, D], BF16).ap()

    with ExitStack() as es:
        K = Kern(nc, es, plan)
        PE, ACT, DVE, POOL, SP = K.PE, K.ACT, K.DVE, K.POOL, K.SP
        ps = Psum(K)
        dram_tk = {}

        def dtk(name, tile):
            key = (name, tile)
            if key not in dram_tk:
                dram_tk[key] = Tk("%s_%d" % key)
            return dram_tk[key]

        ident = K.sb("ident", [128, 128], BF16)
        onesb = K.sb("onesb", [128, 128], BF16)
        rowm = K.sb("rowm", [128, 2], F32)
        sinks = K.sb("sinks", [128, 8], F32)
        gn = K.sb("gn", [128, 1], F32)
        lb = K.sb("lb", [128, 4], F32)
        oml = K.sb("oml", [128, 4], F32)
        a_col = K.sb("a_col", [128, 3, 8], F32)
        sh_col = K.sb("sh_col", [128, 3, 8], F32)
        G = K.sb("G", [128, 3, D], BF16)
        c01 = K.sb("c01", [128, 2, 4], F32)
        gnh = K.sb("gnh", [128, 1], F32)
        const_tk = Tk("consts")
        par_tk = Tk("params")
        rope_tk = Tk("rope")

        with ExitStack() as ss:
            ld = K.newsem("setup_ld")
            ccol = K.sb("ccol", [128, 8], F32, ss)
            ca = K.sb("ca", [128, 8], F32, ss)
            posi = K.sb("posi", [128, NT], I32, ss)
            npre = K.sb("npre", [128, 3, 8], F32, ss)
            npost = K.sb("npost", [128, 3, D], F32, ss)
            lbl = K.sb("lbl", [128, 2, 4], F32, ss)
            invf = K.sb("invf", [128, 32], F32, ss)
            setup_tk = Tk("setup_in")
            cosT = K.sb("cosT", [128, NT, 32], F32, ss)
            sinT = K.sb("sinT", [128, NT, 32], F32, ss)
            loads = [(ident, ident_d), (onesb, onesb_d),
                     (rowm, rowm_d), (sinks, sinks_d), (gn, gn_d), (ccol, ccol_d), (posi, pos_d), (npre, npre_d),
                     (npost, npost_d), (lbl, lbl_d), (invf, invf_d)]
            for (t, d_) in loads:
                nc.sync.dma_start(out=t[:], in_=d_).then_inc(ld.h, 16)
                ld.cnt += 16
            setup_tk.w = (ld, ld.cnt)
            const_tk.w = (ld, ld.cnt)

            K.op(ACT, lambda: nc.scalar.activation(out=ca[:], in_=ccol[:], func=AF.Silu), reads=[setup_tk], writes=[par_tk])
            K.op(DVE, lambda: nc.vector.tensor_tensor(out=lb[:], in0=lbl[:, 0, :], in1=lbl[:, 1, :], op=ALU.subtract),
                 reads=[setup_tk], writes=[par_tk])
            K.op(ACT, lambda: nc.scalar.activation(out=lb[:], in_=lb[:], func=AF.Sigmoid), reads=[par_tk], writes=[par_tk])
            K.op(DVE, lambda: nc.vector.tensor_scalar(out=oml[:], in0=lb[:], scalar1=-1.0, scalar2=1.0, op0=ALU.mult, op1=ALU.add),
                 reads=[par_tk], writes=[par_tk])
            K.op(DVE, lambda: nc.vector.tensor_scalar(out=c01[:, 1, :], in0=oml[:], scalar1=0.5, scalar2=None, op0=ALU.mult),
                 reads=[par_tk], writes=[par_tk])
            K.op(DVE, lambda: nc.vector.tensor_tensor(out=c01[:, 0, :], in0=lb[:], in1=c01[:, 1, :], op=ALU.add),
                 reads=[par_tk], writes=[par_tk])
            K.op(DVE, lambda: nc.vector.tensor_scalar(out=gnh[:], in0=gn[:], scalar1=0.5, scalar2=None, op0=ALU.mult),
                 reads=[setup_tk, par_tk], writes=[par_tk])

            posf = K.sb("posf", [128, NT], F32, ss)
            ang = K.sb("ang", [128, NT, 32], F32, ss)
            rr = K.sb("rr", [128, NT, 32], F32, ss)
            rc = K.sb("rc", [128, NT, 32], F32, ss)
            kf = K.sb("kf", [128, NT, 32], F32, ss)
            ki = K.sb("ki", [128, NT, 32], I32, ss)
            mk = K.sb("mk", [128, NT, 32], F32, ss)
            rt = Tk("ropetmp")

            def V(fn, reads=(), writes=()):
                K.op(DVE, fn, reads=list(reads) + [rt, setup_tk], writes=list(writes) + [rt])

            V(lambda: nc.vector.tensor_copy(out=posf[:], in_=posi[:]))
            V(lambda: nc.vector.tensor_tensor(out=ang[:], in0=posf[:].unsqueeze(2).broadcast_to([128, NT, 32]),
                                              in1=invf[:].unsqueeze(1).broadcast_to([128, NT, 32]), op=ALU.mult))
            V(lambda: nc.vector.tensor_scalar(out=kf[:], in0=ang[:], scalar1=1.0 / TWO_PI, scalar2=None, op0=ALU.mult))
            V(lambda: nc.vector.tensor_copy(out=ki[:], in_=kf[:]))
            V(lambda: nc.vector.tensor_copy(out=kf[:], in_=ki[:]))
            C1 = 6.28125
            C2 = TWO_PI - 6.28125
            V(lambda: nc.vector.scalar_tensor_tensor(out=rr[:], in0=kf[:], scalar=-C1, in1=ang[:], op0=ALU.mult, op1=ALU.add))
            V(lambda: nc.vector.scalar_tensor_tensor(out=rr[:], in0=kf[:], scalar=-C2, in1=rr[:], op0=ALU.mult, op1=ALU.add))

            def fold(t):
                V(lambda: nc.vector.tensor_scalar(out=mk[:], in0=t[:], scalar1=PI, scalar2=-TWO_PI, op0=ALU.is_gt, op1=ALU.mult))
                V(lambda: nc.vector.tensor_tensor(out=t[:], in0=t[:], in1=mk[:], op=ALU.add))
                V(lambda: nc.vector.tensor_scalar(out=mk[:], in0=t[:], scalar1=-PI, scalar2=TWO_PI, op0=ALU.is_lt, op1=ALU.mult))
                V(lambda: nc.vector.tensor_tensor(out=t[:], in0=t[:], in1=mk[:], op=ALU.add))
                V(lambda: nc.vector.tensor_scalar(out=t[:], in0=t[:], scalar1=PI, scalar2=-PI, op0=ALU.min, op1=ALU.max))

            fold(rr)
            V(lambda: nc.vector.tensor_scalar(out=rc[:], in0=rr[:], scalar1=PI / 2, scalar2=None, op0=ALU.add))
            fold(rc)
            K.op(ACT, lambda: nc.scalar.activation(out=sinT[:], in_=rr[:], func=AF.Sin), reads=[rt], writes=[rope_tk])
            K.op(ACT, lambda: nc.scalar.activation(out=cosT[:], in_=rc[:], func=AF.Sin), reads=[rt], writes=[rope_tk])
            cab = K.sb("cab", [128, 8, 128], F32, ss)
            identf = K.sb("identf", [128, 128], F32, ss)
            modbc = K.sb("modbc", [128, 9 * D], F32, ss)
            dg = K.sb("dg", [128, 8, 128], F32, ss)
            K.op(DVE, lambda: nc.vector.tensor_copy(out=cab[:], in_=ca[:].unsqueeze(2).broadcast_to([128, 8, 128])),
                 reads=[par_tk], writes=[par_tk])
            K.op(DVE, lambda: nc.vector.tensor_copy(out=identf[:], in_=ident[:]), reads=[setup_tk], writes=[setup_tk])
            wc_rot = Rot(K, ss, "wc", [128, 8, 512], F32, 2, dma=True)
            bc_rot = Rot(K, ss, "bc", [128, 512], F32, 2, dma=True)
            wc_view = wcond_d.rearrange("(k p) n -> p k n", p=128)
            mod_tk = Tk("modbc")
            for cb in range(18):
                wc, wc_tk, wc_sem = wc_rot.next()
                bc, bc_tk, bc_sem = bc_rot.next()
                K.dma(SP, out=wc[:], in_=wc_view[:, :, cb * 512:(cb + 1) * 512], reads=[], writes=[wc_tk], sem=wc_sem)
                K.dma(SP, out=bc[:], in_=bcond_d[:, cb * 512:(cb + 1) * 512], reads=[], writes=[bc_tk], sem=bc_sem)
                b, btk = ps.alloc(1)
                pv = ps.f32(b)
                K.group(PE, [(lambda k=k: nc.tensor.matmul(pv, lhsT=cab[:, k, :], rhs=wc[:, k, :],
                                                         start=(k == 0), stop=(k == 7))) for k in range(8)],
                        reads=[par_tk, wc_tk], writes=btk)
                K.op(DVE, lambda: nc.vector.tensor_tensor(out=modbc[:, cb * 512:(cb + 1) * 512], in0=pv, in1=bc[:], op=ALU.add),
                     reads=btk + [bc_tk, mod_tk], writes=[mod_tk])
                ps.free(b)
            for k in range(3):
                for which, v in (("sh", 3 * k), ("sc", 3 * k + 1)):
                    K.op(DVE, lambda: nc.vector.tensor_tensor(out=dg[:], in0=modbc[:, v * D:(v + 1) * D].rearrange("p (j q) -> p j q", q=128),
                                                              in1=identf[:].unsqueeze(1).broadcast_to([128, 8, 128]), op=ALU.mult),
                         reads=[mod_tk, setup_tk, par_tk], writes=[par_tk])
                    dstc = sh_col if which == "sh" else a_col
                    K.op(DVE, lambda: nc.vector.tensor_reduce(out=dstc[:, k, :], in_=dg[:], axis=AX.X, op=ALU.add),
                         reads=[par_tk], writes=[par_tk])
                K.op(DVE, lambda: nc.vector.scalar_tensor_tensor(out=a_col[:, k, :], in0=a_col[:, k, :], scalar=1.0, in1=npre[:, k, :],
                                                                 op0=ALU.add, op1=ALU.mult),
                     reads=[par_tk, setup_tk], writes=[par_tk])
                resw = 1.0 if k == 1 else 0.5
                K.op(DVE, lambda: nc.vector.scalar_tensor_tensor(out=G[:, k, :], in0=modbc[:, (3 * k + 2) * D:(3 * k + 3) * D], scalar=resw,
                                                                 in1=npost[:, k, :], op0=ALU.mult, op1=ALU.mult),
                     reads=[mod_tk, setup_tk, par_tk], writes=[par_tk])

            rsem = K.newsem("ropest")
            K.dma(SP, out=rope_d[:, 0, :], in_=cosT[:].rearrange("p t f -> p (t f)"), reads=[rope_tk], writes=[dtk("rope", 0)], sem=rsem)
            K.dma(SP, out=rope_d[:, 1, :], in_=sinT[:].rearrange("p t f -> p (t f)"), reads=[rope_tk], writes=[dtk("rope", 0)], sem=rsem)
            K.barrier()

        def issue_cast_dma(dst, src, tk, sem):
            K.dma(POOL, out=dst, in_=src, reads=[], writes=[tk], sem=sem)

        eps_t = K.sb("eps_t", [128, 2], F32)
        const2_tk = Tk("const2")
        K.op(DVE, lambda: nc.vector.memset(eps_t[:, 0:1], EPS), writes=[const2_tk])
        K.op(DVE, lambda: nc.vector.memset(eps_t[:, 1:2], 0.0), writes=[const2_tk])

        def rstd_ops(st, st_tk, n, lnexp):
            if lnexp:
                K.op(ACT, lambda: nc.scalar.activation(out=st[:, 1:2], in_=st[:, 0:1], func=AF.Ln, scale=1.0 / n, bias=eps_t[:, 0:1]),
                     reads=[st_tk, const2_tk], writes=[st_tk])
                K.op(ACT, lambda: nc.scalar.activation(out=st[:, 2:3], in_=st[:, 1:2], func=AF.Exp, scale=-0.5),
                     reads=[st_tk], writes=[st_tk])
            else:
                K.op(ACT, lambda: nc.scalar.activation(out=st[:, 1:2], in_=st[:, 0:1], func=AF.Sqrt, scale=1.0 / n, bias=eps_t[:, 0:1]),
                     reads=[st_tk, const2_tk], writes=[st_tk])
                K.op(DVE, lambda: nc.vector.reciprocal(out=st[:, 2:3], in_=st[:, 1:2]), reads=[st_tk], writes=[st_tk])

        class Pro:
            def __init__(self, es_, src, srcname, kidx, lnexp, tag, nxin=2, evac_act=False):
                self.src, self.srcname, self.kidx, self.lnexp = src, srcname, kidx, lnexp
                self.evac_act = evac_act
                self.xin_rot = Rot(K, es_, tag + "xin", [128, D], F32, nxin, dma=True)
                self.xn_rot = Rot(K, es_, tag + "xn", [128, D], BF16, 2)
                self.st_rot = Rot(K, es_, tag + "st", [128, 4], F32, 4)
                self.tmpf = K.sb(tag + "tmpf", [128, 8, 128], F32, es_)
                self.tmpf_tk = Tk(tag + "tmpf")

            def elem(self, tile):
                xin, xin_tk, xin_sem = self.xin_rot.next()
                K.dma(SP, out=xin[:], in_=self.src[tile * 128:(tile + 1) * 128, :], reads=[dtk(self.srcname, tile)],
                      writes=[xin_tk], sem=xin_sem)
                st, st_tk, _ = self.st_rot.next()
                xn, xn_tk, _ = self.xn_rot.next()
                K.op(ACT, lambda: nc.scalar.activation(out=xn[:], in_=xin[:], func=AF.Square, accum_out=st[:, 0:1]),
                     reads=[xin_tk], writes=[xn_tk, st_tk])
                rstd_ops(st, st_tk, D, self.lnexp)
                K.op(DVE, lambda: nc.vector.tensor_scalar(out=xn[:], in0=xin[:], scalar1=st[:, 2:3], scalar2=None, op0=ALU.mult),
                     reads=[xin_tk, st_tk], writes=[xn_tk])
                return xn, xn_tk

            def pe(self, xn, xn_tk, h, h_tks, tt):
                b, tks = ps.alloc(1)
                tv = ps.bf(b)
                K.group(PE, [(lambda dc=dc: nc.tensor.transpose(tv[:, dc * 128:(dc + 1) * 128], xn[:, dc * 128:(dc + 1) * 128], ident[:]))
                             for dc in range(8)], reads=[xn_tk, const_tk], writes=tks)
                kidx = self.kidx
                if self.evac_act:
                    for dc in range(8):
                        K.op(ACT, lambda: nc.scalar.activation(out=h[:, dc, tt * 128:(tt + 1) * 128], in_=tv[:, dc * 128:(dc + 1) * 128],
                                                               func=AF.Identity, scale=a_col[:, kidx, dc:dc + 1], bias=sh_col[:, kidx, dc:dc + 1]),
                             reads=tks + [par_tk], writes=h_tks)
                    ps.free(b)
                    return
                K.op(DVE, lambda: nc.vector.tensor_tensor(out=self.tmpf[:], in0=tv.rearrange("p (c t) -> p c t", t=128),
                                                          in1=a_col[:, kidx, :].unsqueeze(2).broadcast_to([128, 8, 128]), op=ALU.mult),
                     reads=tks + [par_tk], writes=[self.tmpf_tk])
                ps.free(b)
                K.op(DVE, lambda: nc.vector.tensor_tensor(out=h[:, :, tt * 128:(tt + 1) * 128], in0=self.tmpf[:],
                                                          in1=sh_col[:, kidx, :].unsqueeze(2).broadcast_to([128, 8, 128]), op=ALU.add),
                     reads=[self.tmpf_tk, par_tk], writes=h_tks)

        class Epi:
            def __init__(self, es_, src, srcname, dst, dstname, kidx, lnexp, tag, junk):
                self.src, self.srcname, self.dst, self.dstname, self.kidx, self.lnexp = src, srcname, dst, dstname, kidx, lnexp
                self.junk = junk
                self.xre_rot = Rot(K, es_, tag + "xre", [128, D], F32, 1, dma=True)
                self.ob_rot = Rot(K, es_, tag + "ob", [128, D], F32, 1, dma=True)
                self.st_rot = Rot(K, es_, tag + "est", [128, 4], F32, 4)

            def run(self, tile, yb, ytk):
                yv = ps.f32(yb, 2)
                kidx = self.kidx
                xre, xre_tk, xre_sem = self.xre_rot.next()
                K.dma(SP, out=xre[:], in_=self.src[tile * 128:(tile + 1) * 128, :], reads=[dtk(self.srcname, tile)], writes=[xre_tk],
                      sem=xre_sem)
                st, st_tk, _ = self.st_rot.next()
                ob, ob_tk, ob_sem = self.ob_rot.next()
                jk, jk_tk, _ = self.junk.next()
                K.op(ACT, lambda: nc.scalar.activation(out=jk[:], in_=yv, func=AF.Square, accum_out=st[:, 0:1]),
                     reads=ytk, writes=[jk_tk, st_tk])
                rstd_ops(st, st_tk, D, self.lnexp)
                K.op(DVE, lambda: nc.vector.scalar_tensor_tensor(out=ob[:], in0=yv, scalar=st[:, 2:3], in1=G[:, kidx, :],
                                                                 op0=ALU.mult, op1=ALU.mult),
                     reads=ytk + [st_tk, par_tk], writes=[ob_tk])
                ps.free(yb, 2)
                K.op(POOL, lambda: nc.gpsimd.tensor_tensor(out=ob[:], in0=ob[:], in1=xre[:], op=ALU.add),
                     reads=[ob_tk, xre_tk], writes=[ob_tk])
                K.dma(POOL, out=self.dst[tile * 128:(tile + 1) * 128, :], in_=ob[:], reads=[ob_tk], writes=[dtk(self.dstname, tile)],
                      sem=ob_sem)

        def interleave(*gens):
            gens = list(gens)
            while gens:
                for g in list(gens):
                    try:
                        next(g)
                    except StopIteration:
                        gens.remove(g)

        def ffn_phase(fi, kidx, src, srcname, dst, dstname):
            with ExitStack() as pe_:
                Win = K.sb("Win", [128, 8, 2 * DFF], BF16, pe_)
                Wout = K.sb("Wout", [128, NF, D], BF16, pe_)
                groups = [(0, 6), (6, 12), (12, 17), (17, 22)]
                win_tk = [Tk("win%d" % g) for g in range(4)]
                wout_tk = [Tk("wout%d" % g) for g in range(2)]
                if fi == 0:
                    wv = win_d[fi].rearrange("(k p) n -> p k n", p=128)
                    wov = wout_d[fi].rearrange("(f p) n -> p f n", p=128)
                    for g, (f0, f1) in enumerate(groups):
                        sem = K.newsem("win%d_%d" % (fi, g))
                        for off in (0, DFF):
                            issue_cast_dma(Win[:, :, off + f0 * 128:off + f1 * 128], wv[:, :, off + f0 * 128:off + f1 * 128],
                                           win_tk[g], sem)
                    for g, (f0, f1) in enumerate([(0, 11), (11, 22)]):
                        sem = K.newsem("wout%d_%d" % (fi, g))
                        issue_cast_dma(Wout[:, f0:f1, :], wov[:, f0:f1, :], wout_tk[g], sem)

                    pcs = K.newsem("precast")

                    def precast(blk):
                        def cp(dst, src_, name):
                            K.dma(POOL, out=dst, in_=src_, reads=[], writes=[dtk(name, 0)], sem=pcs)
                        if blk == 2:
                            cp(wmi_s, wmi_d, "wmi_s")
                        elif blk == 3:
                            cp(wmo_s, wmo_d, "wmo_s")
                            cp(w2out_s[0:1408, :], wout_d[1][0:1408, :], "w2out_s")
                        elif blk == 4:
                            cp(w2out_s[1408:2816, :], wout_d[1][1408:2816, :], "w2out_s")
                            cp(w2in_s[0:256, :], win_d[1][0:256, :], "w2in_s")
                        elif blk == 5:
                            cp(w2in_s[256:512, :], win_d[1][256:512, :], "w2in_s")
                            cp(w2in_s[512:768, :], win_d[1][512:768, :], "w2in_s")
                        elif blk == 6:
                            cp(w2in_s[768:1024, :], win_d[1][768:1024, :], "w2in_s")
                else:
                    wv = w2in_s.rearrange("(k p) n -> p k n", p=128)
                    wov = w2out_s.rearrange("(f p) n -> p f n", p=128)
                    qi = 0
                    for g, (f0, f1) in enumerate(groups):
                        sem = K.newsem("win%d_%d" % (fi, g))
                        for off in (0, DFF):
                            K.dma(POOL, out=Win[:, :, off + f0 * 128:off + f1 * 128],
                                  in_=wv[:, :, off + f0 * 128:off + f1 * 128], reads=[dtk("w2in_s", 0)], writes=[win_tk[g]], sem=sem)
                            qi += 1
                    for g, (f0, f1) in enumerate([(0, 11), (11, 22)]):
                        sem = K.newsem("wout%d_%d" % (fi, g))
                        K.dma(POOL, out=Wout[:, f0:f1, :], in_=wov[:, f0:f1, :], reads=[dtk("w2out_s", 0)],
                              writes=[wout_tk[g]], sem=sem)
                        qi += 1

                def gof(f):
                    for g, (f0, f1) in enumerate(groups):
                        if f0 <= f < f1:
                            return g

                pro = Pro(pe_, src, srcname, kidx, False, "f")
                epi = Epi(pe_, src, srcname, dst, dstname, kidx, False, "f", pro.xn_rot)
                hs = [K.sb("h%d" % i, [128, 8, TB], BF16, pe_) for i in range(2)]
                h_tkss = [[Tk("h%d" % i)] for i in range(2)]
                act = K.sb("act", [128, NF, TB], BF16, pe_)
                act_tk = [Tk("act%d" % f) for f in range(NF)]
                sg_rot = Rot(K, pe_, "sg", [128, TB], F32, 1)

                for tt in range(4):
                    xn, xn_tk = pro.elem(tt)
                    pro.pe(xn, xn_tk, hs[0], h_tkss[0], tt)
                for blk in range(NB):
                    if fi == 0:
                        precast(blk)
                    h = hs[blk % 2]
                    h_tks = h_tkss[blk % 2]
                    hn = hs[(blk + 1) % 2]
                    hn_tks = h_tkss[(blk + 1) % 2]
                    pend = {}
                    for f in range(NF):
                        bg, tg = ps.alloc(1)
                        bu, tu = ps.alloc(1)
                        pg = ps.f32(bg)
                        pu = ps.f32(bu)
                        fns = []
                        for k in range(8):
                            fns.append(lambda k=k: nc.tensor.matmul(pg, lhsT=Win[:, k, f * 128:(f + 1) * 128], rhs=h[:, k, :],
                                                                    start=(k == 0), stop=(k == 7)))
                        for k in range(8):
                            fns.append(lambda k=k: nc.tensor.matmul(pu, lhsT=Win[:, k, DFF + f * 128:DFF + (f + 1) * 128],
                                                                    rhs=h[:, k, :], start=(k == 0), stop=(k == 7)))
                        K.group(PE, fns, reads=h_tks + [win_tk[gof(f)]], writes=tg + tu)
                        sg, sg_tk, _ = sg_rot.next()
                        K.op(ACT, lambda: nc.scalar.activation(out=sg[:], in_=pg, func=AF.Silu), reads=tg, writes=[sg_tk])
                        K.op(DVE, lambda: nc.vector.tensor_tensor(out=act[:, f, :], in0=sg[:], in1=pu, op=ALU.mult),
                             reads=[sg_tk] + tu, writes=[act_tk[f]])
                        ps.free(bg)
                        ps.free(bu)
                        if blk + 1 < NB:
                            if f % 4 == 0 and f // 4 < 4:
                                tt = f // 4
                                pend[tt] = pro.elem((blk + 1) * 4 + tt)
                            if f >= 7 and (f - 7) % 4 == 0 and (f - 7) // 4 < 4:
                                tt = (f - 7) // 4
                                pro.pe(pend[tt][0], pend[tt][1], hn, hn_tks, tt)
                    for tt in range(4):
                        yb, ytk = ps.alloc(2)
                        fns = []
                        for n in range(2):
                            o = ps.f32(yb + n)
                            for f in range(NF):
                                fns.append(lambda o=o, n=n, f=f: nc.tensor.matmul(
                                    o, lhsT=act[:, f, tt * 128:(tt + 1) * 128], rhs=Wout[:, f, n * 512:(n + 1) * 512],
                                    start=(f == 0), stop=(f == NF - 1)))
                        K.group(PE, fns, reads=act_tk + wout_tk, writes=ytk)
                        epi.run(blk * 4 + tt, yb, ytk)
                K.barrier()

        def mixer_phase(src, srcname, dst, dstname):
            kidx = 1
            with ExitStack() as pe_:
                Wmi = K.sb("Wmi", [128, 8, 2816], BF16, pe_)
                Wmo = K.sb("Wmo", [128, 8, D], BF16, pe_)
                wmi_tk = Tk("wmi")
                wmo_tk = Tk("wmo")
                wmiv = wmi_s.rearrange("(k p) n -> p k n", p=128)
                wmov = wmo_s.rearrange("(k p) n -> p k n", p=128)
                s1 = K.newsem("wmi")
                wmi_g = {}
                for (c0, c1) in ((1792, 2304), (1280, 1792), (768, 1280), (2304, 2816), (0, 768)):
                    gtk = Tk("wmi_%d" % c0)
                    gsem = K.newsem("wmi_%d" % c0)
                    K.dma(POOL, out=Wmi[:, :, c0:c1], in_=wmiv[:, :, c0:c1], reads=[dtk("wmi_s", 0)], writes=[gtk], sem=gsem)
                    wmi_g[c0] = gtk
                s2 = K.newsem("wmo")
                K.dma(POOL, out=Wmo[:], in_=wmov, reads=[dtk("wmo_s", 0)], writes=[wmo_tk], sem=s2)
                amask = K.sb("amask", [128, 2, 2, 512], BF16, pe_)
                hmask = K.sb("hmask", [128, 4, 128], BF16, pe_)
                rmask = K.sb("rmask", [128, 512], F32, pe_)
                cs_rot = Rot(K, pe_, "mcs", [128, 2, 128], F32, 2, dma=True)
                mc_tk = Tk("mconst")
                mcs = K.newsem("mconst")
                for (t_, d_) in ((amask[:], amask_d), (hmask[:], hmask_d), (rmask[:], rmask_d)):
                    K.dma(SP, out=t_, in_=d_, reads=[], writes=[mc_tk], sem=mcs)

                pro = Pro(pe_, src, srcname, kidx, True, "m", nxin=1, evac_act=True)
                epi = Epi(pe_, src, srcname, dst, dstname, kidx, True, "m", pro.xn_rot)
                h = K.sb("mh", [128, 8, TB], BF16, pe_)
                h_tks = [Tk("mh")]
                q2 = K.sb("q2", [128, 4, TB], F32, pe_)
                ff = K.sb("ff", [128, 4, TB], F32, pe_)
                bb = K.sb("bb", [128, 4, TB], F32, pe_)
                q2_tk = [Tk("q2_%d" % c) for c in range(4)]
                ff_tk = [Tk("ff_%d" % c) for c in range(4)]
                bb_tk = [Tk("bb_%d" % c) for c in range(4)]
                tA = Rot(K, pe_, "tA", [128, TB], F32, 2)
                tB = Rot(K, pe_, "tB", [128, TB], F32, 2)
                qt = K.sb("qt", [128, 4, TB], BF16, pe_)
                kt = K.sb("kt", [128, 4, TB], BF16, pe_)
                k2T = K.sb("k2T", [128, 4, TB], BF16, pe_)
                gs = K.sb("gs", [128, 4, TB], BF16, pe_)
                dec = K.sb("dec", [128, 4, 8], F32, pe_)
                hb_tk = [Tk("hgblk%d" % c) for c in range(4)]
                ktw_tk = [Tk("ktw%d" % c) for c in range(4)]
                k2w_tk = [Tk("k2w%d" % c) for c in range(4)]
                catT = K.sb("catT", [128, 8, TB], BF16, pe_)
                cat_tk = [Tk("cat%d" % c) for c in range(8)]
                NCH = 2
                RING = 5
                qkr_r = Rot(K, pe_, "qkr", [128, 640], BF16, 1)
                vtm = K.sb("vtm", [128, 4, 512], BF16, pe_)
                vtm_tk = [Tk("vtm%d" % i) for i in range(4)]
                vbe = K.sb("vbe", [128, RING, 2, 65], BF16, pe_)
                vb_tk = [Tk("vb%d" % i) for i in range(RING)]
                qT_r = Rot(K, pe_, "qT", [128, 8, 128], BF16, 4)
                kTa = K.sb("kTa", [128, 2, RING * 128], BF16, pe_)
                kT_tk = [Tk("kT%d" % i) for i in range(RING)]
                PT_r = Rot(K, pe_, "PTs", [128, 2, 2, 512], BF16, NCH)
                atm_r = Rot(K, pe_, "atm", [128, 512], BF16, NCH)
                ra_r = Rot(K, pe_, "ra", [128, 10, 32], F32, 1)
                rb_r = Rot(K, pe_, "rb", [128, 10, 32], F32, 1)
                dn_r = Rot(K, pe_, "dn", [128, 2, 8], F32, NCH)
                esink = K.sb("esink", [128, 8], F32, pe_)
                negc = K.sb("negc", [128, 1], F32, pe_)
                SHIFT = 16.0
                st32 = K.sb("st32", [128, 4, 128], F32, pe_)
                st32_tk = [Tk("st32")]
                stb_rot = Rot(K, pe_, "stb", [128, 4, 128], BF16, 2)
                sqr = K.sb("sqr", [128, 512], BF16, pe_)
                sqr_tk = Tk("sqr")
                rsb = K.sb("rsb", [128, 512], F32, pe_)
                rsb_tk = Tk("rsb")
                t1 = K.sb("t1", [128, 512], F32, pe_)
                t1_tk = Tk("t1")

                for (qT_, qT_tk_, _) in qT_r.items:
                    K.op(POOL, lambda: nc.gpsimd.memset(qT_[:], 0.0), writes=[qT_tk_])
                K.op(POOL, lambda: nc.gpsimd.memset(kTa[:], 0.0), writes=kT_tk)
                K.op(POOL, lambda: nc.gpsimd.memset(vbe[:], 1.0), writes=vb_tk)
                K.op(DVE, lambda: nc.vector.memset(negc[:], -SHIFT), writes=[mc_tk])
                K.op(DVE, lambda: nc.vector.tensor_scalar(out=esink[:], in0=sinks[:], scalar1=-SHIFT, scalar2=None, op0=ALU.add),
                     reads=[const_tk, mc_tk], writes=[mc_tk])
                K.op(ACT, lambda: nc.scalar.activation(out=esink[:], in_=esink[:], func=AF.Exp), reads=[mc_tk], writes=[mc_tk])
                K.op(POOL, lambda: nc.gpsimd.memset(st32[:], 0.0), writes=st32_tk)
                stb0, stb0_tk, _ = stb_rot.next()
                K.op(POOL, lambda: nc.gpsimd.memset(stb0[:], 0.0), writes=[stb0_tk])
                state = {"stb": (stb0, stb0_tk)}
                QSC = float(0.5 * 128 ** -0.5)

                def wg(col0):
                    for c0 in (2304, 1792, 1280, 768, 0):
                        if col0 >= c0:
                            return wmi_g[c0]

                def galloc(n):
                    tries = 0
                    while True:
                        r = ps.try_alloc(n)
                        if r is not None:
                            return r
                        tries += 1
                        assert tries < 10000, "PSUM livelock"
                        yield

                def fm_proj(col0, b, tks):
                    pv = ps.f32(b)
                    K.group(PE, [(lambda k=k: nc.tensor.matmul(pv, lhsT=Wmi[:, k, col0:col0 + 128], rhs=h[:, k, :],
                                                             start=(k == 0), stop=(k == 7))) for k in range(8)],
                            reads=h_tks + [wg(col0)], writes=tks)
                    return pv

                def gen_prologue(blk):
                    for tt in range(4):
                        xn, xn_tk = pro.elem(blk * 4 + tt)
                        yield
                        pro.pe(xn, xn_tk, h, h_tks, tt)
                        yield

                def gen_P(blk):
                    for tt in range(4):
                        cs = slice(tt * 128, (tt + 1) * 128)
                        hb_, htk = yield from galloc(1)
                        K.group(PE, [(lambda k=k: nc.tensor.matmul(ps.f32(hb_), lhsT=h[:, k, cs], rhs=Wmi[:, k, 1792:2304],
                                                                 start=(k == 0), stop=(k == 7))) for k in range(8)],
                                reads=h_tks + [wg(1792)], writes=htk)
                        K.op(ACT, lambda: nc.scalar.activation(out=vtm[:, tt, :], in_=ps.f32(hb_), func=AF.Copy), reads=htk, writes=[vtm_tk[tt]])
                        ps.free(hb_)
                        yield
                    for c in range(4):
                        b, tks = yield from galloc(1)
                        pv = fm_proj(1280 + c * 128, b, tks)
                        a_, a_tk, _ = tA.next()
                        K.op(ACT, lambda: nc.scalar.activation(out=a_[:], in_=pv, func=AF.Tanh, scale=0.5), reads=tks, writes=[a_tk])
                        K.op(DVE, lambda: nc.vector.tensor_scalar(out=ff[:, c, :], in0=a_[:], scalar1=c01[:, 1, c:c + 1], scalar2=c01[:, 0, c:c + 1],
                                                                  op0=ALU.mult, op1=ALU.add), reads=[a_tk, par_tk], writes=[ff_tk[c]])
                        ps.free(b)
                        yield
                        b, tks = yield from galloc(1)
                        pv = fm_proj(768 + c * 128, b, tks)
                        a_, a_tk, _ = tA.next()
                        K.op(ACT, lambda: nc.scalar.activation(out=a_[:], in_=pv, func=AF.Tanh, scale=0.5), reads=tks, writes=[a_tk])
                        K.op(DVE, lambda: nc.vector.scalar_tensor_tensor(out=q2[:, c, :], in0=a_[:], scalar=1.0, in1=pv, op0=ALU.add, op1=ALU.mult),
                             reads=[a_tk] + tks, writes=[q2_tk[c]])
                        ps.free(b)
                        yield
                        b, tks = yield from galloc(1)
                        pv = fm_proj(2304 + c * 128, b, tks)
                        a_, a_tk, _ = tA.next()
                        K.op(ACT, lambda: nc.scalar.activation(out=a_[:], in_=pv, func=AF.Tanh, scale=0.5), reads=tks, writes=[a_tk])
                        K.op(DVE, lambda: nc.vector.scalar_tensor_tensor(out=gs[:, c, :], in0=a_[:], scalar=1.0, in1=pv, op0=ALU.add, op1=ALU.mult),
                             reads=[a_tk] + tks, writes=[hb_tk[c]])
                        ps.free(b)
                        yield

                def gen_Pln(cs_):
                    for c in cs_:
                        wtk = [hb_tk[c]]
                        a_, a_tk, _ = tA.next()
                        K.op(ACT, lambda: nc.scalar.activation(out=a_[:], in_=ff[:, c, :], func=AF.Ln), reads=[ff_tk[c]], writes=[a_tk])
                        K.op(DVE, lambda: nc.vector.tensor_tensor_scan(out=bb[:, c, :], data0=rmask[:], data1=a_[:], initial=0.0,
                                                                       op0=ALU.mult, op1=ALU.add),
                             reads=[a_tk, mc_tk], writes=[bb_tk[c]])
                        yield
                        e_, e_tk, _ = tB.next()
                        K.op(ACT, lambda: nc.scalar.activation(out=e_[:], in_=bb[:, c, :], func=AF.Exp), reads=[bb_tk[c]], writes=[e_tk])
                        K.op(DVE, lambda: nc.vector.scalar_tensor_tensor(out=qt[:, c, :], in0=q2[:, c, :], scalar=-QSC, in1=e_[:],
                                                                         op0=ALU.mult, op1=ALU.mult),
                             reads=[q2_tk[c], e_tk], writes=wtk)
                        K.op(DVE, lambda: nc.vector.tensor_copy(out=dec[:, c, :],
                                                                in_=e_[:].rearrange("p (n t) -> p n t", t=64)[:, :, 63]),
                             reads=[e_tk], writes=wtk)
                        yield
                        x_, x_tk, _ = tB.next()
                        K.op(ACT, lambda: nc.scalar.activation(out=x_[:], in_=bb[:, c, :], func=AF.Exp, scale=-1.0), reads=[bb_tk[c]], writes=[x_tk])
                        K.op(DVE, lambda: nc.vector.scalar_tensor_tensor(out=kt[:, c, :], in0=ff[:, c, :], scalar=1.0, in1=x_[:],
                                                                         op0=ALU.subtract, op1=ALU.mult),
                             reads=[ff_tk[c], x_tk], writes=[ktw_tk[c]])
                        yield
                        d_, d_tk, _ = tA.next()
                        b3 = bb[:, c, :].rearrange("p (n t) -> p n t", t=64)
                        K.op(DVE, lambda: nc.vector.tensor_tensor(out=d_[:].rearrange("p (n t) -> p n t", t=64),
                                                                  in0=b3[:, :, 63:64].broadcast_to([128, 8, 64]), in1=b3, op=ALU.subtract),
                             reads=[bb_tk[c]], writes=[d_tk])
                        K.op(ACT, lambda: nc.scalar.activation(out=d_[:], in_=d_[:], func=AF.Exp), reads=[d_tk], writes=[d_tk])
                        K.op(DVE, lambda: nc.vector.scalar_tensor_tensor(out=k2T[:, c, :], in0=ff[:, c, :], scalar=1.0, in1=d_[:],
                                                                         op0=ALU.subtract, op1=ALU.mult),
                             reads=[ff_tk[c], d_tk], writes=[k2w_tk[c]])
                        yield

                pre_done = {}
                o_emitted = {}
                blkst = {}

                def gen_Apre(blk, tts):
                    cs_t, cs_tk, cs_sem = cs_rot.next()
                    K.dma(SP, out=cs_t[:, 0, :], in_=rope_d[:, 0, blk * 128:(blk + 1) * 128], reads=[dtk("rope", 0)], writes=[cs_tk], sem=cs_sem)
                    K.dma(SP, out=cs_t[:, 1, :], in_=rope_d[:, 1, blk * 128:(blk + 1) * 128], reads=[dtk("rope", 0)], writes=[cs_tk], sem=cs_sem)
                    for tt in tts:
                        tile = blk * 4 + tt
                        cs = slice(tt * 128, (tt + 1) * 128)
                        slot = tile % RING
                        qkr, qkr_tk, _ = qkr_r.next()
                        qT, qT_tk, _ = qT_r.items[tt]
                        ra, ra_tk, _ = ra_r.next()
                        rb, rb_tk, _ = rb_r.next()
                        pb, ptk = yield from galloc(2)
                        fns = []
                        for k in range(8):
                            fns.append(lambda k=k: nc.tensor.matmul(ps.f32(pb), lhsT=h[:, k, cs], rhs=Wmi[:, k, 0:512],
                                                                    start=(k == 0), stop=(k == 7)))
                        for k in range(8):
                            fns.append(lambda k=k: nc.tensor.matmul(ps.f32(pb + 1)[:, 0:256], lhsT=h[:, k, cs], rhs=Wmi[:, k, 512:768],
                                                                    start=(k == 0), stop=(k == 7)))
                        K.group(PE, fns, reads=h_tks + [wg(0)], writes=ptk)
                        yield
                        K.op(ACT, lambda: nc.scalar.activation(out=vbe[:, slot, :, 0:64],
                                                               in_=ps.f32(pb + 1)[:, 128:256].rearrange("p (g d) -> p g d", d=64), func=AF.Copy),
                             reads=ptk, writes=[vb_tk[slot]])
                        qk3 = ps.f32(pb, 2)[:, 0:640].rearrange("p (h d) -> p h d", d=64)
                        cb_ = cs_t[:, 0, tt * 32:(tt + 1) * 32].unsqueeze(1).broadcast_to([128, 10, 32])
                        sb_ = cs_t[:, 1, tt * 32:(tt + 1) * 32].unsqueeze(1).broadcast_to([128, 10, 32])
                        o3 = qkr[:].rearrange("p (h d) -> p h d", d=64)
                        K.op(DVE, lambda: nc.vector.tensor_tensor(out=ra[:], in0=qk3[:, :, 0:32], in1=cb_, op=ALU.mult), reads=ptk + [cs_tk], writes=[ra_tk])
                        K.op(DVE, lambda: nc.vector.tensor_tensor(out=rb[:], in0=qk3[:, :, 32:64], in1=sb_, op=ALU.mult), reads=ptk + [cs_tk], writes=[rb_tk])
                        K.op(DVE, lambda: nc.vector.tensor_tensor(out=o3[:, :, 0:32], in0=ra[:], in1=rb[:], op=ALU.subtract),
                             reads=[ra_tk, rb_tk], writes=[qkr_tk])
                        yield
                        K.op(DVE, lambda: nc.vector.tensor_tensor(out=ra[:], in0=qk3[:, :, 32:64], in1=cb_, op=ALU.mult), reads=ptk + [cs_tk], writes=[ra_tk])
                        K.op(DVE, lambda: nc.vector.tensor_tensor(out=rb[:], in0=qk3[:, :, 0:32], in1=sb_, op=ALU.mult), reads=ptk + [cs_tk], writes=[rb_tk])
                        K.op(DVE, lambda: nc.vector.tensor_tensor(out=o3[:, :, 32:64], in0=ra[:], in1=rb[:], op=ALU.add),
                             reads=[ra_tk, rb_tk], writes=[qkr_tk])
                        ps.free(pb, 2)
                        yield
                        tb2, ttk = yield from galloc(1)
                        tv = ps.bf(tb2)
                        fns = [(lambda j=j: nc.tensor.transpose(tv[:, j * 128:(j + 1) * 128], qkr[:, j * 128:(j + 1) * 128], ident[:]))
                               for j in range(5)]
                        K.group(PE, fns, reads=[qkr_tk, const_tk], writes=ttk)
                        qv = tv[:, 0:512].rearrange("p (j t) -> p j t", t=128)
                        qT4 = qT[:].rearrange("p (j two) t -> p j two t", two=2)
                        K.op(DVE, lambda: nc.vector.tensor_copy(out=qT4[0:64, :, 0, :], in_=qv[0:64]), reads=ttk, writes=[qT_tk])
                        K.op(DVE, lambda: nc.vector.tensor_copy(out=qT4[0:64, :, 1, :], in_=qv[64:128]), reads=ttk, writes=[qT_tk])
                        K.op(ACT, lambda: nc.scalar.activation(out=kTa[0:64, 0, slot * 128:(slot + 1) * 128], in_=tv[0:64, 512:640], func=AF.Copy),
                             reads=ttk, writes=[kT_tk[slot]])
                        K.op(ACT, lambda: nc.scalar.activation(out=kTa[0:64, 1, slot * 128:(slot + 1) * 128], in_=tv[64:128, 512:640],
                                                               func=AF.Copy), reads=ttk, writes=[kT_tk[slot]])
                        ps.free(tb2)
                        pre_done[tile] = True
                        yield

                def gen_Amain(blk, tt):
                    if True:
                        tile = blk * 4 + tt
                        cs = slice(tt * 128, (tt + 1) * 128)
                        slot = tile % RING
                        pslot = (tile - 1) % RING
                        tries = 0
                        while not pre_done.get(tile, False):
                            tries += 1
                            assert tries < 10000, "attention main chain never released"
                            yield
                        qT, qT_tk, _ = qT_r.items[tt]
                        PTs, PT_tk, _ = PT_r.next()
                        atm, atm_tk, _ = atm_r.next()
                        dn, dn_tk, _ = dn_r.next()
                        kbs = [(1, slot)] if tile == 0 else [(0, pslot), (1, slot)]
                        var = 0
                        for g in range(2):
                            sb2, stk = yield from galloc(2)
                            fns = []
                            for (kb, sl) in kbs:
                                o = ps.f32(sb2 + kb)
                                fns.append(lambda o=o, g=g, sl=sl: nc.tensor.matmul(
                                    o, lhsT=kTa[:, g, sl * 128:(sl + 1) * 128], rhs=qT[:, g * 4:(g + 1) * 4, :].rearrange("p h t -> p (h t)"),
                                    start=True, stop=False))
                                fns.append(lambda o=o, kb=kb: nc.tensor.matmul(o, lhsT=ident[:], rhs=amask[:, var, kb, :], start=False, stop=True))
                            K.group(PE, fns, reads=[qT_tk, const_tk, mc_tk] + kT_tk, writes=stk)
                            for (kb, sl) in kbs:
                                K.op(ACT, lambda: nc.scalar.activation(out=PTs[:, g, kb, :], in_=ps.f32(sb2 + kb), func=AF.Exp, scale=0.125,
                                                                       bias=negc[:, 0:1]),
                                     reads=[stk[kb], mc_tk], writes=[PT_tk])
                            ps.free(sb2, 2)
                            yield
                        ob2, otk = yield from galloc(2)
                        fns = []
                        for hh in range(8):
                            g, hq = hh // 4, hh % 4
                            for i, (kb, sl) in enumerate(kbs):
                                fns.append(lambda g=g, hq=hq, kb=kb, sl=sl, i=i: nc.tensor.matmul(
                                    ps.f32(ob2 + g)[:, hq * 65:(hq + 1) * 65], lhsT=PTs[:, g, kb, hq * 128:(hq + 1) * 128],
                                    rhs=vbe[:, sl, g, :], start=(i == 0), stop=(i == len(kbs) - 1)))
                        K.group(PE, fns, reads=[PT_tk] + vb_tk, writes=otk)
                        yield
                        O4 = ps.f32(ob2, 2).rearrange("p (b c) -> p b c", c=512)[:, :, 0:260].rearrange("p b (h e) -> p b h e", e=65)
                        K.op(DVE, lambda: nc.vector.tensor_tensor(out=dn[:, 0, :].rearrange("p (b h) -> p b h", h=4), in0=O4[:, :, :, 64],
                                                                  in1=esink[:].rearrange("p (b h) -> p b h", h=4), op=ALU.add),
                             reads=otk + [mc_tk], writes=[dn_tk])
                        K.op(DVE, lambda: nc.vector.reciprocal(out=dn[:, 1, :], in_=dn[:, 0, :]), reads=[dn_tk], writes=[dn_tk])
                        K.op(DVE, lambda: nc.vector.tensor_tensor(out=atm[:].rearrange("p (b h d) -> p b h d", h=4, d=64), in0=O4[:, :, :, 0:64],
                                                                  in1=dn[:, 1, :].rearrange("p (b h) -> p b h", h=4).unsqueeze(3).broadcast_to([128, 2, 4, 64]),
                                                                  op=ALU.mult),
                             reads=otk + [dn_tk], writes=[atm_tk])
                        ps.free(ob2, 2)
                        yield
                        tries = 0
                        while blk > 0 and not o_emitted.get((blk - 1, tt), False):
                            tries += 1
                            assert tries < 10000
                            yield
                        ab_, atk = yield from galloc(1)
                        av = ps.bf(ab_)
                        K.group(PE, [(lambda j=j: nc.tensor.transpose(av[:, j * 128:(j + 1) * 128], atm[:, j * 128:(j + 1) * 128], ident[:]))
                                     for j in range(4)], reads=[atm_tk, const_tk], writes=atk)
                        K.op(ACT, lambda: nc.scalar.activation(out=catT[:, 0:4, cs], in_=av[:, 0:512].rearrange("p (j t) -> p j t", t=128),
                                                               func=AF.Copy), reads=atk, writes=cat_tk[0:4])
                        ps.free(ab_)
                        yield

                def state_step(n, ubank, utk, sdst, sdst_tk):
                    K.op(DVE, lambda: nc.vector.tensor_tensor(out=st32[:], in0=st32[:], in1=dec[:, :, n:n + 1].broadcast_to([128, 4, 128]),
                                                              op=ALU.mult), reads=st32_tk + hb_tk, writes=st32_tk)
                    K.op(DVE, lambda: nc.vector.tensor_tensor(out=st32[:], in0=st32[:], in1=ps.f32(ubank).rearrange("p (c e) -> p c e", e=128),
                                                              op=ALU.add), reads=st32_tk + utk, writes=st32_tk)
                    K.op(ACT, lambda: nc.scalar.activation(out=sdst[:], in_=st32[:], func=AF.Copy), reads=st32_tk, writes=[sdst_tk])

                k2_r = Rot(K, pe_, "k2AB", [128, 2, 4, 128], BF16, 2)
                scm_r = Rot(K, pe_, "scm2", [128, 4, 128], BF16, 2)

                def gen_H(blk):
                    pre = {}
                    post = {}

                    def h_pre(tt):
                        cs = slice(tt * 128, (tt + 1) * 128)
                        vt = vtm[:, tt, :]
                        vt_tk = vtm_tk[tt]
                        k2, k2_tk, _ = k2_r.next()
                        sc_, sc_tk, _ = scm_r.next()
                        kb_, ktk = yield from galloc(1)
                        kv = ps.bf(kb_)
                        K.group(PE, [(lambda c=c: nc.tensor.transpose(kv[:, c * 128:(c + 1) * 128], k2T[:, c, cs], ident[:])) for c in range(4)],
                                reads=hb_tk + k2w_tk + [const_tk], writes=ktk)
                        kv3 = kv[:, 0:512].rearrange("p (c d) -> p c d", d=128)
                        K.op(DVE, lambda: nc.vector.tensor_scalar(out=k2[:, 0], in0=kv3, scalar1=rowm[:, 0:1], scalar2=None, op0=ALU.mult),
                             reads=ktk + [const_tk], writes=[k2_tk])
                        K.op(DVE, lambda: nc.vector.tensor_scalar(out=k2[:, 1], in0=kv3, scalar1=rowm[:, 1:2], scalar2=None, op0=ALU.mult),
                             reads=ktk + [const_tk], writes=[k2_tk])
                        ps.free(kb_)
                        sb_2, sctk = yield from galloc(1)
                        scv = ps.f32(sb_2)
                        K.group(PE, [(lambda c=c: nc.tensor.matmul(scv[:, c * 128:(c + 1) * 128], lhsT=kt[:, c, cs], rhs=qt[:, c, cs],
                                                                 start=True, stop=True)) for c in range(4)],
                                reads=hb_tk + ktw_tk, writes=sctk)
                        yield
                        K.op(DVE, lambda: nc.vector.tensor_tensor(out=sc_[:], in0=scv.rearrange("p (c t) -> p c t", t=128), in1=hmask[:], op=ALU.mult),
                             reads=sctk + [mc_tk], writes=[sc_tk])
                        ps.free(sb_2)
                        ua, uatk = yield from galloc(1)
                        ub, ubtk = yield from galloc(1)
                        K.group(PE, [(lambda c=c: nc.tensor.matmul(ps.f32(ua)[:, c * 128:(c + 1) * 128], lhsT=k2[:, 0, c, :],
                                                                 rhs=vt[:, c * 128:(c + 1) * 128], start=True, stop=True)) for c in range(4)],
                                reads=[k2_tk, vt_tk], writes=uatk)
                        K.group(PE, [(lambda c=c: nc.tensor.matmul(ps.f32(ub)[:, c * 128:(c + 1) * 128], lhsT=k2[:, 1, c, :],
                                                                 rhs=vt[:, c * 128:(c + 1) * 128], start=True, stop=True)) for c in range(4)],
                                reads=[k2_tk, vt_tk], writes=ubtk)
                        pre[tt] = (sc_, sc_tk, ua, uatk, ub, ubtk)
                        yield

                    def h_chain(tt):
                        vt = vtm[:, tt, :]
                        vt_tk = vtm_tk[tt]
                        sc_, sc_tk, ua, uatk, ub, ubtk = pre[tt]
                        oo, ootk = yield from galloc(1)
                        oov = ps.f32(oo)
                        stA, stA_tk = state["stb"]
                        fns = []
                        for c in range(4):
                            fns.append(lambda c=c: nc.tensor.matmul(oov[:, c * 128:(c + 1) * 128], lhsT=vt[:, c * 128:(c + 1) * 128],
                                                                    rhs=sc_[:, c, :], start=(c == 0), stop=False, skip_group_check=True))
                        for c in range(4):
                            fns.append(lambda c=c: nc.tensor.matmul(oov[:, c * 128:c * 128 + 64], lhsT=stA[:, c, :],
                                                                    rhs=qt[:, c, tt * 128:tt * 128 + 64], start=False, stop=False,
                                                                    skip_group_check=True))
                        K.group(PE, fns, reads=[vt_tk, sc_tk, stA_tk] + hb_tk, writes=ootk)
                        stB, stB_tk, _ = stb_rot.next()
                        state_step(2 * tt, ua, uatk, stB, stB_tk)
                        ps.free(ua)
                        yield
                        K.group(PE, [(lambda c=c: nc.tensor.matmul(oov[:, c * 128 + 64:(c + 1) * 128], lhsT=stB[:, c, :],
                                                                 rhs=qt[:, c, tt * 128 + 64:(tt + 1) * 128], start=False, stop=True,
                                                                 skip_group_check=True)) for c in range(4)],
                                reads=[stB_tk] + hb_tk, writes=ootk)
                        stC, stC_tk, _ = stb_rot.next()
                        state_step(2 * tt + 1, ub, ubtk, stC, stC_tk)
                        ps.free(ub)
                        state["stb"] = (stC, stC_tk)
                        post[tt] = (oo, ootk)
                        yield

                    def h_post(tt):
                        cs = slice(tt * 128, (tt + 1) * 128)
                        oo, ootk = post[tt]
                        oov = ps.f32(oo)
                        tries = 0
                        while blk > 0 and not o_emitted.get((blk - 1, tt), False):
                            tries += 1
                            assert tries < 100000
                            yield
                        K.op(ACT, lambda: nc.scalar.activation(out=sqr[:], in_=oov, func=AF.Square), reads=ootk, writes=[sqr_tk])
                        nb_, ntk = yield from galloc(1)
                        K.group(PE, [lambda: nc.tensor.matmul(ps.f32(nb_), lhsT=onesb[:], rhs=sqr[:], start=True, stop=True)],
                                reads=[sqr_tk, const_tk], writes=ntk)
                        yield
                        K.op(ACT, lambda: nc.scalar.activation(out=rsb[:], in_=ps.f32(nb_), func=AF.Ln, scale=1.0 / 128, bias=eps_t[:, 0:1]),
                             reads=ntk + [const2_tk], writes=[rsb_tk])
                        ps.free(nb_)
                        K.op(ACT, lambda: nc.scalar.activation(out=rsb[:], in_=rsb[:], func=AF.Exp, scale=-0.5), reads=[rsb_tk], writes=[rsb_tk])
                        K.op(DVE, lambda: nc.vector.scalar_tensor_tensor(out=t1[:], in0=oov, scalar=gnh[:, 0:1], in1=rsb[:], op0=ALU.mult, op1=ALU.mult),
                             reads=ootk + [rsb_tk, par_tk], writes=[t1_tk])
                        ps.free(oo)
                        K.op(DVE, lambda: nc.vector.tensor_tensor(out=catT[:, 4:8, cs], in0=t1[:].rearrange("p (c t) -> p c t", t=128),
                                                                  in1=gs[:, :, cs], op=ALU.mult),
                             reads=[t1_tk] + hb_tk, writes=cat_tk[4:8])
                        yield

                    yield from h_pre(0)
                    for tt in range(4):
                        yield from h_chain(tt)
                        if tt + 1 < 4:
                            yield from h_pre(tt + 1)
                        yield from h_post(tt)

                def gen_O(blk):
                    for tt in range(4):
                        yb, ytk = yield from galloc(2)
                        fns = []
                        for n in range(2):
                            o = ps.f32(yb + n)
                            for kc in range(8):
                                fns.append(lambda o=o, n=n, kc=kc: nc.tensor.matmul(
                                    o, lhsT=catT[:, kc, tt * 128:(tt + 1) * 128], rhs=Wmo[:, kc, n * 512:(n + 1) * 512],
                                    start=(kc == 0), stop=(kc == 7)))
                        K.group(PE, fns, reads=cat_tk + [wmo_tk], writes=ytk)
                        o_emitted[(blk, tt)] = True
                        yield
                        epi.run(blk * 4 + tt, yb, ytk)
                        yield

                interleave(gen_prologue(0))
                for blk in range(NB):
                    fl = {"ptanh": False, "pln": False}

                    def gen_Pall(blk=blk, fl=fl):
                        yield from gen_P(blk)
                        fl["ptanh"] = True
                        live = [gen_Pln([0, 2]), gen_Pln([1, 3])]
                        while live:
                            for g in list(live):
                                try:
                                    next(g)
                                except StopIteration:
                                    live.remove(g)
                            yield
                        fl["pln"] = True

                    def gated(cond, gen):
                        tries = 0
                        while not cond():
                            tries += 1
                            assert tries < 100000, "gated chain never released"
                            yield
                        yield from gen

                    def gen_Aseq(tts, blk=blk):
                        for tt in tts:
                            yield from gen_Amain(blk, tt)

                    gens = [gen_Pall(), gen_Apre(blk, [0, 1, 2, 3]), gen_Aseq([0, 2]), gen_Aseq([1, 3]),
                            gated(lambda fl=fl: fl["pln"], gen_H(blk))]
                    if blk > 0:
                        gens.append(gated(lambda fl=fl: fl["ptanh"], gen_O(blk - 1)))
                    if blk + 1 < NB:
                        gens.append(gated(lambda fl=fl, blk=blk: fl["ptanh"] and all(pre_done.get(blk * 4 + t, False) for t in range(4)),
                                          gen_prologue(blk + 1)))
                    interleave(*gens)
                interleave(gen_O(NB - 1))
                K.barrier()

        dsts = {1: out_d if stop_after == 1 else x1_d, 2: out_d if stop_after == 2 else x2_d, 3: out_d}
        names = {1: "out" if stop_after == 1 else "x1", 2: "out" if stop_after == 2 else "x2", 3: "out"}
        if stop_after == 0:
            dbg = K.sb("dbg", [128, D], F32)
            dbg_tk = Tk("dbg")
            dsem = K.newsem("dbg")
            K.op(DVE, lambda: nc.vector.tensor_copy(out=dbg[:], in_=G[:, 0, :]), reads=[par_tk], writes=[dbg_tk])
            K.dma(SP, out=out_d[0:128, :], in_=dbg[:], reads=[dbg_tk], writes=[], sem=dsem)
            K.op(DVE, lambda: nc.vector.tensor_copy(out=dbg[:, 0:24], in_=a_col[:].rearrange("p a b -> p (a b)")), reads=[par_tk, dbg_tk], writes=[dbg_tk])
            K.op(DVE, lambda: nc.vector.tensor_copy(out=dbg[:, 24:48], in_=sh_col[:].rearrange("p a b -> p (a b)")), reads=[par_tk, dbg_tk], writes=[dbg_tk])
            K.op(DVE, lambda: nc.vector.tensor_copy(out=dbg[:, 48:52], in_=lb[:]), reads=[par_tk, dbg_tk], writes=[dbg_tk])
            K.dma(SP, out=out_d[128:256, :], in_=dbg[:], reads=[dbg_tk], writes=[], sem=dsem)
            K.barrier()
            return nc
        ffn_phase(0, 0, x_d, "x", dsts[1], names[1])
        if stop_after >= 2:
            mixer_phase(x1_d, "x1", dsts[2], names[2])
        if stop_after >= 3:
            ffn_phase(1, 2, x2_d, "x2", out_d, "out")
        K.barrier()
        print("instructions:", K.n_inst, "sems:", len(K.sems), "engine sem incs:", K.n_inc)
        new_plan = {sm.idx: sorted(sm.waited) for sm in K.sems if sm.idx in K.eng_sem_idx}
    if want_plan:
        return new_plan
    return nc


def build_thin(stop_after=3):
    plan = build(stop_after, plan=None, want_plan=True)
    return build(stop_after, plan=plan)


def _consts():
    bf = ml_dtypes.bfloat16
    ident = np.eye(128, dtype=np.float32).astype(bf)
    onesb = np.ones((128, 128), np.float32).astype(bf)
    NEG = -1e30
    am = np.zeros((128, 2, 256), np.float32)
    am[0:64, 0, 192:256] = NEG
    am[64:128, 0, 0:64] = NEG
    am[0:64, 1, 64:128] = NEG
    am[64:128, 1, 128:192] = NEG
    s = np.arange(128)[:, None]
    t = np.arange(128)[None, :]
    hm = ((s // 64 == t // 64) & (s <= t)).astype(np.float32)
    hmask = np.broadcast_to(hm[:, None, :], (128, 4, 128)).copy()
    rmask = np.ones((128, 512), np.float32)
    rmask[:, ::64] = 0.0
    inv_freq = (1.0 / (np.float32(10000.0) ** (np.arange(0, 64, 2, dtype=np.float32) / np.float32(64)))).astype(np.float32)
    invf = np.broadcast_to(inv_freq[None, :], (128, 32)).copy()
    rowm = np.zeros((128, 2), np.float32)
    rowm[0:64, 0] = 1.0
    rowm[64:128, 1] = 1.0
    amT = np.zeros((128, 2, 2, 4, 128), np.float32)
    for v in range(2):
        for kb in range(2):
            amT[:, v, kb, :, :] = am[:, v, kb * 128:(kb + 1) * 128].T[:, None, :]
    amT = amT.reshape(128, 2, 2, 512)
    return dict(ident=ident, onesb=onesb, amask=amT.astype(bf), hmask=hmask.astype(bf), rmask=rmask, invf=invf, rowm=rowm)


def make_in_maps(x, c, positions, w_cond, b_cond, norm_pre, norm_post, ffn_w_in, ffn_w_out,
                 w_mix_in, w_mix_out, attn_sinks, hgrn_lb_logits, hgrn_gnorm):
    f = np.float32
    cst = _consts()
    shared = dict(
        w_cond=np.ascontiguousarray(w_cond[0], f), b_cond=np.ascontiguousarray(np.broadcast_to(np.asarray(b_cond[0], f)[None, :], (128, 9 * D))),
        npre=np.ascontiguousarray(np.asarray(norm_pre[0], f).reshape(3, 8, 128).transpose(2, 0, 1)),
        npost=np.ascontiguousarray(np.broadcast_to(np.asarray(norm_post[0], f)[None], (128, 3, D))),
        w_in=np.ascontiguousarray(ffn_w_in[0], f), w_out=np.ascontiguousarray(ffn_w_out[0], f),
        w_mi=np.ascontiguousarray(w_mix_in[0], f), w_mo=np.ascontiguousarray(w_mix_out[0], f),
        sinks=np.ascontiguousarray(np.broadcast_to(np.asarray(attn_sinks[0], f)[None], (128, 8))),
        lbl=np.ascontiguousarray(np.asarray(hgrn_lb_logits, f).reshape(2, 4, 128).transpose(2, 0, 1)),
        gn=np.ascontiguousarray(np.asarray(hgrn_gnorm[0], f)[:, None]),
        **cst)
    maps = []
    for b in range(8):
        m = dict(shared)
        m["x"] = np.ascontiguousarray(x[b], f)
        m["ccol"] = np.ascontiguousarray(np.asarray(c[b], f).reshape(8, 128).T)
        m["pos"] = np.ascontiguousarray(np.asarray(positions[b], np.int32).reshape(NT, 128).T)
        maps.append(m)
    return maps


def kernel(**inputs):
    nc = build_thin(3)
    maps = make_in_maps(**inputs)
    res = run_bass_kernel_spmd(nc, maps, core_ids=list(range(8)))
    return np.stack([np.asarray(r["out"], np.float32) for r in res.results], axis=0)
```

```python
import numpy as np
from contextlib import ExitStack
import ml_dtypes
import concourse.bass as bass
import concourse.mybir as mybir
from concourse.bass_utils import run_bass_kernel_spmd

F32 = mybir.dt.float32
BF16 = mybir.dt.bfloat16
I32 = mybir.dt.int32
AF = mybir.ActivationFunctionType
ALU = mybir.AluOpType
AX = mybir.AxisListType

S = 4096
D = 1024
DFF = 2816
NF = DFF // 128
TB = 512
NB = S // TB
NT = S // 128
EPS = 1e-6
PI = float(np.pi)
TWO_PI = float(2 * np.pi)


class Tk:
    __slots__ = ("w", "r", "name", "excl")

    def __init__(self, name="", excl=False):
        self.w = None
        self.r = {}
        self.name = name
        self.excl = excl


class Sem:
    def __init__(self, h, name, idx, needed):
        self.h = h
        self.cnt = 0
        self.name = name
        self.idx = idx
        self.needed = needed
        self.waited = set()
        self.rank = {n: i + 1 for i, n in enumerate(needed)} if needed is not None else None

    def value(self, n):
        return n if self.rank is None else self.rank[n]

    def signals(self, n):
        return True if self.rank is None else (n in self.rank)


class Eng:
    def __init__(self, name, h, sem):
        self.name = name
        self.h = h
        self.sem = sem
        self.seen = {}


class Kern:
    def __init__(self, nc, es, plan=None):
        self.nc = nc
        self.es = es
        self.plan = plan
        self.sems = []
        self.PE = Eng("pe", nc.tensor, self.newsem("pe"))
        self.ACT = Eng("act", nc.scalar, self.newsem("act"))
        self.DVE = Eng("dve", nc.vector, self.newsem("dve"))
        self.POOL = Eng("pool", nc.gpsimd, self.newsem("pool"))
        self.SP = Eng("sp", nc.sync, self.newsem("sp"))
        self.engs = [self.PE, self.ACT, self.DVE, self.POOL, self.SP]
        self.n_inst = 0
        self.n_inc = 0
        self.eng_sem_idx = [E.sem.idx for E in self.engs]

    def newsem(self, name):
        self.sid = getattr(self, "sid", 0) + 1
        idx = self.sid
        needed = None
        if self.plan is not None and idx in self.plan:
            needed = self.plan[idx]
        s = Sem(self.es.enter_context(self.nc.semaphore("m%d_%s" % (self.sid, name))), name, idx, needed)
        self.sems.append(s)
        return s

    def sb(self, name, shape, dt, es=None):
        self.uid = getattr(self, "uid", 0) + 1
        return (es or self.es).enter_context(self.nc.sbuf_tensor("s%d_%s" % (self.uid, name), shape, dt))

    def _wait(self, E, toks):
        for (s, v) in toks:
            if E.seen.get(id(s), 0) >= v:
                continue
            s.waited.add(v)
            E.h.wait_ge(s.h, s.value(v))
            E.seen[id(s)] = v

    def _deps(self, E, reads, writes):
        toks = []
        for t in reads:
            if t.w is not None:
                toks.append(t.w)
            if t.excl:
                for tok in t.r.values():
                    if tok[0] is not E.sem:
                        toks.append(tok)
        for t in writes:
            if t.w is not None and t.w[0] is not E.sem:
                toks.append(t.w)
            for tok in t.r.values():
                if tok[0] is not E.sem or E is not self.PE:
                    toks.append(tok)
        return toks

    def _mark(self, tok, reads, writes):
        for t in writes:
            t.w = tok
            t.r = {}
        for t in reads:
            t.r[id(tok[0])] = tok

    def op(self, E, fn, reads=(), writes=()):
        self._wait(E, self._deps(E, reads, writes))
        ins = fn()
        E.sem.cnt += 1
        if E.sem.signals(E.sem.cnt):
            ins.then_inc(E.sem.h, 1)
            self.n_inc += 1
        self._mark((E.sem, E.sem.cnt), reads, writes)
        self.n_inst += 1
        return ins

    def group(self, E, fns, reads=(), writes=()):
        self._wait(E, self._deps(E, reads, writes))
        ins = None
        for fn in fns:
            ins = fn()
            self.n_inst += 1
        E.sem.cnt += 1
        if E.sem.signals(E.sem.cnt):
            ins.then_inc(E.sem.h, 1)
            self.n_inc += 1
        self._mark((E.sem, E.sem.cnt), reads, writes)

    def dma(self, E, out, in_, reads, writes, sem, **kw):
        self._wait(E, self._deps(E, reads, writes))
        E.h.dma_start(out=out, in_=in_, **kw).then_inc(sem.h, 16)
        sem.cnt += 16
        self._mark((sem, sem.cnt), reads, writes)
        self.n_inst += 1

    def barrier(self):
        toks = [(s, s.cnt) for s in self.sems if s.cnt > 0]
        for E in self.engs:
            self._wait(E, [t for t in toks if t[0] is not E.sem])


class Psum:
    def __init__(self, K):
        nc = K.nc
        self.t = K.es.enter_context(nc.psum_tensor("psum_all", [128, 8 * 512], F32))
        self.tb = self.t.bitcast(BF16)
        self.tk = [Tk("bank%d" % i, excl=True) for i in range(8)]
        self.p = 0
        self.held = [False] * 8

    def try_alloc(self, n=1):
        for i in range(8):
            b = (self.p + i) % 8
            if b % n or b + n > 8:
                continue
            if any(self.held[b:b + n]):
                continue
            for j in range(b, b + n):
                self.held[j] = True
            self.p = (b + n) % 8
            return b, self.tk[b:b + n]
        return None

    def alloc(self, n=1):
        r = self.try_alloc(n)
        assert r is not None, "PSUM exhausted (straight-line code must free before allocating)"
        return r

    def free(self, b, n=1):
        for j in range(b, b + n):
            assert self.held[j]
            self.held[j] = False

    def f32(self, b, n=1):
        return self.t[:, b * 512:(b + n) * 512]

    def bf(self, b, n=1):
        return self.tb[:, b * 1024:(b + n) * 1024]


class Rot:
    def __init__(self, K, es, name, shape, dt, n, dma=False):
        self.items = []
        for i in range(n):
            t = K.sb("%s%d" % (name, i), shape, dt, es)
            self.items.append((t, Tk("%s%d" % (name, i)), K.newsem("%s%d" % (name, i)) if dma else None))
        self.i = 0

    def next(self):
        it = self.items[self.i % len(self.items)]
        self.i += 1
        return it


def build(stop_after=3, plan=None, want_plan=False):
    nc = bass.Bass("TRN2", target_bir_lowering=False)

    def din(name, shape, dt=F32):
        return nc.dram_tensor(name, shape, dt, kind="ExternalInput").ap()

    x_d = din("x", [S, D])
    ccol_d = din("ccol", [128, 8])
    pos_d = din("pos", [128, NT], I32)
    wcond_d = din("w_cond", [D, 9 * D])
    bcond_d = din("b_cond", [128, 9 * D])
    npre_d = din("npre", [128, 3, 8])
    npost_d = din("npost", [128, 3, D])
    win_d = din("w_in", [2, D, 2 * DFF])
    wout_d = din("w_out", [2, DFF, D])
    wmi_d = din("w_mi", [D, 2816])
    wmo_d = din("w_mo", [D, D])
    sinks_d = din("sinks", [128, 8])
    lbl_d = din("lbl", [128, 2, 4])
    gn_d = din("gn", [128, 1])
    ident_d = din("ident", [128, 128], BF16)
    onesb_d = din("onesb", [128, 128], BF16)
    amask_d = din("amask", [128, 2, 2, 512], BF16)
    hmask_d = din("hmask", [128, 4, 128], BF16)
    rmask_d = din("rmask", [128, 512])
    invf_d = din("invf", [128, 32])
    rowm_d = din("rowm", [128, 2])
    out_d = nc.dram_tensor("out", [S, D], F32, kind="ExternalOutput").ap()
    x1_d = nc.dram_tensor("x1s", [S, D], F32).ap()
    x2_d = nc.dram_tensor("x2s", [S, D], F32).ap()
    rope_d = nc.dram_tensor("rope_s", [128, 2, NT * 32], F32).ap()
    w2in_s = nc.dram_tensor("w2in_s", [D, 2 * DFF], BF16).ap()
    w2out_s = nc.dram_tensor("w2out_s", [DFF, D], BF16).ap()
    wmi_s = nc.dram_tensor("wmi_s", [D, 2816], BF16).ap()
    wmo_s = nc.dram_tensor("wmo_s", [D, D], BF16).ap()

    with ExitStack() as es:
        K = Kern(nc, es, plan)
        PE, ACT, DVE, POOL, SP = K.PE, K.ACT, K.DVE, K.POOL, K.SP
        ps = Psum(K)
        dram_tk = {}

        def dtk(name, tile):
            key = (name, tile)
            if key not in dram_tk:
                dram_tk[key] = Tk("%s_%d" % key)
            return dram_tk[key]

        ident = K.sb("ident", [128, 128], BF16)
        onesb = K.sb("onesb", [128, 128], BF16)
        rowm = K.sb("rowm", [128, 2], F32)
        sinks = K.sb("sinks", [128, 8], F32)
        gn = K.sb("gn", [128, 1], F32)
        lb = K.sb("lb", [128, 4], F32)
        oml = K.sb("oml", [128, 4], F32)
        a_col = K.sb("a_col", [128, 3, 8], F32)
        sh_col = K.sb("sh_col", [128, 3, 8], F32)
        G = K.sb("G", [128, 3, D], BF16)
        c01 = K.sb("c01", [128, 2, 4], F32)
        gnh = K.sb("gnh", [128, 1], F32)
        const_tk = Tk("consts")
        par_tk = Tk("params")
        rope_tk = Tk("rope")

        with ExitStack() as ss:
            ld = K.newsem("setup_ld")
            ccol = K.sb("ccol", [128, 8], F32, ss)
            ca = K.sb("ca", [128, 8], F32, ss)
            posi = K.sb("posi", [128, NT], I32, ss)
            npre = K.sb("npre", [128, 3, 8], F32, ss)
            npost = K.sb("npost", [128, 3, D], F32, ss)
            lbl = K.sb("lbl", [128, 2, 4], F32, ss)
            invf = K.sb("invf", [128, 32], F32, ss)
            setup_tk = Tk("setup_in")
            cosT = K.sb("cosT", [128, NT, 32], F32, ss)
            sinT = K.sb("sinT", [128, NT, 32], F32, ss)
            loads = [(ident, ident_d), (onesb, onesb_d),
                     (rowm, rowm_d), (sinks, sinks_d), (gn, gn_d), (ccol, ccol_d), (posi, pos_d), (npre, npre_d),
                     (npost, npost_d), (lbl, lbl_d), (invf, invf_d)]
            for (t, d_) in loads:
                nc.sync.dma_start(out=t[:], in_=d_).then_inc(ld.h, 16)
                ld.cnt += 16
            setup_tk.w = (ld, ld.cnt)
            const_tk.w = (ld, ld.cnt)

            K.op(ACT, lambda: nc.scalar.activation(out=ca[:], in_=ccol[:], func=AF.Silu), reads=[setup_tk], writes=[par_tk])
            K.op(DVE, lambda: nc.vector.tensor_tensor(out=lb[:], in0=lbl[:, 0, :], in1=lbl[:, 1, :], op=ALU.subtract),
                 reads=[setup_tk], writes=[par_tk])
            K.op(ACT, lambda: nc.scalar.activation(out=lb[:], in_=lb[:], func=AF.Sigmoid), reads=[par_tk], writes=[par_tk])
            K.op(DVE, lambda: nc.vector.tensor_scalar(out=oml[:], in0=lb[:], scalar1=-1.0, scalar2=1.0, op0=ALU.mult, op1=ALU.add),
                 reads=[par_tk], writes=[par_tk])
            K.op(DVE, lambda: nc.vector.tensor_scalar(out=c01[:, 1, :], in0=oml[:], scalar1=0.5, scalar2=None, op0=ALU.mult),
                 reads=[par_tk], writes=[par_tk])
            K.op(DVE, lambda: nc.vector.tensor_tensor(out=c01[:, 0, :], in0=lb[:], in1=c01[:, 1, :], op=ALU.add),
                 reads=[par_tk], writes=[par_tk])
            K.op(DVE, lambda: nc.vector.tensor_scalar(out=gnh[:], in0=gn[:], scalar1=0.5, scalar2=None, op0=ALU.mult),
                 reads=[setup_tk, par_tk], writes=[par_tk])

            posf = K.sb("posf", [128, NT], F32, ss)
            ang = K.sb("ang", [128, NT, 32], F32, ss)
            rr = K.sb("rr", [128, NT, 32], F32, ss)
            rc = K.sb("rc", [128, NT, 32], F32, ss)
            kf = K.sb("kf", [128, NT, 32], F32, ss)
            ki = K.sb("ki", [128, NT, 32], I32, ss)
            mk = K.sb("mk", [128, NT, 32], F32, ss)
            rt = Tk("ropetmp")

            def V(fn, reads=(), writes=()):
                K.op(DVE, fn, reads=list(reads) + [rt, setup_tk], writes=list(writes) + [rt])

            V(lambda: nc.vector.tensor_copy(out=posf[:], in_=posi[:]))
            V(lambda: nc.vector.tensor_tensor(out=ang[:], in0=posf[:].unsqueeze(2).broadcast_to([128, NT, 32]),
                                              in1=invf[:].unsqueeze(1).broadcast_to([128, NT, 32]), op=ALU.mult))
            V(lambda: nc.vector.tensor_scalar(out=kf[:], in0=ang[:], scalar1=1.0 / TWO_PI, scalar2=None, op0=ALU.mult))
            V(lambda: nc.vector.tensor_copy(out=ki[:], in_=kf[:]))
            V(lambda: nc.vector.tensor_copy(out=kf[:], in_=ki[:]))
            C1 = 6.28125
            C2 = TWO_PI - 6.28125
            V(lambda: nc.vector.scalar_tensor_tensor(out=rr[:], in0=kf[:], scalar=-C1, in1=ang[:], op0=ALU.mult, op1=ALU.add))
            V(lambda: nc.vector.scalar_tensor_tensor(out=rr[:], in0=kf[:], scalar=-C2, in1=rr[:], op0=ALU.mult, op1=ALU.add))

            def fold(t):
                V(lambda: nc.vector.tensor_scalar(out=mk[:], in0=t[:], scalar1=PI, scalar2=-TWO_PI, op0=ALU.is_gt, op1=ALU.mult))
                V(lambda: nc.vector.tensor_tensor(out=t[:], in0=t[:], in1=mk[:], op=ALU.add))
                V(lambda: nc.vector.tensor_scalar(out=mk[:], in0=t[:], scalar1=-PI, scalar2=TWO_PI, op0=ALU.is_lt, op1=ALU.mult))
                V(lambda: nc.vector.tensor_tensor(out=t[:], in0=t[:], in1=mk[:], op=ALU.add))
                V(lambda: nc.vector.tensor_scalar(out=t[:], in0=t[:], scalar1=PI, scalar2=-PI, op0=ALU.min, op1=ALU.max))

            fold(rr)
            V(lambda: nc.vector.tensor_scalar(out=rc[:], in0=rr[:], scalar1=PI / 2, scalar2=None, op0=ALU.add))
            fold(rc)
            K.op(ACT, lambda: nc.scalar.activation(out=sinT[:], in_=rr[:], func=AF.Sin), reads=[rt], writes=[rope_tk])
            K.op(ACT, lambda: nc.scalar.activation(out=cosT[:], in_=rc[:], func=AF.Sin), reads=[rt], writes=[rope_tk])
            cab = K.sb("cab", [128, 8, 128], F32, ss)
            identf = K.sb("identf", [128, 128], F32, ss)
            modbc = K.sb("modbc", [128, 9 * D], F32, ss)
            dg = K.sb("dg", [128, 8, 128], F32, ss)
            K.op(DVE, lambda: nc.vector.tensor_copy(out=cab[:], in_=ca[:].unsqueeze(2).broadcast_to([128, 8, 128])),
                 reads=[par_tk], writes=[par_tk])
            K.op(DVE, lambda: nc.vector.tensor_copy(out=identf[:], in_=ident[:]), reads=[setup_tk], writes=[setup_tk])
            wc_rot = Rot(K, ss, "wc", [128, 8, 512], F32, 2, dma=True)
            bc_rot = Rot(K, ss, "bc", [128, 512], F32, 2, dma=True)
            wc_view = wcond_d.rearrange("(k p) n -> p k n", p=128)
            mod_tk = Tk("modbc")
            for cb in range(18):
                wc, wc_tk, wc_sem = wc_rot.next()
                bc, bc_tk, bc_sem = bc_rot.next()
                K.dma(SP, out=wc[:], in_=wc_view[:, :, cb * 512:(cb + 1) * 512], reads=[], writes=[wc_tk], sem=wc_sem)
                K.dma(SP, out=bc[:], in_=bcond_d[:, cb * 512:(cb + 1) * 512], reads=[], writes=[bc_tk], sem=bc_sem)
                b, btk = ps.alloc(1)
                pv = ps.f32(b)
                K.group(PE, [(lambda k=k: nc.tensor.matmul(pv, lhsT=cab[:, k, :], rhs=wc[:, k, :],
                                                         start=(k == 0), stop=(k == 7))) for k in range(8)],
                        reads=[par_tk, wc_tk], writes=btk)
                K.op(DVE, lambda: nc.vector.tensor_tensor(out=modbc[:, cb * 512:(cb + 1) * 512], in0=pv, in1=bc[:], op=ALU.add),
                     reads=btk + [bc_tk, mod_tk], writes=[mod_tk])
                ps.free(b)
            for k in range(3):
                for which, v in (("sh", 3 * k), ("sc", 3 * k + 1)):
                    K.op(DVE, lambda: nc.vector.tensor_tensor(out=dg[:], in0=modbc[:, v * D:(v + 1) * D].rearrange("p (j q) -> p j q", q=128),
                                                              in1=identf[:].unsqueeze(1).broadcast_to([128, 8, 128]), op=ALU.mult),
                         reads=[mod_tk, setup_tk, par_tk], writes=[par_tk])
                    dstc = sh_col if which == "sh" else a_col
                    K.op(DVE, lambda: nc.vector.tensor_reduce(out=dstc[:, k, :], in_=dg[:], axis=AX.X, op=ALU.add),
                         reads=[par_tk], writes=[par_tk])
                K.op(DVE, lambda: nc.vector.scalar_tensor_tensor(out=a_col[:, k, :], in0=a_col[:, k, :], scalar=1.0, in1=npre[:, k, :],
                                                                 op0=ALU.add, op1=ALU.mult),
                     reads=[par_tk, setup_tk], writes=[par_tk])
                resw = 1.0 if k == 1 else 0.5
                K.op(DVE, lambda: nc.vector.scalar_tensor_tensor(out=G[:, k, :], in0=modbc[:, (3 * k + 2) * D:(3 * k + 3) * D], scalar=resw,
                                                                 in1=npost[:, k, :], op0=ALU.mult, op1=ALU.mult),
                     reads=[mod_tk, setup_tk, par_tk], writes=[par_tk])

            rsem = K.newsem("ropest")
            K.dma(SP, out=rope_d[:, 0, :], in_=cosT[:].rearrange("p t f -> p (t f)"), reads=[rope_tk], writes=[dtk("rope", 0)], sem=rsem)
            K.dma(SP, out=rope_d[:, 1, :], in_=sinT[:].rearrange("p t f -> p (t f)"), reads=[rope_tk], writes=[dtk("rope", 0)], sem=rsem)
            K.barrier()

        def issue_cast_dma(dst, src, tk, sem):
            K.dma(POOL, out=dst, in_=src, reads=[], writes=[tk], sem=sem)

        eps_t = K.sb("eps_t", [128, 2], F32)
        const2_tk = Tk("const2")
        K.op(DVE, lambda: nc.vector.memset(eps_t[:, 0:1], EPS), writes=[const2_tk])
        K.op(DVE, lambda: nc.vector.memset(eps_t[:, 1:2], 0.0), writes=[const2_tk])

        def rstd_ops(st, st_tk, n, lnexp):
            if lnexp:
                K.op(ACT, lambda: nc.scalar.activation(out=st[:, 1:2], in_=st[:, 0:1], func=AF.Ln, scale=1.0 / n, bias=eps_t[:, 0:1]),
                     reads=[st_tk, const2_tk], writes=[st_tk])
                K.op(ACT, lambda: nc.scalar.activation(out=st[:, 2:3], in_=st[:, 1:2], func=AF.Exp, scale=-0.5),
                     reads=[st_tk], writes=[st_tk])
            else:
                K.op(ACT, lambda: nc.scalar.activation(out=st[:, 1:2], in_=st[:, 0:1], func=AF.Sqrt, scale=1.0 / n, bias=eps_t[:, 0:1]),
                     reads=[st_tk, const2_tk], writes=[st_tk])
                K.op(DVE, lambda: nc.vector.reciprocal(out=st[:, 2:3], in_=st[:, 1:2]), reads=[st_tk], writes=[st_tk])

        class Pro:
            def __init__(self, es_, src, srcname, kidx, lnexp, tag, nxin=2, evac_act=False):
                self.src, self.srcname, self.kidx, self.lnexp = src, srcname, kidx, lnexp
                self.evac_act = evac_act
                self.xin_rot = Rot(K, es_, tag + "xin", [128, D], F32, nxin, dma=True)
                self.xn_rot = Rot(K, es_, tag + "xn", [128, D], BF16, 2)
                self.st_rot = Rot(K, es_, tag + "st", [128, 4], F32, 4)
                self.tmpf = K.sb(tag + "tmpf", [128, 8, 128], F32, es_)
                self.tmpf_tk = Tk(tag + "tmpf")

            def elem(self, tile):
                xin, xin_tk, xin_sem = self.xin_rot.next()
                K.dma(SP, out=xin[:], in_=self.src[tile * 128:(tile + 1) * 128, :], reads=[dtk(self.srcname, tile)],
                      writes=[xin_tk], sem=xin_sem)
                st, st_tk, _ = self.st_rot.next()
                xn, xn_tk, _ = self.xn_rot.next()
                K.op(ACT, lambda: nc.scalar.activation(out=xn[:], in_=xin[:], func=AF.Square, accum_out=st[:, 0:1]),
                     reads=[xin_tk], writes=[xn_tk, st_tk])
                rstd_ops(st, st_tk, D, self.lnexp)
                K.op(DVE, lambda: nc.vector.tensor_scalar(out=xn[:], in0=xin[:], scalar1=st[:, 2:3], scalar2=None, op0=ALU.mult),
                     reads=[xin_tk, st_tk], writes=[xn_tk])
                return xn, xn_tk

            def pe(self, xn, xn_tk, h, h_tks, tt):
                b, tks = ps.alloc(1)
                tv = ps.bf(b)
                K.group(PE, [(lambda dc=dc: nc.tensor.transpose(tv[:, dc * 128:(dc + 1) * 128], xn[:, dc * 128:(dc + 1) * 128], ident[:]))
                             for dc in range(8)], reads=[xn_tk, const_tk], writes=tks)
                kidx = self.kidx
                if self.evac_act:
                    for dc in range(8):
                        K.op(ACT, lambda: nc.scalar.activation(out=h[:, dc, tt * 128:(tt + 1) * 128], in_=tv[:, dc * 128:(dc + 1) * 128],
                                                               func=AF.Identity, scale=a_col[:, kidx, dc:dc + 1], bias=sh_col[:, kidx, dc:dc + 1]),
                             reads=tks + [par_tk], writes=h_tks)
                    ps.free(b)
                    return
                K.op(DVE, lambda: nc.vector.tensor_tensor(out=self.tmpf[:], in0=tv.rearrange("p (c t) -> p c t", t=128),
                                                          in1=a_col[:, kidx, :].unsqueeze(2).broadcast_to([128, 8, 128]), op=ALU.mult),
                     reads=tks + [par_tk], writes=[self.tmpf_tk])
                ps.free(b)
                K.op(DVE, lambda: nc.vector.tensor_tensor(out=h[:, :, tt * 128:(tt + 1) * 128], in0=self.tmpf[:],
                                                          in1=sh_col[:, kidx, :].unsqueeze(2).broadcast_to([128, 8, 128]), op=ALU.add),
                     reads=[self.tmpf_tk, par_tk], writes=h_tks)

        class Epi:
            def __init__(self, es_, src, srcname, dst, dstname, kidx, lnexp, tag, junk):
                self.src, self.srcname, self.dst, self.dstname, self.kidx, self.lnexp = src, srcname, dst, dstname, kidx, lnexp
                self.junk = junk
                self.xre_rot = Rot(K, es_, tag + "xre", [128, D], F32, 1, dma=True)
                self.ob_rot = Rot(K, es_, tag + "ob", [128, D], F32, 1, dma=True)
                self.st_rot = Rot(K, es_, tag + "est", [128, 4], F32, 4)

            def run(self, tile, yb, ytk):
                yv = ps.f32(yb, 2)
                kidx = self.kidx
                xre, xre_tk, xre_sem = self.xre_rot.next()
                K.dma(SP, out=xre[:], in_=self.src[tile * 128:(tile + 1) * 128, :], reads=[dtk(self.srcname, tile)], writes=[xre_tk],
                      sem=xre_sem)
                st, st_tk, _ = self.st_rot.next()
                ob, ob_tk, ob_sem = self.ob_rot.next()
                jk, jk_tk, _ = self.junk.next()
                K.op(ACT, lambda: nc.scalar.activation(out=jk[:], in_=yv, func=AF.Square, accum_out=st[:, 0:1]),
                     reads=ytk, writes=[jk_tk, st_tk])
                rstd_ops(st, st_tk, D, self.lnexp)
                K.op(DVE, lambda: nc.vector.scalar_tensor_tensor(out=ob[:], in0=yv, scalar=st[:, 2:3], in1=G[:, kidx, :],
                                                                 op0=ALU.mult, op1=ALU.mult),
                     reads=ytk + [st_tk, par_tk], writes=[ob_tk])
                ps.free(yb, 2)
                K.op(POOL, lambda: nc.gpsimd.tensor_tensor(out=ob[:], in0=ob[:], in1=xre[:], op=ALU.add),
                     reads=[ob_tk, xre_tk], writes=[ob_tk])
                K.dma(POOL, out=self.dst[tile * 128:(tile + 1) * 128, :], in_=ob[:], reads=[ob_tk], writes=[dtk(self.dstname, tile)],
                      sem=ob_sem)

        def interleave(*gens):
            gens = list(gens)
            while gens:
                for g in list(gens):
                    try:
                        next(g)
                    except StopIteration:
                        gens.remove(g)

        def ffn_phase(fi, kidx, src, srcname, dst, dstname):
            with ExitStack() as pe_:
                Win = K.sb("Win", [128, 8, 2 * DFF], BF16, pe_)
                Wout = K.sb("Wout", [128, NF, D], BF16, pe_)
                groups = [(0, 6), (6, 12), (12, 17), (17, 22)]
                win_tk = [Tk("win%d" % g) for g in range(4)]
                wout_tk = [Tk("wout%d" % g) for g in range(2)]
                if fi == 0:
                    wv = win_d[fi].rearrange("(k p) n -> p k n", p=128)
                    wov = wout_d[fi].rearrange("(f p) n -> p f n", p=128)
                    for g, (f0, f1) in enumerate(groups):
                        sem = K.newsem("win%d_%d" % (fi, g))
                        for off in (0, DFF):
                            issue_cast_dma(Win[:, :, off + f0 * 128:off + f1 * 128], wv[:, :, off + f0 * 128:off + f1 * 128],
                                           win_tk[g], sem)
                    for g, (f0, f1) in enumerate([(0, 11), (11, 22)]):
                        sem = K.newsem("wout%d_%d" % (fi, g))
                        issue_cast_dma(Wout[:, f0:f1, :], wov[:, f0:f1, :], wout_tk[g], sem)

                    pcs = K.newsem("precast")

                    def precast(blk):
                        def cp(dst, src_, name):
                            K.dma(POOL, out=dst, in_=src_, reads=[], writes=[dtk(name, 0)], sem=pcs)
                        if blk == 2:
                            cp(wmi_s, wmi_d, "wmi_s")
                        elif blk == 3:
                            cp(wmo_s, wmo_d, "wmo_s")
                            cp(w2out_s[0:1408, :], wout_d[1][0:1408, :], "w2out_s")
                        elif blk == 4:
                            cp(w2out_s[1408:2816, :], wout_d[1][1408:2816, :], "w2out_s")
                            cp(w2in_s[0:256, :], win_d[1][0:256, :], "w2in_s")
                        elif blk == 5:
                            cp(w2in_s[256:512, :], win_d[1][256:512, :], "w2in_s")
                            cp(w2in_s[512:768, :], win_d[1][512:768, :], "w2in_s")
                        elif blk == 6:
                            cp(w2in_s[768:1024, :], win_d[1][768:1024, :], "w2in_s")
                else:
                    wv = w2in_s.rearrange("(k p) n -> p k n", p=128)
                    wov = w2out_s.rearrange("(f p) n -> p f n", p=128)
                    qi = 0
                    for g, (f0, f1) in enumerate(groups):
                        sem = K.newsem("win%d_%d" % (fi, g))
                        for off in (0, DFF):
                            K.dma(POOL, out=Win[:, :, off + f0 * 128:off + f1 * 128],
                                  in_=wv[:, :, off + f0 * 128:off + f1 * 128], reads=[dtk("w2in_s", 0)], writes=[win_tk[g]], sem=sem)
                            qi += 1
                    for g, (f0, f1) in enumerate([(0, 11), (11, 22)]):
                        sem = K.newsem("wout%d_%d" % (fi, g))
                        K.dma(POOL, out=Wout[:, f0:f1, :], in_=wov[:, f0:f1, :], reads=[dtk("w2out_s", 0)],
                              writes=[wout_tk[g]], sem=sem)
                        qi += 1

                def gof(f):
                    for g, (f0, f1) in enumerate(groups):
                        if f0 <= f < f1:
                            return g

                pro = Pro(pe_, src, srcname, kidx, False, "f")
                epi = Epi(pe_, src, srcname, dst, dstname, kidx, False, "f", pro.xn_rot)
                hs = [K.sb("h%d" % i, [128, 8, TB], BF16, pe_) for i in range(2)]
                h_tkss = [[Tk("h%d" % i)] for i in range(2)]
                act = K.sb("act", [128, NF, TB], BF16, pe_)
                act_tk = [Tk("act%d" % f) for f in range(NF)]
                sg_rot = Rot(K, pe_, "sg", [128, TB], F32, 1)

                for tt in range(4):
                    xn, xn_tk = pro.elem(tt)
                    pro.pe(xn, xn_tk, hs[0], h_tkss[0], tt)
                for blk in range(NB):
                    if fi == 0:
                        precast(blk)
                    h = hs[blk % 2]
                    h_tks = h_tkss[blk % 2]
                    hn = hs[(blk + 1) % 2]
                    hn_tks = h_tkss[(blk + 1) % 2]
                    pend = {}
                    for f in range(NF):
                        bg, tg = ps.alloc(1)
                        bu, tu = ps.alloc(1)
                        pg = ps.f32(bg)
                        pu = ps.f32(bu)
                        fns = []
                        for k in range(8):
                            fns.append(lambda k=k: nc.tensor.matmul(pg, lhsT=Win[:, k, f * 128:(f + 1) * 128], rhs=h[:, k, :],
                                                                    start=(k == 0), stop=(k == 7)))
                        for k in range(8):
                            fns.append(lambda k=k: nc.tensor.matmul(pu, lhsT=Win[:, k, DFF + f * 128:DFF + (f + 1) * 128],
                                                                    rhs=h[:, k, :], start=(k == 0), stop=(k == 7)))
                        K.group(PE, fns, reads=h_tks + [win_tk[gof(f)]], writes=tg + tu)
                        sg, sg_tk, _ = sg_rot.next()
                        K.op(ACT, lambda: nc.scalar.activation(out=sg[:], in_=pg, func=AF.Silu), reads=tg, writes=[sg_tk])
                        K.op(DVE, lambda: nc.vector.tensor_tensor(out=act[:, f, :], in0=sg[:], in1=pu, op=ALU.mult),
                             reads=[sg_tk] + tu, writes=[act_tk[f]])
                        ps.free(bg)
                        ps.free(bu)
                        if blk + 1 < NB:
                            if f % 4 == 0 and f // 4 < 4:
                                tt = f // 4
                                pend[tt] = pro.elem((blk + 1) * 4 + tt)
                            if f >= 7 and (f - 7) % 4 == 0 and (f - 7) // 4 < 4:
                                tt = (f - 7) // 4
                                pro.pe(pend[tt][0], pend[tt][1], hn, hn_tks, tt)
                    for tt in range(4):
                        yb, ytk = ps.alloc(2)
                        fns = []
                        for n in range(2):
                            o = ps.f32(yb + n)
                            for f in range(NF):
                                fns.append(lambda o=o, n=n, f=f: nc.tensor.matmul(
                                    o, lhsT=act[:, f, tt * 128:(tt + 1) * 128], rhs=Wout[:, f, n * 512:(n + 1) * 512],
                                    start=(f == 0), stop=(f == NF - 1)))
                        K.group(PE, fns, reads=act_tk + wout_tk, writes=ytk)
                        epi.run(blk * 4 + tt, yb, ytk)
                K.barrier()

        def mixer_phase(src, srcname, dst, dstname):
            kidx = 1
            with ExitStack() as pe_:
                Wmi = K.sb("Wmi", [128, 8, 2816], BF16, pe_)
                Wmo = K.sb("Wmo", [128, 8, D], BF16, pe_)
                wmi_tk = Tk("wmi")
                wmo_tk = Tk("wmo")
                wmiv = wmi_s.rearrange("(k p) n -> p k n", p=128)
                wmov = wmo_s.rearrange("(k p) n -> p k n", p=128)
                s1 = K.newsem("wmi")
                wmi_g = {}
                for (c0, c1) in ((1792, 2304), (1280, 1792), (768, 1280), (2304, 2816), (0, 768)):
                    gtk = Tk("wmi_%d" % c0)
                    gsem = K.newsem("wmi_%d" % c0)
                    K.dma(POOL, out=Wmi[:, :, c0:c1], in_=wmiv[:, :, c0:c1], reads=[dtk("wmi_s", 0)], writes=[gtk], sem=gsem)
                    wmi_g[c0] = gtk
                s2 = K.newsem("wmo")
                K.dma(POOL, out=Wmo[:], in_=wmov, reads=[dtk("wmo_s", 0)], writes=[wmo_tk], sem=s2)
                amask = K.sb("amask", [128, 2, 2, 512], BF16, pe_)
                hmask = K.sb("hmask", [128, 4, 128], BF16, pe_)
                rmask = K.sb("rmask", [128, 512], F32, pe_)
                cs_rot = Rot(K, pe_, "mcs", [128, 2, 128], F32, 2, dma=True)
                mc_tk = Tk("mconst")
                mcs = K.newsem("mconst")
                for (t_, d_) in ((amask[:], amask_d), (hmask[:], hmask_d), (rmask[:], rmask_d)):
                    K.dma(SP, out=t_, in_=d_, reads=[], writes=[mc_tk], sem=mcs)

                pro = Pro(pe_, src, srcname, kidx, True, "m", nxin=1, evac_act=True)
                epi = Epi(pe_, src, srcname, dst, dstname, kidx, True, "m", pro.xn_rot)
                h = K.sb("mh", [128, 8, TB], BF16, pe_)
                h_tks = [Tk("mh")]
                q2 = K.sb("q2", [128, 4, TB], F32, pe_)
                ff = K.sb("ff", [128, 4, TB], F32, pe_)
                bb = K.sb("bb", [128, 4, TB], F32, pe_)
                q2_tk = [Tk("q2_%d" % c) for c in range(4)]
                ff_tk = [Tk("ff_%d" % c) for c in range(4)]
                bb_tk = [Tk("bb_%d" % c) for c in range(4)]
                tA = Rot(K, pe_, "tA", [128, TB], F32, 2)
                tB = Rot(K, pe_, "tB", [128, TB], F32, 2)
                qt = K.sb("qt", [128, 4, TB], BF16, pe_)
                kt = K.sb("kt", [128, 4, TB], BF16, pe_)
                k2T = K.sb("k2T", [128, 4, TB], BF16, pe_)
                gs = K.sb("gs", [128, 4, TB], BF16, pe_)
                dec = K.sb("dec", [128, 4, 8], F32, pe_)
                hb_tk = [Tk("hgblk%d" % c) for c in range(4)]
                ktw_tk = [Tk("ktw%d" % c) for c in range(4)]
                k2w_tk = [Tk("k2w%d" % c) for c in range(4)]
                catT = K.sb("catT", [128, 8, TB], BF16, pe_)
                cat_tk = [Tk("cat%d" % c) for c in range(8)]
                NCH = 2
                RING = 5
                qkr_r = Rot(K, pe_, "qkr", [128, 640], BF16, 1)
                vtm = K.sb("vtm", [128, 4, 512], BF16, pe_)
                vtm_tk = [Tk("vtm%d" % i) for i in range(4)]
                vbe = K.sb("vbe", [128, RING, 2, 65], BF16, pe_)
                vb_tk = [Tk("vb%d" % i) for i in range(RING)]
                qT_r = Rot(K, pe_, "qT", [128, 8, 128], BF16, 4)
                kTa = K.sb("kTa", [128, 2, RING * 128], BF16, pe_)
                kT_tk = [Tk("kT%d" % i) for i in range(RING)]
                PT_r = Rot(K, pe_, "PTs", [128, 2, 2, 512], BF16, NCH)
                atm_r = Rot(K, pe_, "atm", [128, 512], BF16, NCH)
                ra_r = Rot(K, pe_, "ra", [128, 10, 32], F32, 1)
                rb_r = Rot(K, pe_, "rb", [128, 10, 32], F32, 1)
                dn_r = Rot(K, pe_, "dn", [128, 2, 8], F32, NCH)
                esink = K.sb("esink", [128, 8], F32, pe_)
                negc = K.sb("negc", [128, 1], F32, pe_)
                SHIFT = 16.0
                st32 = K.sb("st32", [128, 4, 128], F32, pe_)
                st32_tk = [Tk("st32")]
                stb_rot = Rot(K, pe_, "stb", [128, 4, 128], BF16, 2)
                sqr = K.sb("sqr", [128, 512], BF16, pe_)
                sqr_tk = Tk("sqr")
                rsb = K.sb("rsb", [128, 512], F32, pe_)
                rsb_tk = Tk("rsb")
                t1 = K.sb("t1", [128, 512], F32, pe_)
                t1_tk = Tk("t1")

                for (qT_, qT_tk_, _) in qT_r.items:
                    K.op(POOL, lambda: nc.gpsimd.memset(qT_[:], 0.0), writes=[qT_tk_])
                K.op(POOL, lambda: nc.gpsimd.memset(kTa[:], 0.0), writes=kT_tk)
                K.op(POOL, lambda: nc.gpsimd.memset(vbe[:], 1.0), writes=vb_tk)
                K.op(DVE, lambda: nc.vector.memset(negc[:], -SHIFT), writes=[mc_tk])
                K.op(DVE, lambda: nc.vector.tensor_scalar(out=esink[:], in0=sinks[:], scalar1=-SHIFT, scalar2=None, op0=ALU.add),
                     reads=[const_tk, mc_tk], writes=[mc_tk])
                K.op(ACT, lambda: nc.scalar.activation(out=esink[:], in_=esink[:], func=AF.Exp), reads=[mc_tk], writes=[mc_tk])
                K.op(POOL, lambda: nc.gpsimd.memset(st32[:], 0.0), writes=st32_tk)
                stb0, stb0_tk, _ = stb_rot.next()
                K.op(POOL, lambda: nc.gpsimd.memset(stb0[:], 0.0), writes=[stb0_tk])
                state = {"stb": (stb0, stb0_tk)}
                QSC = float(0.5 * 128 ** -0.5)

                def wg(col0):
                    for c0 in (2304, 1792, 1280, 768, 0):
                        if col0 >= c0:
                            return wmi_g[c0]

                def galloc(n):
                    tries = 0
                    while True:
                        r = ps.try_alloc(n)
                        if r is not None:
                            return r
                        tries += 1
                        assert tries < 10000, "PSUM livelock"
                        yield

                def fm_proj(col0, b, tks):
                    pv = ps.f32(b)
                    K.group(PE, [(lambda k=k: nc.tensor.matmul(pv, lhsT=Wmi[:, k, col0:col0 + 128], rhs=h[:, k, :],
                                                             start=(k == 0), stop=(k == 7))) for k in range(8)],
                            reads=h_tks + [wg(col0)], writes=tks)
                    return pv

                def gen_prologue(blk):
                    for tt in range(4):
                        xn, xn_tk = pro.elem(blk * 4 + tt)
                        yield
                        pro.pe(xn, xn_tk, h, h_tks, tt)
                        yield

                def gen_P(blk):
                    for tt in range(4):
                        cs = slice(tt * 128, (tt + 1) * 128)
                        hb_, htk = yield from galloc(1)
                        K.group(PE, [(lambda k=k: nc.tensor.matmul(ps.f32(hb_), lhsT=h[:, k, cs], rhs=Wmi[:, k, 1792:2304],
                                                                 start=(k == 0), stop=(k == 7))) for k in range(8)],
                                reads=h_tks + [wg(1792)], writes=htk)
                        K.op(ACT, lambda: nc.scalar.activation(out=vtm[:, tt, :], in_=ps.f32(hb_), func=AF.Copy), reads=htk, writes=[vtm_tk[tt]])
                        ps.free(hb_)
                        yield
                    for c in range(4):
                        b, tks = yield from galloc(1)
                        pv = fm_proj(1280 + c * 128, b, tks)
                        a_, a_tk, _ = tA.next()
                        K.op(ACT, lambda: nc.scalar.activation(out=a_[:], in_=pv, func=AF.Tanh, scale=0.5), reads=tks, writes=[a_tk])
                        K.op(DVE, lambda: nc.vector.tensor_scalar(out=ff[:, c, :], in0=a_[:], scalar1=c01[:, 1, c:c + 1], scalar2=c01[:, 0, c:c + 1],
                                                                  op0=ALU.mult, op1=ALU.add), reads=[a_tk, par_tk], writes=[ff_tk[c]])
                        ps.free(b)
                        yield
                        b, tks = yield from galloc(1)
                        pv = fm_proj(768 + c * 128, b, tks)
                        a_, a_tk, _ = tA.next()
                        K.op(ACT, lambda: nc.scalar.activation(out=a_[:], in_=pv, func=AF.Tanh, scale=0.5), reads=tks, writes=[a_tk])
                        K.op(DVE, lambda: nc.vector.scalar_tensor_tensor(out=q2[:, c, :], in0=a_[:], scalar=1.0, in1=pv, op0=ALU.add, op1=ALU.mult),
                             reads=[a_tk] + tks, writes=[q2_tk[c]])
                        ps.free(b)
                        yield
                        b, tks = yield from galloc(1)
                        pv = fm_proj(2304 + c * 128, b, tks)
                        a_, a_tk, _ = tA.next()
                        K.op(ACT, lambda: nc.scalar.activation(out=a_[:], in_=pv, func=AF.Tanh, scale=0.5), reads=tks, writes=[a_tk])
                        K.op(DVE, lambda: nc.vector.scalar_tensor_tensor(out=gs[:, c, :], in0=a_[:], scalar=1.0, in1=pv, op0=ALU.add, op1=ALU.mult),
                             reads=[a_tk] + tks, writes=[hb_tk[c]])
                        ps.free(b)
                        yield

                def gen_Pln(cs_):
                    for c in cs_:
                        wtk = [hb_tk[c]]
                        a_, a_tk, _ = tA.next()
                        K.op(ACT, lambda: nc.scalar.activation(out=a_[:], in_=ff[:, c, :], func=AF.Ln), reads=[ff_tk[c]], writes=[a_tk])
                        K.op(DVE, lambda: nc.vector.tensor_tensor_scan(out=bb[:, c, :], data0=rmask[:], data1=a_[:], initial=0.0,
                                                                       op0=ALU.mult, op1=ALU.add),
                             reads=[a_tk, mc_tk], writes=[bb_tk[c]])
                        yield
                        e_, e_tk, _ = tB.next()
                        K.op(ACT, lambda: nc.scalar.activation(out=e_[:], in_=bb[:, c, :], func=AF.Exp), reads=[bb_tk[c]], writes=[e_tk])
                        K.op(DVE, lambda: nc.vector.scalar_tensor_tensor(out=qt[:, c, :], in0=q2[:, c, :], scalar=-QSC, in1=e_[:],
                                                                         op0=ALU.mult, op1=ALU.mult),
                             reads=[q2_tk[c], e_tk], writes=wtk)
                        K.op(DVE, lambda: nc.vector.tensor_copy(out=dec[:, c, :],
                                                                in_=e_[:].rearrange("p (n t) -> p n t", t=64)[:, :, 63]),
                             reads=[e_tk], writes=wtk)
                        yield
                        x_, x_tk, _ = tB.next()
                        K.op(ACT, lambda: nc.scalar.activation(out=x_[:], in_=bb[:, c, :], func=AF.Exp, scale=-1.0), reads=[bb_tk[c]], writes=[x_tk])
                        K.op(DVE, lambda: nc.vector.scalar_tensor_tensor(out=kt[:, c, :], in0=ff[:, c, :], scalar=1.0, in1=x_[:],
                                                                         op0=ALU.subtract, op1=ALU.mult),
                             reads=[ff_tk[c], x_tk], writes=[ktw_tk[c]])
                        yield
                        d_, d_tk, _ = tA.next()
                        b3 = bb[:, c, :].rearrange("p (n t) -> p n t", t=64)
                        K.op(DVE, lambda: nc.vector.tensor_tensor(out=d_[:].rearrange("p (n t) -> p n t", t=64),
                                                                  in0=b3[:, :, 63:64].broadcast_to([128, 8, 64]), in1=b3, op=ALU.subtract),
                             reads=[bb_tk[c]], writes=[d_tk])
                        K.op(ACT, lambda: nc.scalar.activation(out=d_[:], in_=d_[:], func=AF.Exp), reads=[d_tk], writes=[d_tk])
                        K.op(DVE, lambda: nc.vector.scalar_tensor_tensor(out=k2T[:, c, :], in0=ff[:, c, :], scalar=1.0, in1=d_[:],
                                                                         op0=ALU.subtract, op1=ALU.mult),
                             reads=[ff_tk[c], d_tk], writes=[k2w_tk[c]])
                        yield

                pre_done = {}
                o_emitted = {}
                blkst = {}

                def gen_Apre(blk, tts):
                    cs_t, cs_tk, cs_sem = cs_rot.next()
                    K.dma(SP, out=cs_t[:, 0, :], in_=rope_d[:, 0, blk * 128:(blk + 1) * 128], reads=[dtk("rope", 0)], writes=[cs_tk], sem=cs_sem)
                    K.dma(SP, out=cs_t[:, 1, :], in_=rope_d[:, 1, blk * 128:(blk + 1) * 128], reads=[dtk("rope", 0)], writes=[cs_tk], sem=cs_sem)
                    for tt in tts:
                        tile = blk * 4 + tt
                        cs = slice(tt * 128, (tt + 1) * 128)
                        slot = tile % RING
                        qkr, qkr_tk, _ = qkr_r.next()
                        qT, qT_tk, _ = qT_r.items[tt]
                        ra, ra_tk, _ = ra_r.next()
                        rb, rb_tk, _ = rb_r.next()
                        pb, ptk = yield from galloc(2)
                        fns = []
                        for k in range(8):
                            fns.append(lambda k=k: nc.tensor.matmul(ps.f32(pb), lhsT=h[:, k, cs], rhs=Wmi[:, k, 0:512],
                                                                    start=(k == 0), stop=(k == 7)))
                        for k in range(8):
                            fns.append(lambda k=k: nc.tensor.matmul(ps.f32(pb + 1)[:, 0:256], lhsT=h[:, k, cs], rhs=Wmi[:, k, 512:768],
                                                                    start=(k == 0), stop=(k == 7)))
                        K.group(PE, fns, reads=h_tks + [wg(0)], writes=ptk)
                        yield
                        K.op(ACT, lambda: nc.scalar.activation(out=vbe[:, slot, :, 0:64],
                                                               in_=ps.f32(pb + 1)[:, 128:256].rearrange("p (g d) -> p g d", d=64), func=AF.Copy),
                             reads=ptk, writes=[vb_tk[slot]])
                        qk3 = ps.f32(pb, 2)[:, 0:640].rearrange("p (h d) -> p h d", d=64)
                        cb_ = cs_t[:, 0, tt * 32:(tt + 1) * 32].unsqueeze(1).broadcast_to([128, 10, 32])
                        sb_ = cs_t[:, 1, tt * 32:(tt + 1) * 32].unsqueeze(1).broadcast_to([128, 10, 32])
                        o3 = qkr[:].rearrange("p (h d) -> p h d", d=64)
                        K.op(DVE, lambda: nc.vector.tensor_tensor(out=ra[:], in0=qk3[:, :, 0:32], in1=cb_, op=ALU.mult), reads=ptk + [cs_tk], writes=[ra_tk])
                        K.op(DVE, lambda: nc.vector.tensor_tensor(out=rb[:], in0=qk3[:, :, 32:64], in1=sb_, op=ALU.mult), reads=ptk + [cs_tk], writes=[rb_tk])
                        K.op(DVE, lambda: nc.vector.tensor_tensor(out=o3[:, :, 0:32], in0=ra[:], in1=rb[:], op=ALU.subtract),
                             reads=[ra_tk, rb_tk], writes=[qkr_tk])
                        yield
                        K.op(DVE, lambda: nc.vector.tensor_tensor(out=ra[:], in0=qk3[:, :, 32:64], in1=cb_, op=ALU.mult), reads=ptk + [cs_tk], writes=[ra_tk])
                        K.op(DVE, lambda: nc.vector.tensor_tensor(out=rb[:], in0=qk3[:, :, 0:32], in1=sb_, op=ALU.mult), reads=ptk + [cs_tk], writes=[rb_tk])
                        K.op(DVE, lambda: nc.vector.tensor_tensor(out=o3[:, :, 32:64], in0=ra[:], in1=rb[:], op=ALU.add),
                             reads=[ra_tk, rb_tk], writes=[qkr_tk])
                        ps.free(pb, 2)
                        yield
                        tb2, ttk = yield from galloc(1)
                        tv = ps.bf(tb2)
                        fns = [(lambda j=j: nc.tensor.transpose(tv[:, j * 128:(j + 1) * 128], qkr[:, j * 128:(j + 1) * 128], ident[:]))
                               for j in range(5)]
                        K.group(PE, fns, reads=[qkr_tk, const_tk], writes=ttk)
                        qv = tv[:, 0:512].rearrange("p (j t) -> p j t", t=128)
                        qT4 = qT[:].rearrange("p (j two) t -> p j two t", two=2)
                        K.op(DVE, lambda: nc.vector.tensor_copy(out=qT4[0:64, :, 0, :], in_=qv[0:64]), reads=ttk, writes=[qT_tk])
                        K.op(DVE, lambda: nc.vector.tensor_copy(out=qT4[0:64, :, 1, :], in_=qv[64:128]), reads=ttk, writes=[qT_tk])
                        K.op(ACT, lambda: nc.scalar.activation(out=kTa[0:64, 0, slot * 128:(slot + 1) * 128], in_=tv[0:64, 512:640], func=AF.Copy),
                             reads=ttk, writes=[kT_tk[slot]])
                        K.op(ACT, lambda: nc.scalar.activation(out=kTa[0:64, 1, slot * 128:(slot + 1) * 128], in_=tv[64:128, 512:640],
                                                               func=AF.Copy), reads=ttk, writes=[kT_tk[slot]])
                        ps.free(tb2)
                        pre_done[tile] = True
                        yield

                def gen_Amain(blk, tt):
                    if True:
                        tile = blk * 4 + tt
                        cs = slice(tt * 128, (tt + 1) * 128)
                        slot = tile % RING
                        pslot = (tile - 1) % RING
                        tries = 0
                        while not pre_done.get(tile, False):
                            tries += 1
                            assert tries < 10000, "attention main chain never released"
                            yield
                        qT, qT_tk, _ = qT_r.items[tt]
                        PTs, PT_tk, _ = PT_r.next()
                        atm, atm_tk, _ = atm_r.next()
                        dn, dn_tk, _ = dn_r.next()
                        kbs = [(1, slot)] if tile == 0 else [(0, pslot), (1, slot)]
                        var = 0
                        for g in range(2):
                            sb2, stk = yield from galloc(2)
                            fns = []
                            for (kb, sl) in kbs:
                                o = ps.f32(sb2 + kb)
                                fns.append(lambda o=o, g=g, sl=sl: nc.tensor.matmul(
                                    o, lhsT=kTa[:, g, sl * 128:(sl + 1) * 128], rhs=qT[:, g * 4:(g + 1) * 4, :].rearrange("p h t -> p (h t)"),
                                    start=True, stop=False))
                                fns.append(lambda o=o, kb=kb: nc.tensor.matmul(o, lhsT=ident[:], rhs=amask[:, var, kb, :], start=False, stop=True))
                            K.group(PE, fns, reads=[qT_tk, const_tk, mc_tk] + kT_tk, writes=stk)
                            for (kb, sl) in kbs:
                                K.op(ACT, lambda: nc.scalar.activation(out=PTs[:, g, kb, :], in_=ps.f32(sb2 + kb), func=AF.Exp, scale=0.125,
                                                                       bias=negc[:, 0:1]),
                                     reads=[stk[kb], mc_tk], writes=[PT_tk])
                            ps.free(sb2, 2)
                            yield
                        ob2, otk = yield from galloc(2)
                        fns = []
                        for hh in range(8):
                            g, hq = hh // 4, hh % 4
                            for i, (kb, sl) in enumerate(kbs):
                                fns.append(lambda g=g, hq=hq, kb=kb, sl=sl, i=i: nc.tensor.matmul(
                                    ps.f32(ob2 + g)[:, hq * 65:(hq + 1) * 65], lhsT=PTs[:, g, kb, hq * 128:(hq + 1) * 128],
                                    rhs=vbe[:, sl, g, :], start=(i == 0), stop=(i == len(kbs) - 1)))
                        K.group(PE, fns, reads=[PT_tk] + vb_tk, writes=otk)
                        yield
                        O4 = ps.f32(ob2, 2).rearrange("p (b c) -> p b c", c=512)[:, :, 0:260].rearrange("p b (h e) -> p b h e", e=65)
                        K.op(DVE, lambda: nc.vector.tensor_tensor(out=dn[:, 0, :].rearrange("p (b h) -> p b h", h=4), in0=O4[:, :, :, 64],
                                                                  in1=esink[:].rearrange("p (b h) -> p b h", h=4), op=ALU.add),
                             reads=otk + [mc_tk], writes=[dn_tk])
                        K.op(DVE, lambda: nc.vector.reciprocal(out=dn[:, 1, :], in_=dn[:, 0, :]), reads=[dn_tk], writes=[dn_tk])
                        K.op(DVE, lambda: nc.vector.tensor_tensor(out=atm[:].rearrange("p (b h d) -> p b h d", h=4, d=64), in0=O4[:, :, :, 0:64],
                                                                  in1=dn[:, 1, :].rearrange("p (b h) -> p b h", h=4).unsqueeze(3).broadcast_to([128, 2, 4, 64]),
                                                                  op=ALU.mult),
                             reads=otk + [dn_tk], writes=[atm_tk])
                        ps.free(ob2, 2)
                        yield
                        tries = 0
                        while blk > 0 and not o_emitted.get((blk - 1, tt), False):
                            tries += 1
                            assert tries < 10000
                            yield
                        ab_, atk = yield from galloc(1)
                        av = ps.bf(ab_)
                        K.group(PE, [(lambda j=j: nc.tensor.transpose(av[:, j * 128:(j + 1) * 128], atm[:, j * 128:(j + 1) * 128], ident[:]))
                                     for j in range(4)], reads=[atm_tk, const_tk], writes=atk)
                        K.op(ACT, lambda: nc.scalar.activation(out=catT[:, 0:4, cs], in_=av[:, 0:512].rearrange("p (j t) -> p j t", t=128),
                                                               func=AF.Copy), reads=atk, writes=cat_tk[0:4])
                        ps.free(ab_)
                        yield

                def state_step(n, ubank, utk, sdst, sdst_tk):
                    K.op(DVE, lambda: nc.vector.tensor_tensor(out=st32[:], in0=st32[:], in1=dec[:, :, n:n + 1].broadcast_to([128, 4, 128]),
                                                              op=ALU.mult), reads=st32_tk + hb_tk, writes=st32_tk)
                    K.op(DVE, lambda: nc.vector.tensor_tensor(out=st32[:], in0=st32[:], in1=ps.f32(ubank).rearrange("p (c e) -> p c e", e=128),
                                                              op=ALU.add), reads=st32_tk + utk, writes=st32_tk)
                    K.op(DVE, lambda: nc.vector.tensor_copy(out=sdst[:], in_=st32[:]), reads=st32_tk, writes=[sdst_tk])

                k2_r = Rot(K, pe_, "k2AB", [128, 2, 4, 128], BF16, 2)
                scm_r = Rot(K, pe_, "scm2", [128, 4, 128], BF16, 2)

                def gen_H(blk):
                    pre = {}
                    post = {}

                    def h_pre(tt):
                        cs = slice(tt * 128, (tt + 1) * 128)
                        vt = vtm[:, tt, :]
                        vt_tk = vtm_tk[tt]
                        k2, k2_tk, _ = k2_r.next()
                        sc_, sc_tk, _ = scm_r.next()
                        kb_, ktk = yield from galloc(1)
                        kv = ps.bf(kb_)
                        K.group(PE, [(lambda c=c: nc.tensor.transpose(kv[:, c * 128:(c + 1) * 128], k2T[:, c, cs], ident[:])) for c in range(4)],
                                reads=hb_tk + k2w_tk + [const_tk], writes=ktk)
                        kv3 = kv[:, 0:512].rearrange("p (c d) -> p c d", d=128)
                        K.op(DVE, lambda: nc.vector.tensor_scalar(out=k2[:, 0], in0=kv3, scalar1=rowm[:, 0:1], scalar2=None, op0=ALU.mult),
                             reads=ktk + [const_tk], writes=[k2_tk])
                        K.op(DVE, lambda: nc.vector.tensor_scalar(out=k2[:, 1], in0=kv3, scalar1=rowm[:, 1:2], scalar2=None, op0=ALU.mult),
                             reads=ktk + [const_tk], writes=[k2_tk])
                        ps.free(kb_)
                        sb_2, sctk = yield from galloc(1)
                        scv = ps.f32(sb_2)
                        K.group(PE, [(lambda c=c: nc.tensor.matmul(scv[:, c * 128:(c + 1) * 128], lhsT=kt[:, c, cs], rhs=qt[:, c, cs],
                                                                 start=True, stop=True)) for c in range(4)],
                                reads=hb_tk + ktw_tk, writes=sctk)
                        yield
                        K.op(DVE, lambda: nc.vector.tensor_tensor(out=sc_[:], in0=scv.rearrange("p (c t) -> p c t", t=128), in1=hmask[:], op=ALU.mult),
                             reads=sctk + [mc_tk], writes=[sc_tk])
                        ps.free(sb_2)
                        ua, uatk = yield from galloc(1)
                        ub, ubtk = yield from galloc(1)
                        K.group(PE, [(lambda c=c: nc.tensor.matmul(ps.f32(ua)[:, c * 128:(c + 1) * 128], lhsT=k2[:, 0, c, :],
                                                                 rhs=vt[:, c * 128:(c + 1) * 128], start=True, stop=True)) for c in range(4)],
                                reads=[k2_tk, vt_tk], writes=uatk)
                        K.group(PE, [(lambda c=c: nc.tensor.matmul(ps.f32(ub)[:, c * 128:(c + 1) * 128], lhsT=k2[:, 1, c, :],
                                                                 rhs=vt[:, c * 128:(c + 1) * 128], start=True, stop=True)) for c in range(4)],
                                reads=[k2_tk, vt_tk], writes=ubtk)
                        pre[tt] = (sc_, sc_tk, ua, uatk, ub, ubtk)
                        yield

                    def h_chain(tt):
                        vt = vtm[:, tt, :]
                        vt_tk = vtm_tk[tt]
                        sc_, sc_tk, ua, uatk, ub, ubtk = pre[tt]
                        oo, ootk = yield from galloc(1)
                        oov = ps.f32(oo)
                        stA, stA_tk = state["stb"]
                        fns = []
                        for c in range(4):
                            fns.append(lambda c=c: nc.tensor.matmul(oov[:, c * 128:(c + 1) * 128], lhsT=vt[:, c * 128:(c + 1) * 128],
                                                                    rhs=sc_[:, c, :], start=(c == 0), stop=False, skip_group_check=True))
                        for c in range(4):
                            fns.append(lambda c=c: nc.tensor.matmul(oov[:, c * 128:c * 128 + 64], lhsT=stA[:, c, :],
                                                                    rhs=qt[:, c, tt * 128:tt * 128 + 64], start=False, stop=False,
                                                                    skip_group_check=True))
                        K.group(PE, fns, reads=[vt_tk, sc_tk, stA_tk] + hb_tk, writes=ootk)
                        stB, stB_tk, _ = stb_rot.next()
                        state_step(2 * tt, ua, uatk, stB, stB_tk)
                        ps.free(ua)
                        yield
                        K.group(PE, [(lambda c=c: nc.tensor.matmul(oov[:, c * 128 + 64:(c + 1) * 128], lhsT=stB[:, c, :],
                                                                 rhs=qt[:, c, tt * 128 + 64:(tt + 1) * 128], start=False, stop=True,
                                                                 skip_group_check=True)) for c in range(4)],
                                reads=[stB_tk] + hb_tk, writes=ootk)
                        stC, stC_tk, _ = stb_rot.next()
                        state_step(2 * tt + 1, ub, ubtk, stC, stC_tk)
                        ps.free(ub)
                        state["stb"] = (stC, stC_tk)
                        post[tt] = (oo, ootk)
                        yield

                    def h_post(tt):
                        cs = slice(tt * 128, (tt + 1) * 128)
                        oo, ootk = post[tt]
                        oov = ps.f32(oo)
                        tries = 0
                        while blk > 0 and not o_emitted.get((blk - 1, tt), False):
                            tries += 1
                            assert tries < 100000
                            yield
                        K.op(ACT, lambda: nc.scalar.activation(out=sqr[:], in_=oov, func=AF.Square), reads=ootk, writes=[sqr_tk])
                        nb_, ntk = yield from galloc(1)
                        K.group(PE, [lambda: nc.tensor.matmul(ps.f32(nb_), lhsT=onesb[:], rhs=sqr[:], start=True, stop=True)],
                                reads=[sqr_tk, const_tk], writes=ntk)
                        yield
                        K.op(ACT, lambda: nc.scalar.activation(out=rsb[:], in_=ps.f32(nb_), func=AF.Ln, scale=1.0 / 128, bias=eps_t[:, 0:1]),
                             reads=ntk + [const2_tk], writes=[rsb_tk])
                        ps.free(nb_)
                        K.op(ACT, lambda: nc.scalar.activation(out=rsb[:], in_=rsb[:], func=AF.Exp, scale=-0.5), reads=[rsb_tk], writes=[rsb_tk])
                        K.op(DVE, lambda: nc.vector.scalar_tensor_tensor(out=t1[:], in0=oov, scalar=gnh[:, 0:1], in1=rsb[:], op0=ALU.mult, op1=ALU.mult),
                             reads=ootk + [rsb_tk, par_tk], writes=[t1_tk])
                        ps.free(oo)
                        K.op(DVE, lambda: nc.vector.tensor_tensor(out=catT[:, 4:8, cs], in0=t1[:].rearrange("p (c t) -> p c t", t=128),
                                                                  in1=gs[:, :, cs], op=ALU.mult),
                             reads=[t1_tk] + hb_tk, writes=cat_tk[4:8])
                        yield

                    yield from h_pre(0)
                    for tt in range(4):
                        yield from h_chain(tt)
                        if tt + 1 < 4:
                            yield from h_pre(tt + 1)
                        yield from h_post(tt)

                def gen_O(blk):
                    for tt in range(4):
                        yb, ytk = yield from galloc(2)
                        fns = []
                        for n in range(2):
                            o = ps.f32(yb + n)
                            for kc in range(8):
                                fns.append(lambda o=o, n=n, kc=kc: nc.tensor.matmul(
                                    o, lhsT=catT[:, kc, tt * 128:(tt + 1) * 128], rhs=Wmo[:, kc, n * 512:(n + 1) * 512],
                                    start=(kc == 0), stop=(kc == 7)))
                        K.group(PE, fns, reads=cat_tk + [wmo_tk], writes=ytk)
                        o_emitted[(blk, tt)] = True
                        yield
                        epi.run(blk * 4 + tt, yb, ytk)
                        yield

                interleave(gen_prologue(0))
                for blk in range(NB):
                    fl = {"ptanh": False, "pln": False}

                    def gen_Pall(blk=blk, fl=fl):
                        yield from gen_P(blk)
                        fl["ptanh"] = True
                        live = [gen_Pln([0, 2]), gen_Pln([1, 3])]
                        while live:
                            for g in list(live):
                                try:
                                    next(g)
                                except StopIteration:
                                    live.remove(g)
                            yield
                        fl["pln"] = True

                    def gated(cond, gen):
                        tries = 0
                        while not cond():
                            tries += 1
                            assert tries < 100000, "gated chain never released"
                            yield
                        yield from gen

                    def gen_Aseq(tts, blk=blk):
                        for tt in tts:
                            yield from gen_Amain(blk, tt)

                    gens = [gen_Pall(), gen_Apre(blk, [0, 1, 2, 3]), gen_Aseq([0, 2]), gen_Aseq([1, 3]),
                            gated(lambda fl=fl: fl["pln"], gen_H(blk))]
                    if blk > 0:
                        gens.append(gated(lambda fl=fl: fl["ptanh"], gen_O(blk - 1)))
                    if blk + 1 < NB:
                        gens.append(gated(lambda fl=fl, blk=blk: fl["ptanh"] and all(pre_done.get(blk * 4 + t, False) for t in range(4)),
                                          gen_prologue(blk + 1)))
                    interleave(*gens)
                interleave(gen_O(NB - 1))
                K.barrier()

        dsts = {1: out_d if stop_after == 1 else x1_d, 2: out_d if stop_after == 2 else x2_d, 3: out_d}
        names = {1: "out" if stop_after == 1 else "x1", 2: "out" if stop_after == 2 else "x2", 3: "out"}
        if stop_after == 0:
            dbg = K.sb("dbg", [128, D], F32)
            dbg_tk = Tk("dbg")
            dsem = K.newsem("dbg")
            K.op(DVE, lambda: nc.vector.tensor_copy(out=dbg[:], in_=G[:, 0, :]), reads=[par_tk], writes=[dbg_tk])
            K.dma(SP, out=out_d[0:128, :], in_=dbg[:], reads=[dbg_tk], writes=[], sem=dsem)
            K.op(DVE, lambda: nc.vector.tensor_copy(out=dbg[:, 0:24], in_=a_col[:].rearrange("p a b -> p (a b)")), reads=[par_tk, dbg_tk], writes=[dbg_tk])
            K.op(DVE, lambda: nc.vector.tensor_copy(out=dbg[:, 24:48], in_=sh_col[:].rearrange("p a b -> p (a b)")), reads=[par_tk, dbg_tk], writes=[dbg_tk])
            K.op(DVE, lambda: nc.vector.tensor_copy(out=dbg[:, 48:52], in_=lb[:]), reads=[par_tk, dbg_tk], writes=[dbg_tk])
            K.dma(SP, out=out_d[128:256, :], in_=dbg[:], reads=[dbg_tk], writes=[], sem=dsem)
            K.barrier()
            return nc
        ffn_phase(0, 0, x_d, "x", dsts[1], names[1])
        if stop_after >= 2:
            mixer_phase(x1_d, "x1", dsts[2], names[2])
        if stop_after >= 3:
            ffn_phase(1, 2, x2_d, "x2", out_d, "out")
        K.barrier()
        print("instructions:", K.n_inst, "sems:", len(K.sems), "engine sem incs:", K.n_inc)
        new_plan = {sm.idx: sorted(sm.waited) for sm in K.sems if sm.idx in K.eng_sem_idx}
    if want_plan:
        return new_plan
    return nc


def build_thin(stop_after=3):
    plan = build(stop_after, plan=None, want_plan=True)
    return build(stop_after, plan=plan)


def _consts():
    bf = ml_dtypes.bfloat16
    ident = np.eye(128, dtype=np.float32).astype(bf)
    onesb = np.ones((128, 128), np.float32).astype(bf)
    NEG = -1e30
    am = np.zeros((128, 2, 256), np.float32)
    am[0:64, 0, 192:256] = NEG
    am[64:128, 0, 0:64] = NEG
    am[0:64, 1, 64:128] = NEG
    am[64:128, 1, 128:192] = NEG
    s = np.arange(128)[:, None]
    t = np.arange(128)[None, :]
    hm = ((s // 64 == t // 64) & (s <= t)).astype(np.float32)
    hmask = np.broadcast_to(hm[:, None, :], (128, 4, 128)).copy()
    rmask = np.ones((128, 512), np.float32)
    rmask[:, ::64] = 0.0
    inv_freq = (1.0 / (np.float32(10000.0) ** (np.arange(0, 64, 2, dtype=np.float32) / np.float32(64)))).astype(np.float32)
    invf = np.broadcast_to(inv_freq[None, :], (128, 32)).copy()
    rowm = np.zeros((128, 2), np.float32)
    rowm[0:64, 0] = 1.0
    rowm[64:128, 1] = 1.0
    amT = np.zeros((128, 2, 2, 4, 128), np.float32)
    for v in range(2):
        for kb in range(2):
            amT[:, v, kb, :, :] = am[:, v, kb * 128:(kb + 1) * 128].T[:, None, :]
    amT = amT.reshape(128, 2, 2, 512)
    return dict(ident=ident, onesb=onesb, amask=amT.astype(bf), hmask=hmask.astype(bf), rmask=rmask, invf=invf, rowm=rowm)


def make_in_maps(x, c, positions, w_cond, b_cond, norm_pre, norm_post, ffn_w_in, ffn_w_out,
                 w_mix_in, w_mix_out, attn_sinks, hgrn_lb_logits, hgrn_gnorm):
    f = np.float32
    cst = _consts()
    shared = dict(
        w_cond=np.ascontiguousarray(w_cond[0], f), b_cond=np.ascontiguousarray(np.broadcast_to(np.asarray(b_cond[0], f)[None, :], (128, 9 * D))),
        npre=np.ascontiguousarray(np.asarray(norm_pre[0], f).reshape(3, 8, 128).transpose(2, 0, 1)),
        npost=np.ascontiguousarray(np.broadcast_to(np.asarray(norm_post[0], f)[None], (128, 3, D))),
        w_in=np.ascontiguousarray(ffn_w_in[0], f), w_out=np.ascontiguousarray(ffn_w_out[0], f),
        w_mi=np.ascontiguousarray(w_mix_in[0], f), w_mo=np.ascontiguousarray(w_mix_out[0], f),
        sinks=np.ascontiguousarray(np.broadcast_to(np.asarray(attn_sinks[0], f)[None], (128, 8))),
        lbl=np.ascontiguousarray(np.asarray(hgrn_lb_logits, f).reshape(2, 4, 128).transpose(2, 0, 1)),
        gn=np.ascontiguousarray(np.asarray(hgrn_gnorm[0], f)[:, None]),
        **cst)
    maps = []
    for b in range(8):
        m = dict(shared)
        m["x"] = np.ascontiguousarray(x[b], f)
        m["ccol"] = np.ascontiguousarray(np.asarray(c[b], f).reshape(8, 128).T)
        m["pos"] = np.ascontiguousarray(np.asarray(positions[b], np.int32).reshape(NT, 128).T)
        maps.append(m)
    return maps


def kernel(**inputs):
    nc = build_thin(3)
    maps = make_in_maps(**inputs)
    res = run_bass_kernel_spmd(nc, maps, core_ids=list(range(8)))
    return np.stack([np.asarray(r["out"], np.float32) for r in res.results], axis=0)
```
